# Optimizing a Trainium2 kernel written in Bass

```python
import math
import jax
import jax.numpy as jnp
from jax import lax
import numpy as np


D_MODEL = 1024
BATCH = 16
SEQ = 4096
DEPTH = 2

CTX_LEN = 256
GRID_W = 64
EPS = 1e-6
HEAD_DIM = 64
MIX_WIDTH = D_MODEL
A_HEADS = (MIX_WIDTH // 2) // HEAD_DIM
A_DIM = HEAD_DIM
A_WIDTH = A_HEADS * A_DIM
A_IN = 4 * A_WIDTH + 4 * A_HEADS
B_HEADS = (MIX_WIDTH // 2) // HEAD_DIM
B_VDIM = HEAD_DIM
B_QKDIM = HEAD_DIM // 2
B_WIDTH = B_HEADS * B_VDIM
B_IN = 2 * (2 * B_HEADS * B_QKDIM) + B_WIDTH
C_HEADS = (MIX_WIDTH // 2) // HEAD_DIM
C_DIM = HEAD_DIM
C_WIDTH = C_HEADS * C_DIM
C_DECAY_LORA = 64
C_AAA_LORA = 64
C_GATE_LORA = 128
C_IN = 3 * C_WIDTH + 2 * C_DECAY_LORA + 2 * C_AAA_LORA + C_GATE_LORA
RWKV_LN_EPS = 64e-5
D_HEADS = 4
D_VDIM = (MIX_WIDTH // 2) // D_HEADS
D_KDIM = D_VDIM // 2
D_WIDTH = D_HEADS * D_VDIM
D_GATE_LORA = 16
GLA_TAU = 16.0
D_IN = 2 * D_HEADS * D_KDIM + 2 * D_WIDTH + 2 * D_GATE_LORA

EVEN_IN = A_IN + B_IN
ODD_IN = C_IN + D_IN
MLP_HIDDEN = 4 * D_MODEL
MLSTM_CHUNK = 64
GLA_CHUNK = 64
Q_BLOCK = 128
ROPE_BASE = 10000.0

kernel_name = 'hybrid_mlstm_diffattn_rwkv7_gla_dit_trunk'


def rms_norm(x, g, eps=EPS):
    xf = x.astype(jnp.float32)
    y = xf * lax.rsqrt(jnp.mean(xf * xf, axis=-1, keepdims=True) + eps)
    return (y * g.astype(jnp.float32)).astype(x.dtype)


def modulate(h, shift, scale):
    return h * (1 + scale) + shift


def split_cols(u, sizes):
    return jnp.split(u, [int(s) for s in np.cumsum(sizes)[:-1]], axis=-1)


def to_heads(a, n_heads):
    b, t, _ = a.shape
    return a.reshape(b, t, n_heads, -1).transpose(0, 2, 1, 3)


def from_heads(a):
    b, h, t, d = a.shape
    return a.transpose(0, 2, 1, 3).reshape(b, t, h * d)


def to_chunks(a, size):
    b, h, t = a.shape[:3]
    return jnp.moveaxis(a.reshape((b, h, t // size, size) + a.shape[3:]), 2, 0)


def from_chunks(a):
    a = jnp.moveaxis(a, 0, 2)
    return a.reshape(a.shape[:2] + (-1,) + a.shape[4:])


def axial_rope(rows, dim):
    n = dim // 4
    inv = ROPE_BASE ** (-jnp.arange(n, dtype=jnp.float32) / n)
    row = jnp.repeat(jnp.arange(rows, dtype=jnp.float32), GRID_W)
    col = jnp.broadcast_to(jnp.arange(GRID_W, dtype=jnp.float32), (rows, GRID_W)).reshape(-1)
    ang = jnp.concatenate([row[:, None] * inv, col[:, None] * inv], axis=-1)
    return jnp.cos(ang), jnp.sin(ang)


def apply_rope(x, cos, sin):
    shape = (cos.shape[0],) + (1,) * (x.ndim - 3) + (cos.shape[1],)
    c, s = cos.reshape(shape), sin.reshape(shape)
    x1, x2 = x[..., 0::2].astype(jnp.float32), x[..., 1::2].astype(jnp.float32)
    return jnp.stack([x1 * c - x2 * s, x1 * s + x2 * c], axis=-1).reshape(x.shape).astype(x.dtype)


def centred_shift(u, mu):
    prev = jnp.pad(u[:, :-1], ((0, 0), (1, 0), (0, 0)))
    nxt = jnp.pad(u[:, 1:], ((0, 0), (0, 1), (0, 0)))
    return u + mu[0] * (prev - u) + mu[1] * (nxt - u)


def head_layer_norm(y, g, b, eps=RWKV_LN_EPS):
    mu = jnp.mean(y, axis=-1, keepdims=True)
    var = jnp.mean(jnp.square(y - mu), axis=-1, keepdims=True)
    h = y.shape[1]
    return (y - mu) * lax.rsqrt(var + eps) * g.reshape(h, 1, -1) + b.reshape(h, 1, -1)


def sq_relu_mlp(h, w1, w2):
    return jnp.square(jax.nn.relu(h @ w1)) @ w2


def bidirectional(scan_fn, lat_f, ctx_f, lat_b, ctx_b, init):
    hc_f, st_f = scan_fn(*ctx_f, init)
    hx_f, _ = scan_fn(*lat_f, st_f)
    flip = lambda ts: tuple(jnp.flip(t, 2) for t in ts)
    hc_b, st_b = scan_fn(*flip(ctx_b), init)
    hx_b, _ = scan_fn(*flip(lat_b), st_b)
    return hx_f + jnp.flip(hx_b, 2), hc_f + jnp.flip(hc_b, 2)


def mlstm_scan(q, k, v, log_i, log_f, state):
    L = MLSTM_CHUNK
    lower = jnp.tril(jnp.ones((L, L), dtype=bool))

    def step(carry, inp):
        C, n, m = carry
        qc, kc, vc, ic, fc = inp
        b = jnp.cumsum(fc, axis=-1)
        dlog = jnp.where(lower, b[..., :, None] - b[..., None, :] + ic[..., None, :], -jnp.inf)
        inter = b + m[..., None]
        m_row = jnp.maximum(inter, jnp.max(dlog, axis=-1))
        s = jnp.einsum('bhjd,bhsd->bhjs', qc, kc) * jnp.exp(dlog - m_row[..., None])
        w_inter = jnp.exp(inter - m_row)
        num = jnp.einsum('bhjs,bhse->bhje', s, vc) + w_inter[..., None] * jnp.einsum('bhed,bhjd->bhje', C, qc)
        den = jnp.sum(s, axis=-1) + w_inter * jnp.einsum('bhjd,bhd->bhj', qc, n)
        h = num / jnp.maximum(jnp.abs(den), jnp.exp(-m_row))[..., None]
        b_last = b[..., -1]
        upd = b_last[..., None] - b + ic
        m_new = jnp.maximum(b_last + m, jnp.max(upd, axis=-1))
        w_old = jnp.exp(b_last + m - m_new)
        w_upd = jnp.exp(upd - m_new[..., None])
        C = w_old[..., None, None] * C + jnp.einsum('bhs,bhse,bhsd->bhed', w_upd, vc, kc)
        n = w_old[..., None] * n + jnp.einsum('bhs,bhsd->bhd', w_upd, kc)
        return (C, n, m_new), h

    xs = tuple(to_chunks(a, L) for a in (q, k, v, log_i, log_f))
    state, h = lax.scan(step, state, xs)
    return from_chunks(h), state


def gla_scan(q, k, v, log_a, state):
    L = GLA_CHUNK
    lower = jnp.tril(jnp.ones((L, L), dtype=bool))[:, :, None]

    def step(S, inp):
        qc, kc, vc, gc = inp
        b = jnp.cumsum(gc, axis=2)
        decay = jnp.exp(jnp.where(lower, b[:, :, :, None, :] - b[:, :, None, :, :], -jnp.inf))
        A = jnp.einsum('bhjd,bhsd,bhjsd->bhjs', qc, kc, decay)
        o = jnp.einsum('bhjs,bhsv->bhjv', A, vc) + jnp.einsum('bhjd,bhdv->bhjv', qc * jnp.exp(b), S)
        b_last = b[:, :, -1:, :]
        S = jnp.exp(b_last[:, :, 0, :])[..., None] * S + jnp.einsum('bhsd,bhsv->bhdv', kc * jnp.exp(b_last - b), vc)
        return S, o

    xs = tuple(to_chunks(a, L) for a in (q, k, v, log_a))
    state, o = lax.scan(step, state, xs)
    return from_chunks(o), state


def rwkv7_scan(r, decay, k, v, kk, a, state):
    def step(S, inp):
        r_t, w_t, k_t, v_t, kk_t, a_t = inp
        sk = jnp.einsum('bhei,bhi->bhe', S, kk_t)
        S = S * w_t[:, :, None, :] - sk[..., None] * (kk_t * a_t)[:, :, None, :] + v_t[..., None] * k_t[:, :, None, :]
        return S, jnp.einsum('bhei,bhi->bhe', S, r_t)

    xs = tuple(jnp.moveaxis(t, 2, 0) for t in (r, decay, k, v, kk, a))
    state, y = lax.scan(step, state, xs)
    return jnp.moveaxis(y, 0, 2), state


def mlstm_mixer(u_x, u_c, gate_b, norm_g, need_ctx):
    f32 = jnp.float32

    def prep(u):
        b_, t_ = u.shape[:2]
        q, k, v, o, g = split_cols(u.astype(f32), [A_WIDTH] * 4 + [4 * A_HEADS])
        g = (g.reshape(b_, t_, 4, A_HEADS) + gate_b.astype(f32)).transpose(2, 0, 3, 1)
        q, k, v = to_heads(q, A_HEADS), to_heads(k, A_HEADS) * A_DIM ** -0.5, to_heads(v, A_HEADS)
        fwd = (q, k, v, g[0], jax.nn.log_sigmoid(g[1]))
        bwd = (q, k, v, g[2], jax.nn.log_sigmoid(g[3]))
        return fwd, bwd, jax.nn.sigmoid(o)

    fx, bx, ox = prep(u_x)
    fc, bc, oc = prep(u_c)
    b_ = u_x.shape[0]
    init = (jnp.zeros((b_, A_HEADS, A_DIM, A_DIM), f32), jnp.zeros((b_, A_HEADS, A_DIM), f32),
            jnp.zeros((b_, A_HEADS), f32))
    hx, hc = bidirectional(mlstm_scan, fx, fc, bx, bc, init)
    gain = norm_g.reshape(A_HEADS, 1, A_DIM)
    post = lambda h, o: (from_heads(rms_norm(h, gain)) * o).astype(u_x.dtype)
    return post(hx, ox), (post(hc, oc) if need_ctx else None)


def diff_attn_mixer(u_x, u_c, qk_g, lam_p, subln_g, cos, sin, lam_init, need_ctx):
    f32 = jnp.float32

    def prep(u):
        b_, t_ = u.shape[:2]
        q, k, v = split_cols(u, [2 * B_HEADS * B_QKDIM] * 2 + [B_WIDTH])
        q = rms_norm(q.reshape(b_, t_, B_HEADS, 2, B_QKDIM), qk_g[0])
        k = rms_norm(k.reshape(b_, t_, B_HEADS, 2, B_QKDIM), qk_g[1])
        return q, k, v.reshape(b_, t_, B_HEADS, B_VDIM)

    qx, kx, vx = prep(u_x)
    qc, kc, vc = prep(u_c)
    qx, kx = apply_rope(qx, cos, sin), apply_rope(kx, cos, sin)
    lam_p = lam_p.astype(f32)
    lam = jnp.exp(jnp.sum(lam_p[0] * lam_p[1])) - jnp.exp(jnp.sum(lam_p[2] * lam_p[3])) + lam_init
    scale = B_QKDIM ** -0.5

    def attend(q, k, v):
        s = jnp.einsum('bqhcd,bkhcd->bhcqk', q, k).astype(f32) * scale
        p = jax.nn.softmax(s, axis=-1)
        w = p[:, :, 0] - lam * p[:, :, 1]
        return jnp.einsum('bhqk,bkhd->bqhd', w.astype(v.dtype), v)

    k_all = jnp.concatenate([kx, kc], axis=1)
    v_all = jnp.concatenate([vx, vc], axis=1)
    b_, t_ = u_x.shape[:2]
    q_blocks = jnp.swapaxes(qx.reshape(b_, t_ // Q_BLOCK, Q_BLOCK, B_HEADS, 2, B_QKDIM), 0, 1)
    ox = lax.map(lambda qb: attend(qb, k_all, v_all), q_blocks)
    ox = jnp.swapaxes(ox, 0, 1).reshape(b_, t_, B_HEADS, B_VDIM)
    post = lambda o: (rms_norm(o, subln_g) * (1.0 - lam_init)).reshape(o.shape[0], o.shape[1], B_WIDTH)
    return post(ox), (post(attend(qc, kc, vc)) if need_ctx else None)


def rwkv7_mixer(u_x, u_c, mu, w0, w2, a0, a2, g2, kvec, ln, need_ctx):
    f32 = jnp.float32
    hv = lambda p: p.reshape(C_HEADS, C_DIM)

    def prep(u):
        b_, t_ = u.shape[:2]
        u = centred_shift(u.astype(f32), mu.astype(f32))
        r, k, v, wd_f, wd_b, ad_f, ad_b, gd = split_cols(
            u, [C_WIDTH] * 3 + [C_DECAY_LORA] * 2 + [C_AAA_LORA] * 2 + [C_GATE_LORA])
        hd = lambda a: a.reshape(b_, t_, C_HEADS, C_DIM)
        r, k, v = hd(r), hd(k), hd(v)
        kk = k * hv(kvec[0])
        kk = kk * lax.rsqrt(jnp.maximum(jnp.sum(kk * kk, axis=-1, keepdims=True), 1e-24))
        tr = lambda z: z.transpose(0, 2, 1, 3)
        scans, bonus = [], []
        for d, (wd, ad) in enumerate(((wd_f, ad_f), (wd_b, ad_b))):
            wlog = -jax.nn.softplus(-(w0[d] + jnp.tanh(wd) @ w2[d])) - 0.5
            decay = hd(jnp.exp(-jnp.exp(wlog)))
            a = hd(jax.nn.sigmoid(a0[d] + ad @ a2[d]))
            kd = k * (1 + (a - 1) * hv(kvec[1]))
            bonus.append(jnp.sum(r * kd * hv(kvec[2]), axis=-1, keepdims=True) * v)
            scans.append(tuple(tr(z) for z in (r, decay, kd, v, kk, a)))
        g = jax.nn.sigmoid(gd) @ g2
        return scans[0], scans[1], (bonus[0] + bonus[1]).reshape(b_, t_, C_WIDTH), g

    fx, bx, bonx, gx = prep(u_x)
    fc, bc, bonc, gc = prep(u_c)
    init = jnp.zeros((u_x.shape[0], C_HEADS, C_DIM, C_DIM), f32)
    yx, yc = bidirectional(rwkv7_scan, fx, fc, bx, bc, init)
    post = lambda y, bon, g: ((from_heads(head_layer_norm(y, ln[0], ln[1])) + bon) * g).astype(u_x.dtype)
    return post(yx, bonx, gx), (post(yc, bonc, gc) if need_ctx else None)


def gla_mixer(u_x, u_c, gw2, gb, norm_g, need_ctx):
    f32 = jnp.float32

    def prep(u):
        q, k, v, g, gd_f, gd_b = split_cols(
            u.astype(f32), [D_HEADS * D_KDIM] * 2 + [D_WIDTH] * 2 + [D_GATE_LORA] * 2)
        q = to_heads(q, D_HEADS) * D_KDIM ** -0.5
        k, v = to_heads(k, D_HEADS), to_heads(v, D_HEADS)
        la = [to_heads(jax.nn.log_sigmoid(gd @ gw2[d] + gb[d]) / GLA_TAU, D_HEADS)
              for d, gd in enumerate((gd_f, gd_b))]
        return (q, k, v, la[0]), (q, k, v, la[1]), g

    fx, bx, gx = prep(u_x)
    fc, bc, gc = prep(u_c)
    init = jnp.zeros((u_x.shape[0], D_HEADS, D_KDIM, D_VDIM), f32)
    ox, oc = bidirectional(gla_scan, fx, fc, bx, bc, init)
    post = lambda o, g: (from_heads(rms_norm(o, norm_g)) * jax.nn.silu(g)).astype(u_x.dtype)
    return post(ox, gx), (post(oc, gc) if need_ctx else None)


def setup_inputs(seed: int = 0) -> dict:
    key = jax.random.key(seed)
    ks = iter(jax.random.split(key, 32))
    f32 = jnp.float32
    nrm = lambda shape, scale: jax.random.normal(next(ks), shape, f32) * scale
    d = D_MODEL
    n_even, n_odd = (DEPTH + 1) // 2, DEPTH // 2
    fgate = jnp.linspace(3.0, 6.0, A_HEADS, dtype=f32)
    zeros_h = jnp.zeros((A_HEADS,), f32)
    gate_base = jnp.stack([zeros_h, fgate, zeros_h, fgate])
    head_pos = (jnp.arange(C_WIDTH) % C_DIM).astype(f32) / (C_DIM - 1)
    return {
        'x': nrm((BATCH, SEQ, d), 1.0),
        'c': nrm((BATCH, d), 1.0),
        'ctx': nrm((BATCH, CTX_LEN, d), 1.0),
        'c_ctx': nrm((d,), 1.0),
        'ada_w': nrm((DEPTH, d, 6 * d), 0.5 * d ** -0.5),
        'ada_b': nrm((DEPTH, 6 * d), 0.02),
        'norm_g': 1.0 + nrm((DEPTH, 2, d), 0.02),
        'w_in_even': nrm((n_even, d, EVEN_IN), d ** -0.5),
        'w_in_odd': nrm((n_odd, d, ODD_IN), d ** -0.5),
        'w_out': nrm((DEPTH, MIX_WIDTH, d), MIX_WIDTH ** -0.5),
        'w_mlp_in': nrm((DEPTH, d, MLP_HIDDEN), d ** -0.5),
        'w_mlp_out': nrm((DEPTH, MLP_HIDDEN, d), MLP_HIDDEN ** -0.5),
        'mlstm_gate_b': gate_base + nrm((n_even, 4, A_HEADS), 0.1),
        'mlstm_norm_g': 1.0 + nrm((n_even, A_WIDTH), 0.02),
        'diff_qk_g': 1.0 + nrm((n_even, 2, B_QKDIM), 0.02),
        'diff_lam': nrm((n_even, 4, B_QKDIM), 0.1),
        'diff_subln_g': 1.0 + nrm((n_even, B_VDIM), 0.02),
        'rwkv_mu': jax.random.uniform(next(ks), (n_odd, 2, C_IN), f32, 0.0, 0.5),
        'rwkv_w0': -6.0 + 7.0 * head_pos + nrm((n_odd, 2, C_WIDTH), 0.1),
        'rwkv_w2': nrm((n_odd, 2, C_DECAY_LORA, C_WIDTH), 0.5 * C_DECAY_LORA ** -0.5),
        'rwkv_a0': nrm((n_odd, 2, C_WIDTH), 0.1),
        'rwkv_a2': nrm((n_odd, 2, C_AAA_LORA, C_WIDTH), 0.5 * C_AAA_LORA ** -0.5),
        'rwkv_g2': nrm((n_odd, C_GATE_LORA, C_WIDTH), C_GATE_LORA ** -0.5),
        'rwkv_kvec': jnp.array([0.85, 1.0, 0.0], f32)[:, None] + nrm((n_odd, 3, C_WIDTH), 1.0) * jnp.array([0.02, 0.02, 0.1], f32)[:, None],
        'rwkv_ln': jnp.array([1.0, 0.0], f32)[:, None] + nrm((n_odd, 2, C_WIDTH), 0.02),
        'gla_gate_w2': nrm((n_odd, 2, D_GATE_LORA, D_HEADS * D_KDIM), D_GATE_LORA ** -0.5),
        'gla_gate_b': nrm((n_odd, 2, D_HEADS * D_KDIM), 0.1),
        'gla_norm_g': 1.0 + nrm((n_odd, D_VDIM), 0.02),
    }


def reference(x, c, ctx, c_ctx, ada_w, ada_b, norm_g, w_in_even, w_in_odd, w_out, w_mlp_in, w_mlp_out,
              mlstm_gate_b, mlstm_norm_g, diff_qk_g, diff_lam, diff_subln_g,
              rwkv_mu, rwkv_w0, rwkv_w2, rwkv_a0, rwkv_a2, rwkv_g2, rwkv_kvec, rwkv_ln,
              gla_gate_w2, gla_gate_b, gla_norm_g):
    seq = x.shape[1]
    rows = seq // GRID_W
    cos, sin = axial_rope(rows, B_QKDIM)
    s_lat = jax.nn.silu(c)
    s_ctx = jax.nn.silu(c_ctx)
    for layer in range(DEPTH):
        need_ctx = layer < DEPTH - 1
        mx = jnp.split((s_lat @ ada_w[layer] + ada_b[layer])[:, None, :], 6, axis=-1)
        mc = jnp.split(s_ctx @ ada_w[layer] + ada_b[layer], 6, axis=-1)
        hx = modulate(rms_norm(x, norm_g[layer, 0]), mx[0], mx[1])
        hc = modulate(rms_norm(ctx, norm_g[layer, 0]), mc[0], mc[1])
        if layer % 2 == 0:
            e = layer // 2
            uax, ubx = split_cols(hx @ w_in_even[e], [A_IN, B_IN])
            uac, ubc = split_cols(hc @ w_in_even[e], [A_IN, B_IN])
            ax, ac = mlstm_mixer(uax, uac, mlstm_gate_b[e], mlstm_norm_g[e], need_ctx)
            lam_init = 0.8 - 0.6 * math.exp(-0.3 * layer)
            bx, bc = diff_attn_mixer(ubx, ubc, diff_qk_g[e], diff_lam[e], diff_subln_g[e], cos, sin, lam_init, need_ctx)
            ox = jnp.concatenate([ax, bx], axis=-1)
            oc = jnp.concatenate([ac, bc], axis=-1) if need_ctx else None
        else:
            o_ = layer // 2
            ucx, udx = split_cols(hx @ w_in_odd[o_], [C_IN, D_IN])
            ucc, udc = split_cols(hc @ w_in_odd[o_], [C_IN, D_IN])
            cx, cc = rwkv7_mixer(ucx, ucc, rwkv_mu[o_], rwkv_w0[o_], rwkv_w2[o_], rwkv_a0[o_], rwkv_a2[o_],
                                 rwkv_g2[o_], rwkv_kvec[o_], rwkv_ln[o_], need_ctx)
            dx, dc = gla_mixer(udx, udc, gla_gate_w2[o_], gla_gate_b[o_], gla_norm_g[o_], need_ctx)
            ox = jnp.concatenate([cx, dx], axis=-1)
            oc = jnp.concatenate([cc, dc], axis=-1) if need_ctx else None
        x = x + mx[2] * (ox @ w_out[layer])
        x = x + mx[5] * sq_relu_mlp(modulate(rms_norm(x, norm_g[layer, 1]), mx[3], mx[4]),
                                    w_mlp_in[layer], w_mlp_out[layer])
        if need_ctx:
            ctx = ctx + mc[2] * (oc @ w_out[layer])
            ctx = ctx + mc[5] * sq_relu_mlp(modulate(rms_norm(ctx, norm_g[layer, 1]), mc[3], mc[4]),
                                            w_mlp_in[layer], w_mlp_out[layer])
    return x
```

```python
import contextlib
import numpy as np
import concourse.bass as bass
import concourse.mybir as mybir

F32 = mybir.dt.float32
F32R = mybir.dt.float32r
ALU = mybir.AluOpType
AF = mybir.ActivationFunctionType
AX = mybir.AxisListType

ENGS = ("pe", "act", "dve", "pool", "sp")
N_DMA_SEMS = 24


class Op:
    __slots__ = ("eng", "fn", "deps", "signal", "count", "is_dma", "dsem", "dcount", "idx", "rg")


class Prog:
    def __init__(self, nc):
        self.nc = nc
        self.streams = {e: [] for e in ENGS}
        self.last_w = {}
        self.readers = {}
        self.stack = contextlib.ExitStack()
        self.n_dma = {}
        self.pstack = None
        self.bar = {}
        self.uid = 0
        self.recent_dma = {e: {} for e in ENGS}
        self.psum_names = set()

    def push(self):
        self.pstack = contextlib.ExitStack()

    def pop(self):
        self.barrier()
        self.pstack.close()
        self.pstack = None

    def barrier(self):
        lasts = []
        for e in ENGS:
            st = self.streams[e]
            for o in reversed(st):
                if not o.is_dma:
                    lasts.append(o)
                    break
            lasts.extend(self.recent_dma[e].values())
        for e in ENGS:
            self.bar[e] = list(lasts)

    def sbuf(self, name, shape, dtype=F32):
        self.uid += 1
        st = self.pstack if self.pstack is not None else self.stack
        return st.enter_context(self.nc.sbuf_tensor(f"{name}_{self.uid}", list(shape), dtype))

    def psum(self, name, shape, dtype=F32):
        self.uid += 1
        st = self.pstack if self.pstack is not None else self.stack
        self.psum_names.add(f"{name}_{self.uid}")
        return st.enter_context(self.nc.psum_tensor(f"{name}_{self.uid}", list(shape), dtype))

    def dram(self, name, shape, dtype=F32, kind="Internal"):
        return self.nc.dram_tensor(name, list(shape), dtype, kind=kind)

    def dma(self, eng, out, in_, reads=(), writes=(), **kw):
        return self.op(eng, lambda e: e.dma_start(out=out, in_=in_, **kw), reads, writes, is_dma=True)

    def op(self, eng, fn, reads=(), writes=(), is_dma=False, rg=None):
        o = Op()
        o.rg = rg
        o.eng = eng
        o.fn = fn
        o.signal = False
        o.count = 0
        o.is_dma = is_dma
        o.dsem = None
        o.dcount = 0
        reads = [r if isinstance(r, (str, tuple)) else r.name for r in reads]
        writes = [r if isinstance(r, (str, tuple)) else r.name for r in writes]
        deps = {}
        def add(d):
            if d is None:
                return
            if d.eng == "pe" and eng == "pe":
                return
            deps[id(d)] = d
        for r in reads:
            add(self.last_w.get(r))
            if r in self.psum_names:
                for rd in self.readers.get(r, ()):
                    if rd.eng != eng:
                        add(rd)
        for w in writes:
            add(self.last_w.get(w))
            if eng == "pe" and rg is not None:
                lw = self.last_w.get(w)
                if lw is not None and lw.eng == "pe" and lw.rg is not None and lw.rg != rg:
                    deps[id(lw)] = lw
            for rd in self.readers.get(w, ()):
                if rd is not o:
                    add(rd)
        if eng in self.bar:
            for d in self.bar.pop(eng):
                if d is not None and not (d.eng == eng and not d.is_dma and eng != "pe" and False):
                    deps[id(d)] = d
        o.deps = list(deps.values())
        for r in reads:
            self.readers.setdefault(r, []).append(o)
        for w in writes:
            self.last_w[w] = o
            self.readers[w] = []
        o.idx = len(self.streams[eng])
        self.streams[eng].append(o)
        if o.is_dma:
            k = self.n_dma.get(eng, 0)
            o.dsem = k % N_DMA_SEMS
            self.n_dma[eng] = k + 1
            self.recent_dma[eng][o.dsem] = o
        return o

    def emit(self):
        nc = self.nc
        for e in ENGS:
            for o in self.streams[e]:
                for d in o.deps:
                    if not d.is_dma:
                        d.signal = True
        for e in ENGS:
            c = 0
            for o in self.streams[e]:
                if o.signal:
                    c += 1
                    o.count = c
        st = self.stack
        esem = {e: st.enter_context(nc.semaphore("s_" + e)) for e in ("pe", "act", "dve", "pool")}
        dsems = {}
        for q in ENGS:
            if not any(o.is_dma for o in self.streams[q]):
                continue
            dsems[q] = [st.enter_context(nc.semaphore(f"d_{q}_{i}")) for i in range(N_DMA_SEMS)]
            cnt = [0] * N_DMA_SEMS
            for o in self.streams[q]:
                if o.is_dma:
                    cnt[o.dsem] += 16
                    o.dcount = cnt[o.dsem]
        block = st.enter_context(nc.Block())
        streams = self.streams

        def run(engname, eng):
            waited = {}
            for o in streams[engname]:
                need = {}
                for d in o.deps:
                    if d.is_dma:
                        key = ("d", d.eng, d.dsem)
                        val = d.dcount
                    else:
                        key = ("e", d.eng)
                        val = d.count
                    if need.get(key, 0) < val:
                        need[key] = val
                if o.is_dma and o.dcount > 16:
                    key = ("d", o.eng, o.dsem)
                    if need.get(key, 0) < o.dcount - 16:
                        need[key] = o.dcount - 16
                for key, val in need.items():
                    if waited.get(key, 0) >= val:
                        continue
                    waited[key] = val
                    sem = dsems[key[1]][key[2]] if key[0] == "d" else esem[key[1]]
                    eng.wait_ge(sem, val)
                inst = o.fn(eng)
                if o.is_dma:
                    inst.then_inc(dsems[o.eng][o.dsem], 16)
                elif o.signal:
                    inst.then_inc(esem[o.eng], 1)
            return waited

        def fin(engname, eng):
            w = run(engname, eng)
            cnt = {}
            for o in streams[engname]:
                if o.is_dma:
                    cnt[o.dsem] = o.dcount
            for k, v in cnt.items():
                if w.get(("d", engname, k), 0) < v:
                    eng.wait_ge(dsems[engname][k], v)

        @block.tensor
        def _(pe):
            fin("pe", pe)

        @block.scalar
        def _(act):
            fin("act", act)

        @block.vector
        def _(dve):
            fin("dve", dve)

        @block.gpsimd
        def _(pool):
            fin("pool", pool)

        @block.sync
        def _(sp):
            fin("sp", sp)

    def close(self):
        self.stack.close()

import math
import numpy as np

D = 1024
EPS = 1e-6
A_IN, B_IN = 2080, 1536
EVEN_IN = A_IN + B_IN
C_IN, D_IN = 1920, 1568
ODD_IN = C_IN + D_IN
NIN = (EVEN_IN, ODD_IN)
HID = 4096
TC = 256


class Cfg:
    def __init__(self, NB=2, TL=4096, debug=()):
        self.NB = NB
        self.TL = TL
        self.TS = TC + TL
        self.NT = NB * self.TS
        self.ntile = self.NT // 128
        self.debug = set(debug)

    def cls(self, tile):
        b, r = divmod(tile * 128, self.TS)
        return self.NB if r < TC else b


def declare_io(nc, cfg):
    io = {}
    def inp(name, shape):
        io[name] = nc.dram_tensor(name, list(shape), F32, kind="ExternalInput").ap()
    inp("xin", [cfg.NT, D])
    inp("cvec", [cfg.NB + 1, D])
    inp("ada_w", [2, D, 6 * D]); inp("ada_b", [2, 6 * D]); inp("norm_g", [2, 2, D])
    inp("w_in_even", [D, EVEN_IN]); inp("w_in_odd", [D, ODD_IN])
    inp("w_out", [2, D, D]); inp("w_mlp_in", [2, D, HID]); inp("w_mlp_out", [2, HID, D])
    inp("mlstm_gate_b", [1, 32]); inp("mlstm_norm_g", [1, 512])
    inp("diff_qk_g", [2, 32]); inp("diff_lam", [1, 128]); inp("diff_subln_g", [1, 64])
    inp("rwkv_mu", [2, C_IN]); inp("rwkv_w0", [2, 512]); inp("rwkv_w2", [2, 64, 512])
    inp("rwkv_a0", [2, 512]); inp("rwkv_a2", [2, 64, 512]); inp("rwkv_g2", [128, 512])
    inp("rwkv_kvec", [3, 512]); inp("rwkv_ln", [2, 512])
    inp("gla_gate_w2", [2, 16, 256]); inp("gla_gate_b", [2, 256]); inp("gla_norm_g", [1, 128])
    inp("ident", [128, 128]); inp("rope", [cfg.TL, 32])
    inp("masks", [9, 128, 128])
    io["out"] = nc.dram_tensor("out", [cfg.NB * cfg.TL, D], F32, kind="ExternalOutput").ap()
    return io


def build(cfg, stop_after=None):
    nc = bass.Bass("TRN2", target_bir_lowering=False)
    io = declare_io(nc, cfg)
    P = Prog(nc)
    dbg = cfg.debug

    def scratch(name, shape):
        kind = "ExternalOutput" if name in dbg else "Internal"
        return nc.dram_tensor(name, list(shape), F32, kind=kind).ap()

    X1 = scratch("X1", [cfg.NT, D])
    if "U_in" in dbg:
        U = nc.dram_tensor("U", [cfg.NT, EVEN_IN], F32, kind="ExternalInput").ap()
    else:
        U = scratch("U", [cfg.NT, EVEN_IN])
    YS = scratch("YS", [cfg.NT, 520])
    RW = scratch("RW", [cfg.NT, 2432])
    NTS = cfg.TS // 128
    QKT = scratch("QKT", [cfg.NB, 2, 8, 64, cfg.TS])
    masks = P.sbuf("masks", [128, 9, 128])
    P.dma("sp", masks[:], io["masks"].rearrange("m p n -> p m n"), writes=[masks])

    def ukeys(t):
        return [("U", t, nb) for nb in range(8)]

    def bcast_load(dst, src_row, n):
        P.dma("sp", dst, src_row.to_broadcast([128, n]), writes=[dst.tensor.name if hasattr(dst, "tensor") else dst])

    if "O_in" in dbg:
        O = nc.dram_tensor("O", [cfg.NT, D], F32, kind="ExternalInput").ap()
    else:
        O = scratch("O", [cfg.NT, D])
    MODS = scratch("MODS", [2, cfg.NB + 1, 6 * D])
    NC = cfg.NB + 1

    ident = P.sbuf("ident", [128, 128])
    ones = P.sbuf("ones", [128, 128])
    P.dma("sp", ident[:], io["ident"], writes=[ident])
    P.op("dve", lambda e: e.memset(ones[:], 1.0), writes=[ones])
    epsc = P.sbuf("epsc", [128, 1])
    P.op("dve", lambda e: e.memset(epsc[:], EPS), writes=[epsc])
    AB = [P.sbuf(f"AB{l}", [128, 4, 8, NC]) for l in range(2)]

    P.push()
    crow = P.sbuf("crow", [NC, D])
    srow = P.sbuf("srow", [NC, D])
    sT = P.sbuf("sT", [128, 8, NC])
    P.dma("sp", crow[:], io["cvec"], writes=[crow])
    P.op("act", lambda e: e.activation(out=srow[:], in_=crow[:], func=AF.Silu), reads=[crow], writes=[srow])
    pt = P.psum("p0t", [128, 8, NC])
    for k in range(8):
        P.op("pe", lambda e, k=k: e.transpose(out=pt[:, k, :], in_=srow[:, k * 128:(k + 1) * 128], identity=ident[0:NC, 0:NC]),
             reads=[srow, ident], writes=[pt])
    P.op("dve", lambda e: e.tensor_copy(out=sT[:], in_=pt[:]), reads=[pt], writes=[sT])
    modrow = P.sbuf("modrow", [NC, 6 * D])
    gT = P.sbuf("gT", [128, 2, 8])
    wst = [P.sbuf(f"p0w{i}", [128, 8, 512]) for i in range(2)]
    brow = P.sbuf("p0b", [1, 6 * D])
    pm = [P.psum(f"p0m{i}", [NC, 512]) for i in range(2)]
    pmt = P.psum("p0mt", [128, 4, 8, NC])
    pg = P.psum("p0g", [128, 2, 8])
    grow = P.sbuf("p0grow", [1, 2 * D])
    for l in range(2):
        P.dma("sp", brow[:], io["ada_b"][l:l + 1, :], writes=[brow])
        for j in range(12):
            w = wst[j % 2]
            P.dma("sp", w[:], io["ada_w"][l, :, j * 512:(j + 1) * 512].rearrange("(k p) n -> p k n", p=128), writes=[w])
            ps = pm[j % 2]
            for k in range(8):
                P.op("pe", lambda e, k=k, w=w, ps=ps: e.matmul(ps[:], lhsT=sT[:, k, :], rhs=w[:, k, :], start=(k == 0), stop=False),
                     reads=[sT, w], writes=[ps])
            P.op("pe", lambda e, ps=ps, j=j: e.matmul(ps[:], lhsT=ones[0:1, 0:NC], rhs=brow[0:1, j * 512:(j + 1) * 512], start=False, stop=True),
                 reads=[ones, brow], writes=[ps])
            P.op("act", lambda e, ps=ps, j=j: e.activation(out=modrow[:, j * 512:(j + 1) * 512], in_=ps[:], func=AF.Copy),
                 reads=[ps], writes=[modrow])
        P.dma("sp", MODS[l], modrow[:], reads=[modrow], writes=[("MODS", l)])
        for qi, q in enumerate((0, 1, 3, 4)):
            for k in range(8):
                P.op("pe", lambda e, qi=qi, q=q, k=k: e.transpose(out=pmt[:, qi, k, :], in_=modrow[:, q * D + k * 128: q * D + (k + 1) * 128],
                                                                 identity=ident[0:NC, 0:NC]),
                     reads=[modrow, ident], writes=[pmt])
        P.dma("sp", grow[:], io["norm_g"][l:l + 1].rearrange("o j d -> o (j d)"), writes=[grow])
        for j in range(2):
            for k in range(8):
                P.op("pe", lambda e, j=j, k=k: e.transpose(out=pg[:, j, k:k + 1], in_=grow[0:1, j * D + k * 128: j * D + (k + 1) * 128], identity=ident[0:1, 0:1]),
                     reads=[grow, ident], writes=[pg])
        P.op("dve", lambda e: e.tensor_copy(out=gT[:], in_=pg[:]), reads=[pg], writes=[gT])
        ab = AB[l]
        P.op("dve", lambda e, ab=ab: e.tensor_copy(out=ab[:, 1], in_=pmt[:, 0]), reads=[pmt], writes=[ab])
        P.op("dve", lambda e, ab=ab: e.tensor_copy(out=ab[:, 3], in_=pmt[:, 2]), reads=[pmt], writes=[ab])
        for c in range(NC):
            P.op("dve", lambda e, ab=ab, c=c: e.scalar_tensor_tensor(out=ab[:, 0, :, c], in0=pmt[:, 1, :, c], scalar=1.0, in1=gT[:, 0, :],
                                                                   op0=ALU.add, op1=ALU.mult), reads=[pmt, gT], writes=[ab])
            P.op("dve", lambda e, ab=ab, c=c: e.scalar_tensor_tensor(out=ab[:, 2, :, c], in0=pmt[:, 3, :, c], scalar=1.0, in1=gT[:, 1, :],
                                                                   op0=ALU.add, op1=ALU.mult), reads=[pmt, gT], writes=[ab])
    P.pop()
    if stop_after == "P0":
        P.emit(); P.close(); return nc

    def norm_mod_T(xt, hxT, j, l, which, cls, sq_junk, ss, rstd, xn, ptr):
        P.op("act", lambda e: e.activation(out=sq_junk[:], in_=xt, func=AF.Square, accum_out=ss[:]),
             reads=[xt.tensor if hasattr(xt, "tensor") else xt], writes=[sq_junk, ss])
        return

    def p3(l, Xl):
        last = (l == 1)
        P.push()
        MT = 2
        if last:
            tl = [t for t in range(cfg.ntile) if cfg.cls(t) != cfg.NB]
        else:
            tl = list(range(cfg.ntile))
        macs = [tl[i:i + MT] for i in range(0, len(tl), MT)]
        NTK = MT * 128
        gbc = P.sbuf("p3gbc", [128, NC, 2, D])
        for c in range(NC):
            for gi, q in enumerate((2, 5)):
                P.dma("sp", gbc[:, c, gi, :], MODS[l, c:c + 1, q * D:(q + 1) * D].to_broadcast([128, D]),
                      reads=[("MODS", l)], writes=[gbc])
        xts = [P.sbuf(f"p3x{i}", [128, D]) for i in range(2 * MT)]
        obuf = [P.sbuf(f"p3o{i}", [128, D]) for i in range(2)]
        OT = [P.sbuf(f"p3OT{i}", [128, 8, NTK]) for i in range(2)]
        hx2T = P.sbuf("p3hx2T", [128, 8, NTK])
        hT = P.sbuf("p3hT", [128, 32, NTK])
        wstg = [P.sbuf(f"p3ws{i}", [128, 8, 512]) for i in range(2)]
        wr = [P.sbuf(f"p3wr{i}", [128, 8, 512]) for i in range(2)]
        tmp = [P.sbuf(f"p3tmp{i}", [128, 512]) for i in range(2)]
        hrl = [P.sbuf(f"p3hrl{i}", [128, NTK]) for i in range(2)]
        ss = [P.sbuf(f"p3ss{i}", [128, 1]) for i in range(2)]
        rstd = [P.sbuf(f"p3rs{i}", [128, 1]) for i in range(2)]
        ptr = [P.psum(f"p3ptr{i}", [128, 4, 128]) for i in range(2)]
        pu = [P.psum(f"p3pu{i}", [128, 512]) for i in range(2)]
        acc = [P.psum(f"p3acc{i}", [128, 512]) for i in range(2 * MT)]
        cnt = {"o": 0, "pu": 0, "tmp": 0, "hr": 0, "n": 0}
        w_out = io["w_out"][l]; w1 = io["w_mlp_in"][l]; w2 = io["w_mlp_out"][l]
        wblocks = []
        for m in range(len(macs)):
            for nb in range(2):
                wblocks.append(w_out[:, nb * 512:(nb + 1) * 512].rearrange("(k p) n -> p k n", p=128))
            for nb in range(8):
                wblocks.append(w1[:, nb * 512:(nb + 1) * 512].rearrange("(k p) n -> p k n", p=128))
            for nb in range(2):
                for kp in range(4):
                    wblocks.append(w2[kp * 1024:(kp + 1) * 1024, nb * 512:(nb + 1) * 512].rearrange("(k p) n -> p k n", p=128))
        wi = {"i": 0}

        def loadw(i):
            if i >= len(wblocks):
                return
            ws_, wr_ = wstg[i % 2], wr[i % 2]
            P.dma("sp", ws_[:], wblocks[i], writes=[ws_])
            P.op("pool", lambda e, ws_=ws_, wr_=wr_: e.tensor_copy(out=wr_[:].bitcast(F32R), in_=ws_[:]), reads=[ws_], writes=[wr_])

        def nextw():
            i = wi["i"]; wi["i"] += 1
            loadw(i + 1)
            return wr[i % 2]

        def transpose_tile(src, dst, jj, scale_bias=None):
            for half in range(2):
                pt_ = ptr[half]
                for kk in range(4):
                    k = half * 4 + kk
                    P.op("pe", lambda e, pt_=pt_, kk=kk, k=k: e.transpose(out=pt_[:, kk, :], in_=src[:, k * 128:(k + 1) * 128], identity=ident[:]),
                         reads=[src, ident], writes=[pt_])
                if scale_bias is None:
                    dsl = dst[:, half * 4:(half + 1) * 4, jj * 128:(jj + 1) * 128]
                    if half == 0:
                        P.op("act", lambda e, pt_=pt_, dsl=dsl: e.activation(out=dsl.bitcast(F32R), in_=pt_[:], func=AF.Copy), reads=[pt_], writes=[dst])
                    else:
                        P.op("dve", lambda e, pt_=pt_, dsl=dsl: e.tensor_copy(out=dsl.bitcast(F32R), in_=pt_[:]), reads=[pt_], writes=[dst])
                else:
                    ab, ja, jb, cls = scale_bias
                    for kk in range(4):
                        k = half * 4 + kk
                        P.op("act", lambda e, pt_=pt_, kk=kk, k=k: e.activation(
                            out=dst[:, k, jj * 128:(jj + 1) * 128].bitcast(F32R), in_=pt_[:, kk, :], func=AF.Identity,
                            scale=ab[:, ja, k, cls:cls + 1], bias=ab[:, jb, k, cls:cls + 1]), reads=[pt_, ab], writes=[dst])

        def prep_O(m):
            ot = OT[m % 2]
            for jj, t in enumerate(macs[m]):
                ob = obuf[cnt["o"] % 2]; cnt["o"] += 1
                P.dma("sp", ob[:], O[t * 128:(t + 1) * 128, :], reads=[("O", t, 0), ("O", t, 1)], writes=[ob])
                transpose_tile(ob, ot, jj)

        loadw(0)
        prep_O(0)
        for m, tiles in enumerate(macs):
            ntok = len(tiles) * 128
            ot = OT[m % 2]
            xs = [xts[(m % 2) * MT + jj] for jj in range(len(tiles))]
            for jj, t in enumerate(tiles):
                P.dma("sp", xs[jj][:], Xl[t * 128:(t + 1) * 128, :], reads=[("X", l, t)], writes=[xs[jj]])
            for nb in range(2):
                wr_ = nextw()
                for jj, t in enumerate(tiles):
                    cls = cfg.cls(t)
                    ps = pu[cnt["pu"] % 2]; cnt["pu"] += 1
                    tm = tmp[cnt["tmp"] % 2]; cnt["tmp"] += 1
                    for k in range(8):
                        P.op("pe", lambda e, ps=ps, k=k, jj=jj, wr_=wr_, ot=ot: e.matmul(ps[:], lhsT=ot[:, k, jj * 128:(jj + 1) * 128].bitcast(F32R),
                                                                              rhs=wr_[:, k, :].bitcast(F32R), start=(k == 0), stop=(k == 7)),
                             reads=[ot, wr_], writes=[ps])
                    P.op("dve", lambda e, ps=ps, tm=tm, cls=cls, nb=nb: e.tensor_tensor(out=tm[:], in0=ps[:], in1=gbc[:, cls, 0, nb * 512:(nb + 1) * 512], op=ALU.mult),
                         reads=[ps, gbc], writes=[tm])
                    xj = xs[jj]
                    P.op("pool", lambda e, tm=tm, xj=xj, nb=nb: e.tensor_tensor(out=xj[:, nb * 512:(nb + 1) * 512], in0=xj[:, nb * 512:(nb + 1) * 512], in1=tm[:], op=ALU.add),
                         reads=[tm, xj], writes=[xj])
            if m + 1 < len(macs):
                prep_O(m + 1)
            for jj, t in enumerate(tiles):
                cls = cfg.cls(t)
                xj = xs[jj]
                n = cnt["n"]; cnt["n"] += 1
                s_ = ss[n % 2]; r_ = rstd[n % 2]; xn_ = obuf[cnt["o"] % 2]; cnt["o"] += 1
                P.op("act", lambda e, xj=xj, s_=s_, xn_=xn_: e.activation(out=xn_[:], in_=xj[:], func=AF.Square, accum_out=s_[:]),
                     reads=[xj], writes=[xn_, s_])
                P.op("act", lambda e, s_=s_, r_=r_: e.activation(out=r_[:], in_=s_[:], func=AF.Sqrt, scale=1.0 / D, bias=epsc[:, 0:1]),
                     reads=[s_, epsc], writes=[r_])
                P.op("dve", lambda e, r_=r_: e.reciprocal(out=r_[:], in_=r_[:]), reads=[r_], writes=[r_])
                P.op("dve", lambda e, xj=xj, r_=r_, xn_=xn_: e.tensor_scalar(out=xn_[:], in0=xj[:], scalar1=r_[:, 0:1], scalar2=None, op0=ALU.mult),
                     reads=[xj, r_], writes=[xn_])
                transpose_tile(xn_, hx2T, jj, scale_bias=(AB[l], 2, 3, cls))
            for nb in range(8):
                wr_ = nextw()
                for sub in range(4):
                    nchunk = nb * 4 + sub
                    ps = pu[cnt["pu"] % 2]; cnt["pu"] += 1
                    hr = hrl[cnt["hr"] % 2]; cnt["hr"] += 1
                    for k in range(8):
                        P.op("pe", lambda e, ps=ps, k=k, sub=sub, wr_=wr_, ntok=ntok: e.matmul(ps[:, 0:ntok], lhsT=wr_[:, k, sub * 128:(sub + 1) * 128].bitcast(F32R),
                                                                               rhs=hx2T[:, k, 0:ntok].bitcast(F32R), start=(k == 0), stop=(k == 7)),
                             reads=[hx2T, wr_], writes=[ps])
                    P.op("act", lambda e, ps=ps, hr=hr, ntok=ntok: e.activation(out=hr[:, 0:ntok], in_=ps[:, 0:ntok], func=AF.Relu), reads=[ps], writes=[hr])
                    P.op("pool", lambda e, hr=hr, nchunk=nchunk, ntok=ntok: e.tensor_tensor(out=hT[:, nchunk, 0:ntok].bitcast(F32R), in0=hr[:, 0:ntok], in1=hr[:, 0:ntok], op=ALU.mult),
                         reads=[hr], writes=[hT])
            for nb in range(2):
                for kp in range(4):
                    wr_ = nextw()
                    for jj, t in enumerate(tiles):
                        ac = acc[nb * MT + jj]
                        for k in range(8):
                            P.op("pe", lambda e, ac=ac, k=k, kp=kp, jj=jj, wr_=wr_: e.matmul(ac[:], lhsT=hT[:, kp * 8 + k, jj * 128:(jj + 1) * 128].bitcast(F32R),
                                                                                        rhs=wr_[:, k, :].bitcast(F32R), start=(kp == 0 and k == 0), stop=(kp == 3 and k == 7)),
                                 reads=[hT, wr_], writes=[ac])
                for jj, t in enumerate(tiles):
                    cls = cfg.cls(t)
                    ac = acc[nb * MT + jj]
                    tm = tmp[cnt["tmp"] % 2]; cnt["tmp"] += 1
                    xj = xs[jj]
                    P.op("dve", lambda e, ac=ac, tm=tm, cls=cls, nb=nb: e.tensor_tensor(out=tm[:], in0=ac[:], in1=gbc[:, cls, 1, nb * 512:(nb + 1) * 512], op=ALU.mult),
                         reads=[ac, gbc], writes=[tm])
                    P.op("pool", lambda e, tm=tm, xj=xj, nb=nb: e.tensor_tensor(out=xj[:, nb * 512:(nb + 1) * 512], in0=xj[:, nb * 512:(nb + 1) * 512], in1=tm[:], op=ALU.add),
                         reads=[tm, xj], writes=[xj])
            for jj, t in enumerate(tiles):
                xj = xs[jj]
                if last:
                    b, r = divmod(t * 128, cfg.TS)
                    row = b * cfg.TL + (r - TC)
                    P.dma("act", io["out"][row:row + 128, :], xj[:], reads=[xj], writes=[("OUT", t)])
                else:
                    P.dma("act", X1[t * 128:(t + 1) * 128, :], xj[:], reads=[xj], writes=[("X", 1, t)])
        P.pop()

    LN8 = math.log(0.125)

    def mlstm(l):
        P.push()
        NB = cfg.NB
        gb_bc = P.sbuf("mlgb", [128, 32]); ng_bc = P.sbuf("mlng", [128, 512])
        P.dma("sp", gb_bc[:], io["mlstm_gate_b"].to_broadcast([128, 32]), writes=[gb_bc])
        P.dma("sp", ng_bc[:], io["mlstm_norm_g"].to_broadcast([128, 512]), writes=[ng_bc])
        ln8 = P.sbuf("mlln8", [128, 1]); onec = P.sbuf("mlone", [128, 1]); eps_c = P.sbuf("mleps", [128, 1])
        P.op("dve", lambda e: e.memset(ln8[:], LN8), writes=[ln8])
        P.op("dve", lambda e: e.memset(onec[:], 1.0), writes=[onec])
        P.op("dve", lambda e: e.memset(eps_c[:], EPS), writes=[eps_c])
        B = []
        for b in range(NB):
            d = {}
            d["ua"] = [P.sbuf(f"mlua{b}{i}", [128, A_IN]) for i in range(2)]
            d["g"] = P.sbuf(f"mlg{b}", [128, 32])
            d["nlf"] = P.sbuf(f"mlnlf{b}", [128, 8])
            d["eb"] = P.sbuf(f"mleb{b}", [128, 8])
            d["vs"] = P.sbuf(f"mlvs{b}", [128, 8])
            d["eL"] = P.sbuf(f"mleL{b}", [128, 8])
            d["Vt"] = P.sbuf(f"mlVt{b}", [128, 8, 65])
            d["QT"] = P.sbuf(f"mlQT{b}", [128, 4, 128])
            d["KT"] = P.sbuf(f"mlKT{b}", [128, 4, 128])
            d["AT"] = [P.sbuf(f"mlAT{b}{i}", [128, 128]) for i in range(2)]
            d["C"] = P.sbuf(f"mlC{b}", [128, 4, 65])
            d["Ct"] = P.sbuf(f"mlCt{b}", [128, 65])
            d["small"] = P.sbuf(f"mlsm{b}", [128, 6, 8])
            d["hd"] = P.sbuf(f"mlhd{b}", [128, 8, 64])
            d["hf"] = P.sbuf(f"mlhf{b}", [128, 512])
            d["sq"] = P.sbuf(f"mlsq{b}", [128, 512])
            d["sg"] = P.sbuf(f"mlsg{b}", [128, 512])
            d["pT"] = P.psum(f"mlpT{b}", [128, 8, 128]) if b == 0 else None
            d["pa"] = P.psum(f"mlpa{b}", [128, 128])
            d["pn"] = P.psum(f"mlpn{b}", [128, 8, 128]) if b == 0 else None
            B.append(d)
        pn = B[0]["pn"]
        pc = P.psum("mlpc", [128, 2, 65])
        psm = P.psum("mlpsm", [128, 16])
        for dr in range(2):
            mk = masks[:, dr, :]
            for b in range(NB):
                P.op("dve", lambda e, b=b: e.memset(B[b]["C"][:], 0.0), writes=[B[b]["C"]])
            if dr == 0:
                order = list(range(NTS))
            else:
                order = [1, 0] + list(range(NTS - 1, 1, -1))
            for step, tt in enumerate(order):
                for b in range(NB):
                    d = B[b]
                    t = b * NTS + tt
                    ua = d["ua"][step % 2]
                    pT = B[0]["pT"]
                    P.dma("sp", ua[:], U[t * 128:(t + 1) * 128, 0:A_IN], reads=ukeys(t), writes=[ua])
                    g, nlf, eb, vs, eL, Vt, QT, KT, C, sm = d["g"], d["nlf"], d["eb"], d["vs"], d["eL"], d["Vt"], d["QT"], d["KT"], d["C"], d["small"]
                    P.op("dve", lambda e, ua=ua, g=g: e.tensor_tensor(out=g[:], in0=ua[:, 2048:2080], in1=gb_bc[:], op=ALU.add), reads=[ua, gb_bc], writes=[g])
                    ig = g[:, 16 * dr: 16 * dr + 8]
                    fg = g[:, 16 * dr + 8: 16 * dr + 16]
                    P.op("act", lambda e, fg=fg, nlf=nlf: e.activation(out=nlf[:], in_=fg, func=AF.Exp, scale=-1.0), reads=[g], writes=[nlf])
                    P.op("act", lambda e, nlf=nlf: e.activation(out=nlf[:], in_=nlf[:], func=AF.Ln, bias=onec[:, 0:1]), reads=[nlf, onec], writes=[nlf])
                    P.op("pe", lambda e, nlf=nlf, mk=mk: e.matmul(psm[:, 0:8], lhsT=mk, rhs=nlf[:], start=True, stop=True), reads=[masks, nlf], writes=[psm])
                    P.op("pe", lambda e, nlf=nlf: e.matmul(psm[:, 8:16], lhsT=ones[:], rhs=nlf[:], start=True, stop=True), reads=[ones, nlf], writes=[psm])
                    P.op("act", lambda e, eb=eb: e.activation(out=eb[:], in_=psm[:, 0:8], func=AF.Exp, scale=-1.0), reads=[psm], writes=[eb])
                    P.op("act", lambda e, eL=eL: e.activation(out=eL[:], in_=psm[:, 8:16], func=AF.Exp, scale=-1.0), reads=[psm], writes=[eL])
                    P.op("dve", lambda e, vs=vs, ig=ig: e.tensor_tensor(out=vs[:], in0=psm[:, 0:8], in1=ig, op=ALU.add), reads=[psm, g], writes=[vs])
                    P.op("act", lambda e, vs=vs: e.activation(out=vs[:], in_=vs[:], func=AF.Exp, bias=ln8[:, 0:1]), reads=[vs, ln8], writes=[vs])
                    P.op("dve", lambda e, ua=ua, vs=vs, Vt=Vt: e.tensor_tensor(out=Vt[:, :, 0:64], in0=ua[:, 1024:1536].rearrange("p (h d) -> p h d", h=8),
                                                                      in1=vs[:].unsqueeze(2).to_broadcast([128, 8, 64]), op=ALU.mult), reads=[ua, vs], writes=[Vt])
                    P.op("pool", lambda e, vs=vs, Vt=Vt: e.tensor_copy(out=Vt[:, :, 64], in_=vs[:]), reads=[vs], writes=[Vt])
                    for i in range(8):
                        P.op("pe", lambda e, i=i, ua=ua, pT=pT: e.transpose(out=pT[:, i, :], in_=ua[:, i * 128:(i + 1) * 128], identity=ident[:]), reads=[ua, ident], writes=[pT])
                    P.op("act", lambda e, QT=QT, pT=pT: e.activation(out=QT[:], in_=pT[:, 0:4, :], func=AF.Copy), reads=[pT], writes=[QT])
                    P.op("dve", lambda e, KT=KT, pT=pT: e.tensor_copy(out=KT[:], in_=pT[:, 4:8, :]), reads=[pT], writes=[KT])
                    for h in range(8):
                        hp, ho = h // 2, (h % 2) * 64
                        AT = d["AT"][h % 2]
                        pa = d["pa"]
                        P.op("pe", lambda e, KT=KT, QT=QT, hp=hp, ho=ho, pa=pa: e.matmul(pa[:], lhsT=KT[ho:ho + 64, hp, :], rhs=QT[ho:ho + 64, hp, :], start=True, stop=True),
                             reads=[KT, QT], writes=[pa])
                        P.op("dve", lambda e, AT=AT, pa=pa, mk=mk: e.tensor_tensor(out=AT[:], in0=pa[:], in1=mk, op=ALU.mult), reads=[pa, masks], writes=[AT])
                        P.op("pe", lambda e, AT=AT, Vt=Vt, h=h: e.matmul(pn[:, h, 0:65], lhsT=AT[:], rhs=Vt[:, h, :], start=True, stop=False), reads=[AT, Vt], writes=[pn])
                        P.op("pe", lambda e, QT=QT, C=C, h=h, hp=hp, ho=ho: e.matmul(pn[:, h, 0:65], lhsT=QT[ho:ho + 64, hp, :], rhs=C[ho:ho + 64, hp, :], start=False, stop=True),
                             reads=[QT, C], writes=[pn])
                    P.op("dve", lambda e, sm=sm, eb=eb: e.tensor_tensor(out=sm[:, 0, :], in0=pn[:, :, 64], in1=eb[:], op=ALU.mult), reads=[pn, eb], writes=[sm])
                    P.op("dve", lambda e, sm=sm: e.scalar_tensor_tensor(out=sm[:, 1, :], in0=sm[:, 0, :], scalar=-1.0, in1=sm[:, 0, :], op0=ALU.mult, op1=ALU.max), reads=[sm], writes=[sm])
                    P.op("dve", lambda e, sm=sm: e.tensor_scalar_max(out=sm[:, 2, :], in0=sm[:, 1, :], scalar1=1.0), reads=[sm], writes=[sm])
                    P.op("dve", lambda e, sm=sm: e.reciprocal(out=sm[:, 3, :], in_=sm[:, 2, :]), reads=[sm], writes=[sm])
                    P.op("dve", lambda e, sm=sm, eb=eb: e.tensor_tensor(out=sm[:, 4, :], in0=sm[:, 3, :], in1=eb[:], op=ALU.mult), reads=[sm, eb], writes=[sm])
                    hd = d["hd"]
                    P.op("dve", lambda e, sm=sm, hd=hd: e.tensor_tensor(out=hd[:], in0=pn[:, :, 0:64], in1=sm[:, 4, :].unsqueeze(2).to_broadcast([128, 8, 64]), op=ALU.mult),
                         reads=[pn, sm], writes=[hd])
                    for hp in range(4):
                        P.op("pe", lambda e, ua=ua, Vt=Vt, hp=hp: e.matmul(pc[:], lhsT=ua[:, 512 + hp * 128: 512 + (hp + 1) * 128], rhs=Vt[:, 2 * hp:2 * hp + 2, :], start=True, stop=True),
                             reads=[ua, Vt], writes=[pc])
                        for ho_i in range(2):
                            ho = ho_i * 64
                            h = 2 * hp + ho_i
                            Ct = d["Ct"]
                            P.op("dve", lambda e, C=C, Ct=Ct, hp=hp, ho=ho, ho_i=ho_i: e.tensor_tensor(out=Ct[ho:ho + 64, :], in0=pc[ho:ho + 64, ho_i, :], in1=C[ho:ho + 64, hp, :], op=ALU.add),
                                 reads=[pc, C], writes=[Ct])
                            P.op("act", lambda e, C=C, Ct=Ct, eL=eL, hp=hp, ho=ho, h=h: e.activation(out=C[ho:ho + 64, hp, :], in_=Ct[ho:ho + 64, :], func=AF.Copy, scale=eL[ho:ho + 64, h:h + 1]),
                                 reads=[Ct, eL], writes=[C])
                    hdf = hd[:].rearrange("p h d -> p (h d)")
                    if dr == 0:
                        P.dma("act", YS[t * 128:(t + 1) * 128, 0:512], hdf, reads=[hd], writes=[("YS", t)])
                    else:
                        hf, sq, sg = d["hf"], d["sq"], d["sg"]
                        P.dma("sp", hf[:], YS[t * 128:(t + 1) * 128, 0:512], reads=[("YS", t)], writes=[hf])
                        P.op("pool", lambda e, hf=hf, hdf=hdf: e.tensor_tensor(out=hf[:], in0=hf[:], in1=hdf, op=ALU.add), reads=[hf, hd], writes=[hf])
                        P.op("act", lambda e, hf=hf, sq=sq: e.activation(out=sq[:], in_=hf[:], func=AF.Square), reads=[hf], writes=[sq])
                        P.op("dve", lambda e, sq=sq, sm=sm: e.tensor_reduce(out=sm[:, 5, :], in_=sq[:].rearrange("p (h d) -> p h d", h=8), axis=AX.X, op=ALU.add), reads=[sq], writes=[sm])
                        P.op("act", lambda e, sm=sm: e.activation(out=sm[:, 5, :], in_=sm[:, 5, :], func=AF.Sqrt, scale=1.0 / 64, bias=eps_c[:, 0:1]), reads=[sm, eps_c], writes=[sm])
                        P.op("dve", lambda e, sm=sm: e.reciprocal(out=sm[:, 5, :], in_=sm[:, 5, :]), reads=[sm], writes=[sm])
                        P.op("dve", lambda e, hf=hf, sm=sm: e.tensor_tensor(out=hf[:].rearrange("p (h d) -> p h d", h=8), in0=hf[:].rearrange("p (h d) -> p h d", h=8),
                                                                      in1=sm[:, 5, :].unsqueeze(2).to_broadcast([128, 8, 64]), op=ALU.mult), reads=[hf, sm], writes=[hf])
                        P.op("pool", lambda e, hf=hf: e.tensor_tensor(out=hf[:], in0=hf[:], in1=ng_bc[:], op=ALU.mult), reads=[hf, ng_bc], writes=[hf])
                        P.op("act", lambda e, ua=ua, sg=sg: e.activation(out=sg[:], in_=ua[:, 1536:2048], func=AF.Sigmoid), reads=[ua], writes=[sg])
                        P.op("dve", lambda e, hf=hf, sg=sg: e.tensor_tensor(out=sg[:], in0=hf[:], in1=sg[:], op=ALU.mult), reads=[hf, sg], writes=[sg])
                        P.dma("act", O[t * 128:(t + 1) * 128, 0:512], sg[:], reads=[sg], writes=[("O", t, 0)])
        P.pop()

    def diffattn(l):
        NB = cfg.NB
        lam_init = 0.8 - 0.6 * math.exp(-0.3 * l)
        SC = 32 ** -0.5
        CSH = 4.0
        P.push()
        g2 = P.sbuf("dag2", [128, 2, 32]); eps_c = P.sbuf("daeps", [128, 1])
        P.dma("sp", g2[:, 0, :], io["diff_qk_g"][0:1, :].to_broadcast([128, 32]), writes=[g2])
        P.dma("sp", g2[:, 1, :], io["diff_qk_g"][1:2, :].to_broadcast([128, 32]), writes=[g2])
        P.op("dve", lambda e: e.memset(eps_c[:], EPS), writes=[eps_c])
        ubs = [P.sbuf(f"daub{i}", [128, 1024]) for i in range(2)]
        sqs = P.sbuf("dasq", [128, 1024]); rs = [P.sbuf(f"dars{i}", [128, 32]) for i in range(2)]
        qn = [P.sbuf(f"daqn{i}", [128, 1024]) for i in range(2)]
        qr = [P.sbuf(f"daqr{i}", [128, 1024]) for i in range(2)]
        tA = P.sbuf("datA", [128, 2, 512]); tB = P.sbuf("datB", [128, 2, 512])
        cs = [P.sbuf(f"dacs{i}", [128, 32]) for i in range(2)]
        QTt = [P.sbuf(f"daQTt{i}", [64, 16, 128]) for i in range(2)]
        pT = [P.psum(f"dapT{i}", [64, 8, 128]) for i in range(2)]
        for t in range(cfg.ntile):
            b, tt = divmod(t, NTS)
            isctx = tt < TC // 128
            ub = ubs[t % 2]; r_ = rs[t % 2]; qn_ = qn[t % 2]; qr_ = qr[t % 2]; cs_ = cs[t % 2]; qt_ = QTt[t % 2]
            P.dma("sp", ub[:], U[t * 128:(t + 1) * 128, A_IN:A_IN + 1024], reads=ukeys(t), writes=[ub])
            P.op("act", lambda e, ub=ub: e.activation(out=sqs[:], in_=ub[:], func=AF.Square), reads=[ub], writes=[sqs])
            P.op("dve", lambda e, r_=r_: e.tensor_reduce(out=r_[:], in_=sqs[:].rearrange("p (g d) -> p g d", g=32), axis=AX.X, op=ALU.add), reads=[sqs], writes=[r_])
            P.op("act", lambda e, r_=r_: e.activation(out=r_[:], in_=r_[:], func=AF.Sqrt, scale=1.0 / 32, bias=eps_c[:, 0:1]), reads=[r_, eps_c], writes=[r_])
            P.op("dve", lambda e, r_=r_: e.reciprocal(out=r_[:], in_=r_[:]), reads=[r_], writes=[r_])
            P.op("dve", lambda e, ub=ub, r_=r_, qn_=qn_: e.tensor_tensor(out=qn_[:].rearrange("p (g d) -> p g d", g=32), in0=ub[:].rearrange("p (g d) -> p g d", g=32),
                                                                   in1=r_[:].unsqueeze(2).to_broadcast([128, 32, 32]), op=ALU.mult), reads=[ub, r_], writes=[qn_])
            dst = qr_ if isctx else qn_
            P.op("pool", lambda e, qn_=qn_, dst=dst: e.tensor_tensor(out=dst[:].rearrange("p (a g d) -> p a g d", a=2, g=16), in0=qn_[:].rearrange("p (a g d) -> p a g d", a=2, g=16),
                                                                in1=g2[:].unsqueeze(2).to_broadcast([128, 2, 16, 32]), op=ALU.mult), reads=[qn_, g2], writes=[dst])
            if not isctx:
                lrow = (tt - TC // 128) * 128
                P.dma("sp", cs_[:], io["rope"][lrow:lrow + 128, :], writes=[cs_])
                v4 = qn_[:].rearrange("p (g i two) -> p g i two", g=32, two=2)
                o4 = qr_[:].rearrange("p (g i two) -> p g i two", g=32, two=2)
                x1, x2 = v4[:, :, :, 0], v4[:, :, :, 1]
                cosb = cs_[:, 0:16].unsqueeze(1).to_broadcast([128, 32, 16])
                sinb = cs_[:, 16:32].unsqueeze(1).to_broadcast([128, 32, 16])
                tA0 = tA[:, 0, :].rearrange("p (g i) -> p g i", g=32); tA1 = tA[:, 1, :].rearrange("p (g i) -> p g i", g=32)
                tB0 = tB[:, 0, :].rearrange("p (g i) -> p g i", g=32); tB1 = tB[:, 1, :].rearrange("p (g i) -> p g i", g=32)
                P.op("dve", lambda e, x1=x1, cosb=cosb, tA0=tA0: e.tensor_tensor(out=tA0, in0=x1, in1=cosb, op=ALU.mult), reads=[qn_, cs_], writes=[tA])
                P.op("dve", lambda e, x2=x2, sinb=sinb, tA1=tA1: e.tensor_tensor(out=tA1, in0=x2, in1=sinb, op=ALU.mult), reads=[qn_, cs_], writes=[tA])
                P.op("dve", lambda e, o4=o4, tA0=tA0, tA1=tA1: e.tensor_tensor(out=o4[:, :, :, 0], in0=tA0, in1=tA1, op=ALU.subtract), reads=[tA], writes=[qr_])
                P.op("pool", lambda e, x1=x1, sinb=sinb, tB0=tB0: e.tensor_tensor(out=tB0, in0=x1, in1=sinb, op=ALU.mult), reads=[qn_, cs_], writes=[tB])
                P.op("pool", lambda e, x2=x2, cosb=cosb, tB1=tB1: e.tensor_tensor(out=tB1, in0=x2, in1=cosb, op=ALU.mult), reads=[qn_, cs_], writes=[tB])
                P.op("pool", lambda e, o4=o4, tB0=tB0, tB1=tB1: e.tensor_tensor(out=o4[:, :, :, 1], in0=tB0, in1=tB1, op=ALU.add), reads=[tB], writes=[qr_])
            for a in range(2):
                pt_ = pT[a]
                for i in range(8):
                    c0 = a * 512 + i * 64
                    P.op("pe", lambda e, pt_=pt_, i=i, c0=c0, qr_=qr_: e.transpose(out=pt_[:, i, :], in_=qr_[:, c0:c0 + 64], identity=ident[:]), reads=[qr_, ident], writes=[pt_])
                if a == 0:
                    P.op("act", lambda e, pt_=pt_, qt_=qt_: e.activation(out=qt_[:, 0:8, :], in_=pt_[:], func=AF.Copy), reads=[pt_], writes=[qt_])
                else:
                    P.op("dve", lambda e, pt_=pt_, qt_=qt_: e.tensor_copy(out=qt_[:, 8:16, :], in_=pt_[:]), reads=[pt_], writes=[qt_])
            for a in range(2):
                P.dma("act", QKT[b, a, :, :, tt * 128:(tt + 1) * 128].rearrange("h p n -> p h n"), qt_[:, a * 8:(a + 1) * 8, :], reads=[qt_], writes=[("QKT", b, a, tt)])
        P.pop()
        P.push()
        lamr = P.sbuf("dalam", [128, 4, 32]); lt = P.sbuf("dalt", [128, 2, 32]); le = P.sbuf("dale", [128, 2]); nlam = P.sbuf("danlam", [128, 1])
        negc = P.sbuf("danegc", [128, 1]); eps2 = P.sbuf("daeps2", [128, 1]); gs = P.sbuf("dags", [128, 64])
        P.dma("sp", lamr[:].rearrange("p a d -> p (a d)"), io["diff_lam"].to_broadcast([128, 128]), writes=[lamr])
        P.dma("sp", gs[:], io["diff_subln_g"].to_broadcast([128, 64]), writes=[gs])
        P.op("dve", lambda e: e.memset(negc[:], -CSH), writes=[negc])
        P.op("dve", lambda e: e.memset(eps2[:], EPS), writes=[eps2])
        P.op("dve", lambda e: e.tensor_tensor(out=lt[:, 0, :], in0=lamr[:, 0, :], in1=lamr[:, 1, :], op=ALU.mult), reads=[lamr], writes=[lt])
        P.op("dve", lambda e: e.tensor_tensor(out=lt[:, 1, :], in0=lamr[:, 2, :], in1=lamr[:, 3, :], op=ALU.mult), reads=[lamr], writes=[lt])
        P.op("dve", lambda e: e.tensor_reduce(out=le[:], in_=lt[:], axis=AX.X, op=ALU.add), reads=[lt], writes=[le])
        P.op("act", lambda e: e.activation(out=le[:], in_=le[:], func=AF.Exp), reads=[le], writes=[le])
        P.op("dve", lambda e: e.tensor_tensor(out=nlam[:], in0=le[:, 1:2], in1=le[:, 0:1], op=ALU.subtract), reads=[le], writes=[nlam])
        P.op("dve", lambda e: e.tensor_scalar_add(out=nlam[:], in0=nlam[:], scalar1=-lam_init), reads=[nlam], writes=[nlam])
        P.op("dve", lambda e: e.tensor_scalar_mul(out=gs[:], in0=gs[:], scalar1=1.0 - lam_init), reads=[gs], writes=[gs])
        QTh = [P.sbuf(f"daQTh{i}", [64, cfg.TS]) for i in range(2)]
        KTh = [P.sbuf(f"daKTh{i}", [64, cfg.TS]) for i in range(2)]
        Vst = [P.sbuf(f"daVst{i}", [128, NTS, 64]) for i in range(2)]
        Vr = [P.sbuf(f"daVr{i}", [128, NTS, 65]) for i in range(2)]
        for i in range(2):
            for j0 in range(0, NTS, 128):
                j1 = min(NTS, j0 + 128)
                P.op("dve", lambda e, i=i, j0=j0, j1=j1: e.tensor_copy(out=Vr[i][:, j0:j1, 64].bitcast(F32R), in_=ones[:, 0:j1 - j0]), reads=[ones], writes=[Vr[i]])
        PT = [P.sbuf(f"daPT{i}", [128, 512]) for i in range(3)]
        accs = [P.sbuf(f"daaccs{i}", [65, 512]) for i in range(2)]
        Ot = [P.sbuf(f"daOt{i}", [128, 4, 64]) for i in range(2)]
        o2 = P.sbuf("dao2", [128, 4, 64]); sq = P.sbuf("dasq2", [128, 4, 64])
        sm = [P.sbuf(f"dasm{i}", [128, 4, 4]) for i in range(2)]
        ps = [P.psum(f"daps{i}", [128, 512]) for i in range(2)]
        acc = [P.psum(f"daacc{i}", [65, 512]) for i in range(2)]
        pTn = [P.psum(f"dapTn{i}", [128, 4, 65]) for i in range(2)]
        cnt = {"ps": 0, "pt": 0, "blk": 0}
        for b in range(NB):
            for h in range(8):
                it = b * 8 + h
                qth, kth, vst, vr = QTh[it % 2], KTh[it % 2], Vst[it % 2], Vr[it % 2]
                P.dma("sp", qth[:], QKT[b, 0, h], reads=[("QKT", b, 0, tt) for tt in range(NTS)], writes=[qth])
                P.dma("sp", kth[:], QKT[b, 1, h], reads=[("QKT", b, 1, tt) for tt in range(NTS)], writes=[kth])
                c0 = A_IN + 1024 + h * 64
                P.dma("sp", vst[:], U[b * cfg.TS:(b + 1) * cfg.TS, c0:c0 + 64].rearrange("(n p) d -> p n d", p=128),
                      reads=[k_ for tt in range(NTS) for k_ in ukeys(b * NTS + tt)], writes=[vst])
                P.op("pool", lambda e, qth=qth: e.tensor_copy(out=qth[:].bitcast(F32R), in_=qth[:]), reads=[qth], writes=[qth])
                P.op("pool", lambda e, kth=kth: e.tensor_copy(out=kth[:].bitcast(F32R), in_=kth[:]), reads=[kth], writes=[kth])
                P.op("pool", lambda e, vst=vst, vr=vr: e.tensor_copy(out=vr[:, :, 0:64].bitcast(F32R), in_=vst[:]), reads=[vst], writes=[vr])
                blocks = [(0, TC, list(range(TC // 128)))]
                q0 = TC
                while q0 < cfg.TS:
                    nq = min(512, cfg.TS - q0)
                    blocks.append((q0, nq, list(range(NTS))))
                    q0 += nq
                for (q0, nq, kts) in blocks:
                    nsub = nq // 128
                    for c in range(2):
                        for ki, kt in enumerate(kts):
                            p_ = ps[cnt["ps"] % 2]; cnt["ps"] += 1
                            pt_ = PT[cnt["pt"] % 3]; cnt["pt"] += 1
                            P.op("pe", lambda e, p_=p_, kth=kth, qth=qth, c=c, kt=kt, q0=q0, nq=nq: e.matmul(
                                p_[:, 0:nq], lhsT=kth[c * 32:(c + 1) * 32, kt * 128:(kt + 1) * 128].bitcast(F32R), rhs=qth[c * 32:(c + 1) * 32, q0:q0 + nq].bitcast(F32R),
                                start=True, stop=True), reads=[kth, qth], writes=[p_])
                            P.op("act", lambda e, p_=p_, pt_=pt_, nq=nq: e.activation(out=pt_[:, 0:nq].bitcast(F32R), in_=p_[:, 0:nq], func=AF.Exp, scale=SC, bias=negc[:, 0:1]),
                                 reads=[p_, negc], writes=[pt_])
                            P.op("pe", lambda e, pt_=pt_, vr=vr, c=c, kt=kt, nq=nq, ki=ki, nk=len(kts): e.matmul(
                                acc[c][:, 0:nq], lhsT=vr[:, kt, :].bitcast(F32R), rhs=pt_[:, 0:nq].bitcast(F32R), start=(ki == 0), stop=(ki == nk - 1)),
                                reads=[vr, pt_], writes=[acc[c]])
                        if c == 0:
                            P.op("dve", lambda e, c=c, nq=nq: e.tensor_copy(out=accs[c][:, 0:nq], in_=acc[c][:, 0:nq]), reads=[acc[c]], writes=[accs[c]])
                        else:
                            P.op("dve", lambda e, c=c, nq=nq: e.tensor_copy(out=accs[c][:, 0:nq], in_=acc[c][:, 0:nq]), reads=[acc[c]], writes=[accs[c]])
                        for j in range(nsub):
                            P.op("pe", lambda e, c=c, j=j: e.transpose(out=pTn[c][:, j, :], in_=accs[c][:, j * 128:(j + 1) * 128], identity=ident[0:65, 0:65]),
                                 reads=[accs[c], ident], writes=[pTn[c]])
                    bi = cnt["blk"]; cnt["blk"] += 1
                    ot = Ot[bi % 2]; sm_ = sm[bi % 2]
                    n_ = nsub
                    P.op("dve", lambda e, sm_=sm_, n_=n_: e.reciprocal(out=sm_[:, 0, 0:n_], in_=pTn[0][:, 0:n_, 64]), reads=[pTn[0]], writes=[sm_])
                    P.op("dve", lambda e, sm_=sm_, n_=n_: e.reciprocal(out=sm_[:, 1, 0:n_], in_=pTn[1][:, 0:n_, 64]), reads=[pTn[1]], writes=[sm_])
                    P.op("dve", lambda e, sm_=sm_, n_=n_: e.tensor_scalar(out=sm_[:, 1, 0:n_], in0=sm_[:, 1, 0:n_], scalar1=nlam[:, 0:1], scalar2=None, op0=ALU.mult), reads=[sm_, nlam], writes=[sm_])
                    P.op("dve", lambda e, sm_=sm_, n_=n_, ot=ot: e.tensor_tensor(out=ot[:, 0:n_, :], in0=pTn[0][:, 0:n_, 0:64], in1=sm_[:, 0, 0:n_].unsqueeze(2).to_broadcast([128, n_, 64]), op=ALU.mult),
                         reads=[pTn[0], sm_], writes=[ot])
                    P.op("dve", lambda e, sm_=sm_, n_=n_: e.tensor_tensor(out=o2[:, 0:n_, :], in0=pTn[1][:, 0:n_, 0:64], in1=sm_[:, 1, 0:n_].unsqueeze(2).to_broadcast([128, n_, 64]), op=ALU.mult),
                         reads=[pTn[1], sm_], writes=[o2])
                    P.op("pool", lambda e, n_=n_, ot=ot: e.tensor_tensor(out=ot[:, 0:n_, :], in0=ot[:, 0:n_, :], in1=o2[:, 0:n_, :], op=ALU.add), reads=[ot, o2], writes=[ot])
                    P.op("act", lambda e, n_=n_, ot=ot: e.activation(out=sq[:, 0:n_, :], in_=ot[:, 0:n_, :], func=AF.Square), reads=[ot], writes=[sq])
                    P.op("dve", lambda e, sm_=sm_, n_=n_: e.tensor_reduce(out=sm_[:, 2, 0:n_], in_=sq[:, 0:n_, :], axis=AX.X, op=ALU.add), reads=[sq], writes=[sm_])
                    P.op("act", lambda e, sm_=sm_, n_=n_: e.activation(out=sm_[:, 2, 0:n_], in_=sm_[:, 2, 0:n_], func=AF.Sqrt, scale=1.0 / 64, bias=eps2[:, 0:1]), reads=[sm_, eps2], writes=[sm_])
                    P.op("dve", lambda e, sm_=sm_, n_=n_: e.reciprocal(out=sm_[:, 2, 0:n_], in_=sm_[:, 2, 0:n_]), reads=[sm_], writes=[sm_])
                    P.op("dve", lambda e, sm_=sm_, n_=n_, ot=ot: e.tensor_tensor(out=ot[:, 0:n_, :], in0=ot[:, 0:n_, :], in1=sm_[:, 2, 0:n_].unsqueeze(2).to_broadcast([128, n_, 64]), op=ALU.mult),
                         reads=[ot, sm_], writes=[ot])
                    P.op("pool", lambda e, n_=n_, ot=ot: e.tensor_tensor(out=ot[:, 0:n_, :], in0=ot[:, 0:n_, :], in1=gs[:].unsqueeze(1).to_broadcast([128, n_, 64]), op=ALU.mult),
                         reads=[ot, gs], writes=[ot])
                    r0 = b * cfg.TS + q0
                    t0 = r0 // 128
                    P.dma("act", O[r0:r0 + nq, 512 + h * 64: 512 + (h + 1) * 64].rearrange("(n p) d -> p n d", p=128), ot[:, 0:n_, :], reads=[ot],
                          writes=[("O", t0 + j, 1) for j in range(n_)])
        P.pop()

    def gla(l):
        need_ctx = (l == 0) or ("ctx_out" in dbg)
        NB = cfg.NB
        c0 = C_IN
        P.push()
        gw2 = P.sbuf("glgw2", [16, 2, 256]); gbr = P.sbuf("glgb", [1, 2, 256]); ng_bc = P.sbuf("glng", [128, 128])
        onec = P.sbuf("glone", [128, 1]); eps_c = P.sbuf("gleps", [128, 1])
        P.dma("sp", gw2[:], io["gla_gate_w2"].rearrange("d k n -> k d n"), writes=[gw2])
        P.dma("sp", gbr[:], io["gla_gate_b"].rearrange("(o d) n -> o d n", o=1), writes=[gbr])
        P.dma("sp", ng_bc[:], io["gla_norm_g"].to_broadcast([128, 128]), writes=[ng_bc])
        P.op("dve", lambda e: e.memset(onec[:], 1.0), writes=[onec])
        P.op("dve", lambda e: e.memset(eps_c[:], EPS), writes=[eps_c])
        B = []
        for b in range(NB):
            d = {}
            d["ud"] = [P.sbuf(f"glud{b}{i}", [128, D_IN]) for i in range(2)]
            d["gdT"] = P.sbuf(f"glgdT{b}", [16, 128])
            d["nla"] = P.sbuf(f"glnla{b}", [128, 256])
            d["Kh"] = P.sbuf(f"glKh{b}", [128, 256])
            d["eq"] = P.sbuf(f"gleq{b}", [128, 2, 128]); d["ek"] = P.sbuf(f"glek{b}", [128, 2, 128])
            d["QT"] = P.sbuf(f"glQT{b}", [128, 2, 128]); d["KT"] = P.sbuf(f"glKT{b}", [128, 2, 128])
            d["AT"] = [P.sbuf(f"glAT{b}{i}", [128, 128]) for i in range(2)]
            d["S"] = P.sbuf(f"glS{b}", [128, 2, 128])
            d["od"] = P.sbuf(f"glod{b}", [128, 512]); d["of"] = P.sbuf(f"glof{b}", [128, 512])
            d["sq"] = P.sbuf(f"glsq{b}", [128, 512]); d["sm"] = P.sbuf(f"glsm{b}", [128, 4])
            d["sg"] = P.sbuf(f"glsg{b}", [128, 512])
            B.append(d)
        pg = P.psum("glpg", [16, 128]); pz = P.psum("glpz", [128, 256]); pcs = P.psum("glpcs", [128, 2, 256])
        pbT = P.psum("glpbT", [128, 2, 128]); pqk = P.psum("glpqk", [128, 4, 128]); pa = P.psum("glpa", [128, 128])
        po = P.psum("glpo", [128, 4, 128]); pc = P.psum("glpc", [128, 256])
        nct = TC // 128
        for dr in range(2):
            mk = masks[:, dr, :]
            mks = masks[:, 4 + dr, :]
            lastcol = 127 if dr == 0 else 0
            for b in range(NB):
                P.op("dve", lambda e, b=b: e.memset(B[b]["S"][:], 0.0), writes=[B[b]["S"]])
            order = list(range(NTS)) if dr == 0 else [1, 0] + list(range(NTS - 1, 1, -1))
            for step, tt in enumerate(order):
                want_out = need_ctx or tt >= nct
                for b in range(NB):
                    d = B[b]
                    t = b * NTS + tt
                    ud = d["ud"][step % 2]
                    gdT, nla, Kh, eq, ek, QT, KT, S = d["gdT"], d["nla"], d["Kh"], d["eq"], d["ek"], d["QT"], d["KT"], d["S"]
                    P.dma("sp", ud[:], U[t * 128:(t + 1) * 128, c0:c0 + D_IN], reads=ukeys(t), writes=[ud])
                    gc = 1536 + 16 * dr
                    P.op("pe", lambda e, ud=ud, gc=gc: e.transpose(out=pg[:], in_=ud[:, gc:gc + 16], identity=ident[:]), reads=[ud, ident], writes=[pg])
                    P.op("act", lambda e, gdT=gdT: e.activation(out=gdT[:], in_=pg[:], func=AF.Copy), reads=[pg], writes=[gdT])
                    P.op("pe", lambda e, gdT=gdT, dr=dr: e.matmul(pz[:], lhsT=gdT[:], rhs=gw2[:, dr, :], start=True, stop=False), reads=[gdT, gw2], writes=[pz])
                    P.op("pe", lambda e, dr=dr: e.matmul(pz[:], lhsT=ones[0:1, :], rhs=gbr[0:1, dr, :], start=False, stop=True), reads=[ones, gbr], writes=[pz])
                    P.op("act", lambda e, nla=nla: e.activation(out=nla[:], in_=pz[:], func=AF.Exp, scale=-1.0), reads=[pz], writes=[nla])
                    P.op("act", lambda e, nla=nla: e.activation(out=nla[:], in_=nla[:], func=AF.Ln, bias=onec[:, 0:1]), reads=[nla, onec], writes=[nla])
                    P.op("pe", lambda e, nla=nla, mks=mks: e.matmul(pcs[:, 1, :], lhsT=mks, rhs=nla[:], start=True, stop=True), reads=[masks, nla], writes=[pcs])
                    P.op("act", lambda e, Kh=Kh: e.activation(out=Kh[:], in_=pcs[:, 1, :], func=AF.Exp, scale=-1.0 / 16), reads=[pcs], writes=[Kh])
                    P.op("dve", lambda e, Kh=Kh, ud=ud: e.tensor_tensor(out=Kh[:], in0=Kh[:], in1=ud[:, 256:512], op=ALU.mult), reads=[Kh, ud], writes=[Kh])
                    for p in range(2):
                        P.op("pe", lambda e, nla=nla, p=p, mk=mk: e.matmul(pbT[:, p, :], lhsT=nla[:, p * 128:(p + 1) * 128], rhs=mk, start=True, stop=True), reads=[nla, masks], writes=[pbT])
                    P.op("act", lambda e, eq=eq: e.activation(out=eq[:], in_=pbT[:], func=AF.Exp, scale=-1.0 / 16), reads=[pbT], writes=[eq])
                    P.op("act", lambda e, ek=ek: e.activation(out=ek[:], in_=pbT[:], func=AF.Exp, scale=1.0 / 16), reads=[pbT], writes=[ek])
                    for i in range(4):
                        P.op("pe", lambda e, ud=ud, i=i: e.transpose(out=pqk[:, i, :], in_=ud[:, i * 128:(i + 1) * 128], identity=ident[:]), reads=[ud, ident], writes=[pqk])
                    P.op("dve", lambda e, QT=QT, eq=eq: e.scalar_tensor_tensor(out=QT[:], in0=pqk[:, 0:2, :], scalar=0.125, in1=eq[:], op0=ALU.mult, op1=ALU.mult), reads=[pqk, eq], writes=[QT])
                    P.op("dve", lambda e, KT=KT, ek=ek: e.tensor_tensor(out=KT[:], in0=pqk[:, 2:4, :], in1=ek[:], op=ALU.mult), reads=[pqk, ek], writes=[KT])
                    for h in range(4):
                        p, ho = h // 2, (h % 2) * 64
                        if want_out:
                            AT = d["AT"][h % 2]
                            P.op("pe", lambda e, KT=KT, QT=QT, p=p, ho=ho: e.matmul(pa[:], lhsT=KT[ho:ho + 64, p, :], rhs=QT[ho:ho + 64, p, :], start=True, stop=True), reads=[KT, QT], writes=[pa])
                            P.op("dve", lambda e, AT=AT, mk=mk: e.tensor_tensor(out=AT[:], in0=pa[:], in1=mk, op=ALU.mult), reads=[pa, masks], writes=[AT])
                            P.op("pe", lambda e, AT=AT, ud=ud, h=h: e.matmul(po[:, h, :], lhsT=AT[:], rhs=ud[:, 512 + h * 128: 512 + (h + 1) * 128], start=True, stop=False), reads=[AT, ud], writes=[po])
                            P.op("pe", lambda e, QT=QT, S=S, h=h, p=p, ho=ho: e.matmul(po[:, h, :], lhsT=QT[ho:ho + 64, p, :], rhs=S[ho:ho + 64, p, :], start=False, stop=True), reads=[QT, S], writes=[po])
                    od = d["od"]
                    if want_out:
                        P.op("act", lambda e, od=od: e.activation(out=od[:], in_=po[:].rearrange("p h d -> p (h d)"), func=AF.Copy), reads=[po], writes=[od])
                    for p in range(2):
                        P.op("pe", lambda e, Kh=Kh, ud=ud, p=p: e.matmul(pc[:], lhsT=Kh[:, p * 128:(p + 1) * 128], rhs=ud[:, 512 + p * 256: 512 + (p + 1) * 256], start=True, stop=True), reads=[Kh, ud], writes=[pc])
                        for hi in range(2):
                            ho = hi * 64
                            P.op("dve", lambda e, S=S, eq=eq, p=p, ho=ho, hi=hi, lastcol=lastcol: e.scalar_tensor_tensor(out=S[ho:ho + 64, p, :], in0=S[ho:ho + 64, p, :], scalar=eq[ho:ho + 64, p, lastcol:lastcol + 1],
                                                                                                 in1=pc[ho:ho + 64, hi * 128:(hi + 1) * 128], op0=ALU.mult, op1=ALU.add), reads=[S, eq, pc], writes=[S])
                    if not want_out:
                        continue
                    if dr == 0:
                        P.dma("act", YS[t * 128:(t + 1) * 128, 0:512], od[:], reads=[od], writes=[("YS", t)])
                    else:
                        of, sq, sm, sg = d["of"], d["sq"], d["sm"], d["sg"]
                        P.dma("sp", of[:], YS[t * 128:(t + 1) * 128, 0:512], reads=[("YS", t)], writes=[of])
                        P.op("pool", lambda e, of=of, od=od: e.tensor_tensor(out=of[:], in0=of[:], in1=od[:], op=ALU.add), reads=[of, od], writes=[of])
                        P.op("act", lambda e, of=of, sq=sq: e.activation(out=sq[:], in_=of[:], func=AF.Square), reads=[of], writes=[sq])
                        P.op("dve", lambda e, sq=sq, sm=sm: e.tensor_reduce(out=sm[:], in_=sq[:].rearrange("p (h d) -> p h d", h=4), axis=AX.X, op=ALU.add), reads=[sq], writes=[sm])
                        P.op("act", lambda e, sm=sm: e.activation(out=sm[:], in_=sm[:], func=AF.Sqrt, scale=1.0 / 128, bias=eps_c[:, 0:1]), reads=[sm, eps_c], writes=[sm])
                        P.op("dve", lambda e, sm=sm: e.reciprocal(out=sm[:], in_=sm[:]), reads=[sm], writes=[sm])
                        P.op("dve", lambda e, of=of, sm=sm: e.tensor_tensor(out=of[:].rearrange("p (h d) -> p h d", h=4), in0=of[:].rearrange("p (h d) -> p h d", h=4),
                                                                      in1=sm[:].unsqueeze(2).to_broadcast([128, 4, 128]), op=ALU.mult), reads=[of, sm], writes=[of])
                        P.op("pool", lambda e, of=of: e.tensor_tensor(out=of[:].rearrange("p (h d) -> p h d", h=4), in0=of[:].rearrange("p (h d) -> p h d", h=4),
                                                                  in1=ng_bc[:].unsqueeze(1).to_broadcast([128, 4, 128]), op=ALU.mult), reads=[of, ng_bc], writes=[of])
                        P.op("act", lambda e, ud=ud, sg=sg: e.activation(out=sg[:], in_=ud[:, 1024:1536], func=AF.Silu), reads=[ud], writes=[sg])
                        P.op("dve", lambda e, of=of, sg=sg: e.tensor_tensor(out=sg[:], in0=of[:], in1=sg[:], op=ALU.mult), reads=[of, sg], writes=[sg])
                        P.dma("act", O[t * 128:(t + 1) * 128, 512:1024], sg[:], reads=[sg], writes=[("O", t, 1)])
        P.pop()

    E05 = math.exp(-0.5)

    def rwkv(l):
        need_ctx = (l == 0) or ("ctx_out" in dbg)
        NB = cfg.NB
        nct = TC // 128
        P.push()
        mu0 = P.sbuf("rwmu0", [128, C_IN]); mu1 = P.sbuf("rwmu1", [128, C_IN]); muc = P.sbuf("rwmuc", [128, C_IN])
        kv0 = P.sbuf("rwkv0", [128, 512]); tiny = P.sbuf("rwtiny", [128, 1])
        P.dma("sp", mu0[:], io["rwkv_mu"][0:1, :].to_broadcast([128, C_IN]), writes=[mu0])
        P.dma("sp", mu1[:], io["rwkv_mu"][1:2, :].to_broadcast([128, C_IN]), writes=[mu1])
        P.dma("sp", kv0[:], io["rwkv_kvec"][0:1, :].to_broadcast([128, 512]), writes=[kv0])
        P.op("dve", lambda e: e.tensor_tensor(out=muc[:], in0=mu0[:], in1=mu1[:], op=ALU.add), reads=[mu0, mu1], writes=[muc])
        P.op("dve", lambda e: e.tensor_scalar(out=muc[:], in0=muc[:], scalar1=-1.0, scalar2=1.0, op0=ALU.mult, op1=ALU.add), reads=[muc], writes=[muc])
        cur = [P.sbuf(f"rwcur{i}", [128, C_IN]) for i in range(2)]
        prv = [P.sbuf(f"rwprv{i}", [128, C_IN]) for i in range(2)]
        nxt = [P.sbuf(f"rwnxt{i}", [128, C_IN]) for i in range(2)]
        rwt = [P.sbuf(f"rwrwt{i}", [128, 2432]) for i in range(2)]
        t1 = P.sbuf("rwt1", [128, C_IN]); sqk = P.sbuf("rwsqk", [128, 512]); ks = [P.sbuf(f"rwks{i}", [128, 8]) for i in range(2)]
        for t in range(cfg.ntile):
            b, tt = divmod(t, NTS)
            r0 = t * 128
            cu, pv, nx, rt, ks_ = cur[t % 2], prv[t % 2], nxt[t % 2], rwt[t % 2], ks[t % 2]
            seg_start = tt in (0, nct)
            seg_end = tt in (nct - 1, NTS - 1)
            P.dma("sp", cu[:], U[r0:r0 + 128, 0:C_IN], reads=ukeys(t), writes=[cu])
            if seg_start:
                P.op("pool", lambda e, pv=pv: e.memset(pv[:], 0.0), writes=[pv])
                P.dma("sp", pv[1:128, :], U[r0:r0 + 127, 0:C_IN], reads=ukeys(t), writes=[pv])
            else:
                P.dma("sp", pv[:], U[r0 - 1:r0 + 127, 0:C_IN], reads=ukeys(t) + ukeys(t - 1), writes=[pv])
            if seg_end:
                P.op("pool", lambda e, nx=nx: e.memset(nx[:], 0.0), writes=[nx])
                P.dma("sp", nx[0:127, :], U[r0 + 1:r0 + 128, 0:C_IN], reads=ukeys(t), writes=[nx])
            else:
                P.dma("sp", nx[:], U[r0 + 1:r0 + 129, 0:C_IN], reads=ukeys(t) + ukeys(t + 1), writes=[nx])
            us = rt[:, 0:C_IN]
            P.op("dve", lambda e, cu=cu, us=us: e.tensor_tensor(out=us, in0=cu[:], in1=muc[:], op=ALU.mult), reads=[cu, muc], writes=[rt])
            P.op("pool", lambda e, pv=pv: e.tensor_tensor(out=pv[:], in0=pv[:], in1=mu0[:], op=ALU.mult), reads=[pv, mu0], writes=[pv])
            P.op("pool", lambda e, nx=nx: e.tensor_tensor(out=nx[:], in0=nx[:], in1=mu1[:], op=ALU.mult), reads=[nx, mu1], writes=[nx])
            P.op("dve", lambda e, pv=pv, us=us: e.tensor_tensor(out=us, in0=us, in1=pv[:], op=ALU.add), reads=[rt, pv], writes=[rt])
            P.op("dve", lambda e, nx=nx, us=us: e.tensor_tensor(out=us, in0=us, in1=nx[:], op=ALU.add), reads=[rt, nx], writes=[rt])
            kkc = rt[:, 1920:2432]
            P.op("pool", lambda e, rt=rt, kkc=kkc: e.tensor_tensor(out=kkc, in0=rt[:, 512:1024], in1=kv0[:], op=ALU.mult), reads=[rt, kv0], writes=[rt])
            P.op("act", lambda e, kkc=kkc: e.activation(out=sqk[:], in_=kkc, func=AF.Square), reads=[rt], writes=[sqk])
            P.op("dve", lambda e, ks_=ks_: e.tensor_reduce(out=ks_[:], in_=sqk[:].rearrange("p (h d) -> p h d", h=8), axis=AX.X, op=ALU.add), reads=[sqk], writes=[ks_])
            P.op("dve", lambda e, ks_=ks_: e.tensor_scalar_max(out=ks_[:], in0=ks_[:], scalar1=1e-24), reads=[ks_], writes=[ks_])
            P.op("act", lambda e, ks_=ks_: e.activation(out=ks_[:], in_=ks_[:], func=AF.Sqrt), reads=[ks_], writes=[ks_])
            P.op("dve", lambda e, ks_=ks_: e.reciprocal(out=ks_[:], in_=ks_[:]), reads=[ks_], writes=[ks_])
            P.op("dve", lambda e, kkc=kkc, ks_=ks_: e.tensor_tensor(out=kkc.rearrange("p (h d) -> p h d", h=8), in0=kkc.rearrange("p (h d) -> p h d", h=8),
                                                              in1=ks_[:].unsqueeze(2).to_broadcast([128, 8, 64]), op=ALU.mult), reads=[rt, ks_], writes=[rt])
            P.dma("act", RW[r0:r0 + 128, :], rt[:], reads=[rt], writes=[("RW", t)])
        P.pop()
        if "rwA" in dbg:
            return
        P.push()
        w2s = P.sbuf("rww2", [64, 2, 512]); a2s = P.sbuf("rwa2", [64, 2, 512]); w0r = P.sbuf("rww0", [1, 2, 512]); a0r = P.sbuf("rwa0", [1, 2, 512])
        g2s = P.sbuf("rwg2", [128, 512]); kv1 = P.sbuf("rwkv1", [128, 512]); omk1 = P.sbuf("rwomk1", [128, 512]); kv2 = P.sbuf("rwkv2", [128, 512])
        ln0 = P.sbuf("rwln0", [128, 512]); ln1 = P.sbuf("rwln1", [128, 512]); lneps = P.sbuf("rwlneps", [128, 1])
        P.dma("sp", w2s[:], io["rwkv_w2"].rearrange("d k n -> k d n"), writes=[w2s])
        P.dma("sp", a2s[:], io["rwkv_a2"].rearrange("d k n -> k d n"), writes=[a2s])
        P.dma("sp", w0r[:], io["rwkv_w0"].rearrange("(o d) n -> o d n", o=1), writes=[w0r])
        P.dma("sp", a0r[:], io["rwkv_a0"].rearrange("(o d) n -> o d n", o=1), writes=[a0r])
        P.dma("sp", g2s[:], io["rwkv_g2"], writes=[g2s])
        P.dma("sp", kv1[:], io["rwkv_kvec"][1:2, :].to_broadcast([128, 512]), writes=[kv1])
        P.dma("sp", kv2[:], io["rwkv_kvec"][2:3, :].to_broadcast([128, 512]), writes=[kv2])
        P.dma("sp", ln0[:], io["rwkv_ln"][0:1, :].to_broadcast([128, 512]), writes=[ln0])
        P.dma("sp", ln1[:], io["rwkv_ln"][1:2, :].to_broadcast([128, 512]), writes=[ln1])
        P.op("dve", lambda e: e.tensor_scalar(out=omk1[:], in0=kv1[:], scalar1=-1.0, scalar2=1.0, op0=ALU.mult, op1=ALU.add), reads=[kv1], writes=[omk1])
        P.op("dve", lambda e: e.memset(lneps[:], 64e-5), writes=[lneps])
        nm = ["sg", "a_", "tt_", "kd", "beta", "eI", "eInv", "eE", "eR", "Rt", "Kt", "Bt", "At", "Kh", "nBh", "tmp"]
        T = {n: P.sbuf("rw_" + n, [128, 512]) for n in nm}
        twT = P.sbuf("rwtwT", [64, 128]); adT = P.sbuf("rwadT", [64, 128]); WL = P.sbuf("rwWL", [128, 4, 2])
        Acur = [P.sbuf(f"rwA{i}", [128, 128]) for i in range(2)]; Atc = [P.sbuf(f"rwAt{i}", [128, 128]) for i in range(2)]
        Pc = [P.sbuf(f"rwP{i}", [128, 128]) for i in range(2)]
        Th = [P.sbuf(f"rwTh{i}", [128, 128]) for i in range(2)]; Mh = [P.sbuf(f"rwMh{i}", [128, 128]) for i in range(2)]
        G3h = [P.sbuf(f"rwG3h{i}", [128, 128]) for i in range(2)]; nG4h = [P.sbuf(f"rwG4h{i}", [128, 128]) for i in range(2)]
        sgT = P.sbuf("rwsgT", [128, 128])
        B = []
        for b in range(NB):
            d = {}
            d["rw"] = P.sbuf(f"rwrw{b}", [128, 2432])
            d["FT"] = P.sbuf(f"rwFT{b}", [128, 4, 4, 128])
            d["ST"] = P.sbuf(f"rwST{b}", [128, 4, 64])
            d["XT"] = P.sbuf(f"rwXT{b}", [128, 2, 64]); d["UT"] = P.sbuf(f"rwUT{b}", [128, 2, 64])
            d["y"] = P.sbuf(f"rwy{b}", [128, 512]); d["bd"] = P.sbuf(f"rwbd{b}", [128, 8])
            d["yf"] = P.sbuf(f"rwyf{b}", [128, 520]); d["sm"] = P.sbuf(f"rwsm{b}", [128, 3, 8])
            B.append(d)
        pw = P.psum("rwpw", [128, 264]); pz = P.psum("rwpz", [128, 512]); pcw = P.psum("rwpcw", [128, 512]); pft = P.psum("rwpft", [128, 4, 128])
        pG = [P.psum(f"rwpG{i}", [128, 128]) for i in range(2)]; pxy = P.psum("rwpxy", [128, 3, 2, 64]); pS = P.psum("rwpS", [128, 128])
        gcnt = {"g": 0, "e": 0}

        def gmm(lhsT, rhs, reads):
            pg_ = pG[gcnt["g"] % 2]; gcnt["g"] += 1
            P.op("pe", lambda e, pg_=pg_, lhsT=lhsT, rhs=rhs: e.matmul(pg_[:], lhsT=lhsT, rhs=rhs, start=True, stop=True), reads=reads, writes=[pg_])
            return pg_

        def evac_copy(dst, src):
            gcnt["e"] += 1
            if gcnt["e"] % 2:
                P.op("act", lambda e, dst=dst, src=src: e.activation(out=dst[:], in_=src[:], func=AF.Copy), reads=[src], writes=[dst])
            else:
                P.op("dve", lambda e, dst=dst, src=src: e.tensor_copy(out=dst[:], in_=src[:]), reads=[src], writes=[dst])

        for dr in range(2):
            for b in range(NB):
                P.op("dve", lambda e, b=b: e.memset(B[b]["ST"][:], 0.0), writes=[B[b]["ST"]])
            order = list(range(NTS)) if dr == 0 else [1, 0] + list(range(NTS - 1, 1, -1))
            m_incl = masks[:, 2 + dr, :]; m_strict = masks[:, 6 + dr, :]; m_strictT = masks[:, 7 - dr, :]
            chunks = (0, 1) if dr == 0 else (1, 0)
            for step, tt in enumerate(order):
                want_out = need_ctx or tt >= nct
                for b in range(NB):
                    d = B[b]
                    t = b * NTS + tt
                    rw, FT, ST, XT, UT, y, bd, sm = d["rw"], d["FT"], d["ST"], d["XT"], d["UT"], d["y"], d["bd"], d["sm"]
                    P.dma("sp", rw[:], RW[t * 128:(t + 1) * 128, :], reads=[("RW", t)], writes=[rw])
                    wc = 1536 + 64 * dr; ac = 1664 + 64 * dr
                    P.op("pe", lambda e, rw=rw, wc=wc: e.transpose(out=pw[0:64, 0:128], in_=rw[:, wc:wc + 64], identity=ident[:]), reads=[rw, ident], writes=[pw])
                    P.op("pe", lambda e, rw=rw, ac=ac: e.transpose(out=pw[0:64, 128:256], in_=rw[:, ac:ac + 64], identity=ident[:]), reads=[rw, ident], writes=[pw])
                    P.op("act", lambda e: e.activation(out=twT[:], in_=pw[0:64, 0:128], func=AF.Tanh), reads=[pw], writes=[twT])
                    P.op("dve", lambda e: e.tensor_copy(out=adT[:], in_=pw[0:64, 128:256]), reads=[pw], writes=[adT])
                    P.op("pe", lambda e, dr=dr: e.matmul(pz[:], lhsT=twT[:], rhs=w2s[:, dr, :], start=True, stop=False), reads=[twT, w2s], writes=[pz])
                    P.op("pe", lambda e, dr=dr: e.matmul(pz[:], lhsT=ones[0:1, :], rhs=w0r[0:1, dr, :], start=False, stop=True), reads=[ones, w0r], writes=[pz])
                    P.op("act", lambda e: e.activation(out=T["sg"][:], in_=pz[:], func=AF.Sigmoid), reads=[pz], writes=[T["sg"]])
                    P.op("pe", lambda e, dr=dr: e.matmul(pz[:], lhsT=adT[:], rhs=a2s[:, dr, :], start=True, stop=False), reads=[adT, a2s], writes=[pz])
                    P.op("pe", lambda e, dr=dr: e.matmul(pz[:], lhsT=ones[0:1, :], rhs=a0r[0:1, dr, :], start=False, stop=True), reads=[ones, a0r], writes=[pz])
                    P.op("act", lambda e: e.activation(out=T["a_"][:], in_=pz[:], func=AF.Sigmoid), reads=[pz], writes=[T["a_"]])
                    P.op("pe", lambda e, m_incl=m_incl: e.matmul(pcw[:], lhsT=m_incl, rhs=T["sg"][:], start=True, stop=True), reads=[masks, T["sg"]], writes=[pcw])
                    P.op("act", lambda e: e.activation(out=T["eI"][:], in_=pcw[:], func=AF.Exp, scale=-E05), reads=[pcw], writes=[T["eI"]])
                    P.op("act", lambda e: e.activation(out=T["eInv"][:], in_=pcw[:], func=AF.Exp, scale=E05), reads=[pcw], writes=[T["eInv"]])
                    P.op("dve", lambda e: e.tensor_tensor(out=T["tmp"][:], in0=pcw[:], in1=T["sg"][:], op=ALU.subtract), reads=[pcw, T["sg"]], writes=[T["tmp"]])
                    P.op("act", lambda e: e.activation(out=T["eE"][:], in_=T["tmp"][:], func=AF.Exp, scale=-E05), reads=[T["tmp"]], writes=[T["eE"]])
                    P.op("pe", lambda e, m_strictT=m_strictT: e.matmul(pcw[:], lhsT=m_strictT, rhs=T["sg"][:], start=True, stop=True), reads=[masks, T["sg"]], writes=[pcw])
                    P.op("act", lambda e: e.activation(out=T["eR"][:], in_=pcw[:], func=AF.Exp, scale=-E05), reads=[pcw], writes=[T["eR"]])
                    for p in range(4):
                        P.op("pe", lambda e, p=p: e.matmul(pw[:, 256 + 2 * p:258 + 2 * p], lhsT=T["sg"][:, p * 128:(p + 1) * 128], rhs=masks[:, 8, 0:2], start=True, stop=True),
                             reads=[T["sg"], masks], writes=[pw])
                    P.op("act", lambda e: e.activation(out=WL[:].rearrange("p a c -> p (a c)"), in_=pw[:, 256:264], func=AF.Exp, scale=-E05), reads=[pw], writes=[WL])
                    r_, k_, v_, kk_ = rw[:, 0:512], rw[:, 512:1024], rw[:, 1024:1536], rw[:, 1920:2432]
                    P.op("dve", lambda e: e.tensor_tensor(out=T["tt_"][:], in0=T["a_"][:], in1=kv1[:], op=ALU.mult), reads=[T["a_"], kv1], writes=[T["tt_"]])
                    P.op("pool", lambda e: e.tensor_tensor(out=T["tt_"][:], in0=T["tt_"][:], in1=omk1[:], op=ALU.add), reads=[T["tt_"], omk1], writes=[T["tt_"]])
                    P.op("dve", lambda e, k_=k_: e.tensor_tensor(out=T["kd"][:], in0=k_, in1=T["tt_"][:], op=ALU.mult), reads=[rw, T["tt_"]], writes=[T["kd"]])
                    P.op("pool", lambda e, kk_=kk_: e.tensor_tensor(out=T["beta"][:], in0=T["a_"][:], in1=kk_, op=ALU.mult), reads=[rw, T["a_"]], writes=[T["beta"]])
                    P.op("dve", lambda e, r_=r_: e.tensor_tensor(out=T["Rt"][:], in0=r_, in1=T["eI"][:], op=ALU.mult), reads=[rw, T["eI"]], writes=[T["Rt"]])
                    P.op("pool", lambda e: e.tensor_tensor(out=T["Kt"][:], in0=T["kd"][:], in1=T["eInv"][:], op=ALU.mult), reads=[T["kd"], T["eInv"]], writes=[T["Kt"]])
                    P.op("dve", lambda e: e.tensor_tensor(out=T["Bt"][:], in0=T["beta"][:], in1=T["eInv"][:], op=ALU.mult), reads=[T["beta"], T["eInv"]], writes=[T["Bt"]])
                    P.op("pool", lambda e, kk_=kk_: e.tensor_tensor(out=T["At"][:], in0=kk_, in1=T["eE"][:], op=ALU.mult), reads=[rw, T["eE"]], writes=[T["At"]])
                    P.op("dve", lambda e: e.tensor_tensor(out=T["Kh"][:], in0=T["kd"][:], in1=T["eR"][:], op=ALU.mult), reads=[T["kd"], T["eR"]], writes=[T["Kh"]])
                    P.op("dve", lambda e: e.scalar_tensor_tensor(out=T["nBh"][:], in0=T["beta"][:], scalar=-1.0, in1=T["eR"][:], op0=ALU.mult, op1=ALU.mult), reads=[T["beta"], T["eR"]], writes=[T["nBh"]])
                    P.op("pool", lambda e, r_=r_: e.tensor_tensor(out=T["tmp"][:], in0=r_, in1=T["kd"][:], op=ALU.mult), reads=[rw, T["kd"]], writes=[T["tmp"]])
                    P.op("dve", lambda e: e.tensor_tensor(out=T["tmp"][:], in0=T["tmp"][:], in1=kv2[:], op=ALU.mult), reads=[T["tmp"], kv2], writes=[T["tmp"]])
                    P.op("dve", lambda e, bd=bd: e.tensor_reduce(out=bd[:], in_=T["tmp"][:].rearrange("p (h d) -> p h d", h=8), axis=AX.X, op=ALU.add), reads=[T["tmp"]], writes=[bd])
                    for ki, kn in enumerate(("Kt", "Bt", "At", "Rt")):
                        src = T[kn]
                        for p in range(4):
                            P.op("pe", lambda e, src=src, p=p: e.transpose(out=pft[:, p, :], in_=src[:, p * 128:(p + 1) * 128], identity=ident[:]), reads=[src, ident], writes=[pft])
                        if ki % 2:
                            P.op("act", lambda e, FT=FT, ki=ki: e.activation(out=FT[:, ki, :, :], in_=pft[:], func=AF.Copy), reads=[pft], writes=[FT])
                        else:
                            P.op("dve", lambda e, FT=FT, ki=ki: e.tensor_copy(out=FT[:, ki, :, :], in_=pft[:]), reads=[pft], writes=[FT])
                    KI, BI, AI, RI = 0, 1, 2, 3
                    for p in range(4 if "rwB1" not in dbg else 0):
                        for hi in range(2):
                            ho = hi * 64
                            fK, fB, fA, fR = (FT[ho:ho + 64, i, p, :] for i in (KI, BI, AI, RI))
                            A0, At0, P0 = Acur[0], Atc[0], Pc[0]
                            g = gmm(fB, fA, [FT])
                            P.op("dve", lambda e, g=g, A0=A0, m_strict=m_strict: e.tensor_tensor(out=A0[:], in0=g[:], in1=m_strict, op=ALU.mult), reads=[g, masks], writes=[A0])
                            P.op("pool", lambda e, A0=A0, P0=P0: e.tensor_tensor(out=P0[:], in0=ident[:], in1=A0[:], op=ALU.subtract), reads=[ident, A0], writes=[P0])
                            g = gmm(fA, fB, [FT])
                            P.op("dve", lambda e, g=g, At0=At0, m_strictT=m_strictT: e.tensor_tensor(out=At0[:], in0=g[:], in1=m_strictT, op=ALU.mult), reads=[g, masks], writes=[At0])
                            g = gmm(fK, fA, [FT])
                            P.op("dve", lambda e, g=g, hi=hi, m_strict=m_strict: e.tensor_tensor(out=Mh[hi][:], in0=g[:], in1=m_strict, op=ALU.mult), reads=[g, masks], writes=[Mh[hi]])
                            if want_out:
                                g = gmm(fK, fR, [FT])
                                P.op("dve", lambda e, g=g, hi=hi, m_incl=m_incl: e.tensor_tensor(out=G3h[hi][:], in0=g[:], in1=m_incl, op=ALU.mult), reads=[g, masks], writes=[G3h[hi]])
                                g = gmm(fB, fR, [FT])
                                P.op("dve", lambda e, g=g, hi=hi, m_incl=m_incl: e.scalar_tensor_tensor(out=nG4h[hi][:], in0=g[:], scalar=-1.0, in1=m_incl, op0=ALU.mult, op1=ALU.mult),
                                     reads=[g, masks], writes=[nG4h[hi]])
                            ia = 0
                            for kq in range(1, 6):
                                Ap, Atp, Pp = Acur[ia], Atc[ia], Pc[ia]
                                An, Atn = Acur[1 - ia], Atc[1 - ia]
                                Pn = Pc[1 - ia] if kq < 5 else Th[hi]
                                g = gmm(Ap[:], Atp[:], [Ap, Atp])
                                evac_copy(Atn, g)
                                if kq < 5:
                                    g = gmm(Atp[:], Ap[:], [Ap, Atp])
                                    evac_copy(An, g)
                                g = gmm(Atn[:], Pp[:], [Atn, Pp])
                                P.op("dve", lambda e, g=g, Pn=Pn, Pp=Pp: e.tensor_tensor(out=Pn[:], in0=g[:], in1=Pp[:], op=ALU.add), reads=[g, Pp], writes=[Pn])
                                ia = 1 - ia
                        for c in (chunks if "rwB2" not in dbg else ()):
                            rc0 = 64 * c
                            for hi in range(2):
                                ho = hi * 64; h = 2 * p + hi
                                P.op("pe", lambda e, FT=FT, ST=ST, ho=ho, p=p, hi=hi: e.matmul(pxy[:, 0, hi, :], lhsT=FT[ho:ho + 64, AI, p, :], rhs=ST[ho:ho + 64, p, :], start=True, stop=False),
                                     reads=[FT, ST], writes=[pxy], rg=ho)
                                P.op("pe", lambda e, rw=rw, hi=hi, h=h, rc0=rc0: e.matmul(pxy[:, 0, hi, :], lhsT=Mh[hi][rc0:rc0 + 64, :], rhs=rw[rc0:rc0 + 64, 1024 + h * 64:1024 + (h + 1) * 64], start=False, stop=True),
                                     reads=[Mh[hi], rw], writes=[pxy], rg=rc0)
                            P.op("act", lambda e, XT=XT, rc0=rc0: e.activation(out=XT[rc0:rc0 + 64], in_=pxy[rc0:rc0 + 64, 0], func=AF.Copy), reads=[pxy], writes=[XT])
                            for hi in range(2):
                                P.op("pe", lambda e, XT=XT, hi=hi, rc0=rc0: e.matmul(pxy[:, 1, hi, :], lhsT=Th[hi][rc0:rc0 + 64, :], rhs=XT[rc0:rc0 + 64, hi, :], start=True, stop=True),
                                     reads=[Th[hi], XT], writes=[pxy], rg=rc0)
                            P.op("dve", lambda e, UT=UT, rc0=rc0: e.tensor_copy(out=UT[rc0:rc0 + 64], in_=pxy[rc0:rc0 + 64, 1]), reads=[pxy], writes=[UT])
                            if want_out:
                                for hi in range(2):
                                    ho = hi * 64; h = 2 * p + hi
                                    P.op("pe", lambda e, FT=FT, ST=ST, ho=ho, p=p, hi=hi: e.matmul(pxy[:, 2, hi, :], lhsT=FT[ho:ho + 64, RI, p, :], rhs=ST[ho:ho + 64, p, :], start=True, stop=False),
                                         reads=[FT, ST], writes=[pxy], rg=ho)
                                    P.op("pe", lambda e, rw=rw, hi=hi, h=h, rc0=rc0: e.matmul(pxy[:, 2, hi, :], lhsT=G3h[hi][rc0:rc0 + 64, :], rhs=rw[rc0:rc0 + 64, 1024 + h * 64:1024 + (h + 1) * 64], start=False, stop=False),
                                         reads=[G3h[hi], rw], writes=[pxy], rg=rc0)
                                    P.op("pe", lambda e, UT=UT, hi=hi, rc0=rc0: e.matmul(pxy[:, 2, hi, :], lhsT=nG4h[hi][rc0:rc0 + 64, :], rhs=UT[rc0:rc0 + 64, hi, :], start=False, stop=True),
                                         reads=[nG4h[hi], UT], writes=[pxy], rg=rc0)
                                P.op("act", lambda e, y=y, rc0=rc0, p=p: e.activation(out=y[rc0:rc0 + 64, p * 128:(p + 1) * 128], in_=pxy[rc0:rc0 + 64, 2].rearrange("p a d -> p (a d)"), func=AF.Copy),
                                     reads=[pxy], writes=[y])
                            P.op("pe", lambda e, rw=rw, p=p, rc0=rc0: e.matmul(pS[:], lhsT=T["Kh"][rc0:rc0 + 64, p * 128:(p + 1) * 128], rhs=rw[rc0:rc0 + 64, 1024 + p * 128:1024 + (p + 1) * 128], start=True, stop=False),
                                 reads=[T["Kh"], rw], writes=[pS])
                            P.op("pe", lambda e, UT=UT, p=p, rc0=rc0: e.matmul(pS[:], lhsT=T["nBh"][rc0:rc0 + 64, p * 128:(p + 1) * 128], rhs=UT[rc0:rc0 + 64].rearrange("p a d -> p (a d)"), start=False, stop=True),
                                 reads=[T["nBh"], UT], writes=[pS])
                            for hi in range(2):
                                ho = hi * 64
                                P.op("dve", lambda e, ST=ST, ho=ho, p=p, c=c, hi=hi: e.scalar_tensor_tensor(out=ST[ho:ho + 64, p, :], in0=ST[ho:ho + 64, p, :], scalar=WL[ho:ho + 64, p, c:c + 1],
                                                                                                     in1=pS[ho:ho + 64, hi * 64:(hi + 1) * 64], op0=ALU.mult, op1=ALU.add), reads=[ST, WL, pS], writes=[ST])
                    if not want_out:
                        continue
                    if dr == 0:
                        P.dma("act", YS[t * 128:(t + 1) * 128, 0:512], y[:], reads=[y], writes=[("YS", t)])
                        P.dma("act", YS[t * 128:(t + 1) * 128, 512:520], bd[:], reads=[bd], writes=[("YSb", t)])
                    else:
                        yf = d["yf"]
                        P.dma("sp", yf[:], YS[t * 128:(t + 1) * 128, :], reads=[("YS", t), ("YSb", t)], writes=[yf])
                        y3 = y[:].rearrange("p (h d) -> p h d", h=8)
                        P.op("pool", lambda e, y=y, yf=yf: e.tensor_tensor(out=y[:], in0=y[:], in1=yf[:, 0:512], op=ALU.add), reads=[y, yf], writes=[y])
                        P.op("pool", lambda e, bd=bd, yf=yf: e.tensor_tensor(out=bd[:], in0=bd[:], in1=yf[:, 512:520], op=ALU.add), reads=[bd, yf], writes=[bd])
                        P.op("dve", lambda e, sm=sm, y3=y3: e.tensor_reduce(out=sm[:, 0, :], in_=y3, axis=AX.X, op=ALU.add), reads=[y], writes=[sm])
                        P.op("dve", lambda e, sm=sm: e.tensor_scalar_mul(out=sm[:, 0, :], in0=sm[:, 0, :], scalar1=-1.0 / 64), reads=[sm], writes=[sm])
                        P.op("dve", lambda e, sm=sm, y3=y3: e.tensor_tensor(out=y3, in0=y3, in1=sm[:, 0, :].unsqueeze(2).to_broadcast([128, 8, 64]), op=ALU.add), reads=[y, sm], writes=[y])
                        P.op("act", lambda e, y=y: e.activation(out=T["tmp"][:], in_=y[:], func=AF.Square), reads=[y], writes=[T["tmp"]])
                        P.op("dve", lambda e, sm=sm: e.tensor_reduce(out=sm[:, 1, :], in_=T["tmp"][:].rearrange("p (h d) -> p h d", h=8), axis=AX.X, op=ALU.add), reads=[T["tmp"]], writes=[sm])
                        P.op("act", lambda e, sm=sm: e.activation(out=sm[:, 1, :], in_=sm[:, 1, :], func=AF.Sqrt, scale=1.0 / 64, bias=lneps[:, 0:1]), reads=[sm, lneps], writes=[sm])
                        P.op("dve", lambda e, sm=sm: e.reciprocal(out=sm[:, 1, :], in_=sm[:, 1, :]), reads=[sm], writes=[sm])
                        P.op("dve", lambda e, sm=sm, y3=y3: e.tensor_tensor(out=y3, in0=y3, in1=sm[:, 1, :].unsqueeze(2).to_broadcast([128, 8, 64]), op=ALU.mult), reads=[y, sm], writes=[y])
                        P.op("pool", lambda e, y=y: e.tensor_tensor(out=y[:], in0=y[:], in1=ln0[:], op=ALU.mult), reads=[y, ln0], writes=[y])
                        P.op("pool", lambda e, y=y: e.tensor_tensor(out=y[:], in0=y[:], in1=ln1[:], op=ALU.add), reads=[y, ln1], writes=[y])
                        P.op("dve", lambda e, rw=rw, bd=bd: e.tensor_tensor(out=T["tmp"][:].rearrange("p (h d) -> p h d", h=8), in0=rw[:, 1024:1536].rearrange("p (h d) -> p h d", h=8),
                                                                      in1=bd[:].unsqueeze(2).to_broadcast([128, 8, 64]), op=ALU.mult), reads=[rw, bd], writes=[T["tmp"]])
                        P.op("pool", lambda e, y=y: e.tensor_tensor(out=y[:], in0=y[:], in1=T["tmp"][:], op=ALU.add), reads=[y, T["tmp"]], writes=[y])
                        P.op("pe", lambda e, rw=rw: e.transpose(out=pft[:, 0, :], in_=rw[:, 1792:1920], identity=ident[:]), reads=[rw, ident], writes=[pft])
                        P.op("act", lambda e: e.activation(out=sgT[:], in_=pft[:, 0, :], func=AF.Sigmoid), reads=[pft], writes=[sgT])
                        P.op("pe", lambda e: e.matmul(pz[:], lhsT=sgT[:], rhs=g2s[:], start=True, stop=True), reads=[sgT, g2s], writes=[pz])
                        P.op("dve", lambda e, y=y: e.tensor_tensor(out=y[:], in0=y[:], in1=pz[:], op=ALU.mult), reads=[y, pz], writes=[y])
                        P.dma("act", O[t * 128:(t + 1) * 128, 0:512], y[:], reads=[y], writes=[("O", t, 0)])
        P.pop()

    for l in range(2):
        nin = NIN[l]
        w_in = io["w_in_even"] if l == 0 else io["w_in_odd"]
        Xl = io["xin"] if l == 0 else X1
        if "U_in" not in dbg:
            P.push()
            MT = 4
            xts = [P.sbuf(f"p1x{i}", [128, D]) for i in range(3)]
            junk = P.sbuf("p1junk", [128, D])
            ss = [P.sbuf(f"p1ss{i}", [128, 1]) for i in range(3)]
            rstd = [P.sbuf(f"p1rs{i}", [128, 1]) for i in range(3)]
            xn = [P.sbuf(f"p1xn{i}", [128, D]) for i in range(2)]
            hxT = [P.sbuf(f"p1hxT{i}", [128, 8, MT * 128]) for i in range(2)]
            ptr = [P.psum(f"p1ptr{i}", [128, 4, 128]) for i in range(2)]
            wstg = [P.sbuf(f"p1ws{i}", [128, 8, 512]) for i in range(2)]
            wr = [P.sbuf(f"p1wr{i}", [128, 8, 512]) for i in range(2)]
            pu = [P.psum(f"p1pu{i}", [128, 512]) for i in range(4)]
            ut = [P.sbuf(f"p1ut{i}", [128, 512]) for i in range(4)]
            nblk = (nin + 511) // 512
            nmac = (cfg.ntile + MT - 1) // MT
            cnt = {"ti": 0, "ei": 0}

            def p1_norm(m):
                tiles = list(range(m * MT, min((m + 1) * MT, cfg.ntile)))
                hx = hxT[m % 2]
                for jj, t in enumerate(tiles):
                    cls = cfg.cls(t)
                    ti = cnt["ti"]; cnt["ti"] += 1
                    xt = xts[ti % 3]; s_ = ss[ti % 3]; r_ = rstd[ti % 3]; xn_ = xn[ti % 2]
                    P.dma("sp", xt[:], Xl[t * 128:(t + 1) * 128, :], reads=[("X", l, t)], writes=[xt])
                    P.op("act", lambda e, xt=xt, s_=s_: e.activation(out=junk[:], in_=xt[:], func=AF.Square, accum_out=s_[:]),
                         reads=[xt], writes=[junk, s_])
                    P.op("act", lambda e, s_=s_, r_=r_: e.activation(out=r_[:], in_=s_[:], func=AF.Sqrt, scale=1.0 / D, bias=epsc[:, 0:1]),
                         reads=[s_, epsc], writes=[r_])
                    P.op("dve", lambda e, r_=r_: e.reciprocal(out=r_[:], in_=r_[:]), reads=[r_], writes=[r_])
                    P.op("dve", lambda e, xt=xt, r_=r_, xn_=xn_: e.tensor_scalar(out=xn_[:], in0=xt[:], scalar1=r_[:, 0:1], scalar2=None, op0=ALU.mult),
                         reads=[xt, r_], writes=[xn_])
                    for half in range(2):
                        pt_ = ptr[half]
                        for kk in range(4):
                            k = half * 4 + kk
                            P.op("pe", lambda e, pt_=pt_, kk=kk, k=k, xn_=xn_: e.transpose(out=pt_[:, kk, :], in_=xn_[:, k * 128:(k + 1) * 128], identity=ident[:]),
                                 reads=[xn_, ident], writes=[pt_])
                        for kk in range(4):
                            k = half * 4 + kk
                            P.op("act", lambda e, pt_=pt_, kk=kk, k=k, hx=hx, jj=jj, cls=cls, ab=AB[l]: e.activation(
                                out=hx[:, k, jj * 128:(jj + 1) * 128].bitcast(F32R), in_=pt_[:, kk, :], func=AF.Identity,
                                scale=ab[:, 0, k, cls:cls + 1], bias=ab[:, 1, k, cls:cls + 1]),
                                reads=[pt_, AB[l]], writes=[hx])

            blocks = [(m, nb) for m in range(nmac) for nb in range(nblk)]

            def p1_loadw(i):
                m, nb = blocks[i]
                n0 = nb * 512
                nw = min(512, nin - n0)
                ws_, wr_ = wstg[i % 2], wr[i % 2]
                P.dma("sp", ws_[:, :, 0:nw], w_in[:, n0:n0 + nw].rearrange("(k p) n -> p k n", p=128), writes=[ws_])
                P.op("pool", lambda e, ws_=ws_, wr_=wr_, nw=nw: e.tensor_copy(out=wr_[:, :, 0:nw].bitcast(F32R), in_=ws_[:, :, 0:nw]),
                     reads=[ws_], writes=[wr_])

            p1_norm(0)
            p1_loadw(0)
            for i, (m, nb) in enumerate(blocks):
                if nb == 0 and m + 1 < nmac:
                    p1_norm(m + 1)
                if i + 1 < len(blocks):
                    p1_loadw(i + 1)
                tiles = list(range(m * MT, min((m + 1) * MT, cfg.ntile)))
                hx = hxT[m % 2]
                n0 = nb * 512
                nw = min(512, nin - n0)
                wr_ = wr[i % 2]
                for jj, t in enumerate(tiles):
                    ei = cnt["ei"]; cnt["ei"] += 1
                    ps = pu[ei % 4]; u_ = ut[ei % 4]
                    for k in range(8):
                        P.op("pe", lambda e, ps=ps, k=k, hx=hx, jj=jj, wr_=wr_, nw=nw: e.matmul(
                            ps[:, 0:nw], lhsT=hx[:, k, jj * 128:(jj + 1) * 128].bitcast(F32R), rhs=wr_[:, k, 0:nw].bitcast(F32R),
                            start=(k == 0), stop=(k == 7)), reads=[hx, wr_], writes=[ps])
                    if ei % 2:
                        P.op("act", lambda e, ps=ps, u_=u_, nw=nw: e.activation(out=u_[:, 0:nw], in_=ps[:, 0:nw], func=AF.Copy), reads=[ps], writes=[u_])
                    else:
                        P.op("dve", lambda e, ps=ps, u_=u_, nw=nw: e.tensor_copy(out=u_[:, 0:nw], in_=ps[:, 0:nw]), reads=[ps], writes=[u_])
                    P.dma("act", U[t * 128:(t + 1) * 128, n0:n0 + nw], u_[:, 0:nw], reads=[u_], writes=[("U", t, nb)])
            P.pop()
        if stop_after == f"P1_{l}":
            break
        if stop_after is None:
            if l == 0:
                mlstm(l); diffattn(l)
            else:
                rwkv(l); gla(l)
            p3(l, Xl)
        elif stop_after == f"P3_{l}":
            p3(l, Xl); break
        elif l == 0 and stop_after in ("mlstm", "diff"):
            (mlstm if stop_after == "mlstm" else diffattn)(l); break
        elif l == 1 and stop_after in ("rwkv", "gla"):
            (rwkv if stop_after == "rwkv" else gla)(l); break

    P.emit()
    P.close()
    return nc

import numpy as np
TC = 256
def rope_table(TL):
    n = 8
    inv = (10000.0 ** (-np.arange(n, dtype=np.float32) / n)).astype(np.float32)
    row = np.repeat(np.arange(TL // 64, dtype=np.float32), 64)
    col = np.tile(np.arange(64, dtype=np.float32), TL // 64)
    ang = np.concatenate([row[:, None] * inv, col[:, None] * inv], axis=-1).astype(np.float32)
    return np.concatenate([np.cos(ang), np.sin(ang)], axis=-1).astype(np.float32)

def make_masks():
    m = np.zeros((9, 128, 128), np.float32)
    i = np.arange(128)
    S, T = i[:, None], i[None, :]
    blk = (S // 64 == T // 64)
    m[0] = (S <= T); m[1] = (S >= T)
    m[2] = (S <= T) * blk; m[3] = (S >= T) * blk
    m[4] = (S > T); m[5] = (S < T)
    m[6] = (S < T) * blk; m[7] = (S > T) * blk
    m[8, :64, 0] = 1.0; m[8, 64:, 1] = 1.0
    return m

def core_inputs(inp, core, NB, TL):
    b0 = core * NB
    xs = []
    for b in range(b0, b0 + NB):
        xs.append(inp["ctx"][b]); xs.append(inp["x"][b][:TL])
    m = {}
    m["xin"] = np.ascontiguousarray(np.concatenate(xs, axis=0))
    m["cvec"] = np.ascontiguousarray(np.concatenate([inp["c"][b0:b0 + NB], inp["c_ctx"][None, :]], axis=0))
    for k in ("ada_w", "ada_b", "norm_g", "w_out", "w_mlp_in", "w_mlp_out"):
        m[k] = inp[k]
    m["w_in_even"] = inp["w_in_even"][0]; m["w_in_odd"] = inp["w_in_odd"][0]
    m["mlstm_gate_b"] = inp["mlstm_gate_b"].reshape(1, 32); m["mlstm_norm_g"] = inp["mlstm_norm_g"].reshape(1, 512)
    m["diff_qk_g"] = inp["diff_qk_g"][0]; m["diff_lam"] = inp["diff_lam"].reshape(1, 128); m["diff_subln_g"] = inp["diff_subln_g"].reshape(1, 64)
    m["rwkv_mu"] = inp["rwkv_mu"][0]; m["rwkv_w0"] = inp["rwkv_w0"][0]; m["rwkv_w2"] = inp["rwkv_w2"][0]
    m["rwkv_a0"] = inp["rwkv_a0"][0]; m["rwkv_a2"] = inp["rwkv_a2"][0]; m["rwkv_g2"] = inp["rwkv_g2"][0]
    m["rwkv_kvec"] = inp["rwkv_kvec"][0]; m["rwkv_ln"] = inp["rwkv_ln"][0]
    m["gla_gate_w2"] = inp["gla_gate_w2"][0]; m["gla_gate_b"] = inp["gla_gate_b"][0]; m["gla_norm_g"] = inp["gla_norm_g"].reshape(1, 128)
    m["ident"] = np.eye(128, dtype=np.float32)
    m["rope"] = rope_table(TL)
    m["masks"] = make_masks()
    return {k: np.ascontiguousarray(np.asarray(v, dtype=np.float32)) for k, v in m.items()}


def kernel(**inputs):
    from concourse.bass_utils import run_bass_kernel_spmd
    inp = {k: np.asarray(v) for k, v in inputs.items()}
    NB, TL = 2, 4096
    cfg = Cfg(NB=NB, TL=TL)
    nc = build(cfg)
    in_maps = [core_inputs(inp, core, NB, TL) for core in range(8)]
    res = run_bass_kernel_spmd(nc, in_maps, core_ids=list(range(8)))
    outs = [np.asarray(r["out"]).reshape(NB, TL, D) for r in res.results]
    return np.concatenate(outs, axis=0).astype(np.float32)
```

```python
import contextlib
import numpy as np
import concourse.bass as bass
import concourse.mybir as mybir

F32 = mybir.dt.float32
F32R = mybir.dt.float32r
ALU = mybir.AluOpType
AF = mybir.ActivationFunctionType
AX = mybir.AxisListType

ENGS = ("pe", "act", "dve", "pool", "sp")
N_DMA_SEMS = 24


class Op:
    __slots__ = ("eng", "fn", "deps", "signal", "count", "is_dma", "dsem", "dcount", "idx", "rg")


class Prog:
    def __init__(self, nc):
        self.nc = nc
        self.streams = {e: [] for e in ENGS}
        self.last_w = {}
        self.readers = {}
        self.stack = contextlib.ExitStack()
        self.n_dma = {}
        self.pstack = None
        self.bar = {}
        self.uid = 0
        self.recent_dma = {e: {} for e in ENGS}
        self.psum_names = set()

    def push(self):
        self.pstack = contextlib.ExitStack()

    def pop(self):
        self.barrier()
        self.pstack.close()
        self.pstack = None

    def barrier(self):
        lasts = []
        for e in ENGS:
            st = self.streams[e]
            for o in reversed(st):
                if not o.is_dma:
                    lasts.append(o)
                    break
            lasts.extend(self.recent_dma[e].values())
        for e in ENGS:
            self.bar[e] = list(lasts)

    def sbuf(self, name, shape, dtype=F32):
        self.uid += 1
        st = self.pstack if self.pstack is not None else self.stack
        return st.enter_context(self.nc.sbuf_tensor(f"{name}_{self.uid}", list(shape), dtype))

    def psum(self, name, shape, dtype=F32):
        self.uid += 1
        st = self.pstack if self.pstack is not None else self.stack
        self.psum_names.add(f"{name}_{self.uid}")
        return st.enter_context(self.nc.psum_tensor(f"{name}_{self.uid}", list(shape), dtype))

    def dram(self, name, shape, dtype=F32, kind="Internal"):
        return self.nc.dram_tensor(name, list(shape), dtype, kind=kind)

    def dma(self, eng, out, in_, reads=(), writes=(), **kw):
        return self.op(eng, lambda e: e.dma_start(out=out, in_=in_, **kw), reads, writes, is_dma=True)

    def op(self, eng, fn, reads=(), writes=(), is_dma=False, rg=None):
        o = Op()
        o.rg = rg
        o.eng = eng
        o.fn = fn
        o.signal = False
        o.count = 0
        o.is_dma = is_dma
        o.dsem = None
        o.dcount = 0
        reads = [r if isinstance(r, (str, tuple)) else r.name for r in reads]
        writes = [r if isinstance(r, (str, tuple)) else r.name for r in writes]
        deps = {}
        def add(d):
            if d is None:
                return
            if d.eng == "pe" and eng == "pe":
                return
            deps[id(d)] = d
        for r in reads:
            add(self.last_w.get(r))
            if r in self.psum_names:
                for rd in self.readers.get(r, ()):
                    if rd.eng != eng:
                        add(rd)
        for w in writes:
            add(self.last_w.get(w))
            if eng == "pe" and rg is not None:
                lw = self.last_w.get(w)
                if lw is not None and lw.eng == "pe" and lw.rg is not None and lw.rg != rg:
                    deps[id(lw)] = lw
            for rd in self.readers.get(w, ()):
                if rd is not o:
                    add(rd)
        if eng in self.bar:
            for d in self.bar.pop(eng):
                if d is not None and not (d.eng == eng and not d.is_dma and eng != "pe" and False):
                    deps[id(d)] = d
        o.deps = list(deps.values())
        for r in reads:
            self.readers.setdefault(r, []).append(o)
        for w in writes:
            self.last_w[w] = o
            self.readers[w] = []
        o.idx = len(self.streams[eng])
        self.streams[eng].append(o)
        if o.is_dma:
            k = self.n_dma.get(eng, 0)
            o.dsem = k % N_DMA_SEMS
            self.n_dma[eng] = k + 1
            self.recent_dma[eng][o.dsem] = o
        return o

    def emit(self):
        nc = self.nc
        for e in ENGS:
            for o in self.streams[e]:
                for d in o.deps:
                    if not d.is_dma:
                        d.signal = True
        for e in ENGS:
            c = 0
            for o in self.streams[e]:
                if o.signal:
                    c += 1
                    o.count = c
        st = self.stack
        esem = {e: st.enter_context(nc.semaphore("s_" + e)) for e in ("pe", "act", "dve", "pool")}
        dsems = {}
        for q in ENGS:
            if not any(o.is_dma for o in self.streams[q]):
                continue
            dsems[q] = [st.enter_context(nc.semaphore(f"d_{q}_{i}")) for i in range(N_DMA_SEMS)]
            cnt = [0] * N_DMA_SEMS
            for o in self.streams[q]:
                if o.is_dma:
                    cnt[o.dsem] += 16
                    o.dcount = cnt[o.dsem]
        block = st.enter_context(nc.Block())
        streams = self.streams

        def run(engname, eng):
            waited = {}
            for o in streams[engname]:
                need = {}
                for d in o.deps:
                    if d.is_dma:
                        key = ("d", d.eng, d.dsem)
                        val = d.dcount
                    else:
                        key = ("e", d.eng)
                        val = d.count
                    if need.get(key, 0) < val:
                        need[key] = val
                if o.is_dma and o.dcount > 16:
                    key = ("d", o.eng, o.dsem)
                    if need.get(key, 0) < o.dcount - 16:
                        need[key] = o.dcount - 16
                for key, val in need.items():
                    if waited.get(key, 0) >= val:
                        continue
                    waited[key] = val
                    sem = dsems[key[1]][key[2]] if key[0] == "d" else esem[key[1]]
                    eng.wait_ge(sem, val)
                inst = o.fn(eng)
                if o.is_dma:
                    inst.then_inc(dsems[o.eng][o.dsem], 16)
                elif o.signal:
                    inst.then_inc(esem[o.eng], 1)
            return waited

        def fin(engname, eng):
            w = run(engname, eng)
            cnt = {}
            for o in streams[engname]:
                if o.is_dma:
                    cnt[o.dsem] = o.dcount
            for k, v in cnt.items():
                if w.get(("d", engname, k), 0) < v:
                    eng.wait_ge(dsems[engname][k], v)

        @block.tensor
        def _(pe):
            fin("pe", pe)

        @block.scalar
        def _(act):
            fin("act", act)

        @block.vector
        def _(dve):
            fin("dve", dve)

        @block.gpsimd
        def _(pool):
            fin("pool", pool)

        @block.sync
        def _(sp):
            fin("sp", sp)

    def close(self):
        self.stack.close()

import math
import numpy as np

D = 1024
EPS = 1e-6
A_IN, B_IN = 2080, 1536
EVEN_IN = A_IN + B_IN
C_IN, D_IN = 1920, 1568
ODD_IN = C_IN + D_IN
NIN = (EVEN_IN, ODD_IN)
HID = 4096
TC = 256


class Cfg:
    def __init__(self, NB=2, TL=4096, debug=()):
        self.NB = NB
        self.TL = TL
        self.TS = TC + TL
        self.NT = NB * self.TS
        self.ntile = self.NT // 128
        self.debug = set(debug)

    def cls(self, tile):
        b, r = divmod(tile * 128, self.TS)
        return self.NB if r < TC else b


def declare_io(nc, cfg):
    io = {}
    def inp(name, shape):
        io[name] = nc.dram_tensor(name, list(shape), F32, kind="ExternalInput").ap()
    inp("xin", [cfg.NT, D])
    inp("cvec", [cfg.NB + 1, D])
    inp("ada_w", [2, D, 6 * D]); inp("ada_b", [2, 6 * D]); inp("norm_g", [2, 2, D])
    inp("w_in_even", [D, EVEN_IN]); inp("w_in_odd", [D, ODD_IN])
    inp("w_out", [2, D, D]); inp("w_mlp_in", [2, D, HID]); inp("w_mlp_out", [2, HID, D])
    inp("mlstm_gate_b", [1, 32]); inp("mlstm_norm_g", [1, 512])
    inp("diff_qk_g", [2, 32]); inp("diff_lam", [1, 128]); inp("diff_subln_g", [1, 64])
    inp("rwkv_mu", [2, C_IN]); inp("rwkv_w0", [2, 512]); inp("rwkv_w2", [2, 64, 512])
    inp("rwkv_a0", [2, 512]); inp("rwkv_a2", [2, 64, 512]); inp("rwkv_g2", [128, 512])
    inp("rwkv_kvec", [3, 512]); inp("rwkv_ln", [2, 512])
    inp("gla_gate_w2", [2, 16, 256]); inp("gla_gate_b", [2, 256]); inp("gla_norm_g", [1, 128])
    inp("ident", [128, 128]); inp("rope", [cfg.TL, 32])
    inp("masks", [9, 128, 128])
    io["out"] = nc.dram_tensor("out", [cfg.NB * cfg.TL, D], F32, kind="ExternalOutput").ap()
    return io


def build(cfg, stop_after=None):
    nc = bass.Bass("TRN2", target_bir_lowering=False)
    io = declare_io(nc, cfg)
    P = Prog(nc)
    dbg = cfg.debug

    def scratch(name, shape):
        kind = "ExternalOutput" if name in dbg else "Internal"
        return nc.dram_tensor(name, list(shape), F32, kind=kind).ap()

    X1 = scratch("X1", [cfg.NT, D])
    if "U_in" in dbg:
        U = nc.dram_tensor("U", [cfg.NT, EVEN_IN], F32, kind="ExternalInput").ap()
    else:
        U = scratch("U", [cfg.NT, EVEN_IN])
    YS = scratch("YS", [cfg.NT, 520])
    RW = scratch("RW", [cfg.NT, 2432])
    NTS = cfg.TS // 128
    QKT = scratch("QKT", [cfg.NB, 2, 8, 64, cfg.TS])
    masks = P.sbuf("masks", [128, 9, 128])
    P.dma("sp", masks[:], io["masks"].rearrange("m p n -> p m n"), writes=[masks])

    def ukeys(t):
        return [("U", t, nb) for nb in range(8)]

    def bcast_load(dst, src_row, n):
        P.dma("sp", dst, src_row.to_broadcast([128, n]), writes=[dst.tensor.name if hasattr(dst, "tensor") else dst])

    if "O_in" in dbg:
        O = nc.dram_tensor("O", [cfg.NT, D], F32, kind="ExternalInput").ap()
    else:
        O = scratch("O", [cfg.NT, D])
    MODS = scratch("MODS", [2, cfg.NB + 1, 6 * D])
    NC = cfg.NB + 1

    ident = P.sbuf("ident", [128, 128])
    ones = P.sbuf("ones", [128, 128])
    P.dma("sp", ident[:], io["ident"], writes=[ident])
    P.op("dve", lambda e: e.memset(ones[:], 1.0), writes=[ones])
    epsc = P.sbuf("epsc", [128, 1])
    P.op("dve", lambda e: e.memset(epsc[:], EPS), writes=[epsc])
    AB = [P.sbuf(f"AB{l}", [128, 4, 8, NC]) for l in range(2)]

    P.push()
    crow = P.sbuf("crow", [NC, D])
    srow = P.sbuf("srow", [NC, D])
    sT = P.sbuf("sT", [128, 8, NC])
    P.dma("sp", crow[:], io["cvec"], writes=[crow])
    P.op("act", lambda e: e.activation(out=srow[:], in_=crow[:], func=AF.Silu), reads=[crow], writes=[srow])
    pt = P.psum("p0t", [128, 8, NC])
    for k in range(8):
        P.op("pe", lambda e, k=k: e.transpose(out=pt[:, k, :], in_=srow[:, k * 128:(k + 1) * 128], identity=ident[0:NC, 0:NC]),
             reads=[srow, ident], writes=[pt])
    P.op("dve", lambda e: e.tensor_copy(out=sT[:], in_=pt[:]), reads=[pt], writes=[sT])
    modrow = P.sbuf("modrow", [NC, 6 * D])
    gT = P.sbuf("gT", [128, 2, 8])
    wst = [P.sbuf(f"p0w{i}", [128, 8, 512]) for i in range(2)]
    brow = P.sbuf("p0b", [1, 6 * D])
    pm = [P.psum(f"p0m{i}", [NC, 512]) for i in range(2)]
    pmt = P.psum("p0mt", [128, 4, 8, NC])
    pg = P.psum("p0g", [128, 2, 8])
    grow = P.sbuf("p0grow", [1, 2 * D])
    for l in range(2):
        P.dma("sp", brow[:], io["ada_b"][l:l + 1, :], writes=[brow])
        for j in range(12):
            w = wst[j % 2]
            P.dma("sp", w[:], io["ada_w"][l, :, j * 512:(j + 1) * 512].rearrange("(k p) n -> p k n", p=128), writes=[w])
            ps = pm[j % 2]
            for k in range(8):
                P.op("pe", lambda e, k=k, w=w, ps=ps: e.matmul(ps[:], lhsT=sT[:, k, :], rhs=w[:, k, :], start=(k == 0), stop=False),
                     reads=[sT, w], writes=[ps])
            P.op("pe", lambda e, ps=ps, j=j: e.matmul(ps[:], lhsT=ones[0:1, 0:NC], rhs=brow[0:1, j * 512:(j + 1) * 512], start=False, stop=True),
                 reads=[ones, brow], writes=[ps])
            P.op("act", lambda e, ps=ps, j=j: e.activation(out=modrow[:, j * 512:(j + 1) * 512], in_=ps[:], func=AF.Copy),
                 reads=[ps], writes=[modrow])
        P.dma("sp", MODS[l], modrow[:], reads=[modrow], writes=[("MODS", l)])
        for qi, q in enumerate((0, 1, 3, 4)):
            for k in range(8):
                P.op("pe", lambda e, qi=qi, q=q, k=k: e.transpose(out=pmt[:, qi, k, :], in_=modrow[:, q * D + k * 128: q * D + (k + 1) * 128],
                                                                 identity=ident[0:NC, 0:NC]),
                     reads=[modrow, ident], writes=[pmt])
        P.dma("sp", grow[:], io["norm_g"][l:l + 1].rearrange("o j d -> o (j d)"), writes=[grow])
        for j in range(2):
            for k in range(8):
                P.op("pe", lambda e, j=j, k=k: e.transpose(out=pg[:, j, k:k + 1], in_=grow[0:1, j * D + k * 128: j * D + (k + 1) * 128], identity=ident[0:1, 0:1]),
                     reads=[grow, ident], writes=[pg])
        P.op("dve", lambda e: e.tensor_copy(out=gT[:], in_=pg[:]), reads=[pg], writes=[gT])
        ab = AB[l]
        P.op("dve", lambda e, ab=ab: e.tensor_copy(out=ab[:, 1], in_=pmt[:, 0]), reads=[pmt], writes=[ab])
        P.op("dve", lambda e, ab=ab: e.tensor_copy(out=ab[:, 3], in_=pmt[:, 2]), reads=[pmt], writes=[ab])
        for c in range(NC):
            P.op("dve", lambda e, ab=ab, c=c: e.scalar_tensor_tensor(out=ab[:, 0, :, c], in0=pmt[:, 1, :, c], scalar=1.0, in1=gT[:, 0, :],
                                                                   op0=ALU.add, op1=ALU.mult), reads=[pmt, gT], writes=[ab])
            P.op("dve", lambda e, ab=ab, c=c: e.scalar_tensor_tensor(out=ab[:, 2, :, c], in0=pmt[:, 3, :, c], scalar=1.0, in1=gT[:, 1, :],
                                                                   op0=ALU.add, op1=ALU.mult), reads=[pmt, gT], writes=[ab])
    P.pop()
    if stop_after == "P0":
        P.emit(); P.close(); return nc

    def norm_mod_T(xt, hxT, j, l, which, cls, sq_junk, ss, rstd, xn, ptr):
        P.op("act", lambda e: e.activation(out=sq_junk[:], in_=xt, func=AF.Square, accum_out=ss[:]),
             reads=[xt.tensor if hasattr(xt, "tensor") else xt], writes=[sq_junk, ss])
        return

    def p3(l, Xl):
        last = (l == 1)
        P.push()
        MT = 2
        if last:
            tl = [t for t in range(cfg.ntile) if cfg.cls(t) != cfg.NB]
        else:
            tl = list(range(cfg.ntile))
        macs = [tl[i:i + MT] for i in range(0, len(tl), MT)]
        NTK = MT * 128
        gbc = P.sbuf("p3gbc", [128, NC, 2, D])
        for c in range(NC):
            for gi, q in enumerate((2, 5)):
                P.dma("sp", gbc[:, c, gi, :], MODS[l, c:c + 1, q * D:(q + 1) * D].to_broadcast([128, D]),
                      reads=[("MODS", l)], writes=[gbc])
        xts = [P.sbuf(f"p3x{i}", [128, D]) for i in range(2 * MT)]
        obuf = [P.sbuf(f"p3o{i}", [128, D]) for i in range(2)]
        OT = [P.sbuf(f"p3OT{i}", [128, 8, NTK]) for i in range(2)]
        hx2T = P.sbuf("p3hx2T", [128, 8, NTK])
        hT = P.sbuf("p3hT", [128, 32, NTK])
        wstg = [P.sbuf(f"p3ws{i}", [128, 8, 512]) for i in range(2)]
        wr = [P.sbuf(f"p3wr{i}", [128, 8, 512]) for i in range(2)]
        tmp = [P.sbuf(f"p3tmp{i}", [128, 512]) for i in range(2)]
        hrl = [P.sbuf(f"p3hrl{i}", [128, NTK]) for i in range(2)]
        ss = [P.sbuf(f"p3ss{i}", [128, 1]) for i in range(2)]
        rstd = [P.sbuf(f"p3rs{i}", [128, 1]) for i in range(2)]
        ptr = [P.psum(f"p3ptr{i}", [128, 4, 128]) for i in range(2)]
        pu = [P.psum(f"p3pu{i}", [128, 512]) for i in range(2)]
        acc = [P.psum(f"p3acc{i}", [128, 512]) for i in range(2 * MT)]
        cnt = {"o": 0, "pu": 0, "tmp": 0, "hr": 0, "n": 0}
        w_out = io["w_out"][l]; w1 = io["w_mlp_in"][l]; w2 = io["w_mlp_out"][l]
        wblocks = []
        for m in range(len(macs)):
            for nb in range(2):
                wblocks.append(w_out[:, nb * 512:(nb + 1) * 512].rearrange("(k p) n -> p k n", p=128))
            for nb in range(8):
                wblocks.append(w1[:, nb * 512:(nb + 1) * 512].rearrange("(k p) n -> p k n", p=128))
            for nb in range(2):
                for kp in range(4):
                    wblocks.append(w2[kp * 1024:(kp + 1) * 1024, nb * 512:(nb + 1) * 512].rearrange("(k p) n -> p k n", p=128))
        wi = {"i": 0}

        def loadw(i):
            if i >= len(wblocks):
                return
            ws_, wr_ = wstg[i % 2], wr[i % 2]
            P.dma("sp", ws_[:], wblocks[i], writes=[ws_])
            P.op("dve", lambda e, ws_=ws_, wr_=wr_: e.tensor_copy(out=wr_[:, 0:4, :].bitcast(F32R), in_=ws_[:, 0:4, :]), reads=[ws_], writes=[(wr_.name, "a")])
            P.op("act", lambda e, ws_=ws_, wr_=wr_: e.activation(out=wr_[:, 4:8, :].bitcast(F32R), in_=ws_[:, 4:8, :], func=AF.Copy), reads=[ws_], writes=[(wr_.name, "b")])

        def nextw():
            i = wi["i"]; wi["i"] += 1
            loadw(i + 1)
            return wr[i % 2]

        def transpose_tile(src, dst, jj, scale_bias=None):
            for half in range(2):
                pt_ = ptr[half]
                for kk in range(4):
                    k = half * 4 + kk
                    P.op("pe", lambda e, pt_=pt_, kk=kk, k=k: e.transpose(out=pt_[:, kk, :], in_=src[:, k * 128:(k + 1) * 128], identity=ident[:]),
                         reads=[src, ident], writes=[pt_])
                if scale_bias is None:
                    dsl = dst[:, half * 4:(half + 1) * 4, jj * 128:(jj + 1) * 128]
                    if half == 0:
                        P.op("act", lambda e, pt_=pt_, dsl=dsl: e.activation(out=dsl.bitcast(F32R), in_=pt_[:], func=AF.Copy), reads=[pt_], writes=[dst])
                    else:
                        P.op("dve", lambda e, pt_=pt_, dsl=dsl: e.tensor_copy(out=dsl.bitcast(F32R), in_=pt_[:]), reads=[pt_], writes=[dst])
                else:
                    ab, ja, jb, cls = scale_bias
                    for kk in range(4):
                        k = half * 4 + kk
                        P.op("act", lambda e, pt_=pt_, kk=kk, k=k: e.activation(
                            out=dst[:, k, jj * 128:(jj + 1) * 128].bitcast(F32R), in_=pt_[:, kk, :], func=AF.Identity,
                            scale=ab[:, ja, k, cls:cls + 1], bias=ab[:, jb, k, cls:cls + 1]), reads=[pt_, ab], writes=[dst])

        def prep_O(m):
            ot = OT[m % 2]
            for jj, t in enumerate(macs[m]):
                ob = obuf[cnt["o"] % 2]; cnt["o"] += 1
                P.dma("sp", ob[:], O[t * 128:(t + 1) * 128, :], reads=[("O", t, 0), ("O", t, 1)], writes=[ob])
                transpose_tile(ob, ot, jj)

        loadw(0)
        prep_O(0)
        for m, tiles in enumerate(macs):
            ntok = len(tiles) * 128
            ot = OT[m % 2]
            xs = [xts[(m % 2) * MT + jj] for jj in range(len(tiles))]
            for jj, t in enumerate(tiles):
                P.dma("sp", xs[jj][:], Xl[t * 128:(t + 1) * 128, :], reads=[("X", l, t)], writes=[xs[jj]])
            for nb in range(2):
                wr_ = nextw()
                for jj, t in enumerate(tiles):
                    cls = cfg.cls(t)
                    ps = pu[cnt["pu"] % 2]; cnt["pu"] += 1
                    tm = tmp[cnt["tmp"] % 2]; cnt["tmp"] += 1
                    for k in range(8):
                        P.op("pe", lambda e, ps=ps, k=k, jj=jj, wr_=wr_, ot=ot: e.matmul(ps[:], lhsT=ot[:, k, jj * 128:(jj + 1) * 128].bitcast(F32R),
                                                                              rhs=wr_[:, k, :].bitcast(F32R), start=(k == 0), stop=(k == 7)),
                             reads=[ot, (wr_.name, 'a' if k < 4 else 'b')], writes=[ps])
                    P.op("dve", lambda e, ps=ps, tm=tm, cls=cls, nb=nb: e.tensor_tensor(out=tm[:], in0=ps[:], in1=gbc[:, cls, 0, nb * 512:(nb + 1) * 512], op=ALU.mult),
                         reads=[ps, gbc], writes=[tm])
                    xj = xs[jj]
                    P.op("pool", lambda e, tm=tm, xj=xj, nb=nb: e.tensor_tensor(out=xj[:, nb * 512:(nb + 1) * 512], in0=xj[:, nb * 512:(nb + 1) * 512], in1=tm[:], op=ALU.add),
                         reads=[tm, xj], writes=[xj])
            if m + 1 < len(macs):
                prep_O(m + 1)
            for jj, t in enumerate(tiles):
                cls = cfg.cls(t)
                xj = xs[jj]
                n = cnt["n"]; cnt["n"] += 1
                s_ = ss[n % 2]; r_ = rstd[n % 2]; xn_ = obuf[cnt["o"] % 2]; cnt["o"] += 1
                P.op("act", lambda e, xj=xj, s_=s_, xn_=xn_: e.activation(out=xn_[:], in_=xj[:], func=AF.Square, accum_out=s_[:]),
                     reads=[xj], writes=[xn_, s_])
                P.op("act", lambda e, s_=s_, r_=r_: e.activation(out=r_[:], in_=s_[:], func=AF.Sqrt, scale=1.0 / D, bias=epsc[:, 0:1]),
                     reads=[s_, epsc], writes=[r_])
                P.op("dve", lambda e, r_=r_: e.reciprocal(out=r_[:], in_=r_[:]), reads=[r_], writes=[r_])
                P.op("dve", lambda e, xj=xj, r_=r_, xn_=xn_: e.tensor_scalar(out=xn_[:], in0=xj[:], scalar1=r_[:, 0:1], scalar2=None, op0=ALU.mult),
                     reads=[xj, r_], writes=[xn_])
                transpose_tile(xn_, hx2T, jj, scale_bias=(AB[l], 2, 3, cls))
            for nb in range(8):
                wr_ = nextw()
                for sub in range(4):
                    nchunk = nb * 4 + sub
                    ps = pu[cnt["pu"] % 2]; cnt["pu"] += 1
                    hr = hrl[cnt["hr"] % 2]; cnt["hr"] += 1
                    for k in range(8):
                        P.op("pe", lambda e, ps=ps, k=k, sub=sub, wr_=wr_, ntok=ntok: e.matmul(ps[:, 0:ntok], lhsT=wr_[:, k, sub * 128:(sub + 1) * 128].bitcast(F32R),
                                                                               rhs=hx2T[:, k, 0:ntok].bitcast(F32R), start=(k == 0), stop=(k == 7)),
                             reads=[hx2T, (wr_.name, 'a' if k < 4 else 'b')], writes=[ps])
                    P.op("act", lambda e, ps=ps, hr=hr, ntok=ntok: e.activation(out=hr[:, 0:ntok], in_=ps[:, 0:ntok], func=AF.Relu), reads=[ps], writes=[hr])
                    P.op("pool", lambda e, hr=hr, nchunk=nchunk, ntok=ntok: e.tensor_tensor(out=hT[:, nchunk, 0:ntok].bitcast(F32R), in0=hr[:, 0:ntok], in1=hr[:, 0:ntok], op=ALU.mult),
                         reads=[hr], writes=[hT])
            for nb in range(2):
                for kp in range(4):
                    wr_ = nextw()
                    for jj, t in enumerate(tiles):
                        ac = acc[nb * MT + jj]
                        for k in range(8):
                            P.op("pe", lambda e, ac=ac, k=k, kp=kp, jj=jj, wr_=wr_: e.matmul(ac[:], lhsT=hT[:, kp * 8 + k, jj * 128:(jj + 1) * 128].bitcast(F32R),
                                                                                        rhs=wr_[:, k, :].bitcast(F32R), start=(kp == 0 and k == 0), stop=(kp == 3 and k == 7)),
                                 reads=[hT, (wr_.name, 'a' if k < 4 else 'b')], writes=[ac])
                for jj, t in enumerate(tiles):
                    cls = cfg.cls(t)
                    ac = acc[nb * MT + jj]
                    tm = tmp[cnt["tmp"] % 2]; cnt["tmp"] += 1
                    xj = xs[jj]
                    P.op("dve", lambda e, ac=ac, tm=tm, cls=cls, nb=nb: e.tensor_tensor(out=tm[:], in0=ac[:], in1=gbc[:, cls, 1, nb * 512:(nb + 1) * 512], op=ALU.mult),
                         reads=[ac, gbc], writes=[tm])
                    P.op("pool", lambda e, tm=tm, xj=xj, nb=nb: e.tensor_tensor(out=xj[:, nb * 512:(nb + 1) * 512], in0=xj[:, nb * 512:(nb + 1) * 512], in1=tm[:], op=ALU.add),
                         reads=[tm, xj], writes=[xj])
            for jj, t in enumerate(tiles):
                xj = xs[jj]
                if last:
                    b, r = divmod(t * 128, cfg.TS)
                    row = b * cfg.TL + (r - TC)
                    P.dma("act", io["out"][row:row + 128, :], xj[:], reads=[xj], writes=[("OUT", t)])
                else:
                    P.dma("act", X1[t * 128:(t + 1) * 128, :], xj[:], reads=[xj], writes=[("X", 1, t)])
        P.pop()

    LN8 = math.log(0.125)

    def mlstm(l):
        P.push()
        NB = cfg.NB
        gb_bc = P.sbuf("mlgb", [128, 32]); ng_bc = P.sbuf("mlng", [128, 512])
        P.dma("sp", gb_bc[:], io["mlstm_gate_b"].to_broadcast([128, 32]), writes=[gb_bc])
        P.dma("sp", ng_bc[:], io["mlstm_norm_g"].to_broadcast([128, 512]), writes=[ng_bc])
        ln8 = P.sbuf("mlln8", [128, 1]); onec = P.sbuf("mlone", [128, 1]); eps_c = P.sbuf("mleps", [128, 1])
        P.op("dve", lambda e: e.memset(ln8[:], LN8), writes=[ln8])
        P.op("dve", lambda e: e.memset(onec[:], 1.0), writes=[onec])
        P.op("dve", lambda e: e.memset(eps_c[:], EPS), writes=[eps_c])
        B = []
        for b in range(NB):
            d = {}
            d["ua"] = [P.sbuf(f"mlua{b}{i}", [128, A_IN]) for i in range(2)]
            d["g"] = P.sbuf(f"mlg{b}", [128, 32])
            d["nlf"] = P.sbuf(f"mlnlf{b}", [128, 8])
            d["eb"] = P.sbuf(f"mleb{b}", [128, 8])
            d["vs"] = P.sbuf(f"mlvs{b}", [128, 8])
            d["eL"] = P.sbuf(f"mleL{b}", [128, 8])
            d["Vt"] = P.sbuf(f"mlVt{b}", [128, 8, 65])
            d["QT"] = P.sbuf(f"mlQT{b}", [128, 4, 128])
            d["KT"] = P.sbuf(f"mlKT{b}", [128, 4, 128])
            d["AT"] = [P.sbuf(f"mlAT{b}{i}", [128, 128]) for i in range(2)]
            d["C"] = P.sbuf(f"mlC{b}", [128, 4, 65])
            d["Ct"] = P.sbuf(f"mlCt{b}", [128, 65])
            d["small"] = P.sbuf(f"mlsm{b}", [128, 6, 8])
            d["hd"] = P.sbuf(f"mlhd{b}", [128, 8, 64])
            d["hf"] = P.sbuf(f"mlhf{b}", [128, 512])
            d["sq"] = P.sbuf(f"mlsq{b}", [128, 512])
            d["sg"] = P.sbuf(f"mlsg{b}", [128, 512])
            d["pT"] = P.psum(f"mlpT{b}", [128, 8, 128]) if b == 0 else None
            d["pa"] = P.psum(f"mlpa{b}", [128, 128])
            d["pn"] = P.psum(f"mlpn{b}", [128, 8, 128]) if b == 0 else None
            B.append(d)
        pn = B[0]["pn"]
        pc = P.psum("mlpc", [128, 2, 65])
        psm = P.psum("mlpsm", [128, 16])
        for dr in range(2):
            mk = masks[:, dr, :]
            for b in range(NB):
                P.op("dve", lambda e, b=b: e.memset(B[b]["C"][:], 0.0), writes=[B[b]["C"]])
            if dr == 0:
                order = list(range(NTS))
            else:
                order = [1, 0] + list(range(NTS - 1, 1, -1))
            for step, tt in enumerate(order):
                for b in range(NB):
                    d = B[b]
                    t = b * NTS + tt
                    ua = d["ua"][step % 2]
                    pT = B[0]["pT"]
                    P.dma("sp", ua[:], U[t * 128:(t + 1) * 128, 0:A_IN], reads=ukeys(t), writes=[ua])
                    g, nlf, eb, vs, eL, Vt, QT, KT, C, sm = d["g"], d["nlf"], d["eb"], d["vs"], d["eL"], d["Vt"], d["QT"], d["KT"], d["C"], d["small"]
                    P.op("dve", lambda e, ua=ua, g=g: e.tensor_tensor(out=g[:], in0=ua[:, 2048:2080], in1=gb_bc[:], op=ALU.add), reads=[ua, gb_bc], writes=[g])
                    ig = g[:, 16 * dr: 16 * dr + 8]
                    fg = g[:, 16 * dr + 8: 16 * dr + 16]
                    P.op("act", lambda e, fg=fg, nlf=nlf: e.activation(out=nlf[:], in_=fg, func=AF.Exp, scale=-1.0), reads=[g], writes=[nlf])
                    P.op("act", lambda e, nlf=nlf: e.activation(out=nlf[:], in_=nlf[:], func=AF.Ln, bias=onec[:, 0:1]), reads=[nlf, onec], writes=[nlf])
                    P.op("pe", lambda e, nlf=nlf, mk=mk: e.matmul(psm[:, 0:8], lhsT=mk, rhs=nlf[:], start=True, stop=True), reads=[masks, nlf], writes=[psm])
                    P.op("pe", lambda e, nlf=nlf: e.matmul(psm[:, 8:16], lhsT=ones[:], rhs=nlf[:], start=True, stop=True), reads=[ones, nlf], writes=[psm])
                    P.op("act", lambda e, eb=eb: e.activation(out=eb[:], in_=psm[:, 0:8], func=AF.Exp, scale=-1.0), reads=[psm], writes=[eb])
                    P.op("act", lambda e, eL=eL: e.activation(out=eL[:], in_=psm[:, 8:16], func=AF.Exp, scale=-1.0), reads=[psm], writes=[eL])
                    P.op("dve", lambda e, vs=vs, ig=ig: e.tensor_tensor(out=vs[:], in0=psm[:, 0:8], in1=ig, op=ALU.add), reads=[psm, g], writes=[vs])
                    P.op("act", lambda e, vs=vs: e.activation(out=vs[:], in_=vs[:], func=AF.Exp, bias=ln8[:, 0:1]), reads=[vs, ln8], writes=[vs])
                    P.op("dve", lambda e, ua=ua, vs=vs, Vt=Vt: e.tensor_tensor(out=Vt[:, :, 0:64], in0=ua[:, 1024:1536].rearrange("p (h d) -> p h d", h=8),
                                                                      in1=vs[:].unsqueeze(2).to_broadcast([128, 8, 64]), op=ALU.mult), reads=[ua, vs], writes=[Vt])
                    P.op("pool", lambda e, vs=vs, Vt=Vt: e.tensor_copy(out=Vt[:, :, 64], in_=vs[:]), reads=[vs], writes=[Vt])
                    for i in range(8):
                        P.op("pe", lambda e, i=i, ua=ua, pT=pT: e.transpose(out=pT[:, i, :], in_=ua[:, i * 128:(i + 1) * 128], identity=ident[:]), reads=[ua, ident], writes=[pT])
                    P.op("act", lambda e, QT=QT, pT=pT: e.activation(out=QT[:], in_=pT[:, 0:4, :], func=AF.Copy), reads=[pT], writes=[QT])
                    P.op("dve", lambda e, KT=KT, pT=pT: e.tensor_copy(out=KT[:], in_=pT[:, 4:8, :]), reads=[pT], writes=[KT])
                    for h in range(8):
                        hp, ho = h // 2, (h % 2) * 64
                        AT = d["AT"][h % 2]
                        pa = d["pa"]
                        P.op("pe", lambda e, KT=KT, QT=QT, hp=hp, ho=ho, pa=pa: e.matmul(pa[:], lhsT=KT[ho:ho + 64, hp, :], rhs=QT[ho:ho + 64, hp, :], start=True, stop=True),
                             reads=[KT, QT], writes=[pa])
                        P.op("dve", lambda e, AT=AT, pa=pa, mk=mk: e.tensor_tensor(out=AT[:], in0=pa[:], in1=mk, op=ALU.mult), reads=[pa, masks], writes=[AT])
                        P.op("pe", lambda e, AT=AT, Vt=Vt, h=h: e.matmul(pn[:, h, 0:65], lhsT=AT[:], rhs=Vt[:, h, :], start=True, stop=False), reads=[AT, Vt], writes=[pn])
                        P.op("pe", lambda e, QT=QT, C=C, h=h, hp=hp, ho=ho: e.matmul(pn[:, h, 0:65], lhsT=QT[ho:ho + 64, hp, :], rhs=C[ho:ho + 64, hp, :], start=False, stop=True),
                             reads=[QT, C], writes=[pn])
                    P.op("dve", lambda e, sm=sm, eb=eb: e.tensor_tensor(out=sm[:, 0, :], in0=pn[:, :, 64], in1=eb[:], op=ALU.mult), reads=[pn, eb], writes=[sm])
                    P.op("dve", lambda e, sm=sm: e.scalar_tensor_tensor(out=sm[:, 1, :], in0=sm[:, 0, :], scalar=-1.0, in1=sm[:, 0, :], op0=ALU.mult, op1=ALU.max), reads=[sm], writes=[sm])
                    P.op("dve", lambda e, sm=sm: e.tensor_scalar_max(out=sm[:, 2, :], in0=sm[:, 1, :], scalar1=1.0), reads=[sm], writes=[sm])
                    P.op("dve", lambda e, sm=sm: e.reciprocal(out=sm[:, 3, :], in_=sm[:, 2, :]), reads=[sm], writes=[sm])
                    P.op("dve", lambda e, sm=sm, eb=eb: e.tensor_tensor(out=sm[:, 4, :], in0=sm[:, 3, :], in1=eb[:], op=ALU.mult), reads=[sm, eb], writes=[sm])
                    hd = d["hd"]
                    P.op("dve", lambda e, sm=sm, hd=hd: e.tensor_tensor(out=hd[:], in0=pn[:, :, 0:64], in1=sm[:, 4, :].unsqueeze(2).to_broadcast([128, 8, 64]), op=ALU.mult),
                         reads=[pn, sm], writes=[hd])
                    for hp in range(4):
                        P.op("pe", lambda e, ua=ua, Vt=Vt, hp=hp: e.matmul(pc[:], lhsT=ua[:, 512 + hp * 128: 512 + (hp + 1) * 128], rhs=Vt[:, 2 * hp:2 * hp + 2, :], start=True, stop=True),
                             reads=[ua, Vt], writes=[pc])
                        for ho_i in range(2):
                            ho = ho_i * 64
                            h = 2 * hp + ho_i
                            Ct = d["Ct"]
                            P.op("dve", lambda e, C=C, Ct=Ct, hp=hp, ho=ho, ho_i=ho_i: e.tensor_tensor(out=Ct[ho:ho + 64, :], in0=pc[ho:ho + 64, ho_i, :], in1=C[ho:ho + 64, hp, :], op=ALU.add),
                                 reads=[pc, C], writes=[Ct])
                            P.op("act", lambda e, C=C, Ct=Ct, eL=eL, hp=hp, ho=ho, h=h: e.activation(out=C[ho:ho + 64, hp, :], in_=Ct[ho:ho + 64, :], func=AF.Copy, scale=eL[ho:ho + 64, h:h + 1]),
                                 reads=[Ct, eL], writes=[C])
                    hdf = hd[:].rearrange("p h d -> p (h d)")
                    if dr == 0:
                        P.dma("act", YS[t * 128:(t + 1) * 128, 0:512], hdf, reads=[hd], writes=[("YS", t)])
                    else:
                        hf, sq, sg = d["hf"], d["sq"], d["sg"]
                        P.dma("sp", hf[:], YS[t * 128:(t + 1) * 128, 0:512], reads=[("YS", t)], writes=[hf])
                        P.op("pool", lambda e, hf=hf, hdf=hdf: e.tensor_tensor(out=hf[:], in0=hf[:], in1=hdf, op=ALU.add), reads=[hf, hd], writes=[hf])
                        P.op("act", lambda e, hf=hf, sq=sq: e.activation(out=sq[:], in_=hf[:], func=AF.Square), reads=[hf], writes=[sq])
                        P.op("dve", lambda e, sq=sq, sm=sm: e.tensor_reduce(out=sm[:, 5, :], in_=sq[:].rearrange("p (h d) -> p h d", h=8), axis=AX.X, op=ALU.add), reads=[sq], writes=[sm])
                        P.op("act", lambda e, sm=sm: e.activation(out=sm[:, 5, :], in_=sm[:, 5, :], func=AF.Sqrt, scale=1.0 / 64, bias=eps_c[:, 0:1]), reads=[sm, eps_c], writes=[sm])
                        P.op("dve", lambda e, sm=sm: e.reciprocal(out=sm[:, 5, :], in_=sm[:, 5, :]), reads=[sm], writes=[sm])
                        P.op("dve", lambda e, hf=hf, sm=sm: e.tensor_tensor(out=hf[:].rearrange("p (h d) -> p h d", h=8), in0=hf[:].rearrange("p (h d) -> p h d", h=8),
                                                                      in1=sm[:, 5, :].unsqueeze(2).to_broadcast([128, 8, 64]), op=ALU.mult), reads=[hf, sm], writes=[hf])
                        P.op("pool", lambda e, hf=hf: e.tensor_tensor(out=hf[:], in0=hf[:], in1=ng_bc[:], op=ALU.mult), reads=[hf, ng_bc], writes=[hf])
                        P.op("act", lambda e, ua=ua, sg=sg: e.activation(out=sg[:], in_=ua[:, 1536:2048], func=AF.Sigmoid), reads=[ua], writes=[sg])
                        P.op("dve", lambda e, hf=hf, sg=sg: e.tensor_tensor(out=sg[:], in0=hf[:], in1=sg[:], op=ALU.mult), reads=[hf, sg], writes=[sg])
                        P.dma("act", O[t * 128:(t + 1) * 128, 0:512], sg[:], reads=[sg], writes=[("O", t, 0)])
        P.pop()

    def diffattn(l):
        NB = cfg.NB
        lam_init = 0.8 - 0.6 * math.exp(-0.3 * l)
        SC = 32 ** -0.5
        CSH = 4.0
        P.push()
        g2 = P.sbuf("dag2", [128, 2, 32]); eps_c = P.sbuf("daeps", [128, 1])
        P.dma("sp", g2[:, 0, :], io["diff_qk_g"][0:1, :].to_broadcast([128, 32]), writes=[g2])
        P.dma("sp", g2[:, 1, :], io["diff_qk_g"][1:2, :].to_broadcast([128, 32]), writes=[g2])
        P.op("dve", lambda e: e.memset(eps_c[:], EPS), writes=[eps_c])
        ubs = [P.sbuf(f"daub{i}", [128, 1024]) for i in range(2)]
        sqs = P.sbuf("dasq", [128, 1024]); rs = [P.sbuf(f"dars{i}", [128, 32]) for i in range(2)]
        qn = [P.sbuf(f"daqn{i}", [128, 1024]) for i in range(2)]
        qr = [P.sbuf(f"daqr{i}", [128, 1024]) for i in range(2)]
        tA = P.sbuf("datA", [128, 2, 512]); tB = P.sbuf("datB", [128, 2, 512])
        cs = [P.sbuf(f"dacs{i}", [128, 32]) for i in range(2)]
        QTt = [P.sbuf(f"daQTt{i}", [64, 16, 128]) for i in range(2)]
        pT = [P.psum(f"dapT{i}", [64, 8, 128]) for i in range(2)]
        for t in range(cfg.ntile):
            b, tt = divmod(t, NTS)
            isctx = tt < TC // 128
            ub = ubs[t % 2]; r_ = rs[t % 2]; qn_ = qn[t % 2]; qr_ = qr[t % 2]; cs_ = cs[t % 2]; qt_ = QTt[t % 2]
            P.dma("sp", ub[:], U[t * 128:(t + 1) * 128, A_IN:A_IN + 1024], reads=ukeys(t), writes=[ub])
            P.op("act", lambda e, ub=ub: e.activation(out=sqs[:], in_=ub[:], func=AF.Square), reads=[ub], writes=[sqs])
            P.op("dve", lambda e, r_=r_: e.tensor_reduce(out=r_[:], in_=sqs[:].rearrange("p (g d) -> p g d", g=32), axis=AX.X, op=ALU.add), reads=[sqs], writes=[r_])
            P.op("act", lambda e, r_=r_: e.activation(out=r_[:], in_=r_[:], func=AF.Sqrt, scale=1.0 / 32, bias=eps_c[:, 0:1]), reads=[r_, eps_c], writes=[r_])
            P.op("dve", lambda e, r_=r_: e.reciprocal(out=r_[:], in_=r_[:]), reads=[r_], writes=[r_])
            P.op("dve", lambda e, ub=ub, r_=r_, qn_=qn_: e.tensor_tensor(out=qn_[:].rearrange("p (g d) -> p g d", g=32), in0=ub[:].rearrange("p (g d) -> p g d", g=32),
                                                                   in1=r_[:].unsqueeze(2).to_broadcast([128, 32, 32]), op=ALU.mult), reads=[ub, r_], writes=[qn_])
            dst = qr_ if isctx else qn_
            P.op("pool", lambda e, qn_=qn_, dst=dst: e.tensor_tensor(out=dst[:].rearrange("p (a g d) -> p a g d", a=2, g=16), in0=qn_[:].rearrange("p (a g d) -> p a g d", a=2, g=16),
                                                                in1=g2[:].unsqueeze(2).to_broadcast([128, 2, 16, 32]), op=ALU.mult), reads=[qn_, g2], writes=[dst])
            if not isctx:
                lrow = (tt - TC // 128) * 128
                P.dma("sp", cs_[:], io["rope"][lrow:lrow + 128, :], writes=[cs_])
                v4 = qn_[:].rearrange("p (g i two) -> p g i two", g=32, two=2)
                o4 = qr_[:].rearrange("p (g i two) -> p g i two", g=32, two=2)
                x1, x2 = v4[:, :, :, 0], v4[:, :, :, 1]
                cosb = cs_[:, 0:16].unsqueeze(1).to_broadcast([128, 32, 16])
                sinb = cs_[:, 16:32].unsqueeze(1).to_broadcast([128, 32, 16])
                tA0 = tA[:, 0, :].rearrange("p (g i) -> p g i", g=32); tA1 = tA[:, 1, :].rearrange("p (g i) -> p g i", g=32)
                tB0 = tB[:, 0, :].rearrange("p (g i) -> p g i", g=32); tB1 = tB[:, 1, :].rearrange("p (g i) -> p g i", g=32)
                P.op("dve", lambda e, x1=x1, cosb=cosb, tA0=tA0: e.tensor_tensor(out=tA0, in0=x1, in1=cosb, op=ALU.mult), reads=[qn_, cs_], writes=[tA])
                P.op("dve", lambda e, x2=x2, sinb=sinb, tA1=tA1: e.tensor_tensor(out=tA1, in0=x2, in1=sinb, op=ALU.mult), reads=[qn_, cs_], writes=[tA])
                P.op("dve", lambda e, o4=o4, tA0=tA0, tA1=tA1: e.tensor_tensor(out=o4[:, :, :, 0], in0=tA0, in1=tA1, op=ALU.subtract), reads=[tA], writes=[qr_])
                P.op("pool", lambda e, x1=x1, sinb=sinb, tB0=tB0: e.tensor_tensor(out=tB0, in0=x1, in1=sinb, op=ALU.mult), reads=[qn_, cs_], writes=[tB])
                P.op("pool", lambda e, x2=x2, cosb=cosb, tB1=tB1: e.tensor_tensor(out=tB1, in0=x2, in1=cosb, op=ALU.mult), reads=[qn_, cs_], writes=[tB])
                P.op("pool", lambda e, o4=o4, tB0=tB0, tB1=tB1: e.tensor_tensor(out=o4[:, :, :, 1], in0=tB0, in1=tB1, op=ALU.add), reads=[tB], writes=[qr_])
            for a in range(2):
                pt_ = pT[a]
                for i in range(8):
                    c0 = a * 512 + i * 64
                    P.op("pe", lambda e, pt_=pt_, i=i, c0=c0, qr_=qr_: e.transpose(out=pt_[:, i, :], in_=qr_[:, c0:c0 + 64], identity=ident[:]), reads=[qr_, ident], writes=[pt_])
                if a == 0:
                    P.op("act", lambda e, pt_=pt_, qt_=qt_: e.activation(out=qt_[:, 0:8, :], in_=pt_[:], func=AF.Copy), reads=[pt_], writes=[qt_])
                else:
                    P.op("dve", lambda e, pt_=pt_, qt_=qt_: e.tensor_copy(out=qt_[:, 8:16, :], in_=pt_[:]), reads=[pt_], writes=[qt_])
            for a in range(2):
                P.dma("act", QKT[b, a, :, :, tt * 128:(tt + 1) * 128].rearrange("h p n -> p h n"), qt_[:, a * 8:(a + 1) * 8, :], reads=[qt_], writes=[("QKT", b, a, tt)])
        P.pop()
        P.push()
        lamr = P.sbuf("dalam", [128, 4, 32]); lt = P.sbuf("dalt", [128, 2, 32]); le = P.sbuf("dale", [128, 2]); nlam = P.sbuf("danlam", [128, 1])
        negc = P.sbuf("danegc", [128, 1]); eps2 = P.sbuf("daeps2", [128, 1]); gs = P.sbuf("dags", [128, 64])
        P.dma("sp", lamr[:].rearrange("p a d -> p (a d)"), io["diff_lam"].to_broadcast([128, 128]), writes=[lamr])
        P.dma("sp", gs[:], io["diff_subln_g"].to_broadcast([128, 64]), writes=[gs])
        P.op("dve", lambda e: e.memset(negc[:], -CSH), writes=[negc])
        P.op("dve", lambda e: e.memset(eps2[:], EPS), writes=[eps2])
        P.op("dve", lambda e: e.tensor_tensor(out=lt[:, 0, :], in0=lamr[:, 0, :], in1=lamr[:, 1, :], op=ALU.mult), reads=[lamr], writes=[lt])
        P.op("dve", lambda e: e.tensor_tensor(out=lt[:, 1, :], in0=lamr[:, 2, :], in1=lamr[:, 3, :], op=ALU.mult), reads=[lamr], writes=[lt])
        P.op("dve", lambda e: e.tensor_reduce(out=le[:], in_=lt[:], axis=AX.X, op=ALU.add), reads=[lt], writes=[le])
        P.op("act", lambda e: e.activation(out=le[:], in_=le[:], func=AF.Exp), reads=[le], writes=[le])
        P.op("dve", lambda e: e.tensor_tensor(out=nlam[:], in0=le[:, 1:2], in1=le[:, 0:1], op=ALU.subtract), reads=[le], writes=[nlam])
        P.op("dve", lambda e: e.tensor_scalar_add(out=nlam[:], in0=nlam[:], scalar1=-lam_init), reads=[nlam], writes=[nlam])
        P.op("dve", lambda e: e.tensor_scalar_mul(out=gs[:], in0=gs[:], scalar1=1.0 - lam_init), reads=[gs], writes=[gs])
        QTh = [P.sbuf(f"daQTh{i}", [64, cfg.TS]) for i in range(2)]
        KTh = [P.sbuf(f"daKTh{i}", [64, cfg.TS]) for i in range(2)]
        Vst = [P.sbuf(f"daVst{i}", [128, NTS, 64]) for i in range(2)]
        Vr = [P.sbuf(f"daVr{i}", [128, NTS, 65]) for i in range(2)]
        for i in range(2):
            for j0 in range(0, NTS, 128):
                j1 = min(NTS, j0 + 128)
                P.op("dve", lambda e, i=i, j0=j0, j1=j1: e.tensor_copy(out=Vr[i][:, j0:j1, 64].bitcast(F32R), in_=ones[:, 0:j1 - j0]), reads=[ones], writes=[Vr[i]])
        PT = [P.sbuf(f"daPT{i}", [128, 512]) for i in range(3)]
        accs = [P.sbuf(f"daaccs{i}", [65, 512]) for i in range(2)]
        Ot = [P.sbuf(f"daOt{i}", [128, 4, 64]) for i in range(2)]
        o2 = P.sbuf("dao2", [128, 4, 64]); sq = P.sbuf("dasq2", [128, 4, 64])
        sm = [P.sbuf(f"dasm{i}", [128, 4, 4]) for i in range(2)]
        ps = [P.psum(f"daps{i}", [128, 512]) for i in range(2)]
        acc = [P.psum(f"daacc{i}", [65, 512]) for i in range(2)]
        pTn = [P.psum(f"dapTn{i}", [128, 4, 65]) for i in range(2)]
        cnt = {"ps": 0, "pt": 0, "blk": 0}
        for b in range(NB):
            for h in range(8):
                it = b * 8 + h
                qth, kth, vst, vr = QTh[it % 2], KTh[it % 2], Vst[it % 2], Vr[it % 2]
                P.dma("sp", qth[:], QKT[b, 0, h], reads=[("QKT", b, 0, tt) for tt in range(NTS)], writes=[qth])
                P.dma("sp", kth[:], QKT[b, 1, h], reads=[("QKT", b, 1, tt) for tt in range(NTS)], writes=[kth])
                c0 = A_IN + 1024 + h * 64
                P.dma("sp", vst[:], U[b * cfg.TS:(b + 1) * cfg.TS, c0:c0 + 64].rearrange("(n p) d -> p n d", p=128),
                      reads=[k_ for tt in range(NTS) for k_ in ukeys(b * NTS + tt)], writes=[vst])
                P.op("pool", lambda e, qth=qth: e.tensor_copy(out=qth[:].bitcast(F32R), in_=qth[:]), reads=[qth], writes=[qth])
                P.op("pool", lambda e, kth=kth: e.tensor_copy(out=kth[:].bitcast(F32R), in_=kth[:]), reads=[kth], writes=[kth])
                P.op("pool", lambda e, vst=vst, vr=vr: e.tensor_copy(out=vr[:, :, 0:64].bitcast(F32R), in_=vst[:]), reads=[vst], writes=[vr])
                blocks = [(0, TC, list(range(TC // 128)))]
                q0 = TC
                while q0 < cfg.TS:
                    nq = min(512, cfg.TS - q0)
                    blocks.append((q0, nq, list(range(NTS))))
                    q0 += nq
                for (q0, nq, kts) in blocks:
                    nsub = nq // 128
                    for c in range(2):
                        for ki, kt in enumerate(kts):
                            p_ = ps[cnt["ps"] % 2]; cnt["ps"] += 1
                            pt_ = PT[cnt["pt"] % 3]; cnt["pt"] += 1
                            P.op("pe", lambda e, p_=p_, kth=kth, qth=qth, c=c, kt=kt, q0=q0, nq=nq: e.matmul(
                                p_[:, 0:nq], lhsT=kth[c * 32:(c + 1) * 32, kt * 128:(kt + 1) * 128].bitcast(F32R), rhs=qth[c * 32:(c + 1) * 32, q0:q0 + nq].bitcast(F32R),
                                start=True, stop=True), reads=[kth, qth], writes=[p_])
                            P.op("act", lambda e, p_=p_, pt_=pt_, nq=nq: e.activation(out=pt_[:, 0:nq].bitcast(F32R), in_=p_[:, 0:nq], func=AF.Exp, scale=SC, bias=negc[:, 0:1]),
                                 reads=[p_, negc], writes=[pt_])
                            P.op("pe", lambda e, pt_=pt_, vr=vr, c=c, kt=kt, nq=nq, ki=ki, nk=len(kts): e.matmul(
                                acc[c][:, 0:nq], lhsT=vr[:, kt, :].bitcast(F32R), rhs=pt_[:, 0:nq].bitcast(F32R), start=(ki == 0), stop=(ki == nk - 1)),
                                reads=[vr, pt_], writes=[acc[c]])
                        if c == 0:
                            P.op("dve", lambda e, c=c, nq=nq: e.tensor_copy(out=accs[c][:, 0:nq], in_=acc[c][:, 0:nq]), reads=[acc[c]], writes=[accs[c]])
                        else:
                            P.op("dve", lambda e, c=c, nq=nq: e.tensor_copy(out=accs[c][:, 0:nq], in_=acc[c][:, 0:nq]), reads=[acc[c]], writes=[accs[c]])
                        for j in range(nsub):
                            P.op("pe", lambda e, c=c, j=j: e.transpose(out=pTn[c][:, j, :], in_=accs[c][:, j * 128:(j + 1) * 128], identity=ident[0:65, 0:65]),
                                 reads=[accs[c], ident], writes=[pTn[c]])
                    bi = cnt["blk"]; cnt["blk"] += 1
                    ot = Ot[bi % 2]; sm_ = sm[bi % 2]
                    n_ = nsub
                    P.op("dve", lambda e, sm_=sm_, n_=n_: e.reciprocal(out=sm_[:, 0, 0:n_], in_=pTn[0][:, 0:n_, 64]), reads=[pTn[0]], writes=[sm_])
                    P.op("dve", lambda e, sm_=sm_, n_=n_: e.reciprocal(out=sm_[:, 1, 0:n_], in_=pTn[1][:, 0:n_, 64]), reads=[pTn[1]], writes=[sm_])
                    P.op("dve", lambda e, sm_=sm_, n_=n_: e.tensor_scalar(out=sm_[:, 1, 0:n_], in0=sm_[:, 1, 0:n_], scalar1=nlam[:, 0:1], scalar2=None, op0=ALU.mult), reads=[sm_, nlam], writes=[sm_])
                    P.op("dve", lambda e, sm_=sm_, n_=n_, ot=ot: e.tensor_tensor(out=ot[:, 0:n_, :], in0=pTn[0][:, 0:n_, 0:64], in1=sm_[:, 0, 0:n_].unsqueeze(2).to_broadcast([128, n_, 64]), op=ALU.mult),
                         reads=[pTn[0], sm_], writes=[ot])
                    P.op("dve", lambda e, sm_=sm_, n_=n_: e.tensor_tensor(out=o2[:, 0:n_, :], in0=pTn[1][:, 0:n_, 0:64], in1=sm_[:, 1, 0:n_].unsqueeze(2).to_broadcast([128, n_, 64]), op=ALU.mult),
                         reads=[pTn[1], sm_], writes=[o2])
                    P.op("pool", lambda e, n_=n_, ot=ot: e.tensor_tensor(out=ot[:, 0:n_, :], in0=ot[:, 0:n_, :], in1=o2[:, 0:n_, :], op=ALU.add), reads=[ot, o2], writes=[ot])
                    P.op("act", lambda e, n_=n_, ot=ot: e.activation(out=sq[:, 0:n_, :], in_=ot[:, 0:n_, :], func=AF.Square), reads=[ot], writes=[sq])
                    P.op("dve", lambda e, sm_=sm_, n_=n_: e.tensor_reduce(out=sm_[:, 2, 0:n_], in_=sq[:, 0:n_, :], axis=AX.X, op=ALU.add), reads=[sq], writes=[sm_])
                    P.op("act", lambda e, sm_=sm_, n_=n_: e.activation(out=sm_[:, 2, 0:n_], in_=sm_[:, 2, 0:n_], func=AF.Sqrt, scale=1.0 / 64, bias=eps2[:, 0:1]), reads=[sm_, eps2], writes=[sm_])
                    P.op("dve", lambda e, sm_=sm_, n_=n_: e.reciprocal(out=sm_[:, 2, 0:n_], in_=sm_[:, 2, 0:n_]), reads=[sm_], writes=[sm_])
                    P.op("dve", lambda e, sm_=sm_, n_=n_, ot=ot: e.tensor_tensor(out=ot[:, 0:n_, :], in0=ot[:, 0:n_, :], in1=sm_[:, 2, 0:n_].unsqueeze(2).to_broadcast([128, n_, 64]), op=ALU.mult),
                         reads=[ot, sm_], writes=[ot])
                    P.op("pool", lambda e, n_=n_, ot=ot: e.tensor_tensor(out=ot[:, 0:n_, :], in0=ot[:, 0:n_, :], in1=gs[:].unsqueeze(1).to_broadcast([128, n_, 64]), op=ALU.mult),
                         reads=[ot, gs], writes=[ot])
                    r0 = b * cfg.TS + q0
                    t0 = r0 // 128
                    P.dma("act", O[r0:r0 + nq, 512 + h * 64: 512 + (h + 1) * 64].rearrange("(n p) d -> p n d", p=128), ot[:, 0:n_, :], reads=[ot],
                          writes=[("O", t0 + j, 1) for j in range(n_)])
        P.pop()

    def gla(l):
        need_ctx = (l == 0) or ("ctx_out" in dbg)
        NB = cfg.NB
        c0 = C_IN
        P.push()
        gw2 = P.sbuf("glgw2", [16, 2, 256]); gbr = P.sbuf("glgb", [1, 2, 256]); ng_bc = P.sbuf("glng", [128, 128])
        onec = P.sbuf("glone", [128, 1]); eps_c = P.sbuf("gleps", [128, 1])
        P.dma("sp", gw2[:], io["gla_gate_w2"].rearrange("d k n -> k d n"), writes=[gw2])
        P.dma("sp", gbr[:], io["gla_gate_b"].rearrange("(o d) n -> o d n", o=1), writes=[gbr])
        P.dma("sp", ng_bc[:], io["gla_norm_g"].to_broadcast([128, 128]), writes=[ng_bc])
        P.op("dve", lambda e: e.memset(onec[:], 1.0), writes=[onec])
        P.op("dve", lambda e: e.memset(eps_c[:], EPS), writes=[eps_c])
        B = []
        for b in range(NB):
            d = {}
            d["ud"] = [P.sbuf(f"glud{b}{i}", [128, D_IN]) for i in range(2)]
            d["gdT"] = P.sbuf(f"glgdT{b}", [16, 128])
            d["nla"] = P.sbuf(f"glnla{b}", [128, 256])
            d["Kh"] = P.sbuf(f"glKh{b}", [128, 256])
            d["eq"] = P.sbuf(f"gleq{b}", [128, 2, 128]); d["ek"] = P.sbuf(f"glek{b}", [128, 2, 128])
            d["QT"] = P.sbuf(f"glQT{b}", [128, 2, 128]); d["KT"] = P.sbuf(f"glKT{b}", [128, 2, 128])
            d["AT"] = [P.sbuf(f"glAT{b}{i}", [128, 128]) for i in range(2)]
            d["S"] = P.sbuf(f"glS{b}", [128, 2, 128])
            d["od"] = P.sbuf(f"glod{b}", [128, 512]); d["of"] = P.sbuf(f"glof{b}", [128, 512])
            d["sq"] = P.sbuf(f"glsq{b}", [128, 512]); d["sm"] = P.sbuf(f"glsm{b}", [128, 4])
            d["sg"] = P.sbuf(f"glsg{b}", [128, 512])
            B.append(d)
        pg = P.psum("glpg", [16, 128]); pz = P.psum("glpz", [128, 256]); pcs = P.psum("glpcs", [128, 2, 256])
        pbT = P.psum("glpbT", [128, 2, 128]); pqk = P.psum("glpqk", [128, 4, 128]); pa = P.psum("glpa", [128, 128])
        po = P.psum("glpo", [128, 4, 128]); pc = P.psum("glpc", [128, 256])
        nct = TC // 128
        for dr in range(2):
            mk = masks[:, dr, :]
            mks = masks[:, 4 + dr, :]
            lastcol = 127 if dr == 0 else 0
            for b in range(NB):
                P.op("dve", lambda e, b=b: e.memset(B[b]["S"][:], 0.0), writes=[B[b]["S"]])
            order = list(range(NTS)) if dr == 0 else [1, 0] + list(range(NTS - 1, 1, -1))
            for step, tt in enumerate(order):
                want_out = need_ctx or tt >= nct
                for b in range(NB):
                    d = B[b]
                    t = b * NTS + tt
                    ud = d["ud"][step % 2]
                    gdT, nla, Kh, eq, ek, QT, KT, S = d["gdT"], d["nla"], d["Kh"], d["eq"], d["ek"], d["QT"], d["KT"], d["S"]
                    P.dma("sp", ud[:], U[t * 128:(t + 1) * 128, c0:c0 + D_IN], reads=ukeys(t), writes=[ud])
                    gc = 1536 + 16 * dr
                    P.op("pe", lambda e, ud=ud, gc=gc: e.transpose(out=pg[:], in_=ud[:, gc:gc + 16], identity=ident[:]), reads=[ud, ident], writes=[pg])
                    P.op("act", lambda e, gdT=gdT: e.activation(out=gdT[:], in_=pg[:], func=AF.Copy), reads=[pg], writes=[gdT])
                    P.op("pe", lambda e, gdT=gdT, dr=dr: e.matmul(pz[:], lhsT=gdT[:], rhs=gw2[:, dr, :], start=True, stop=False), reads=[gdT, gw2], writes=[pz])
                    P.op("pe", lambda e, dr=dr: e.matmul(pz[:], lhsT=ones[0:1, :], rhs=gbr[0:1, dr, :], start=False, stop=True), reads=[ones, gbr], writes=[pz])
                    P.op("act", lambda e, nla=nla: e.activation(out=nla[:], in_=pz[:], func=AF.Exp, scale=-1.0), reads=[pz], writes=[nla])
                    P.op("act", lambda e, nla=nla: e.activation(out=nla[:], in_=nla[:], func=AF.Ln, bias=onec[:, 0:1]), reads=[nla, onec], writes=[nla])
                    P.op("pe", lambda e, nla=nla, mks=mks: e.matmul(pcs[:, 1, :], lhsT=mks, rhs=nla[:], start=True, stop=True), reads=[masks, nla], writes=[pcs])
                    P.op("act", lambda e, Kh=Kh: e.activation(out=Kh[:], in_=pcs[:, 1, :], func=AF.Exp, scale=-1.0 / 16), reads=[pcs], writes=[Kh])
                    P.op("dve", lambda e, Kh=Kh, ud=ud: e.tensor_tensor(out=Kh[:], in0=Kh[:], in1=ud[:, 256:512], op=ALU.mult), reads=[Kh, ud], writes=[Kh])
                    for p in range(2):
                        P.op("pe", lambda e, nla=nla, p=p, mk=mk: e.matmul(pbT[:, p, :], lhsT=nla[:, p * 128:(p + 1) * 128], rhs=mk, start=True, stop=True), reads=[nla, masks], writes=[pbT])
                    P.op("act", lambda e, eq=eq: e.activation(out=eq[:], in_=pbT[:], func=AF.Exp, scale=-1.0 / 16), reads=[pbT], writes=[eq])
                    P.op("act", lambda e, ek=ek: e.activation(out=ek[:], in_=pbT[:], func=AF.Exp, scale=1.0 / 16), reads=[pbT], writes=[ek])
                    for i in range(4):
                        P.op("pe", lambda e, ud=ud, i=i: e.transpose(out=pqk[:, i, :], in_=ud[:, i * 128:(i + 1) * 128], identity=ident[:]), reads=[ud, ident], writes=[pqk])
                    P.op("dve", lambda e, QT=QT, eq=eq: e.scalar_tensor_tensor(out=QT[:], in0=pqk[:, 0:2, :], scalar=0.125, in1=eq[:], op0=ALU.mult, op1=ALU.mult), reads=[pqk, eq], writes=[QT])
                    P.op("dve", lambda e, KT=KT, ek=ek: e.tensor_tensor(out=KT[:], in0=pqk[:, 2:4, :], in1=ek[:], op=ALU.mult), reads=[pqk, ek], writes=[KT])
                    for h in range(4):
                        p, ho = h // 2, (h % 2) * 64
                        if want_out:
                            AT = d["AT"][h % 2]
                            P.op("pe", lambda e, KT=KT, QT=QT, p=p, ho=ho: e.matmul(pa[:], lhsT=KT[ho:ho + 64, p, :], rhs=QT[ho:ho + 64, p, :], start=True, stop=True), reads=[KT, QT], writes=[pa])
                            P.op("dve", lambda e, AT=AT, mk=mk: e.tensor_tensor(out=AT[:], in0=pa[:], in1=mk, op=ALU.mult), reads=[pa, masks], writes=[AT])
                            P.op("pe", lambda e, AT=AT, ud=ud, h=h: e.matmul(po[:, h, :], lhsT=AT[:], rhs=ud[:, 512 + h * 128: 512 + (h + 1) * 128], start=True, stop=False), reads=[AT, ud], writes=[po])
                            P.op("pe", lambda e, QT=QT, S=S, h=h, p=p, ho=ho: e.matmul(po[:, h, :], lhsT=QT[ho:ho + 64, p, :], rhs=S[ho:ho + 64, p, :], start=False, stop=True), reads=[QT, S], writes=[po])
                    od = d["od"]
                    if want_out:
                        P.op("act", lambda e, od=od: e.activation(out=od[:], in_=po[:].rearrange("p h d -> p (h d)"), func=AF.Copy), reads=[po], writes=[od])
                    for p in range(2):
                        P.op("pe", lambda e, Kh=Kh, ud=ud, p=p: e.matmul(pc[:], lhsT=Kh[:, p * 128:(p + 1) * 128], rhs=ud[:, 512 + p * 256: 512 + (p + 1) * 256], start=True, stop=True), reads=[Kh, ud], writes=[pc])
                        for hi in range(2):
                            ho = hi * 64
                            P.op("dve", lambda e, S=S, eq=eq, p=p, ho=ho, hi=hi, lastcol=lastcol: e.scalar_tensor_tensor(out=S[ho:ho + 64, p, :], in0=S[ho:ho + 64, p, :], scalar=eq[ho:ho + 64, p, lastcol:lastcol + 1],
                                                                                                 in1=pc[ho:ho + 64, hi * 128:(hi + 1) * 128], op0=ALU.mult, op1=ALU.add), reads=[S, eq, pc], writes=[S])
                    if not want_out:
                        continue
                    if dr == 0:
                        P.dma("act", YS[t * 128:(t + 1) * 128, 0:512], od[:], reads=[od], writes=[("YS", t)])
                    else:
                        of, sq, sm, sg = d["of"], d["sq"], d["sm"], d["sg"]
                        P.dma("sp", of[:], YS[t * 128:(t + 1) * 128, 0:512], reads=[("YS", t)], writes=[of])
                        P.op("pool", lambda e, of=of, od=od: e.tensor_tensor(out=of[:], in0=of[:], in1=od[:], op=ALU.add), reads=[of, od], writes=[of])
                        P.op("act", lambda e, of=of, sq=sq: e.activation(out=sq[:], in_=of[:], func=AF.Square), reads=[of], writes=[sq])
                        P.op("dve", lambda e, sq=sq, sm=sm: e.tensor_reduce(out=sm[:], in_=sq[:].rearrange("p (h d) -> p h d", h=4), axis=AX.X, op=ALU.add), reads=[sq], writes=[sm])
                        P.op("act", lambda e, sm=sm: e.activation(out=sm[:], in_=sm[:], func=AF.Sqrt, scale=1.0 / 128, bias=eps_c[:, 0:1]), reads=[sm, eps_c], writes=[sm])
                        P.op("dve", lambda e, sm=sm: e.reciprocal(out=sm[:], in_=sm[:]), reads=[sm], writes=[sm])
                        P.op("dve", lambda e, of=of, sm=sm: e.tensor_tensor(out=of[:].rearrange("p (h d) -> p h d", h=4), in0=of[:].rearrange("p (h d) -> p h d", h=4),
                                                                      in1=sm[:].unsqueeze(2).to_broadcast([128, 4, 128]), op=ALU.mult), reads=[of, sm], writes=[of])
                        P.op("pool", lambda e, of=of: e.tensor_tensor(out=of[:].rearrange("p (h d) -> p h d", h=4), in0=of[:].rearrange("p (h d) -> p h d", h=4),
                                                                  in1=ng_bc[:].unsqueeze(1).to_broadcast([128, 4, 128]), op=ALU.mult), reads=[of, ng_bc], writes=[of])
                        P.op("act", lambda e, ud=ud, sg=sg: e.activation(out=sg[:], in_=ud[:, 1024:1536], func=AF.Silu), reads=[ud], writes=[sg])
                        P.op("dve", lambda e, of=of, sg=sg: e.tensor_tensor(out=sg[:], in0=of[:], in1=sg[:], op=ALU.mult), reads=[of, sg], writes=[sg])
                        P.dma("act", O[t * 128:(t + 1) * 128, 512:1024], sg[:], reads=[sg], writes=[("O", t, 1)])
        P.pop()

    E05 = math.exp(-0.5)

    def rwkv(l):
        need_ctx = (l == 0) or ("ctx_out" in dbg)
        NB = cfg.NB
        nct = TC // 128
        P.push()
        mu0 = P.sbuf("rwmu0", [128, C_IN]); mu1 = P.sbuf("rwmu1", [128, C_IN]); muc = P.sbuf("rwmuc", [128, C_IN])
        kv0 = P.sbuf("rwkv0", [128, 512]); tiny = P.sbuf("rwtiny", [128, 1])
        P.dma("sp", mu0[:], io["rwkv_mu"][0:1, :].to_broadcast([128, C_IN]), writes=[mu0])
        P.dma("sp", mu1[:], io["rwkv_mu"][1:2, :].to_broadcast([128, C_IN]), writes=[mu1])
        P.dma("sp", kv0[:], io["rwkv_kvec"][0:1, :].to_broadcast([128, 512]), writes=[kv0])
        P.op("dve", lambda e: e.tensor_tensor(out=muc[:], in0=mu0[:], in1=mu1[:], op=ALU.add), reads=[mu0, mu1], writes=[muc])
        P.op("dve", lambda e: e.tensor_scalar(out=muc[:], in0=muc[:], scalar1=-1.0, scalar2=1.0, op0=ALU.mult, op1=ALU.add), reads=[muc], writes=[muc])
        cur = [P.sbuf(f"rwcur{i}", [128, C_IN]) for i in range(2)]
        prv = [P.sbuf(f"rwprv{i}", [128, C_IN]) for i in range(2)]
        nxt = [P.sbuf(f"rwnxt{i}", [128, C_IN]) for i in range(2)]
        rwt = [P.sbuf(f"rwrwt{i}", [128, 2432]) for i in range(2)]
        t1 = P.sbuf("rwt1", [128, C_IN]); sqk = P.sbuf("rwsqk", [128, 512]); ks = [P.sbuf(f"rwks{i}", [128, 8]) for i in range(2)]
        for t in range(cfg.ntile):
            b, tt = divmod(t, NTS)
            r0 = t * 128
            cu, pv, nx, rt, ks_ = cur[t % 2], prv[t % 2], nxt[t % 2], rwt[t % 2], ks[t % 2]
            seg_start = tt in (0, nct)
            seg_end = tt in (nct - 1, NTS - 1)
            P.dma("sp", cu[:], U[r0:r0 + 128, 0:C_IN], reads=ukeys(t), writes=[cu])
            if seg_start:
                P.op("pool", lambda e, pv=pv: e.memset(pv[:], 0.0), writes=[pv])
                P.dma("sp", pv[1:128, :], U[r0:r0 + 127, 0:C_IN], reads=ukeys(t), writes=[pv])
            else:
                P.dma("sp", pv[:], U[r0 - 1:r0 + 127, 0:C_IN], reads=ukeys(t) + ukeys(t - 1), writes=[pv])
            if seg_end:
                P.op("pool", lambda e, nx=nx: e.memset(nx[:], 0.0), writes=[nx])
                P.dma("sp", nx[0:127, :], U[r0 + 1:r0 + 128, 0:C_IN], reads=ukeys(t), writes=[nx])
            else:
                P.dma("sp", nx[:], U[r0 + 1:r0 + 129, 0:C_IN], reads=ukeys(t) + ukeys(t + 1), writes=[nx])
            us = rt[:, 0:C_IN]
            P.op("dve", lambda e, cu=cu, us=us: e.tensor_tensor(out=us, in0=cu[:], in1=muc[:], op=ALU.mult), reads=[cu, muc], writes=[rt])
            P.op("pool", lambda e, pv=pv: e.tensor_tensor(out=pv[:], in0=pv[:], in1=mu0[:], op=ALU.mult), reads=[pv, mu0], writes=[pv])
            P.op("pool", lambda e, nx=nx: e.tensor_tensor(out=nx[:], in0=nx[:], in1=mu1[:], op=ALU.mult), reads=[nx, mu1], writes=[nx])
            P.op("dve", lambda e, pv=pv, us=us: e.tensor_tensor(out=us, in0=us, in1=pv[:], op=ALU.add), reads=[rt, pv], writes=[rt])
            P.op("dve", lambda e, nx=nx, us=us: e.tensor_tensor(out=us, in0=us, in1=nx[:], op=ALU.add), reads=[rt, nx], writes=[rt])
            kkc = rt[:, 1920:2432]
            P.op("pool", lambda e, rt=rt, kkc=kkc: e.tensor_tensor(out=kkc, in0=rt[:, 512:1024], in1=kv0[:], op=ALU.mult), reads=[rt, kv0], writes=[rt])
            P.op("act", lambda e, kkc=kkc: e.activation(out=sqk[:], in_=kkc, func=AF.Square), reads=[rt], writes=[sqk])
            P.op("dve", lambda e, ks_=ks_: e.tensor_reduce(out=ks_[:], in_=sqk[:].rearrange("p (h d) -> p h d", h=8), axis=AX.X, op=ALU.add), reads=[sqk], writes=[ks_])
            P.op("dve", lambda e, ks_=ks_: e.tensor_scalar_max(out=ks_[:], in0=ks_[:], scalar1=1e-24), reads=[ks_], writes=[ks_])
            P.op("act", lambda e, ks_=ks_: e.activation(out=ks_[:], in_=ks_[:], func=AF.Sqrt), reads=[ks_], writes=[ks_])
            P.op("dve", lambda e, ks_=ks_: e.reciprocal(out=ks_[:], in_=ks_[:]), reads=[ks_], writes=[ks_])
            P.op("dve", lambda e, kkc=kkc, ks_=ks_: e.tensor_tensor(out=kkc.rearrange("p (h d) -> p h d", h=8), in0=kkc.rearrange("p (h d) -> p h d", h=8),
                                                              in1=ks_[:].unsqueeze(2).to_broadcast([128, 8, 64]), op=ALU.mult), reads=[rt, ks_], writes=[rt])
            P.dma("act", RW[r0:r0 + 128, :], rt[:], reads=[rt], writes=[("RW", t)])
        P.pop()
        if "rwA" in dbg:
            return
        P.push()
        w2s = P.sbuf("rww2", [64, 2, 512]); a2s = P.sbuf("rwa2", [64, 2, 512]); w0r = P.sbuf("rww0", [1, 2, 512]); a0r = P.sbuf("rwa0", [1, 2, 512])
        g2s = P.sbuf("rwg2", [128, 512]); kv1 = P.sbuf("rwkv1", [128, 512]); omk1 = P.sbuf("rwomk1", [128, 512]); kv2 = P.sbuf("rwkv2", [128, 512])
        ln0 = P.sbuf("rwln0", [128, 512]); ln1 = P.sbuf("rwln1", [128, 512]); lneps = P.sbuf("rwlneps", [128, 1])
        P.dma("sp", w2s[:], io["rwkv_w2"].rearrange("d k n -> k d n"), writes=[w2s])
        P.dma("sp", a2s[:], io["rwkv_a2"].rearrange("d k n -> k d n"), writes=[a2s])
        P.dma("sp", w0r[:], io["rwkv_w0"].rearrange("(o d) n -> o d n", o=1), writes=[w0r])
        P.dma("sp", a0r[:], io["rwkv_a0"].rearrange("(o d) n -> o d n", o=1), writes=[a0r])
        P.dma("sp", g2s[:], io["rwkv_g2"], writes=[g2s])
        P.dma("sp", kv1[:], io["rwkv_kvec"][1:2, :].to_broadcast([128, 512]), writes=[kv1])
        P.dma("sp", kv2[:], io["rwkv_kvec"][2:3, :].to_broadcast([128, 512]), writes=[kv2])
        P.dma("sp", ln0[:], io["rwkv_ln"][0:1, :].to_broadcast([128, 512]), writes=[ln0])
        P.dma("sp", ln1[:], io["rwkv_ln"][1:2, :].to_broadcast([128, 512]), writes=[ln1])
        P.op("dve", lambda e: e.tensor_scalar(out=omk1[:], in0=kv1[:], scalar1=-1.0, scalar2=1.0, op0=ALU.mult, op1=ALU.add), reads=[kv1], writes=[omk1])
        P.op("dve", lambda e: e.memset(lneps[:], 64e-5), writes=[lneps])
        nm = ["sg", "a_", "tt_", "kd", "beta", "eI", "eInv", "eE", "eR", "Rt", "Kt", "Bt", "At", "Kh", "nBh", "tmp"]
        T = {n: P.sbuf("rw_" + n, [128, 512]) for n in nm}
        twT = P.sbuf("rwtwT", [64, 128]); adT = P.sbuf("rwadT", [64, 128]); WL = P.sbuf("rwWL", [128, 4, 2])
        Acur = [P.sbuf(f"rwA{i}", [128, 128]) for i in range(2)]; Atc = [P.sbuf(f"rwAt{i}", [128, 128]) for i in range(2)]
        Pc = [P.sbuf(f"rwP{i}", [128, 128]) for i in range(2)]
        Th = [P.sbuf(f"rwTh{i}", [128, 128]) for i in range(2)]; Mh = [P.sbuf(f"rwMh{i}", [128, 128]) for i in range(2)]
        G3h = [P.sbuf(f"rwG3h{i}", [128, 128]) for i in range(2)]; nG4h = [P.sbuf(f"rwG4h{i}", [128, 128]) for i in range(2)]
        sgT = P.sbuf("rwsgT", [128, 128])
        B = []
        for b in range(NB):
            d = {}
            d["rw"] = P.sbuf(f"rwrw{b}", [128, 2432])
            d["FT"] = P.sbuf(f"rwFT{b}", [128, 4, 4, 128])
            d["ST"] = P.sbuf(f"rwST{b}", [128, 4, 64])
            d["XT"] = P.sbuf(f"rwXT{b}", [128, 2, 64]); d["UT"] = P.sbuf(f"rwUT{b}", [128, 2, 64])
            d["y"] = P.sbuf(f"rwy{b}", [128, 512]); d["bd"] = P.sbuf(f"rwbd{b}", [128, 8])
            d["yf"] = P.sbuf(f"rwyf{b}", [128, 520]); d["sm"] = P.sbuf(f"rwsm{b}", [128, 3, 8])
            B.append(d)
        pw = P.psum("rwpw", [128, 264]); pz = P.psum("rwpz", [128, 512]); pcw = P.psum("rwpcw", [128, 512]); pft = P.psum("rwpft", [128, 4, 128])
        pG = [P.psum(f"rwpG{i}", [128, 128]) for i in range(2)]; pxy = P.psum("rwpxy", [128, 3, 2, 64]); pS = P.psum("rwpS", [128, 128])
        gcnt = {"g": 0, "e": 0}

        def gmm(lhsT, rhs, reads):
            pg_ = pG[gcnt["g"] % 2]; gcnt["g"] += 1
            P.op("pe", lambda e, pg_=pg_, lhsT=lhsT, rhs=rhs: e.matmul(pg_[:], lhsT=lhsT, rhs=rhs, start=True, stop=True), reads=reads, writes=[pg_])
            return pg_

        def evac_copy(dst, src):
            gcnt["e"] += 1
            if gcnt["e"] % 2:
                P.op("act", lambda e, dst=dst, src=src: e.activation(out=dst[:], in_=src[:], func=AF.Copy), reads=[src], writes=[dst])
            else:
                P.op("dve", lambda e, dst=dst, src=src: e.tensor_copy(out=dst[:], in_=src[:]), reads=[src], writes=[dst])

        for dr in range(2):
            for b in range(NB):
                P.op("dve", lambda e, b=b: e.memset(B[b]["ST"][:], 0.0), writes=[B[b]["ST"]])
            order = list(range(NTS)) if dr == 0 else [1, 0] + list(range(NTS - 1, 1, -1))
            m_incl = masks[:, 2 + dr, :]; m_strict = masks[:, 6 + dr, :]; m_strictT = masks[:, 7 - dr, :]
            chunks = (0, 1) if dr == 0 else (1, 0)
            for step, tt in enumerate(order):
                want_out = need_ctx or tt >= nct
                for b in range(NB):
                    d = B[b]
                    t = b * NTS + tt
                    rw, FT, ST, XT, UT, y, bd, sm = d["rw"], d["FT"], d["ST"], d["XT"], d["UT"], d["y"], d["bd"], d["sm"]
                    P.dma("sp", rw[:], RW[t * 128:(t + 1) * 128, :], reads=[("RW", t)], writes=[rw])
                    wc = 1536 + 64 * dr; ac = 1664 + 64 * dr
                    P.op("pe", lambda e, rw=rw, wc=wc: e.transpose(out=pw[0:64, 0:128], in_=rw[:, wc:wc + 64], identity=ident[:]), reads=[rw, ident], writes=[pw])
                    P.op("pe", lambda e, rw=rw, ac=ac: e.transpose(out=pw[0:64, 128:256], in_=rw[:, ac:ac + 64], identity=ident[:]), reads=[rw, ident], writes=[pw])
                    P.op("act", lambda e: e.activation(out=twT[:], in_=pw[0:64, 0:128], func=AF.Tanh), reads=[pw], writes=[twT])
                    P.op("dve", lambda e: e.tensor_copy(out=adT[:], in_=pw[0:64, 128:256]), reads=[pw], writes=[adT])
                    P.op("pe", lambda e, dr=dr: e.matmul(pz[:], lhsT=twT[:], rhs=w2s[:, dr, :], start=True, stop=False), reads=[twT, w2s], writes=[pz])
                    P.op("pe", lambda e, dr=dr: e.matmul(pz[:], lhsT=ones[0:1, :], rhs=w0r[0:1, dr, :], start=False, stop=True), reads=[ones, w0r], writes=[pz])
                    P.op("act", lambda e: e.activation(out=T["sg"][:], in_=pz[:], func=AF.Sigmoid), reads=[pz], writes=[T["sg"]])
                    P.op("pe", lambda e, dr=dr: e.matmul(pz[:], lhsT=adT[:], rhs=a2s[:, dr, :], start=True, stop=False), reads=[adT, a2s], writes=[pz])
                    P.op("pe", lambda e, dr=dr: e.matmul(pz[:], lhsT=ones[0:1, :], rhs=a0r[0:1, dr, :], start=False, stop=True), reads=[ones, a0r], writes=[pz])
                    P.op("act", lambda e: e.activation(out=T["a_"][:], in_=pz[:], func=AF.Sigmoid), reads=[pz], writes=[T["a_"]])
                    P.op("pe", lambda e, m_incl=m_incl: e.matmul(pcw[:], lhsT=m_incl, rhs=T["sg"][:], start=True, stop=True), reads=[masks, T["sg"]], writes=[pcw])
                    P.op("act", lambda e: e.activation(out=T["eI"][:], in_=pcw[:], func=AF.Exp, scale=-E05), reads=[pcw], writes=[T["eI"]])
                    P.op("act", lambda e: e.activation(out=T["eInv"][:], in_=pcw[:], func=AF.Exp, scale=E05), reads=[pcw], writes=[T["eInv"]])
                    P.op("dve", lambda e: e.tensor_tensor(out=T["tmp"][:], in0=pcw[:], in1=T["sg"][:], op=ALU.subtract), reads=[pcw, T["sg"]], writes=[T["tmp"]])
                    P.op("act", lambda e: e.activation(out=T["eE"][:], in_=T["tmp"][:], func=AF.Exp, scale=-E05), reads=[T["tmp"]], writes=[T["eE"]])
                    P.op("pe", lambda e, m_strictT=m_strictT: e.matmul(pcw[:], lhsT=m_strictT, rhs=T["sg"][:], start=True, stop=True), reads=[masks, T["sg"]], writes=[pcw])
                    P.op("act", lambda e: e.activation(out=T["eR"][:], in_=pcw[:], func=AF.Exp, scale=-E05), reads=[pcw], writes=[T["eR"]])
                    for p in range(4):
                        P.op("pe", lambda e, p=p: e.matmul(pw[:, 256 + 2 * p:258 + 2 * p], lhsT=T["sg"][:, p * 128:(p + 1) * 128], rhs=masks[:, 8, 0:2], start=True, stop=True),
                             reads=[T["sg"], masks], writes=[pw])
                    P.op("act", lambda e: e.activation(out=WL[:].rearrange("p a c -> p (a c)"), in_=pw[:, 256:264], func=AF.Exp, scale=-E05), reads=[pw], writes=[WL])
                    r_, k_, v_, kk_ = rw[:, 0:512], rw[:, 512:1024], rw[:, 1024:1536], rw[:, 1920:2432]
                    P.op("dve", lambda e: e.tensor_tensor(out=T["tt_"][:], in0=T["a_"][:], in1=kv1[:], op=ALU.mult), reads=[T["a_"], kv1], writes=[T["tt_"]])
                    P.op("pool", lambda e: e.tensor_tensor(out=T["tt_"][:], in0=T["tt_"][:], in1=omk1[:], op=ALU.add), reads=[T["tt_"], omk1], writes=[T["tt_"]])
                    P.op("dve", lambda e, k_=k_: e.tensor_tensor(out=T["kd"][:], in0=k_, in1=T["tt_"][:], op=ALU.mult), reads=[rw, T["tt_"]], writes=[T["kd"]])
                    P.op("pool", lambda e, kk_=kk_: e.tensor_tensor(out=T["beta"][:], in0=T["a_"][:], in1=kk_, op=ALU.mult), reads=[rw, T["a_"]], writes=[T["beta"]])
                    P.op("dve", lambda e, r_=r_: e.tensor_tensor(out=T["Rt"][:], in0=r_, in1=T["eI"][:], op=ALU.mult), reads=[rw, T["eI"]], writes=[T["Rt"]])
                    P.op("pool", lambda e: e.tensor_tensor(out=T["Kt"][:], in0=T["kd"][:], in1=T["eInv"][:], op=ALU.mult), reads=[T["kd"], T["eInv"]], writes=[T["Kt"]])
                    P.op("dve", lambda e: e.tensor_tensor(out=T["Bt"][:], in0=T["beta"][:], in1=T["eInv"][:], op=ALU.mult), reads=[T["beta"], T["eInv"]], writes=[T["Bt"]])
                    P.op("pool", lambda e, kk_=kk_: e.tensor_tensor(out=T["At"][:], in0=kk_, in1=T["eE"][:], op=ALU.mult), reads=[rw, T["eE"]], writes=[T["At"]])
                    P.op("dve", lambda e: e.tensor_tensor(out=T["Kh"][:], in0=T["kd"][:], in1=T["eR"][:], op=ALU.mult), reads=[T["kd"], T["eR"]], writes=[T["Kh"]])
                    P.op("dve", lambda e: e.scalar_tensor_tensor(out=T["nBh"][:], in0=T["beta"][:], scalar=-1.0, in1=T["eR"][:], op0=ALU.mult, op1=ALU.mult), reads=[T["beta"], T["eR"]], writes=[T["nBh"]])
                    P.op("pool", lambda e, r_=r_: e.tensor_tensor(out=T["tmp"][:], in0=r_, in1=T["kd"][:], op=ALU.mult), reads=[rw, T["kd"]], writes=[T["tmp"]])
                    P.op("dve", lambda e: e.tensor_tensor(out=T["tmp"][:], in0=T["tmp"][:], in1=kv2[:], op=ALU.mult), reads=[T["tmp"], kv2], writes=[T["tmp"]])
                    P.op("dve", lambda e, bd=bd: e.tensor_reduce(out=bd[:], in_=T["tmp"][:].rearrange("p (h d) -> p h d", h=8), axis=AX.X, op=ALU.add), reads=[T["tmp"]], writes=[bd])
                    for ki, kn in enumerate(("Kt", "Bt", "At", "Rt")):
                        src = T[kn]
                        for p in range(4):
                            P.op("pe", lambda e, src=src, p=p: e.transpose(out=pft[:, p, :], in_=src[:, p * 128:(p + 1) * 128], identity=ident[:]), reads=[src, ident], writes=[pft])
                        if ki % 2:
                            P.op("act", lambda e, FT=FT, ki=ki: e.activation(out=FT[:, ki, :, :], in_=pft[:], func=AF.Copy), reads=[pft], writes=[FT])
                        else:
                            P.op("dve", lambda e, FT=FT, ki=ki: e.tensor_copy(out=FT[:, ki, :, :], in_=pft[:]), reads=[pft], writes=[FT])
                    KI, BI, AI, RI = 0, 1, 2, 3
                    for p in range(4 if "rwB1" not in dbg else 0):
                        for hi in range(2):
                            ho = hi * 64
                            fK, fB, fA, fR = (FT[ho:ho + 64, i, p, :] for i in (KI, BI, AI, RI))
                            A0, At0, P0 = Acur[0], Atc[0], Pc[0]
                            g = gmm(fB, fA, [FT])
                            P.op("dve", lambda e, g=g, A0=A0, m_strict=m_strict: e.tensor_tensor(out=A0[:], in0=g[:], in1=m_strict, op=ALU.mult), reads=[g, masks], writes=[A0])
                            P.op("pool", lambda e, A0=A0, P0=P0: e.tensor_tensor(out=P0[:], in0=ident[:], in1=A0[:], op=ALU.subtract), reads=[ident, A0], writes=[P0])
                            g = gmm(fA, fB, [FT])
                            P.op("dve", lambda e, g=g, At0=At0, m_strictT=m_strictT: e.tensor_tensor(out=At0[:], in0=g[:], in1=m_strictT, op=ALU.mult), reads=[g, masks], writes=[At0])
                            g = gmm(fK, fA, [FT])
                            P.op("dve", lambda e, g=g, hi=hi, m_strict=m_strict: e.tensor_tensor(out=Mh[hi][:], in0=g[:], in1=m_strict, op=ALU.mult), reads=[g, masks], writes=[Mh[hi]])
                            if want_out:
                                g = gmm(fK, fR, [FT])
                                P.op("dve", lambda e, g=g, hi=hi, m_incl=m_incl: e.tensor_tensor(out=G3h[hi][:], in0=g[:], in1=m_incl, op=ALU.mult), reads=[g, masks], writes=[G3h[hi]])
                                g = gmm(fB, fR, [FT])
                                P.op("dve", lambda e, g=g, hi=hi, m_incl=m_incl: e.scalar_tensor_tensor(out=nG4h[hi][:], in0=g[:], scalar=-1.0, in1=m_incl, op0=ALU.mult, op1=ALU.mult),
                                     reads=[g, masks], writes=[nG4h[hi]])
                            ia = 0
                            for kq in range(1, 6):
                                Ap, Atp, Pp = Acur[ia], Atc[ia], Pc[ia]
                                An, Atn = Acur[1 - ia], Atc[1 - ia]
                                Pn = Pc[1 - ia] if kq < 5 else Th[hi]
                                g = gmm(Ap[:], Atp[:], [Ap, Atp])
                                evac_copy(Atn, g)
                                if kq < 5:
                                    g = gmm(Atp[:], Ap[:], [Ap, Atp])
                                    evac_copy(An, g)
                                g = gmm(Atn[:], Pp[:], [Atn, Pp])
                                P.op("dve", lambda e, g=g, Pn=Pn, Pp=Pp: e.tensor_tensor(out=Pn[:], in0=g[:], in1=Pp[:], op=ALU.add), reads=[g, Pp], writes=[Pn])
                                ia = 1 - ia
                        for c in (chunks if "rwB2" not in dbg else ()):
                            rc0 = 64 * c
                            for hi in range(2):
                                ho = hi * 64; h = 2 * p + hi
                                P.op("pe", lambda e, FT=FT, ST=ST, ho=ho, p=p, hi=hi: e.matmul(pxy[:, 0, hi, :], lhsT=FT[ho:ho + 64, AI, p, :], rhs=ST[ho:ho + 64, p, :], start=True, stop=False),
                                     reads=[FT, ST], writes=[pxy], rg=ho)
                                P.op("pe", lambda e, rw=rw, hi=hi, h=h, rc0=rc0: e.matmul(pxy[:, 0, hi, :], lhsT=Mh[hi][rc0:rc0 + 64, :], rhs=rw[rc0:rc0 + 64, 1024 + h * 64:1024 + (h + 1) * 64], start=False, stop=True),
                                     reads=[Mh[hi], rw], writes=[pxy], rg=rc0)
                            P.op("act", lambda e, XT=XT, rc0=rc0: e.activation(out=XT[rc0:rc0 + 64], in_=pxy[rc0:rc0 + 64, 0], func=AF.Copy), reads=[pxy], writes=[XT])
                            for hi in range(2):
                                P.op("pe", lambda e, XT=XT, hi=hi, rc0=rc0: e.matmul(pxy[:, 1, hi, :], lhsT=Th[hi][rc0:rc0 + 64, :], rhs=XT[rc0:rc0 + 64, hi, :], start=True, stop=True),
                                     reads=[Th[hi], XT], writes=[pxy], rg=rc0)
                            P.op("dve", lambda e, UT=UT, rc0=rc0: e.tensor_copy(out=UT[rc0:rc0 + 64], in_=pxy[rc0:rc0 + 64, 1]), reads=[pxy], writes=[UT])
                            if want_out:
                                for hi in range(2):
                                    ho = hi * 64; h = 2 * p + hi
                                    P.op("pe", lambda e, FT=FT, ST=ST, ho=ho, p=p, hi=hi: e.matmul(pxy[:, 2, hi, :], lhsT=FT[ho:ho + 64, RI, p, :], rhs=ST[ho:ho + 64, p, :], start=True, stop=False),
                                         reads=[FT, ST], writes=[pxy], rg=ho)
                                    P.op("pe", lambda e, rw=rw, hi=hi, h=h, rc0=rc0: e.matmul(pxy[:, 2, hi, :], lhsT=G3h[hi][rc0:rc0 + 64, :], rhs=rw[rc0:rc0 + 64, 1024 + h * 64:1024 + (h + 1) * 64], start=False, stop=False),
                                         reads=[G3h[hi], rw], writes=[pxy], rg=rc0)
                                    P.op("pe", lambda e, UT=UT, hi=hi, rc0=rc0: e.matmul(pxy[:, 2, hi, :], lhsT=nG4h[hi][rc0:rc0 + 64, :], rhs=UT[rc0:rc0 + 64, hi, :], start=False, stop=True),
                                         reads=[nG4h[hi], UT], writes=[pxy], rg=rc0)
                                P.op("act", lambda e, y=y, rc0=rc0, p=p: e.activation(out=y[rc0:rc0 + 64, p * 128:(p + 1) * 128], in_=pxy[rc0:rc0 + 64, 2].rearrange("p a d -> p (a d)"), func=AF.Copy),
                                     reads=[pxy], writes=[y])
                            P.op("pe", lambda e, rw=rw, p=p, rc0=rc0: e.matmul(pS[:], lhsT=T["Kh"][rc0:rc0 + 64, p * 128:(p + 1) * 128], rhs=rw[rc0:rc0 + 64, 1024 + p * 128:1024 + (p + 1) * 128], start=True, stop=False),
                                 reads=[T["Kh"], rw], writes=[pS])
                            P.op("pe", lambda e, UT=UT, p=p, rc0=rc0: e.matmul(pS[:], lhsT=T["nBh"][rc0:rc0 + 64, p * 128:(p + 1) * 128], rhs=UT[rc0:rc0 + 64].rearrange("p a d -> p (a d)"), start=False, stop=True),
                                 reads=[T["nBh"], UT], writes=[pS])
                            for hi in range(2):
                                ho = hi * 64
                                P.op("dve", lambda e, ST=ST, ho=ho, p=p, c=c, hi=hi: e.scalar_tensor_tensor(out=ST[ho:ho + 64, p, :], in0=ST[ho:ho + 64, p, :], scalar=WL[ho:ho + 64, p, c:c + 1],
                                                                                                     in1=pS[ho:ho + 64, hi * 64:(hi + 1) * 64], op0=ALU.mult, op1=ALU.add), reads=[ST, WL, pS], writes=[ST])
                    if not want_out:
                        continue
                    if dr == 0:
                        P.dma("act", YS[t * 128:(t + 1) * 128, 0:512], y[:], reads=[y], writes=[("YS", t)])
                        P.dma("act", YS[t * 128:(t + 1) * 128, 512:520], bd[:], reads=[bd], writes=[("YSb", t)])
                    else:
                        yf = d["yf"]
                        P.dma("sp", yf[:], YS[t * 128:(t + 1) * 128, :], reads=[("YS", t), ("YSb", t)], writes=[yf])
                        y3 = y[:].rearrange("p (h d) -> p h d", h=8)
                        P.op("pool", lambda e, y=y, yf=yf: e.tensor_tensor(out=y[:], in0=y[:], in1=yf[:, 0:512], op=ALU.add), reads=[y, yf], writes=[y])
                        P.op("pool", lambda e, bd=bd, yf=yf: e.tensor_tensor(out=bd[:], in0=bd[:], in1=yf[:, 512:520], op=ALU.add), reads=[bd, yf], writes=[bd])
                        P.op("dve", lambda e, sm=sm, y3=y3: e.tensor_reduce(out=sm[:, 0, :], in_=y3, axis=AX.X, op=ALU.add), reads=[y], writes=[sm])
                        P.op("dve", lambda e, sm=sm: e.tensor_scalar_mul(out=sm[:, 0, :], in0=sm[:, 0, :], scalar1=-1.0 / 64), reads=[sm], writes=[sm])
                        P.op("dve", lambda e, sm=sm, y3=y3: e.tensor_tensor(out=y3, in0=y3, in1=sm[:, 0, :].unsqueeze(2).to_broadcast([128, 8, 64]), op=ALU.add), reads=[y, sm], writes=[y])
                        P.op("act", lambda e, y=y: e.activation(out=T["tmp"][:], in_=y[:], func=AF.Square), reads=[y], writes=[T["tmp"]])
                        P.op("dve", lambda e, sm=sm: e.tensor_reduce(out=sm[:, 1, :], in_=T["tmp"][:].rearrange("p (h d) -> p h d", h=8), axis=AX.X, op=ALU.add), reads=[T["tmp"]], writes=[sm])
                        P.op("act", lambda e, sm=sm: e.activation(out=sm[:, 1, :], in_=sm[:, 1, :], func=AF.Sqrt, scale=1.0 / 64, bias=lneps[:, 0:1]), reads=[sm, lneps], writes=[sm])
                        P.op("dve", lambda e, sm=sm: e.reciprocal(out=sm[:, 1, :], in_=sm[:, 1, :]), reads=[sm], writes=[sm])
                        P.op("dve", lambda e, sm=sm, y3=y3: e.tensor_tensor(out=y3, in0=y3, in1=sm[:, 1, :].unsqueeze(2).to_broadcast([128, 8, 64]), op=ALU.mult), reads=[y, sm], writes=[y])
                        P.op("pool", lambda e, y=y: e.tensor_tensor(out=y[:], in0=y[:], in1=ln0[:], op=ALU.mult), reads=[y, ln0], writes=[y])
                        P.op("pool", lambda e, y=y: e.tensor_tensor(out=y[:], in0=y[:], in1=ln1[:], op=ALU.add), reads=[y, ln1], writes=[y])
                        P.op("dve", lambda e, rw=rw, bd=bd: e.tensor_tensor(out=T["tmp"][:].rearrange("p (h d) -> p h d", h=8), in0=rw[:, 1024:1536].rearrange("p (h d) -> p h d", h=8),
                                                                      in1=bd[:].unsqueeze(2).to_broadcast([128, 8, 64]), op=ALU.mult), reads=[rw, bd], writes=[T["tmp"]])
                        P.op("pool", lambda e, y=y: e.tensor_tensor(out=y[:], in0=y[:], in1=T["tmp"][:], op=ALU.add), reads=[y, T["tmp"]], writes=[y])
                        P.op("pe", lambda e, rw=rw: e.transpose(out=pft[:, 0, :], in_=rw[:, 1792:1920], identity=ident[:]), reads=[rw, ident], writes=[pft])
                        P.op("act", lambda e: e.activation(out=sgT[:], in_=pft[:, 0, :], func=AF.Sigmoid), reads=[pft], writes=[sgT])
                        P.op("pe", lambda e: e.matmul(pz[:], lhsT=sgT[:], rhs=g2s[:], start=True, stop=True), reads=[sgT, g2s], writes=[pz])
                        P.op("dve", lambda e, y=y: e.tensor_tensor(out=y[:], in0=y[:], in1=pz[:], op=ALU.mult), reads=[y, pz], writes=[y])
                        P.dma("act", O[t * 128:(t + 1) * 128, 0:512], y[:], reads=[y], writes=[("O", t, 0)])
        P.pop()

    for l in range(2):
        nin = NIN[l]
        w_in = io["w_in_even"] if l == 0 else io["w_in_odd"]
        Xl = io["xin"] if l == 0 else X1
        if "U_in" not in dbg:
            P.push()
            MT = 4
            xts = [P.sbuf(f"p1x{i}", [128, D]) for i in range(3)]
            junk = P.sbuf("p1junk", [128, D])
            ss = [P.sbuf(f"p1ss{i}", [128, 1]) for i in range(3)]
            rstd = [P.sbuf(f"p1rs{i}", [128, 1]) for i in range(3)]
            xn = [P.sbuf(f"p1xn{i}", [128, D]) for i in range(2)]
            hxT = [P.sbuf(f"p1hxT{i}", [128, 8, MT * 128]) for i in range(2)]
            ptr = [P.psum(f"p1ptr{i}", [128, 4, 128]) for i in range(2)]
            wstg = [P.sbuf(f"p1ws{i}", [128, 8, 512]) for i in range(2)]
            wr = [P.sbuf(f"p1wr{i}", [128, 8, 512]) for i in range(2)]
            pu = [P.psum(f"p1pu{i}", [128, 512]) for i in range(4)]
            ut = [P.sbuf(f"p1ut{i}", [128, 512]) for i in range(4)]
            nblk = (nin + 511) // 512
            nmac = (cfg.ntile + MT - 1) // MT
            cnt = {"ti": 0, "ei": 0}

            def p1_norm(m):
                tiles = list(range(m * MT, min((m + 1) * MT, cfg.ntile)))
                hx = hxT[m % 2]
                for jj, t in enumerate(tiles):
                    cls = cfg.cls(t)
                    ti = cnt["ti"]; cnt["ti"] += 1
                    xt = xts[ti % 3]; s_ = ss[ti % 3]; r_ = rstd[ti % 3]; xn_ = xn[ti % 2]
                    P.dma("sp", xt[:], Xl[t * 128:(t + 1) * 128, :], reads=[("X", l, t)], writes=[xt])
                    P.op("act", lambda e, xt=xt, s_=s_: e.activation(out=junk[:], in_=xt[:], func=AF.Square, accum_out=s_[:]),
                         reads=[xt], writes=[junk, s_])
                    P.op("act", lambda e, s_=s_, r_=r_: e.activation(out=r_[:], in_=s_[:], func=AF.Sqrt, scale=1.0 / D, bias=epsc[:, 0:1]),
                         reads=[s_, epsc], writes=[r_])
                    P.op("dve", lambda e, r_=r_: e.reciprocal(out=r_[:], in_=r_[:]), reads=[r_], writes=[r_])
                    P.op("dve", lambda e, xt=xt, r_=r_, xn_=xn_: e.tensor_scalar(out=xn_[:], in0=xt[:], scalar1=r_[:, 0:1], scalar2=None, op0=ALU.mult),
                         reads=[xt, r_], writes=[xn_])
                    for half in range(2):
                        pt_ = ptr[half]
                        for kk in range(4):
                            k = half * 4 + kk
                            P.op("pe", lambda e, pt_=pt_, kk=kk, k=k, xn_=xn_: e.transpose(out=pt_[:, kk, :], in_=xn_[:, k * 128:(k + 1) * 128], identity=ident[:]),
                                 reads=[xn_, ident], writes=[pt_])
                        for kk in range(4):
                            k = half * 4 + kk
                            P.op("act", lambda e, pt_=pt_, kk=kk, k=k, hx=hx, jj=jj, cls=cls, ab=AB[l]: e.activation(
                                out=hx[:, k, jj * 128:(jj + 1) * 128].bitcast(F32R), in_=pt_[:, kk, :], func=AF.Identity,
                                scale=ab[:, 0, k, cls:cls + 1], bias=ab[:, 1, k, cls:cls + 1]),
                                reads=[pt_, AB[l]], writes=[hx])

            blocks = [(m, nb) for m in range(nmac) for nb in range(nblk)]

            def p1_loadw(i):
                m, nb = blocks[i]
                n0 = nb * 512
                nw = min(512, nin - n0)
                ws_, wr_ = wstg[i % 2], wr[i % 2]
                P.dma("sp", ws_[:, :, 0:nw], w_in[:, n0:n0 + nw].rearrange("(k p) n -> p k n", p=128), writes=[ws_])
                P.op("dve", lambda e, ws_=ws_, wr_=wr_, nw=nw: e.tensor_copy(out=wr_[:, 0:4, 0:nw].bitcast(F32R), in_=ws_[:, 0:4, 0:nw]),
                     reads=[ws_], writes=[(wr_.name, "a")])
                P.op("act", lambda e, ws_=ws_, wr_=wr_, nw=nw: e.activation(out=wr_[:, 4:8, 0:nw].bitcast(F32R), in_=ws_[:, 4:8, 0:nw], func=AF.Copy),
                     reads=[ws_], writes=[(wr_.name, "b")])

            p1_norm(0)
            p1_loadw(0)
            for i, (m, nb) in enumerate(blocks):
                if nb == 0 and m + 1 < nmac:
                    p1_norm(m + 1)
                if i + 1 < len(blocks):
                    p1_loadw(i + 1)
                tiles = list(range(m * MT, min((m + 1) * MT, cfg.ntile)))
                hx = hxT[m % 2]
                n0 = nb * 512
                nw = min(512, nin - n0)
                wr_ = wr[i % 2]
                for jj, t in enumerate(tiles):
                    ei = cnt["ei"]; cnt["ei"] += 1
                    ps = pu[ei % 4]; u_ = ut[ei % 4]
                    for k in range(8):
                        P.op("pe", lambda e, ps=ps, k=k, hx=hx, jj=jj, wr_=wr_, nw=nw: e.matmul(
                            ps[:, 0:nw], lhsT=hx[:, k, jj * 128:(jj + 1) * 128].bitcast(F32R), rhs=wr_[:, k, 0:nw].bitcast(F32R),
                            start=(k == 0), stop=(k == 7)), reads=[hx, (wr_.name, 'a' if k < 4 else 'b')], writes=[ps])
                    if ei % 2:
                        P.op("act", lambda e, ps=ps, u_=u_, nw=nw: e.activation(out=u_[:, 0:nw], in_=ps[:, 0:nw], func=AF.Copy), reads=[ps], writes=[u_])
                    else:
                        P.op("dve", lambda e, ps=ps, u_=u_, nw=nw: e.tensor_copy(out=u_[:, 0:nw], in_=ps[:, 0:nw]), reads=[ps], writes=[u_])
                    P.dma("act", U[t * 128:(t + 1) * 128, n0:n0 + nw], u_[:, 0:nw], reads=[u_], writes=[("U", t, nb)])
            P.pop()
        if stop_after == f"P1_{l}":
            break
        if stop_after is None:
            if l == 0:
                mlstm(l); diffattn(l)
            else:
                rwkv(l); gla(l)
            p3(l, Xl)
        elif stop_after == f"P3_{l}":
            p3(l, Xl); break
        elif l == 0 and stop_after in ("mlstm", "diff"):
            (mlstm if stop_after == "mlstm" else diffattn)(l); break
        elif l == 1 and stop_after in ("rwkv", "gla"):
            (rwkv if stop_after == "rwkv" else gla)(l); break

    P.emit()
    P.close()
    return nc

import numpy as np
TC = 256
def rope_table(TL):
    n = 8
    inv = (10000.0 ** (-np.arange(n, dtype=np.float32) / n)).astype(np.float32)
    row = np.repeat(np.arange(TL // 64, dtype=np.float32), 64)
    col = np.tile(np.arange(64, dtype=np.float32), TL // 64)
    ang = np.concatenate([row[:, None] * inv, col[:, None] * inv], axis=-1).astype(np.float32)
    return np.concatenate([np.cos(ang), np.sin(ang)], axis=-1).astype(np.float32)

def make_masks():
    m = np.zeros((9, 128, 128), np.float32)
    i = np.arange(128)
    S, T = i[:, None], i[None, :]
    blk = (S // 64 == T // 64)
    m[0] = (S <= T); m[1] = (S >= T)
    m[2] = (S <= T) * blk; m[3] = (S >= T) * blk
    m[4] = (S > T); m[5] = (S < T)
    m[6] = (S < T) * blk; m[7] = (S > T) * blk
    m[8, :64, 0] = 1.0; m[8, 64:, 1] = 1.0
    return m

def core_inputs(inp, core, NB, TL):
    b0 = core * NB
    xs = []
    for b in range(b0, b0 + NB):
        xs.append(inp["ctx"][b]); xs.append(inp["x"][b][:TL])
    m = {}
    m["xin"] = np.ascontiguousarray(np.concatenate(xs, axis=0))
    m["cvec"] = np.ascontiguousarray(np.concatenate([inp["c"][b0:b0 + NB], inp["c_ctx"][None, :]], axis=0))
    for k in ("ada_w", "ada_b", "norm_g", "w_out", "w_mlp_in", "w_mlp_out"):
        m[k] = inp[k]
    m["w_in_even"] = inp["w_in_even"][0]; m["w_in_odd"] = inp["w_in_odd"][0]
    m["mlstm_gate_b"] = inp["mlstm_gate_b"].reshape(1, 32); m["mlstm_norm_g"] = inp["mlstm_norm_g"].reshape(1, 512)
    m["diff_qk_g"] = inp["diff_qk_g"][0]; m["diff_lam"] = inp["diff_lam"].reshape(1, 128); m["diff_subln_g"] = inp["diff_subln_g"].reshape(1, 64)
    m["rwkv_mu"] = inp["rwkv_mu"][0]; m["rwkv_w0"] = inp["rwkv_w0"][0]; m["rwkv_w2"] = inp["rwkv_w2"][0]
    m["rwkv_a0"] = inp["rwkv_a0"][0]; m["rwkv_a2"] = inp["rwkv_a2"][0]; m["rwkv_g2"] = inp["rwkv_g2"][0]
    m["rwkv_kvec"] = inp["rwkv_kvec"][0]; m["rwkv_ln"] = inp["rwkv_ln"][0]
    m["gla_gate_w2"] = inp["gla_gate_w2"][0]; m["gla_gate_b"] = inp["gla_gate_b"][0]; m["gla_norm_g"] = inp["gla_norm_g"].reshape(1, 128)
    m["ident"] = np.eye(128, dtype=np.float32)
    m["rope"] = rope_table(TL)
    m["masks"] = make_masks()
    return {k: np.ascontiguousarray(np.asarray(v, dtype=np.float32)) for k, v in m.items()}


def kernel(**inputs):
    from concourse.bass_utils import run_bass_kernel_spmd
    inp = {k: np.asarray(v) for k, v in inputs.items()}
    NB, TL = 2, 4096
    cfg = Cfg(NB=NB, TL=TL)
    nc = build(cfg)
    in_maps = [core_inputs(inp, core, NB, TL) for core in range(8)]
    res = run_bass_kernel_spmd(nc, in_maps, core_ids=list(range(8)))
    outs = [np.asarray(r["out"]).reshape(NB, TL, D) for r in res.results]
    return np.concatenate(outs, axis=0).astype(np.float32)
```

```python
import contextlib
import numpy as np
import concourse.bass as bass
import concourse.mybir as mybir

F32 = mybir.dt.float32
F32R = mybir.dt.float32r
ALU = mybir.AluOpType
AF = mybir.ActivationFunctionType
AX = mybir.AxisListType

ENGS = ("pe", "act", "dve", "pool", "sp")
N_DMA_SEMS = 24
ATTACH_WAIT = True


class Op:
    __slots__ = ("eng", "fn", "deps", "signal", "count", "is_dma", "dsem", "dcount", "idx", "rg")


class Prog:
    def __init__(self, nc):
        self.nc = nc
        self.streams = {e: [] for e in ENGS}
        self.last_w = {}
        self.readers = {}
        self.stack = contextlib.ExitStack()
        self.n_dma = {}
        self.pstack = None
        self.bar = {}
        self.uid = 0
        self.recent_dma = {e: {} for e in ENGS}
        self.psum_names = set()

    def push(self):
        self.pstack = contextlib.ExitStack()

    def pop(self):
        self.barrier()
        self.pstack.close()
        self.pstack = None

    def barrier(self):
        lasts = []
        for e in ENGS:
            st = self.streams[e]
            for o in reversed(st):
                if not o.is_dma:
                    lasts.append(o)
                    break
            lasts.extend(self.recent_dma[e].values())
        for e in ENGS:
            self.bar[e] = list(lasts)

    def sbuf(self, name, shape, dtype=F32):
        self.uid += 1
        st = self.pstack if self.pstack is not None else self.stack
        return st.enter_context(self.nc.sbuf_tensor(f"{name}_{self.uid}", list(shape), dtype))

    def psum(self, name, shape, dtype=F32):
        self.uid += 1
        st = self.pstack if self.pstack is not None else self.stack
        self.psum_names.add(f"{name}_{self.uid}")
        return st.enter_context(self.nc.psum_tensor(f"{name}_{self.uid}", list(shape), dtype))

    def dram(self, name, shape, dtype=F32, kind="Internal"):
        return self.nc.dram_tensor(name, list(shape), dtype, kind=kind)

    def dma(self, eng, out, in_, reads=(), writes=(), **kw):
        return self.op(eng, lambda e: e.dma_start(out=out, in_=in_, **kw), reads, writes, is_dma=True)

    def op(self, eng, fn, reads=(), writes=(), is_dma=False, rg=None):
        o = Op()
        o.rg = rg
        o.eng = eng
        o.fn = fn
        o.signal = False
        o.count = 0
        o.is_dma = is_dma
        o.dsem = None
        o.dcount = 0
        reads = [r if isinstance(r, (str, tuple)) else r.name for r in reads]
        writes = [r if isinstance(r, (str, tuple)) else r.name for r in writes]
        deps = {}
        def add(d):
            if d is None:
                return
            if d.eng == "pe" and eng == "pe":
                return
            deps[id(d)] = d
        for r in reads:
            add(self.last_w.get(r))
            if r in self.psum_names:
                for rd in self.readers.get(r, ()):
                    if rd.eng != eng:
                        add(rd)
        for w in writes:
            add(self.last_w.get(w))
            if eng == "pe" and rg is not None:
                lw = self.last_w.get(w)
                if lw is not None and lw.eng == "pe" and lw.rg is not None and lw.rg != rg:
                    deps[id(lw)] = lw
            for rd in self.readers.get(w, ()):
                if rd is not o:
                    add(rd)
        if eng in self.bar:
            for d in self.bar.pop(eng):
                if d is not None and not (d.eng == eng and not d.is_dma and eng != "pe" and False):
                    deps[id(d)] = d
        o.deps = list(deps.values())
        for r in reads:
            self.readers.setdefault(r, []).append(o)
        for w in writes:
            self.last_w[w] = o
            self.readers[w] = []
        o.idx = len(self.streams[eng])
        self.streams[eng].append(o)
        if o.is_dma:
            k = self.n_dma.get(eng, 0)
            o.dsem = k % N_DMA_SEMS
            self.n_dma[eng] = k + 1
            self.recent_dma[eng][o.dsem] = o
        return o

    def emit(self):
        nc = self.nc
        for e in ENGS:
            for o in self.streams[e]:
                for d in o.deps:
                    if not d.is_dma:
                        d.signal = True
        for e in ENGS:
            c = 0
            for o in self.streams[e]:
                if o.signal:
                    c += 1
                    o.count = c
        st = self.stack
        esem = {e: st.enter_context(nc.semaphore("s_" + e)) for e in ("pe", "act", "dve", "pool")}
        dsems = {}
        for q in ENGS:
            if not any(o.is_dma for o in self.streams[q]):
                continue
            dsems[q] = [st.enter_context(nc.semaphore(f"d_{q}_{i}")) for i in range(N_DMA_SEMS)]
            cnt = [0] * N_DMA_SEMS
            for o in self.streams[q]:
                if o.is_dma:
                    cnt[o.dsem] += 16
                    o.dcount = cnt[o.dsem]
        block = st.enter_context(nc.Block())
        streams = self.streams

        def run(engname, eng):
            waited = {}
            for o in streams[engname]:
                need = {}
                for d in o.deps:
                    if d.is_dma:
                        key = ("d", d.eng, d.dsem)
                        val = d.dcount
                    else:
                        key = ("e", d.eng)
                        val = d.count
                    if need.get(key, 0) < val:
                        need[key] = val
                if o.is_dma and o.dcount > 16:
                    key = ("d", o.eng, o.dsem)
                    if need.get(key, 0) < o.dcount - 16:
                        need[key] = o.dcount - 16
                pend = []
                for key, val in need.items():
                    if waited.get(key, 0) >= val:
                        continue
                    waited[key] = val
                    sem = dsems[key[1]][key[2]] if key[0] == "d" else esem[key[1]]
                    pend.append((sem, val))
                attach = None
                if pend and ATTACH_WAIT and not o.is_dma:
                    attach = pend.pop()
                for sem, val in pend:
                    eng.wait_ge(sem, val)
                inst = o.fn(eng)
                if attach is not None:
                    inst._wait_ge(attach[0], eng.lower_val(attach[1]))
                if o.is_dma:
                    inst.then_inc(dsems[o.eng][o.dsem], 16)
                elif o.signal:
                    inst.then_inc(esem[o.eng], 1)
            return waited

        def fin(engname, eng):
            w = run(engname, eng)
            cnt = {}
            for o in streams[engname]:
                if o.is_dma:
                    cnt[o.dsem] = o.dcount
            for k, v in cnt.items():
                if w.get(("d", engname, k), 0) < v:
                    eng.wait_ge(dsems[engname][k], v)

        @block.tensor
        def _(pe):
            fin("pe", pe)

        @block.scalar
        def _(act):
            fin("act", act)

        @block.vector
        def _(dve):
            fin("dve", dve)

        @block.gpsimd
        def _(pool):
            fin("pool", pool)

        @block.sync
        def _(sp):
            fin("sp", sp)

    def close(self):
        self.stack.close()

import math
import numpy as np

D = 1024
EPS = 1e-6
A_IN, B_IN = 2080, 1536
EVEN_IN = A_IN + B_IN
C_IN, D_IN = 1920, 1568
ODD_IN = C_IN + D_IN
NIN = (EVEN_IN, ODD_IN)
HID = 4096
TC = 256


class Cfg:
    def __init__(self, NB=2, TL=4096, debug=()):
        self.NB = NB
        self.TL = TL
        self.TS = TC + TL
        self.NT = NB * self.TS
        self.ntile = self.NT // 128
        self.debug = set(debug)

    def cls(self, tile):
        b, r = divmod(tile * 128, self.TS)
        return self.NB if r < TC else b


def declare_io(nc, cfg):
    io = {}
    def inp(name, shape):
        io[name] = nc.dram_tensor(name, list(shape), F32, kind="ExternalInput").ap()
    inp("xin", [cfg.NT, D])
    inp("cvec", [cfg.NB + 1, D])
    inp("ada_w", [2, D, 6 * D]); inp("ada_b", [2, 6 * D]); inp("norm_g", [2, 2, D])
    inp("w_in_even", [D, EVEN_IN]); inp("w_in_odd", [D, ODD_IN])
    inp("w_out", [2, D, D]); inp("w_mlp_in", [2, D, HID]); inp("w_mlp_out", [2, HID, D])
    inp("mlstm_gate_b", [1, 32]); inp("mlstm_norm_g", [1, 512])
    inp("diff_qk_g", [2, 32]); inp("diff_lam", [1, 128]); inp("diff_subln_g", [1, 64])
    inp("rwkv_mu", [2, C_IN]); inp("rwkv_w0", [2, 512]); inp("rwkv_w2", [2, 64, 512])
    inp("rwkv_a0", [2, 512]); inp("rwkv_a2", [2, 64, 512]); inp("rwkv_g2", [128, 512])
    inp("rwkv_kvec", [3, 512]); inp("rwkv_ln", [2, 512])
    inp("gla_gate_w2", [2, 16, 256]); inp("gla_gate_b", [2, 256]); inp("gla_norm_g", [1, 128])
    inp("ident", [128, 128]); inp("rope", [cfg.TL, 32])
    inp("masks", [9, 128, 128])
    io["out"] = nc.dram_tensor("out", [cfg.NB * cfg.TL, D], F32, kind="ExternalOutput").ap()
    return io


def build(cfg, stop_after=None):
    nc = bass.Bass("TRN2", target_bir_lowering=False)
    io = declare_io(nc, cfg)
    P = Prog(nc)
    dbg = cfg.debug

    def scratch(name, shape):
        kind = "ExternalOutput" if name in dbg else "Internal"
        return nc.dram_tensor(name, list(shape), F32, kind=kind).ap()

    X1 = scratch("X1", [cfg.NT, D])
    if "U_in" in dbg:
        U = nc.dram_tensor("U", [cfg.NT, EVEN_IN], F32, kind="ExternalInput").ap()
    else:
        U = scratch("U", [cfg.NT, EVEN_IN])
    YS = scratch("YS", [cfg.NT, 520])
    RW = scratch("RW", [cfg.NT, 2432])
    NTS = cfg.TS // 128
    QKT = scratch("QKT", [cfg.NB, 2, 8, 64, cfg.TS])
    masks = P.sbuf("masks", [128, 9, 128])
    P.dma("sp", masks[:], io["masks"].rearrange("m p n -> p m n"), writes=[masks])

    def ukeys(t):
        return [("U", t, nb) for nb in range(8)]

    def bcast_load(dst, src_row, n):
        P.dma("sp", dst, src_row.to_broadcast([128, n]), writes=[dst.tensor.name if hasattr(dst, "tensor") else dst])

    if "O_in" in dbg:
        O = nc.dram_tensor("O", [cfg.NT, D], F32, kind="ExternalInput").ap()
    else:
        O = scratch("O", [cfg.NT, D])
    MODS = scratch("MODS", [2, cfg.NB + 1, 6 * D])
    NC = cfg.NB + 1

    ident = P.sbuf("ident", [128, 128])
    ones = P.sbuf("ones", [128, 128])
    P.dma("sp", ident[:], io["ident"], writes=[ident])
    P.op("dve", lambda e: e.memset(ones[:], 1.0), writes=[ones])
    epsc = P.sbuf("epsc", [128, 1])
    P.op("dve", lambda e: e.memset(epsc[:], EPS), writes=[epsc])
    AB = [P.sbuf(f"AB{l}", [128, 4, 8, NC]) for l in range(2)]

    P.push()
    crow = P.sbuf("crow", [NC, D])
    srow = P.sbuf("srow", [NC, D])
    sT = P.sbuf("sT", [128, 8, NC])
    P.dma("sp", crow[:], io["cvec"], writes=[crow])
    P.op("act", lambda e: e.activation(out=srow[:], in_=crow[:], func=AF.Silu), reads=[crow], writes=[srow])
    pt = P.psum("p0t", [128, 8, NC])
    for k in range(8):
        P.op("pe", lambda e, k=k: e.transpose(out=pt[:, k, :], in_=srow[:, k * 128:(k + 1) * 128], identity=ident[0:NC, 0:NC]),
             reads=[srow, ident], writes=[pt])
    P.op("dve", lambda e: e.tensor_copy(out=sT[:], in_=pt[:]), reads=[pt], writes=[sT])
    modrow = P.sbuf("modrow", [NC, 6 * D])
    gT = P.sbuf("gT", [128, 2, 8])
    wst = [P.sbuf(f"p0w{i}", [128, 8, 512]) for i in range(2)]
    brow = P.sbuf("p0b", [1, 6 * D])
    pm = [P.psum(f"p0m{i}", [NC, 512]) for i in range(2)]
    pmt = P.psum("p0mt", [128, 4, 8, NC])
    pg = P.psum("p0g", [128, 2, 8])
    grow = P.sbuf("p0grow", [1, 2 * D])
    for l in range(2):
        P.dma("sp", brow[:], io["ada_b"][l:l + 1, :], writes=[brow])
        for j in range(12):
            w = wst[j % 2]
            P.dma("sp", w[:], io["ada_w"][l, :, j * 512:(j + 1) * 512].rearrange("(k p) n -> p k n", p=128), writes=[w])
            ps = pm[j % 2]
            for k in range(8):
                P.op("pe", lambda e, k=k, w=w, ps=ps: e.matmul(ps[:], lhsT=sT[:, k, :], rhs=w[:, k, :], start=(k == 0), stop=False),
                     reads=[sT, w], writes=[ps])
            P.op("pe", lambda e, ps=ps, j=j: e.matmul(ps[:], lhsT=ones[0:1, 0:NC], rhs=brow[0:1, j * 512:(j + 1) * 512], start=False, stop=True),
                 reads=[ones, brow], writes=[ps])
            P.op("act", lambda e, ps=ps, j=j: e.activation(out=modrow[:, j * 512:(j + 1) * 512], in_=ps[:], func=AF.Copy),
                 reads=[ps], writes=[modrow])
        P.dma("sp", MODS[l], modrow[:], reads=[modrow], writes=[("MODS", l)])
        for qi, q in enumerate((0, 1, 3, 4)):
            for k in range(8):
                P.op("pe", lambda e, qi=qi, q=q, k=k: e.transpose(out=pmt[:, qi, k, :], in_=modrow[:, q * D + k * 128: q * D + (k + 1) * 128],
                                                                 identity=ident[0:NC, 0:NC]),
                     reads=[modrow, ident], writes=[pmt])
        P.dma("sp", grow[:], io["norm_g"][l:l + 1].rearrange("o j d -> o (j d)"), writes=[grow])
        for j in range(2):
            for k in range(8):
                P.op("pe", lambda e, j=j, k=k: e.transpose(out=pg[:, j, k:k + 1], in_=grow[0:1, j * D + k * 128: j * D + (k + 1) * 128], identity=ident[0:1, 0:1]),
                     reads=[grow, ident], writes=[pg])
        P.op("dve", lambda e: e.tensor_copy(out=gT[:], in_=pg[:]), reads=[pg], writes=[gT])
        ab = AB[l]
        P.op("dve", lambda e, ab=ab: e.tensor_copy(out=ab[:, 1], in_=pmt[:, 0]), reads=[pmt], writes=[ab])
        P.op("dve", lambda e, ab=ab: e.tensor_copy(out=ab[:, 3], in_=pmt[:, 2]), reads=[pmt], writes=[ab])
        for c in range(NC):
            P.op("dve", lambda e, ab=ab, c=c: e.scalar_tensor_tensor(out=ab[:, 0, :, c], in0=pmt[:, 1, :, c], scalar=1.0, in1=gT[:, 0, :],
                                                                   op0=ALU.add, op1=ALU.mult), reads=[pmt, gT], writes=[ab])
            P.op("dve", lambda e, ab=ab, c=c: e.scalar_tensor_tensor(out=ab[:, 2, :, c], in0=pmt[:, 3, :, c], scalar=1.0, in1=gT[:, 1, :],
                                                                   op0=ALU.add, op1=ALU.mult), reads=[pmt, gT], writes=[ab])
    P.pop()
    if stop_after == "P0":
        P.emit(); P.close(); return nc

    def norm_mod_T(xt, hxT, j, l, which, cls, sq_junk, ss, rstd, xn, ptr):
        P.op("act", lambda e: e.activation(out=sq_junk[:], in_=xt, func=AF.Square, accum_out=ss[:]),
             reads=[xt.tensor if hasattr(xt, "tensor") else xt], writes=[sq_junk, ss])
        return

    def p3(l, Xl):
        last = (l == 1)
        P.push()
        MT = 2
        if last:
            tl = [t for t in range(cfg.ntile) if cfg.cls(t) != cfg.NB]
        else:
            tl = list(range(cfg.ntile))
        macs = [tl[i:i + MT] for i in range(0, len(tl), MT)]
        NTK = MT * 128
        gbc = P.sbuf("p3gbc", [128, NC, 2, D])
        for c in range(NC):
            for gi, q in enumerate((2, 5)):
                P.dma("sp", gbc[:, c, gi, :], MODS[l, c:c + 1, q * D:(q + 1) * D].to_broadcast([128, D]),
                      reads=[("MODS", l)], writes=[gbc])
        xts = [P.sbuf(f"p3x{i}", [128, D]) for i in range(2 * MT)]
        obuf = [P.sbuf(f"p3o{i}", [128, D]) for i in range(2)]
        OT = [P.sbuf(f"p3OT{i}", [128, 8, NTK]) for i in range(2)]
        hx2T = P.sbuf("p3hx2T", [128, 8, NTK])
        hT = P.sbuf("p3hT", [128, 32, NTK])
        wstg = [P.sbuf(f"p3ws{i}", [128, 8, 512]) for i in range(2)]
        wr = [P.sbuf(f"p3wr{i}", [128, 8, 512]) for i in range(2)]
        tmp = [P.sbuf(f"p3tmp{i}", [128, 512]) for i in range(2)]
        hrl = [P.sbuf(f"p3hrl{i}", [128, NTK]) for i in range(2)]
        ss = [P.sbuf(f"p3ss{i}", [128, 1]) for i in range(2)]
        rstd = [P.sbuf(f"p3rs{i}", [128, 1]) for i in range(2)]
        ptr = [P.psum(f"p3ptr{i}", [128, 4, 128]) for i in range(2)]
        pu = [P.psum(f"p3pu{i}", [128, 512]) for i in range(2)]
        acc = [P.psum(f"p3acc{i}", [128, 512]) for i in range(2 * MT)]
        cnt = {"o": 0, "pu": 0, "tmp": 0, "hr": 0, "n": 0}
        w_out = io["w_out"][l]; w1 = io["w_mlp_in"][l]; w2 = io["w_mlp_out"][l]
        wblocks = []
        for m in range(len(macs)):
            for nb in range(2):
                wblocks.append(w_out[:, nb * 512:(nb + 1) * 512].rearrange("(k p) n -> p k n", p=128))
            for nb in range(8):
                wblocks.append(w1[:, nb * 512:(nb + 1) * 512].rearrange("(k p) n -> p k n", p=128))
            for nb in range(2):
                for kp in range(4):
                    wblocks.append(w2[kp * 1024:(kp + 1) * 1024, nb * 512:(nb + 1) * 512].rearrange("(k p) n -> p k n", p=128))
        wi = {"i": 0}

        def loadw(i):
            if i >= len(wblocks):
                return
            ws_, wr_ = wstg[i % 2], wr[i % 2]
            P.dma("sp", ws_[:, 0:4, :], wblocks[i][:, 0:4, :], writes=[(ws_.name, "a")])
            P.dma("sp", ws_[:, 4:8, :], wblocks[i][:, 4:8, :], writes=[(ws_.name, "b")])
            P.op("dve", lambda e, ws_=ws_, wr_=wr_: e.tensor_copy(out=wr_[:, 0:4, :].bitcast(F32R), in_=ws_[:, 0:4, :]), reads=[(ws_.name, "a")], writes=[(wr_.name, "a")])
            P.op("act", lambda e, ws_=ws_, wr_=wr_: e.activation(out=wr_[:, 4:8, :].bitcast(F32R), in_=ws_[:, 4:8, :], func=AF.Copy), reads=[(ws_.name, "b")], writes=[(wr_.name, "b")])

        def nextw():
            i = wi["i"]; wi["i"] += 1
            loadw(i + 1)
            return wr[i % 2]

        def transpose_tile(src, dst, jj, scale_bias=None):
            for half in range(2):
                pt_ = ptr[half]
                for kk in range(4):
                    k = half * 4 + kk
                    P.op("pe", lambda e, pt_=pt_, kk=kk, k=k: e.transpose(out=pt_[:, kk, :], in_=src[:, k * 128:(k + 1) * 128], identity=ident[:]),
                         reads=[src, ident], writes=[pt_])
                if scale_bias is None:
                    dsl = dst[:, half * 4:(half + 1) * 4, jj * 128:(jj + 1) * 128]
                    if half == 0:
                        P.op("act", lambda e, pt_=pt_, dsl=dsl: e.activation(out=dsl.bitcast(F32R), in_=pt_[:], func=AF.Copy), reads=[pt_], writes=[dst])
                    else:
                        P.op("dve", lambda e, pt_=pt_, dsl=dsl: e.tensor_copy(out=dsl.bitcast(F32R), in_=pt_[:]), reads=[pt_], writes=[dst])
                else:
                    ab, ja, jb, cls = scale_bias
                    for kk in range(4):
                        k = half * 4 + kk
                        P.op("act", lambda e, pt_=pt_, kk=kk, k=k: e.activation(
                            out=dst[:, k, jj * 128:(jj + 1) * 128].bitcast(F32R), in_=pt_[:, kk, :], func=AF.Identity,
                            scale=ab[:, ja, k, cls:cls + 1], bias=ab[:, jb, k, cls:cls + 1]), reads=[pt_, ab], writes=[dst])

        def prep_O(m):
            ot = OT[m % 2]
            for jj, t in enumerate(macs[m]):
                ob = obuf[cnt["o"] % 2]; cnt["o"] += 1
                P.dma("sp", ob[:], O[t * 128:(t + 1) * 128, :], reads=[("O", t, 0), ("O", t, 1)], writes=[ob])
                transpose_tile(ob, ot, jj)

        loadw(0)
        prep_O(0)
        for m, tiles in enumerate(macs):
            ntok = len(tiles) * 128
            ot = OT[m % 2]
            xs = [xts[(m % 2) * MT + jj] for jj in range(len(tiles))]
            for jj, t in enumerate(tiles):
                P.dma("sp", xs[jj][:], Xl[t * 128:(t + 1) * 128, :], reads=[("X", l, t)], writes=[xs[jj]])
            for nb in range(2):
                wr_ = nextw()
                for jj, t in enumerate(tiles):
                    cls = cfg.cls(t)
                    ps = pu[cnt["pu"] % 2]; cnt["pu"] += 1
                    tm = tmp[cnt["tmp"] % 2]; cnt["tmp"] += 1
                    for k in range(8):
                        P.op("pe", lambda e, ps=ps, k=k, jj=jj, wr_=wr_, ot=ot: e.matmul(ps[:], lhsT=ot[:, k, jj * 128:(jj + 1) * 128].bitcast(F32R),
                                                                              rhs=wr_[:, k, :].bitcast(F32R), start=(k == 0), stop=(k == 7)),
                             reads=[ot, (wr_.name, 'a' if k < 4 else 'b')], writes=[ps])
                    P.op("dve", lambda e, ps=ps, tm=tm, cls=cls, nb=nb: e.tensor_tensor(out=tm[:], in0=ps[:], in1=gbc[:, cls, 0, nb * 512:(nb + 1) * 512], op=ALU.mult),
                         reads=[ps, gbc], writes=[tm])
                    xj = xs[jj]
                    P.op("pool", lambda e, tm=tm, xj=xj, nb=nb: e.tensor_tensor(out=xj[:, nb * 512:(nb + 1) * 512], in0=xj[:, nb * 512:(nb + 1) * 512], in1=tm[:], op=ALU.add),
                         reads=[tm, xj], writes=[xj])
            if m + 1 < len(macs):
                prep_O(m + 1)
            for jj, t in enumerate(tiles):
                cls = cfg.cls(t)
                xj = xs[jj]
                n = cnt["n"]; cnt["n"] += 1
                s_ = ss[n % 2]; r_ = rstd[n % 2]; xn_ = obuf[cnt["o"] % 2]; cnt["o"] += 1
                P.op("act", lambda e, xj=xj, s_=s_, xn_=xn_: e.activation(out=xn_[:], in_=xj[:], func=AF.Square, accum_out=s_[:]),
                     reads=[xj], writes=[xn_, s_])
                P.op("act", lambda e, s_=s_, r_=r_: e.activation(out=r_[:], in_=s_[:], func=AF.Sqrt, scale=1.0 / D, bias=epsc[:, 0:1]),
                     reads=[s_, epsc], writes=[r_])
                P.op("dve", lambda e, r_=r_: e.reciprocal(out=r_[:], in_=r_[:]), reads=[r_], writes=[r_])
                P.op("dve", lambda e, xj=xj, r_=r_, xn_=xn_: e.tensor_scalar(out=xn_[:], in0=xj[:], scalar1=r_[:, 0:1], scalar2=None, op0=ALU.mult),
                     reads=[xj, r_], writes=[xn_])
                transpose_tile(xn_, hx2T, jj, scale_bias=(AB[l], 2, 3, cls))
            for nb in range(8):
                wr_ = nextw()
                for sub in range(4):
                    nchunk = nb * 4 + sub
                    ps = pu[cnt["pu"] % 2]; cnt["pu"] += 1
                    hr = hrl[cnt["hr"] % 2]; cnt["hr"] += 1
                    for k in range(8):
                        P.op("pe", lambda e, ps=ps, k=k, sub=sub, wr_=wr_, ntok=ntok: e.matmul(ps[:, 0:ntok], lhsT=wr_[:, k, sub * 128:(sub + 1) * 128].bitcast(F32R),
                                                                               rhs=hx2T[:, k, 0:ntok].bitcast(F32R), start=(k == 0), stop=(k == 7)),
                             reads=[hx2T, (wr_.name, 'a' if k < 4 else 'b')], writes=[ps])
                    P.op("act", lambda e, ps=ps, hr=hr, ntok=ntok: e.activation(out=hr[:, 0:ntok], in_=ps[:, 0:ntok], func=AF.Relu), reads=[ps], writes=[hr])
                    P.op("pool", lambda e, hr=hr, nchunk=nchunk, ntok=ntok: e.tensor_tensor(out=hT[:, nchunk, 0:ntok].bitcast(F32R), in0=hr[:, 0:ntok], in1=hr[:, 0:ntok], op=ALU.mult),
                         reads=[hr], writes=[hT])
            for nb in range(2):
                for kp in range(4):
                    wr_ = nextw()
                    for jj, t in enumerate(tiles):
                        ac = acc[nb * MT + jj]
                        for k in range(8):
                            P.op("pe", lambda e, ac=ac, k=k, kp=kp, jj=jj, wr_=wr_: e.matmul(ac[:], lhsT=hT[:, kp * 8 + k, jj * 128:(jj + 1) * 128].bitcast(F32R),
                                                                                        rhs=wr_[:, k, :].bitcast(F32R), start=(kp == 0 and k == 0), stop=(kp == 3 and k == 7)),
                                 reads=[hT, (wr_.name, 'a' if k < 4 else 'b')], writes=[ac])
                for jj, t in enumerate(tiles):
                    cls = cfg.cls(t)
                    ac = acc[nb * MT + jj]
                    tm = tmp[cnt["tmp"] % 2]; cnt["tmp"] += 1
                    xj = xs[jj]
                    P.op("dve", lambda e, ac=ac, tm=tm, cls=cls, nb=nb: e.tensor_tensor(out=tm[:], in0=ac[:], in1=gbc[:, cls, 1, nb * 512:(nb + 1) * 512], op=ALU.mult),
                         reads=[ac, gbc], writes=[tm])
                    P.op("pool", lambda e, tm=tm, xj=xj, nb=nb: e.tensor_tensor(out=xj[:, nb * 512:(nb + 1) * 512], in0=xj[:, nb * 512:(nb + 1) * 512], in1=tm[:], op=ALU.add),
                         reads=[tm, xj], writes=[xj])
            for jj, t in enumerate(tiles):
                xj = xs[jj]
                if last:
                    b, r = divmod(t * 128, cfg.TS)
                    row = b * cfg.TL + (r - TC)
                    P.dma("act", io["out"][row:row + 128, :], xj[:], reads=[xj], writes=[("OUT", t)])
                else:
                    P.dma("act", X1[t * 128:(t + 1) * 128, :], xj[:], reads=[xj], writes=[("X", 1, t)])
        P.pop()

    LN8 = math.log(0.125)

    def mlstm(l):
        P.push()
        NB = cfg.NB
        gb_bc = P.sbuf("mlgb", [128, 32]); ng_bc = P.sbuf("mlng", [128, 512])
        P.dma("sp", gb_bc[:], io["mlstm_gate_b"].to_broadcast([128, 32]), writes=[gb_bc])
        P.dma("sp", ng_bc[:], io["mlstm_norm_g"].to_broadcast([128, 512]), writes=[ng_bc])
        ln8 = P.sbuf("mlln8", [128, 1]); onec = P.sbuf("mlone", [128, 1]); eps_c = P.sbuf("mleps", [128, 1])
        P.op("dve", lambda e: e.memset(ln8[:], LN8), writes=[ln8])
        P.op("dve", lambda e: e.memset(onec[:], 1.0), writes=[onec])
        P.op("dve", lambda e: e.memset(eps_c[:], EPS), writes=[eps_c])
        B = []
        for b in range(NB):
            d = {}
            d["ua"] = [P.sbuf(f"mlua{b}{i}", [128, A_IN]) for i in range(2)]
            d["g"] = P.sbuf(f"mlg{b}", [128, 32])
            d["nlf"] = P.sbuf(f"mlnlf{b}", [128, 8])
            d["eb"] = P.sbuf(f"mleb{b}", [128, 8])
            d["vs"] = P.sbuf(f"mlvs{b}", [128, 8])
            d["eL"] = P.sbuf(f"mleL{b}", [128, 8])
            d["Vt"] = P.sbuf(f"mlVt{b}", [128, 8, 65])
            d["QT"] = P.sbuf(f"mlQT{b}", [128, 4, 128])
            d["KT"] = P.sbuf(f"mlKT{b}", [128, 4, 128])
            d["AT"] = [P.sbuf(f"mlAT{b}{i}", [128, 128]) for i in range(2)]
            d["C"] = P.sbuf(f"mlC{b}", [128, 4, 65])
            d["Ct"] = P.sbuf(f"mlCt{b}", [128, 65])
            d["small"] = P.sbuf(f"mlsm{b}", [128, 6, 8])
            d["hd"] = P.sbuf(f"mlhd{b}", [128, 8, 64])
            d["hf"] = P.sbuf(f"mlhf{b}", [128, 512])
            d["sq"] = P.sbuf(f"mlsq{b}", [128, 512])
            d["sg"] = P.sbuf(f"mlsg{b}", [128, 512])
            d["pT"] = P.psum(f"mlpT{b}", [128, 8, 128]) if b == 0 else None
            d["pa"] = P.psum(f"mlpa{b}", [128, 128])
            d["pn"] = P.psum(f"mlpn{b}", [128, 8, 128]) if b == 0 else None
            B.append(d)
        pn = B[0]["pn"]
        pc = P.psum("mlpc", [128, 2, 65])
        psm = P.psum("mlpsm", [128, 16])
        for dr in range(2):
            mk = masks[:, dr, :]
            for b in range(NB):
                P.op("dve", lambda e, b=b: e.memset(B[b]["C"][:], 0.0), writes=[B[b]["C"]])
            if dr == 0:
                order = list(range(NTS))
            else:
                order = [1, 0] + list(range(NTS - 1, 1, -1))
            for step, tt in enumerate(order):
                for b in range(NB):
                    d = B[b]
                    t = b * NTS + tt
                    ua = d["ua"][step % 2]
                    pT = B[0]["pT"]
                    P.dma("sp", ua[:], U[t * 128:(t + 1) * 128, 0:A_IN], reads=ukeys(t), writes=[ua])
                    g, nlf, eb, vs, eL, Vt, QT, KT, C, sm = d["g"], d["nlf"], d["eb"], d["vs"], d["eL"], d["Vt"], d["QT"], d["KT"], d["C"], d["small"]
                    P.op("dve", lambda e, ua=ua, g=g: e.tensor_tensor(out=g[:], in0=ua[:, 2048:2080], in1=gb_bc[:], op=ALU.add), reads=[ua, gb_bc], writes=[g])
                    ig = g[:, 16 * dr: 16 * dr + 8]
                    fg = g[:, 16 * dr + 8: 16 * dr + 16]
                    P.op("act", lambda e, fg=fg, nlf=nlf: e.activation(out=nlf[:], in_=fg, func=AF.Exp, scale=-1.0), reads=[g], writes=[nlf])
                    P.op("act", lambda e, nlf=nlf: e.activation(out=nlf[:], in_=nlf[:], func=AF.Ln, bias=onec[:, 0:1]), reads=[nlf, onec], writes=[nlf])
                    P.op("pe", lambda e, nlf=nlf, mk=mk: e.matmul(psm[:, 0:8], lhsT=mk, rhs=nlf[:], start=True, stop=True), reads=[masks, nlf], writes=[psm])
                    P.op("pe", lambda e, nlf=nlf: e.matmul(psm[:, 8:16], lhsT=ones[:], rhs=nlf[:], start=True, stop=True), reads=[ones, nlf], writes=[psm])
                    P.op("act", lambda e, eb=eb: e.activation(out=eb[:], in_=psm[:, 0:8], func=AF.Exp, scale=-1.0), reads=[psm], writes=[eb])
                    P.op("act", lambda e, eL=eL: e.activation(out=eL[:], in_=psm[:, 8:16], func=AF.Exp, scale=-1.0), reads=[psm], writes=[eL])
                    P.op("dve", lambda e, vs=vs, ig=ig: e.tensor_tensor(out=vs[:], in0=psm[:, 0:8], in1=ig, op=ALU.add), reads=[psm, g], writes=[vs])
                    P.op("act", lambda e, vs=vs: e.activation(out=vs[:], in_=vs[:], func=AF.Exp, bias=ln8[:, 0:1]), reads=[vs, ln8], writes=[vs])
                    P.op("dve", lambda e, ua=ua, vs=vs, Vt=Vt: e.tensor_tensor(out=Vt[:, :, 0:64], in0=ua[:, 1024:1536].rearrange("p (h d) -> p h d", h=8),
                                                                      in1=vs[:].unsqueeze(2).to_broadcast([128, 8, 64]), op=ALU.mult), reads=[ua, vs], writes=[Vt])
                    P.op("pool", lambda e, vs=vs, Vt=Vt: e.tensor_copy(out=Vt[:, :, 64], in_=vs[:]), reads=[vs], writes=[Vt])
                    for i in range(8):
                        P.op("pe", lambda e, i=i, ua=ua, pT=pT: e.transpose(out=pT[:, i, :], in_=ua[:, i * 128:(i + 1) * 128], identity=ident[:]), reads=[ua, ident], writes=[pT])
                    P.op("act", lambda e, QT=QT, pT=pT: e.activation(out=QT[:], in_=pT[:, 0:4, :], func=AF.Copy), reads=[pT], writes=[QT])
                    P.op("dve", lambda e, KT=KT, pT=pT: e.tensor_copy(out=KT[:], in_=pT[:, 4:8, :]), reads=[pT], writes=[KT])
                    for h in range(8):
                        hp, ho = h // 2, (h % 2) * 64
                        AT = d["AT"][h % 2]
                        pa = d["pa"]
                        P.op("pe", lambda e, KT=KT, QT=QT, hp=hp, ho=ho, pa=pa: e.matmul(pa[:], lhsT=KT[ho:ho + 64, hp, :], rhs=QT[ho:ho + 64, hp, :], start=True, stop=True),
                             reads=[KT, QT], writes=[pa])
                        P.op("dve", lambda e, AT=AT, pa=pa, mk=mk: e.tensor_tensor(out=AT[:], in0=pa[:], in1=mk, op=ALU.mult), reads=[pa, masks], writes=[AT])
                        P.op("pe", lambda e, AT=AT, Vt=Vt, h=h: e.matmul(pn[:, h, 0:65], lhsT=AT[:], rhs=Vt[:, h, :], start=True, stop=False), reads=[AT, Vt], writes=[pn])
                        P.op("pe", lambda e, QT=QT, C=C, h=h, hp=hp, ho=ho: e.matmul(pn[:, h, 0:65], lhsT=QT[ho:ho + 64, hp, :], rhs=C[ho:ho + 64, hp, :], start=False, stop=True),
                             reads=[QT, C], writes=[pn])
                    P.op("dve", lambda e, sm=sm, eb=eb: e.tensor_tensor(out=sm[:, 0, :], in0=pn[:, :, 64], in1=eb[:], op=ALU.mult), reads=[pn, eb], writes=[sm])
                    P.op("dve", lambda e, sm=sm: e.scalar_tensor_tensor(out=sm[:, 1, :], in0=sm[:, 0, :], scalar=-1.0, in1=sm[:, 0, :], op0=ALU.mult, op1=ALU.max), reads=[sm], writes=[sm])
                    P.op("dve", lambda e, sm=sm: e.tensor_scalar_max(out=sm[:, 2, :], in0=sm[:, 1, :], scalar1=1.0), reads=[sm], writes=[sm])
                    P.op("dve", lambda e, sm=sm: e.reciprocal(out=sm[:, 3, :], in_=sm[:, 2, :]), reads=[sm], writes=[sm])
                    P.op("dve", lambda e, sm=sm, eb=eb: e.tensor_tensor(out=sm[:, 4, :], in0=sm[:, 3, :], in1=eb[:], op=ALU.mult), reads=[sm, eb], writes=[sm])
                    hd = d["hd"]
                    P.op("dve", lambda e, sm=sm, hd=hd: e.tensor_tensor(out=hd[:], in0=pn[:, :, 0:64], in1=sm[:, 4, :].unsqueeze(2).to_broadcast([128, 8, 64]), op=ALU.mult),
                         reads=[pn, sm], writes=[hd])
                    for hp in range(4):
                        P.op("pe", lambda e, ua=ua, Vt=Vt, hp=hp: e.matmul(pc[:], lhsT=ua[:, 512 + hp * 128: 512 + (hp + 1) * 128], rhs=Vt[:, 2 * hp:2 * hp + 2, :], start=True, stop=True),
                             reads=[ua, Vt], writes=[pc])
                        for ho_i in range(2):
                            ho = ho_i * 64
                            h = 2 * hp + ho_i
                            Ct = d["Ct"]
                            P.op("dve", lambda e, C=C, Ct=Ct, hp=hp, ho=ho, ho_i=ho_i: e.tensor_tensor(out=Ct[ho:ho + 64, :], in0=pc[ho:ho + 64, ho_i, :], in1=C[ho:ho + 64, hp, :], op=ALU.add),
                                 reads=[pc, C], writes=[Ct])
                            P.op("act", lambda e, C=C, Ct=Ct, eL=eL, hp=hp, ho=ho, h=h: e.activation(out=C[ho:ho + 64, hp, :], in_=Ct[ho:ho + 64, :], func=AF.Copy, scale=eL[ho:ho + 64, h:h + 1]),
                                 reads=[Ct, eL], writes=[C])
                    hdf = hd[:].rearrange("p h d -> p (h d)")
                    if dr == 0:
                        P.dma("act", YS[t * 128:(t + 1) * 128, 0:512], hdf, reads=[hd], writes=[("YS", t)])
                    else:
                        hf, sq, sg = d["hf"], d["sq"], d["sg"]
                        P.dma("sp", hf[:], YS[t * 128:(t + 1) * 128, 0:512], reads=[("YS", t)], writes=[hf])
                        P.op("pool", lambda e, hf=hf, hdf=hdf: e.tensor_tensor(out=hf[:], in0=hf[:], in1=hdf, op=ALU.add), reads=[hf, hd], writes=[hf])
                        P.op("act", lambda e, hf=hf, sq=sq: e.activation(out=sq[:], in_=hf[:], func=AF.Square), reads=[hf], writes=[sq])
                        P.op("dve", lambda e, sq=sq, sm=sm: e.tensor_reduce(out=sm[:, 5, :], in_=sq[:].rearrange("p (h d) -> p h d", h=8), axis=AX.X, op=ALU.add), reads=[sq], writes=[sm])
                        P.op("act", lambda e, sm=sm: e.activation(out=sm[:, 5, :], in_=sm[:, 5, :], func=AF.Sqrt, scale=1.0 / 64, bias=eps_c[:, 0:1]), reads=[sm, eps_c], writes=[sm])
                        P.op("dve", lambda e, sm=sm: e.reciprocal(out=sm[:, 5, :], in_=sm[:, 5, :]), reads=[sm], writes=[sm])
                        P.op("dve", lambda e, hf=hf, sm=sm: e.tensor_tensor(out=hf[:].rearrange("p (h d) -> p h d", h=8), in0=hf[:].rearrange("p (h d) -> p h d", h=8),
                                                                      in1=sm[:, 5, :].unsqueeze(2).to_broadcast([128, 8, 64]), op=ALU.mult), reads=[hf, sm], writes=[hf])
                        P.op("pool", lambda e, hf=hf: e.tensor_tensor(out=hf[:], in0=hf[:], in1=ng_bc[:], op=ALU.mult), reads=[hf, ng_bc], writes=[hf])
                        P.op("act", lambda e, ua=ua, sg=sg: e.activation(out=sg[:], in_=ua[:, 1536:2048], func=AF.Sigmoid), reads=[ua], writes=[sg])
                        P.op("dve", lambda e, hf=hf, sg=sg: e.tensor_tensor(out=sg[:], in0=hf[:], in1=sg[:], op=ALU.mult), reads=[hf, sg], writes=[sg])
                        P.dma("act", O[t * 128:(t + 1) * 128, 0:512], sg[:], reads=[sg], writes=[("O", t, 0)])
        P.pop()

    def diffattn(l):
        NB = cfg.NB
        lam_init = 0.8 - 0.6 * math.exp(-0.3 * l)
        SC = 32 ** -0.5
        CSH = 4.0
        P.push()
        g2 = P.sbuf("dag2", [128, 2, 32]); eps_c = P.sbuf("daeps", [128, 1])
        P.dma("sp", g2[:, 0, :], io["diff_qk_g"][0:1, :].to_broadcast([128, 32]), writes=[g2])
        P.dma("sp", g2[:, 1, :], io["diff_qk_g"][1:2, :].to_broadcast([128, 32]), writes=[g2])
        P.op("dve", lambda e: e.memset(eps_c[:], EPS), writes=[eps_c])
        ubs = [P.sbuf(f"daub{i}", [128, 1024]) for i in range(2)]
        sqs = P.sbuf("dasq", [128, 1024]); rs = [P.sbuf(f"dars{i}", [128, 32]) for i in range(2)]
        qn = [P.sbuf(f"daqn{i}", [128, 1024]) for i in range(2)]
        qr = [P.sbuf(f"daqr{i}", [128, 1024]) for i in range(2)]
        tA = P.sbuf("datA", [128, 2, 512]); tB = P.sbuf("datB", [128, 2, 512])
        cs = [P.sbuf(f"dacs{i}", [128, 32]) for i in range(2)]
        QTt = [P.sbuf(f"daQTt{i}", [64, 16, 128]) for i in range(2)]
        pT = [P.psum(f"dapT{i}", [64, 8, 128]) for i in range(2)]
        for t in range(cfg.ntile):
            b, tt = divmod(t, NTS)
            isctx = tt < TC // 128
            ub = ubs[t % 2]; r_ = rs[t % 2]; qn_ = qn[t % 2]; qr_ = qr[t % 2]; cs_ = cs[t % 2]; qt_ = QTt[t % 2]
            P.dma("sp", ub[:], U[t * 128:(t + 1) * 128, A_IN:A_IN + 1024], reads=ukeys(t), writes=[ub])
            P.op("act", lambda e, ub=ub: e.activation(out=sqs[:], in_=ub[:], func=AF.Square), reads=[ub], writes=[sqs])
            P.op("dve", lambda e, r_=r_: e.tensor_reduce(out=r_[:], in_=sqs[:].rearrange("p (g d) -> p g d", g=32), axis=AX.X, op=ALU.add), reads=[sqs], writes=[r_])
            P.op("act", lambda e, r_=r_: e.activation(out=r_[:], in_=r_[:], func=AF.Sqrt, scale=1.0 / 32, bias=eps_c[:, 0:1]), reads=[r_, eps_c], writes=[r_])
            P.op("dve", lambda e, r_=r_: e.reciprocal(out=r_[:], in_=r_[:]), reads=[r_], writes=[r_])
            P.op("dve", lambda e, ub=ub, r_=r_, qn_=qn_: e.tensor_tensor(out=qn_[:].rearrange("p (g d) -> p g d", g=32), in0=ub[:].rearrange("p (g d) -> p g d", g=32),
                                                                   in1=r_[:].unsqueeze(2).to_broadcast([128, 32, 32]), op=ALU.mult), reads=[ub, r_], writes=[qn_])
            dst = qr_ if isctx else qn_
            P.op("pool", lambda e, qn_=qn_, dst=dst: e.tensor_tensor(out=dst[:].rearrange("p (a g d) -> p a g d", a=2, g=16), in0=qn_[:].rearrange("p (a g d) -> p a g d", a=2, g=16),
                                                                in1=g2[:].unsqueeze(2).to_broadcast([128, 2, 16, 32]), op=ALU.mult), reads=[qn_, g2], writes=[dst])
            if not isctx:
                lrow = (tt - TC // 128) * 128
                P.dma("sp", cs_[:], io["rope"][lrow:lrow + 128, :], writes=[cs_])
                v4 = qn_[:].rearrange("p (g i two) -> p g i two", g=32, two=2)
                o4 = qr_[:].rearrange("p (g i two) -> p g i two", g=32, two=2)
                x1, x2 = v4[:, :, :, 0], v4[:, :, :, 1]
                cosb = cs_[:, 0:16].unsqueeze(1).to_broadcast([128, 32, 16])
                sinb = cs_[:, 16:32].unsqueeze(1).to_broadcast([128, 32, 16])
                tA0 = tA[:, 0, :].rearrange("p (g i) -> p g i", g=32); tA1 = tA[:, 1, :].rearrange("p (g i) -> p g i", g=32)
                tB0 = tB[:, 0, :].rearrange("p (g i) -> p g i", g=32); tB1 = tB[:, 1, :].rearrange("p (g i) -> p g i", g=32)
                P.op("dve", lambda e, x1=x1, cosb=cosb, tA0=tA0: e.tensor_tensor(out=tA0, in0=x1, in1=cosb, op=ALU.mult), reads=[qn_, cs_], writes=[tA])
                P.op("dve", lambda e, x2=x2, sinb=sinb, tA1=tA1: e.tensor_tensor(out=tA1, in0=x2, in1=sinb, op=ALU.mult), reads=[qn_, cs_], writes=[tA])
                P.op("dve", lambda e, o4=o4, tA0=tA0, tA1=tA1: e.tensor_tensor(out=o4[:, :, :, 0], in0=tA0, in1=tA1, op=ALU.subtract), reads=[tA], writes=[qr_])
                P.op("pool", lambda e, x1=x1, sinb=sinb, tB0=tB0: e.tensor_tensor(out=tB0, in0=x1, in1=sinb, op=ALU.mult), reads=[qn_, cs_], writes=[tB])
                P.op("pool", lambda e, x2=x2, cosb=cosb, tB1=tB1: e.tensor_tensor(out=tB1, in0=x2, in1=cosb, op=ALU.mult), reads=[qn_, cs_], writes=[tB])
                P.op("pool", lambda e, o4=o4, tB0=tB0, tB1=tB1: e.tensor_tensor(out=o4[:, :, :, 1], in0=tB0, in1=tB1, op=ALU.add), reads=[tB], writes=[qr_])
            for a in range(2):
                pt_ = pT[a]
                for i in range(8):
                    c0 = a * 512 + i * 64
                    P.op("pe", lambda e, pt_=pt_, i=i, c0=c0, qr_=qr_: e.transpose(out=pt_[:, i, :], in_=qr_[:, c0:c0 + 64], identity=ident[:]), reads=[qr_, ident], writes=[pt_])
                if a == 0:
                    P.op("act", lambda e, pt_=pt_, qt_=qt_: e.activation(out=qt_[:, 0:8, :], in_=pt_[:], func=AF.Copy), reads=[pt_], writes=[qt_])
                else:
                    P.op("dve", lambda e, pt_=pt_, qt_=qt_: e.tensor_copy(out=qt_[:, 8:16, :], in_=pt_[:]), reads=[pt_], writes=[qt_])
            for a in range(2):
                P.dma("act", QKT[b, a, :, :, tt * 128:(tt + 1) * 128].rearrange("h p n -> p h n"), qt_[:, a * 8:(a + 1) * 8, :], reads=[qt_], writes=[("QKT", b, a, tt)])
        P.pop()
        P.push()
        lamr = P.sbuf("dalam", [128, 4, 32]); lt = P.sbuf("dalt", [128, 2, 32]); le = P.sbuf("dale", [128, 2]); nlam = P.sbuf("danlam", [128, 1])
        negc = P.sbuf("danegc", [128, 1]); eps2 = P.sbuf("daeps2", [128, 1]); gs = P.sbuf("dags", [128, 64])
        P.dma("sp", lamr[:].rearrange("p a d -> p (a d)"), io["diff_lam"].to_broadcast([128, 128]), writes=[lamr])
        P.dma("sp", gs[:], io["diff_subln_g"].to_broadcast([128, 64]), writes=[gs])
        P.op("dve", lambda e: e.memset(negc[:], -CSH), writes=[negc])
        P.op("dve", lambda e: e.memset(eps2[:], EPS), writes=[eps2])
        P.op("dve", lambda e: e.tensor_tensor(out=lt[:, 0, :], in0=lamr[:, 0, :], in1=lamr[:, 1, :], op=ALU.mult), reads=[lamr], writes=[lt])
        P.op("dve", lambda e: e.tensor_tensor(out=lt[:, 1, :], in0=lamr[:, 2, :], in1=lamr[:, 3, :], op=ALU.mult), reads=[lamr], writes=[lt])
        P.op("dve", lambda e: e.tensor_reduce(out=le[:], in_=lt[:], axis=AX.X, op=ALU.add), reads=[lt], writes=[le])
        P.op("act", lambda e: e.activation(out=le[:], in_=le[:], func=AF.Exp), reads=[le], writes=[le])
        P.op("dve", lambda e: e.tensor_tensor(out=nlam[:], in0=le[:, 1:2], in1=le[:, 0:1], op=ALU.subtract), reads=[le], writes=[nlam])
        P.op("dve", lambda e: e.tensor_scalar_add(out=nlam[:], in0=nlam[:], scalar1=-lam_init), reads=[nlam], writes=[nlam])
        P.op("dve", lambda e: e.tensor_scalar_mul(out=gs[:], in0=gs[:], scalar1=1.0 - lam_init), reads=[gs], writes=[gs])
        QTh = [P.sbuf(f"daQTh{i}", [64, cfg.TS]) for i in range(2)]
        KTh = [P.sbuf(f"daKTh{i}", [64, cfg.TS]) for i in range(2)]
        Vst = [P.sbuf(f"daVst{i}", [128, NTS, 64]) for i in range(2)]
        Vr = [P.sbuf(f"daVr{i}", [128, NTS, 65]) for i in range(2)]
        for i in range(2):
            for j0 in range(0, NTS, 128):
                j1 = min(NTS, j0 + 128)
                P.op("dve", lambda e, i=i, j0=j0, j1=j1: e.tensor_copy(out=Vr[i][:, j0:j1, 64].bitcast(F32R), in_=ones[:, 0:j1 - j0]), reads=[ones], writes=[Vr[i]])
        PT = [P.sbuf(f"daPT{i}", [128, 512]) for i in range(3)]
        accs = [P.sbuf(f"daaccs{i}", [65, 512]) for i in range(2)]
        Ot = [P.sbuf(f"daOt{i}", [128, 4, 64]) for i in range(2)]
        o2 = P.sbuf("dao2", [128, 4, 64]); sq = P.sbuf("dasq2", [128, 4, 64])
        sm = [P.sbuf(f"dasm{i}", [128, 4, 4]) for i in range(2)]
        ps = [P.psum(f"daps{i}", [128, 512]) for i in range(2)]
        acc = [P.psum(f"daacc{i}", [65, 512]) for i in range(2)]
        pTn = [P.psum(f"dapTn{i}", [128, 4, 65]) for i in range(2)]
        cnt = {"ps": 0, "pt": 0, "blk": 0}
        for b in range(NB):
            for h in range(8):
                it = b * 8 + h
                qth, kth, vst, vr = QTh[it % 2], KTh[it % 2], Vst[it % 2], Vr[it % 2]
                P.dma("sp", qth[:], QKT[b, 0, h], reads=[("QKT", b, 0, tt) for tt in range(NTS)], writes=[qth])
                P.dma("sp", kth[:], QKT[b, 1, h], reads=[("QKT", b, 1, tt) for tt in range(NTS)], writes=[kth])
                c0 = A_IN + 1024 + h * 64
                P.dma("sp", vst[:], U[b * cfg.TS:(b + 1) * cfg.TS, c0:c0 + 64].rearrange("(n p) d -> p n d", p=128),
                      reads=[k_ for tt in range(NTS) for k_ in ukeys(b * NTS + tt)], writes=[vst])
                P.op("pool", lambda e, qth=qth: e.tensor_copy(out=qth[:].bitcast(F32R), in_=qth[:]), reads=[qth], writes=[qth])
                P.op("pool", lambda e, kth=kth: e.tensor_copy(out=kth[:].bitcast(F32R), in_=kth[:]), reads=[kth], writes=[kth])
                P.op("pool", lambda e, vst=vst, vr=vr: e.tensor_copy(out=vr[:, :, 0:64].bitcast(F32R), in_=vst[:]), reads=[vst], writes=[vr])
                blocks = [(0, TC, list(range(TC // 128)))]
                q0 = TC
                while q0 < cfg.TS:
                    nq = min(512, cfg.TS - q0)
                    blocks.append((q0, nq, list(range(NTS))))
                    q0 += nq
                for (q0, nq, kts) in blocks:
                    nsub = nq // 128
                    for c in range(2):
                        for ki, kt in enumerate(kts):
                            p_ = ps[cnt["ps"] % 2]; cnt["ps"] += 1
                            pt_ = PT[cnt["pt"] % 3]; cnt["pt"] += 1
                            P.op("pe", lambda e, p_=p_, kth=kth, qth=qth, c=c, kt=kt, q0=q0, nq=nq: e.matmul(
                                p_[:, 0:nq], lhsT=kth[c * 32:(c + 1) * 32, kt * 128:(kt + 1) * 128].bitcast(F32R), rhs=qth[c * 32:(c + 1) * 32, q0:q0 + nq].bitcast(F32R),
                                start=True, stop=True), reads=[kth, qth], writes=[p_])
                            P.op("act", lambda e, p_=p_, pt_=pt_, nq=nq: e.activation(out=pt_[:, 0:nq].bitcast(F32R), in_=p_[:, 0:nq], func=AF.Exp, scale=SC, bias=negc[:, 0:1]),
                                 reads=[p_, negc], writes=[pt_])
                            P.op("pe", lambda e, pt_=pt_, vr=vr, c=c, kt=kt, nq=nq, ki=ki, nk=len(kts): e.matmul(
                                acc[c][:, 0:nq], lhsT=vr[:, kt, :].bitcast(F32R), rhs=pt_[:, 0:nq].bitcast(F32R), start=(ki == 0), stop=(ki == nk - 1)),
                                reads=[vr, pt_], writes=[acc[c]])
                        if c == 0:
                            P.op("dve", lambda e, c=c, nq=nq: e.tensor_copy(out=accs[c][:, 0:nq], in_=acc[c][:, 0:nq]), reads=[acc[c]], writes=[accs[c]])
                        else:
                            P.op("dve", lambda e, c=c, nq=nq: e.tensor_copy(out=accs[c][:, 0:nq], in_=acc[c][:, 0:nq]), reads=[acc[c]], writes=[accs[c]])
                        for j in range(nsub):
                            P.op("pe", lambda e, c=c, j=j: e.transpose(out=pTn[c][:, j, :], in_=accs[c][:, j * 128:(j + 1) * 128], identity=ident[0:65, 0:65]),
                                 reads=[accs[c], ident], writes=[pTn[c]])
                    bi = cnt["blk"]; cnt["blk"] += 1
                    ot = Ot[bi % 2]; sm_ = sm[bi % 2]
                    n_ = nsub
                    P.op("dve", lambda e, sm_=sm_, n_=n_: e.reciprocal(out=sm_[:, 0, 0:n_], in_=pTn[0][:, 0:n_, 64]), reads=[pTn[0]], writes=[sm_])
                    P.op("dve", lambda e, sm_=sm_, n_=n_: e.reciprocal(out=sm_[:, 1, 0:n_], in_=pTn[1][:, 0:n_, 64]), reads=[pTn[1]], writes=[sm_])
                    P.op("dve", lambda e, sm_=sm_, n_=n_: e.tensor_scalar(out=sm_[:, 1, 0:n_], in0=sm_[:, 1, 0:n_], scalar1=nlam[:, 0:1], scalar2=None, op0=ALU.mult), reads=[sm_, nlam], writes=[sm_])
                    P.op("dve", lambda e, sm_=sm_, n_=n_, ot=ot: e.tensor_tensor(out=ot[:, 0:n_, :], in0=pTn[0][:, 0:n_, 0:64], in1=sm_[:, 0, 0:n_].unsqueeze(2).to_broadcast([128, n_, 64]), op=ALU.mult),
                         reads=[pTn[0], sm_], writes=[ot])
                    P.op("dve", lambda e, sm_=sm_, n_=n_: e.tensor_tensor(out=o2[:, 0:n_, :], in0=pTn[1][:, 0:n_, 0:64], in1=sm_[:, 1, 0:n_].unsqueeze(2).to_broadcast([128, n_, 64]), op=ALU.mult),
                         reads=[pTn[1], sm_], writes=[o2])
                    P.op("pool", lambda e, n_=n_, ot=ot: e.tensor_tensor(out=ot[:, 0:n_, :], in0=ot[:, 0:n_, :], in1=o2[:, 0:n_, :], op=ALU.add), reads=[ot, o2], writes=[ot])
                    P.op("act", lambda e, n_=n_, ot=ot: e.activation(out=sq[:, 0:n_, :], in_=ot[:, 0:n_, :], func=AF.Square), reads=[ot], writes=[sq])
                    P.op("dve", lambda e, sm_=sm_, n_=n_: e.tensor_reduce(out=sm_[:, 2, 0:n_], in_=sq[:, 0:n_, :], axis=AX.X, op=ALU.add), reads=[sq], writes=[sm_])
                    P.op("act", lambda e, sm_=sm_, n_=n_: e.activation(out=sm_[:, 2, 0:n_], in_=sm_[:, 2, 0:n_], func=AF.Sqrt, scale=1.0 / 64, bias=eps2[:, 0:1]), reads=[sm_, eps2], writes=[sm_])
                    P.op("dve", lambda e, sm_=sm_, n_=n_: e.reciprocal(out=sm_[:, 2, 0:n_], in_=sm_[:, 2, 0:n_]), reads=[sm_], writes=[sm_])
                    P.op("dve", lambda e, sm_=sm_, n_=n_, ot=ot: e.tensor_tensor(out=ot[:, 0:n_, :], in0=ot[:, 0:n_, :], in1=sm_[:, 2, 0:n_].unsqueeze(2).to_broadcast([128, n_, 64]), op=ALU.mult),
                         reads=[ot, sm_], writes=[ot])
                    P.op("pool", lambda e, n_=n_, ot=ot: e.tensor_tensor(out=ot[:, 0:n_, :], in0=ot[:, 0:n_, :], in1=gs[:].unsqueeze(1).to_broadcast([128, n_, 64]), op=ALU.mult),
                         reads=[ot, gs], writes=[ot])
                    r0 = b * cfg.TS + q0
                    t0 = r0 // 128
                    P.dma("act", O[r0:r0 + nq, 512 + h * 64: 512 + (h + 1) * 64].rearrange("(n p) d -> p n d", p=128), ot[:, 0:n_, :], reads=[ot],
                          writes=[("O", t0 + j, 1) for j in range(n_)])
        P.pop()

    def gla(l):
        need_ctx = (l == 0) or ("ctx_out" in dbg)
        NB = cfg.NB
        c0 = C_IN
        P.push()
        gw2 = P.sbuf("glgw2", [16, 2, 256]); gbr = P.sbuf("glgb", [1, 2, 256]); ng_bc = P.sbuf("glng", [128, 128])
        onec = P.sbuf("glone", [128, 1]); eps_c = P.sbuf("gleps", [128, 1])
        P.dma("sp", gw2[:], io["gla_gate_w2"].rearrange("d k n -> k d n"), writes=[gw2])
        P.dma("sp", gbr[:], io["gla_gate_b"].rearrange("(o d) n -> o d n", o=1), writes=[gbr])
        P.dma("sp", ng_bc[:], io["gla_norm_g"].to_broadcast([128, 128]), writes=[ng_bc])
        P.op("dve", lambda e: e.memset(onec[:], 1.0), writes=[onec])
        P.op("dve", lambda e: e.memset(eps_c[:], EPS), writes=[eps_c])
        B = []
        for b in range(NB):
            d = {}
            d["ud"] = [P.sbuf(f"glud{b}{i}", [128, D_IN]) for i in range(2)]
            d["gdT"] = P.sbuf(f"glgdT{b}", [16, 128])
            d["nla"] = P.sbuf(f"glnla{b}", [128, 256])
            d["Kh"] = P.sbuf(f"glKh{b}", [128, 256])
            d["eq"] = P.sbuf(f"gleq{b}", [128, 2, 128]); d["ek"] = P.sbuf(f"glek{b}", [128, 2, 128])
            d["QT"] = P.sbuf(f"glQT{b}", [128, 2, 128]); d["KT"] = P.sbuf(f"glKT{b}", [128, 2, 128])
            d["AT"] = [P.sbuf(f"glAT{b}{i}", [128, 128]) for i in range(2)]
            d["S"] = P.sbuf(f"glS{b}", [128, 2, 128])
            d["od"] = P.sbuf(f"glod{b}", [128, 512]); d["of"] = P.sbuf(f"glof{b}", [128, 512])
            d["sq"] = P.sbuf(f"glsq{b}", [128, 512]); d["sm"] = P.sbuf(f"glsm{b}", [128, 4])
            d["sg"] = P.sbuf(f"glsg{b}", [128, 512])
            B.append(d)
        pg = P.psum("glpg", [16, 128]); pz = P.psum("glpz", [128, 256]); pcs = P.psum("glpcs", [128, 2, 256])
        pbT = P.psum("glpbT", [128, 2, 128]); pqk = P.psum("glpqk", [128, 4, 128]); pa = P.psum("glpa", [128, 128])
        po = P.psum("glpo", [128, 4, 128]); pc = P.psum("glpc", [128, 256])
        nct = TC // 128
        for dr in range(2):
            mk = masks[:, dr, :]
            mks = masks[:, 4 + dr, :]
            lastcol = 127 if dr == 0 else 0
            for b in range(NB):
                P.op("dve", lambda e, b=b: e.memset(B[b]["S"][:], 0.0), writes=[B[b]["S"]])
            order = list(range(NTS)) if dr == 0 else [1, 0] + list(range(NTS - 1, 1, -1))
            for step, tt in enumerate(order):
                want_out = need_ctx or tt >= nct
                for b in range(NB):
                    d = B[b]
                    t = b * NTS + tt
                    ud = d["ud"][step % 2]
                    gdT, nla, Kh, eq, ek, QT, KT, S = d["gdT"], d["nla"], d["Kh"], d["eq"], d["ek"], d["QT"], d["KT"], d["S"]
                    P.dma("sp", ud[:], U[t * 128:(t + 1) * 128, c0:c0 + D_IN], reads=ukeys(t), writes=[ud])
                    gc = 1536 + 16 * dr
                    P.op("pe", lambda e, ud=ud, gc=gc: e.transpose(out=pg[:], in_=ud[:, gc:gc + 16], identity=ident[:]), reads=[ud, ident], writes=[pg])
                    P.op("act", lambda e, gdT=gdT: e.activation(out=gdT[:], in_=pg[:], func=AF.Copy), reads=[pg], writes=[gdT])
                    P.op("pe", lambda e, gdT=gdT, dr=dr: e.matmul(pz[:], lhsT=gdT[:], rhs=gw2[:, dr, :], start=True, stop=False), reads=[gdT, gw2], writes=[pz])
                    P.op("pe", lambda e, dr=dr: e.matmul(pz[:], lhsT=ones[0:1, :], rhs=gbr[0:1, dr, :], start=False, stop=True), reads=[ones, gbr], writes=[pz])
                    P.op("act", lambda e, nla=nla: e.activation(out=nla[:], in_=pz[:], func=AF.Exp, scale=-1.0), reads=[pz], writes=[nla])
                    P.op("act", lambda e, nla=nla: e.activation(out=nla[:], in_=nla[:], func=AF.Ln, bias=onec[:, 0:1]), reads=[nla, onec], writes=[nla])
                    P.op("pe", lambda e, nla=nla, mks=mks: e.matmul(pcs[:, 1, :], lhsT=mks, rhs=nla[:], start=True, stop=True), reads=[masks, nla], writes=[pcs])
                    P.op("act", lambda e, Kh=Kh: e.activation(out=Kh[:], in_=pcs[:, 1, :], func=AF.Exp, scale=-1.0 / 16), reads=[pcs], writes=[Kh])
                    P.op("dve", lambda e, Kh=Kh, ud=ud: e.tensor_tensor(out=Kh[:], in0=Kh[:], in1=ud[:, 256:512], op=ALU.mult), reads=[Kh, ud], writes=[Kh])
                    for p in range(2):
                        P.op("pe", lambda e, nla=nla, p=p, mk=mk: e.matmul(pbT[:, p, :], lhsT=nla[:, p * 128:(p + 1) * 128], rhs=mk, start=True, stop=True), reads=[nla, masks], writes=[pbT])
                    P.op("act", lambda e, eq=eq: e.activation(out=eq[:], in_=pbT[:], func=AF.Exp, scale=-1.0 / 16), reads=[pbT], writes=[eq])
                    P.op("act", lambda e, ek=ek: e.activation(out=ek[:], in_=pbT[:], func=AF.Exp, scale=1.0 / 16), reads=[pbT], writes=[ek])
                    for i in range(4):
                        P.op("pe", lambda e, ud=ud, i=i: e.transpose(out=pqk[:, i, :], in_=ud[:, i * 128:(i + 1) * 128], identity=ident[:]), reads=[ud, ident], writes=[pqk])
                    P.op("dve", lambda e, QT=QT, eq=eq: e.scalar_tensor_tensor(out=QT[:], in0=pqk[:, 0:2, :], scalar=0.125, in1=eq[:], op0=ALU.mult, op1=ALU.mult), reads=[pqk, eq], writes=[QT])
                    P.op("dve", lambda e, KT=KT, ek=ek: e.tensor_tensor(out=KT[:], in0=pqk[:, 2:4, :], in1=ek[:], op=ALU.mult), reads=[pqk, ek], writes=[KT])
                    for h in range(4):
                        p, ho = h // 2, (h % 2) * 64
                        if want_out:
                            AT = d["AT"][h % 2]
                            P.op("pe", lambda e, KT=KT, QT=QT, p=p, ho=ho: e.matmul(pa[:], lhsT=KT[ho:ho + 64, p, :], rhs=QT[ho:ho + 64, p, :], start=True, stop=True), reads=[KT, QT], writes=[pa])
                            P.op("dve", lambda e, AT=AT, mk=mk: e.tensor_tensor(out=AT[:], in0=pa[:], in1=mk, op=ALU.mult), reads=[pa, masks], writes=[AT])
                            P.op("pe", lambda e, AT=AT, ud=ud, h=h: e.matmul(po[:, h, :], lhsT=AT[:], rhs=ud[:, 512 + h * 128: 512 + (h + 1) * 128], start=True, stop=False), reads=[AT, ud], writes=[po])
                            P.op("pe", lambda e, QT=QT, S=S, h=h, p=p, ho=ho: e.matmul(po[:, h, :], lhsT=QT[ho:ho + 64, p, :], rhs=S[ho:ho + 64, p, :], start=False, stop=True), reads=[QT, S], writes=[po])
                    od = d["od"]
                    if want_out:
                        P.op("act", lambda e, od=od: e.activation(out=od[:], in_=po[:].rearrange("p h d -> p (h d)"), func=AF.Copy), reads=[po], writes=[od])
                    for p in range(2):
                        P.op("pe", lambda e, Kh=Kh, ud=ud, p=p: e.matmul(pc[:], lhsT=Kh[:, p * 128:(p + 1) * 128], rhs=ud[:, 512 + p * 256: 512 + (p + 1) * 256], start=True, stop=True), reads=[Kh, ud], writes=[pc])
                        for hi in range(2):
                            ho = hi * 64
                            P.op("dve", lambda e, S=S, eq=eq, p=p, ho=ho, hi=hi, lastcol=lastcol: e.scalar_tensor_tensor(out=S[ho:ho + 64, p, :], in0=S[ho:ho + 64, p, :], scalar=eq[ho:ho + 64, p, lastcol:lastcol + 1],
                                                                                                 in1=pc[ho:ho + 64, hi * 128:(hi + 1) * 128], op0=ALU.mult, op1=ALU.add), reads=[S, eq, pc], writes=[S])
                    if not want_out:
                        continue
                    if dr == 0:
                        P.dma("act", YS[t * 128:(t + 1) * 128, 0:512], od[:], reads=[od], writes=[("YS", t)])
                    else:
                        of, sq, sm, sg = d["of"], d["sq"], d["sm"], d["sg"]
                        P.dma("sp", of[:], YS[t * 128:(t + 1) * 128, 0:512], reads=[("YS", t)], writes=[of])
                        P.op("pool", lambda e, of=of, od=od: e.tensor_tensor(out=of[:], in0=of[:], in1=od[:], op=ALU.add), reads=[of, od], writes=[of])
                        P.op("act", lambda e, of=of, sq=sq: e.activation(out=sq[:], in_=of[:], func=AF.Square), reads=[of], writes=[sq])
                        P.op("dve", lambda e, sq=sq, sm=sm: e.tensor_reduce(out=sm[:], in_=sq[:].rearrange("p (h d) -> p h d", h=4), axis=AX.X, op=ALU.add), reads=[sq], writes=[sm])
                        P.op("act", lambda e, sm=sm: e.activation(out=sm[:], in_=sm[:], func=AF.Sqrt, scale=1.0 / 128, bias=eps_c[:, 0:1]), reads=[sm, eps_c], writes=[sm])
                        P.op("dve", lambda e, sm=sm: e.reciprocal(out=sm[:], in_=sm[:]), reads=[sm], writes=[sm])
                        P.op("dve", lambda e, of=of, sm=sm: e.tensor_tensor(out=of[:].rearrange("p (h d) -> p h d", h=4), in0=of[:].rearrange("p (h d) -> p h d", h=4),
                                                                      in1=sm[:].unsqueeze(2).to_broadcast([128, 4, 128]), op=ALU.mult), reads=[of, sm], writes=[of])
                        P.op("pool", lambda e, of=of: e.tensor_tensor(out=of[:].rearrange("p (h d) -> p h d", h=4), in0=of[:].rearrange("p (h d) -> p h d", h=4),
                                                                  in1=ng_bc[:].unsqueeze(1).to_broadcast([128, 4, 128]), op=ALU.mult), reads=[of, ng_bc], writes=[of])
                        P.op("act", lambda e, ud=ud, sg=sg: e.activation(out=sg[:], in_=ud[:, 1024:1536], func=AF.Silu), reads=[ud], writes=[sg])
                        P.op("dve", lambda e, of=of, sg=sg: e.tensor_tensor(out=sg[:], in0=of[:], in1=sg[:], op=ALU.mult), reads=[of, sg], writes=[sg])
                        P.dma("act", O[t * 128:(t + 1) * 128, 512:1024], sg[:], reads=[sg], writes=[("O", t, 1)])
        P.pop()

    E05 = math.exp(-0.5)

    def rwkv(l):
        need_ctx = (l == 0) or ("ctx_out" in dbg)
        NB = cfg.NB
        nct = TC // 128
        P.push()
        mu0 = P.sbuf("rwmu0", [128, C_IN]); mu1 = P.sbuf("rwmu1", [128, C_IN]); muc = P.sbuf("rwmuc", [128, C_IN])
        kv0 = P.sbuf("rwkv0", [128, 512]); tiny = P.sbuf("rwtiny", [128, 1])
        P.dma("sp", mu0[:], io["rwkv_mu"][0:1, :].to_broadcast([128, C_IN]), writes=[mu0])
        P.dma("sp", mu1[:], io["rwkv_mu"][1:2, :].to_broadcast([128, C_IN]), writes=[mu1])
        P.dma("sp", kv0[:], io["rwkv_kvec"][0:1, :].to_broadcast([128, 512]), writes=[kv0])
        P.op("dve", lambda e: e.tensor_tensor(out=muc[:], in0=mu0[:], in1=mu1[:], op=ALU.add), reads=[mu0, mu1], writes=[muc])
        P.op("dve", lambda e: e.tensor_scalar(out=muc[:], in0=muc[:], scalar1=-1.0, scalar2=1.0, op0=ALU.mult, op1=ALU.add), reads=[muc], writes=[muc])
        cur = [P.sbuf(f"rwcur{i}", [128, C_IN]) for i in range(2)]
        prv = [P.sbuf(f"rwprv{i}", [128, C_IN]) for i in range(2)]
        nxt = [P.sbuf(f"rwnxt{i}", [128, C_IN]) for i in range(2)]
        rwt = [P.sbuf(f"rwrwt{i}", [128, 2432]) for i in range(2)]
        t1 = P.sbuf("rwt1", [128, C_IN]); sqk = P.sbuf("rwsqk", [128, 512]); ks = [P.sbuf(f"rwks{i}", [128, 8]) for i in range(2)]
        for t in range(cfg.ntile):
            b, tt = divmod(t, NTS)
            r0 = t * 128
            cu, pv, nx, rt, ks_ = cur[t % 2], prv[t % 2], nxt[t % 2], rwt[t % 2], ks[t % 2]
            seg_start = tt in (0, nct)
            seg_end = tt in (nct - 1, NTS - 1)
            P.dma("sp", cu[:], U[r0:r0 + 128, 0:C_IN], reads=ukeys(t), writes=[cu])
            if seg_start:
                P.op("pool", lambda e, pv=pv: e.memset(pv[:], 0.0), writes=[pv])
                P.dma("sp", pv[1:128, :], U[r0:r0 + 127, 0:C_IN], reads=ukeys(t), writes=[pv])
            else:
                P.dma("sp", pv[:], U[r0 - 1:r0 + 127, 0:C_IN], reads=ukeys(t) + ukeys(t - 1), writes=[pv])
            if seg_end:
                P.op("pool", lambda e, nx=nx: e.memset(nx[:], 0.0), writes=[nx])
                P.dma("sp", nx[0:127, :], U[r0 + 1:r0 + 128, 0:C_IN], reads=ukeys(t), writes=[nx])
            else:
                P.dma("sp", nx[:], U[r0 + 1:r0 + 129, 0:C_IN], reads=ukeys(t) + ukeys(t + 1), writes=[nx])
            us = rt[:, 0:C_IN]
            P.op("dve", lambda e, cu=cu, us=us: e.tensor_tensor(out=us, in0=cu[:], in1=muc[:], op=ALU.mult), reads=[cu, muc], writes=[rt])
            P.op("pool", lambda e, pv=pv: e.tensor_tensor(out=pv[:], in0=pv[:], in1=mu0[:], op=ALU.mult), reads=[pv, mu0], writes=[pv])
            P.op("dve", lambda e, nx=nx: e.tensor_tensor(out=nx[:], in0=nx[:], in1=mu1[:], op=ALU.mult), reads=[nx, mu1], writes=[nx])
            P.op("dve", lambda e, pv=pv, us=us: e.tensor_tensor(out=us, in0=us, in1=pv[:], op=ALU.add), reads=[rt, pv], writes=[rt])
            P.op("dve", lambda e, nx=nx, us=us: e.tensor_tensor(out=us, in0=us, in1=nx[:], op=ALU.add), reads=[rt, nx], writes=[rt])
            kkc = rt[:, 1920:2432]
            P.op("pool", lambda e, rt=rt, kkc=kkc: e.tensor_tensor(out=kkc, in0=rt[:, 512:1024], in1=kv0[:], op=ALU.mult), reads=[rt, kv0], writes=[rt])
            P.op("act", lambda e, kkc=kkc: e.activation(out=sqk[:], in_=kkc, func=AF.Square), reads=[rt], writes=[sqk])
            P.op("dve", lambda e, ks_=ks_: e.tensor_reduce(out=ks_[:], in_=sqk[:].rearrange("p (h d) -> p h d", h=8), axis=AX.X, op=ALU.add), reads=[sqk], writes=[ks_])
            P.op("dve", lambda e, ks_=ks_: e.tensor_scalar_max(out=ks_[:], in0=ks_[:], scalar1=1e-24), reads=[ks_], writes=[ks_])
            P.op("act", lambda e, ks_=ks_: e.activation(out=ks_[:], in_=ks_[:], func=AF.Sqrt), reads=[ks_], writes=[ks_])
            P.op("dve", lambda e, ks_=ks_: e.reciprocal(out=ks_[:], in_=ks_[:]), reads=[ks_], writes=[ks_])
            P.op("dve", lambda e, kkc=kkc, ks_=ks_: e.tensor_tensor(out=kkc.rearrange("p (h d) -> p h d", h=8), in0=kkc.rearrange("p (h d) -> p h d", h=8),
                                                              in1=ks_[:].unsqueeze(2).to_broadcast([128, 8, 64]), op=ALU.mult), reads=[rt, ks_], writes=[rt])
            P.dma("act", RW[r0:r0 + 128, :], rt[:], reads=[rt], writes=[("RW", t)])
        P.pop()
        if "rwA" in dbg:
            return
        P.push()
        w2s = P.sbuf("rww2", [64, 2, 512]); a2s = P.sbuf("rwa2", [64, 2, 512]); w0r = P.sbuf("rww0", [1, 2, 512]); a0r = P.sbuf("rwa0", [1, 2, 512])
        g2s = P.sbuf("rwg2", [128, 512]); kv1 = P.sbuf("rwkv1", [128, 512]); omk1 = P.sbuf("rwomk1", [128, 512]); kv2 = P.sbuf("rwkv2", [128, 512])
        ln0 = P.sbuf("rwln0", [128, 512]); ln1 = P.sbuf("rwln1", [128, 512]); lneps = P.sbuf("rwlneps", [128, 1])
        P.dma("sp", w2s[:], io["rwkv_w2"].rearrange("d k n -> k d n"), writes=[w2s])
        P.dma("sp", a2s[:], io["rwkv_a2"].rearrange("d k n -> k d n"), writes=[a2s])
        P.dma("sp", w0r[:], io["rwkv_w0"].rearrange("(o d) n -> o d n", o=1), writes=[w0r])
        P.dma("sp", a0r[:], io["rwkv_a0"].rearrange("(o d) n -> o d n", o=1), writes=[a0r])
        P.dma("sp", g2s[:], io["rwkv_g2"], writes=[g2s])
        P.dma("sp", kv1[:], io["rwkv_kvec"][1:2, :].to_broadcast([128, 512]), writes=[kv1])
        P.dma("sp", kv2[:], io["rwkv_kvec"][2:3, :].to_broadcast([128, 512]), writes=[kv2])
        P.dma("sp", ln0[:], io["rwkv_ln"][0:1, :].to_broadcast([128, 512]), writes=[ln0])
        P.dma("sp", ln1[:], io["rwkv_ln"][1:2, :].to_broadcast([128, 512]), writes=[ln1])
        P.op("dve", lambda e: e.tensor_scalar(out=omk1[:], in0=kv1[:], scalar1=-1.0, scalar2=1.0, op0=ALU.mult, op1=ALU.add), reads=[kv1], writes=[omk1])
        P.op("dve", lambda e: e.memset(lneps[:], 64e-5), writes=[lneps])
        nm = ["sg", "a_", "tt_", "kd", "beta", "eI", "eInv", "eE", "eR", "Rt", "Kt", "Bt", "At", "Kh", "nBh", "tmp"]
        T = {n: P.sbuf("rw_" + n, [128, 512]) for n in nm}
        twT = P.sbuf("rwtwT", [64, 128]); adT = P.sbuf("rwadT", [64, 128]); WL = P.sbuf("rwWL", [128, 4, 2])
        Acur = [P.sbuf(f"rwA{i}", [128, 128]) for i in range(2)]; Atc = [P.sbuf(f"rwAt{i}", [128, 128]) for i in range(2)]
        Pc = [P.sbuf(f"rwP{i}", [128, 128]) for i in range(2)]
        Th = [P.sbuf(f"rwTh{i}", [128, 128]) for i in range(2)]; Mh = [P.sbuf(f"rwMh{i}", [128, 128]) for i in range(2)]
        G3h = [P.sbuf(f"rwG3h{i}", [128, 128]) for i in range(2)]; nG4h = [P.sbuf(f"rwG4h{i}", [128, 128]) for i in range(2)]
        sgT = P.sbuf("rwsgT", [128, 128])
        B = []
        for b in range(NB):
            d = {}
            d["rw"] = P.sbuf(f"rwrw{b}", [128, 2432])
            d["FT"] = P.sbuf(f"rwFT{b}", [128, 4, 4, 128])
            d["ST"] = P.sbuf(f"rwST{b}", [128, 4, 64])
            d["XT"] = P.sbuf(f"rwXT{b}", [128, 2, 64]); d["UT"] = P.sbuf(f"rwUT{b}", [128, 2, 64])
            d["y"] = P.sbuf(f"rwy{b}", [128, 512]); d["bd"] = P.sbuf(f"rwbd{b}", [128, 8])
            d["yf"] = P.sbuf(f"rwyf{b}", [128, 520]); d["sm"] = P.sbuf(f"rwsm{b}", [128, 3, 8])
            B.append(d)
        pw = P.psum("rwpw", [128, 264]); pz = P.psum("rwpz", [128, 512]); pcw = P.psum("rwpcw", [128, 512]); pft = P.psum("rwpft", [128, 4, 128])
        pG = [P.psum(f"rwpG{i}", [128, 128]) for i in range(2)]; pxy = P.psum("rwpxy", [128, 3, 2, 64]); pS = P.psum("rwpS", [128, 128])
        gcnt = {"g": 0, "e": 0}

        def gmm(lhsT, rhs, reads):
            pg_ = pG[gcnt["g"] % 2]; gcnt["g"] += 1
            P.op("pe", lambda e, pg_=pg_, lhsT=lhsT, rhs=rhs: e.matmul(pg_[:], lhsT=lhsT, rhs=rhs, start=True, stop=True), reads=reads, writes=[pg_])
            return pg_

        def evac_copy(dst, src):
            gcnt["e"] += 1
            if gcnt["e"] % 2:
                P.op("act", lambda e, dst=dst, src=src: e.activation(out=dst[:], in_=src[:], func=AF.Copy), reads=[src], writes=[dst])
            else:
                P.op("dve", lambda e, dst=dst, src=src: e.tensor_copy(out=dst[:], in_=src[:]), reads=[src], writes=[dst])

        for dr in range(2):
            for b in range(NB):
                P.op("dve", lambda e, b=b: e.memset(B[b]["ST"][:], 0.0), writes=[B[b]["ST"]])
            order = list(range(NTS)) if dr == 0 else [1, 0] + list(range(NTS - 1, 1, -1))
            m_incl = masks[:, 2 + dr, :]; m_strict = masks[:, 6 + dr, :]; m_strictT = masks[:, 7 - dr, :]
            chunks = (0, 1) if dr == 0 else (1, 0)
            for step, tt in enumerate(order):
                want_out = need_ctx or tt >= nct
                for b in range(NB):
                    d = B[b]
                    t = b * NTS + tt
                    rw, FT, ST, XT, UT, y, bd, sm = d["rw"], d["FT"], d["ST"], d["XT"], d["UT"], d["y"], d["bd"], d["sm"]
                    P.dma("sp", rw[:], RW[t * 128:(t + 1) * 128, :], reads=[("RW", t)], writes=[rw])
                    wc = 1536 + 64 * dr; ac = 1664 + 64 * dr
                    P.op("pe", lambda e, rw=rw, wc=wc: e.transpose(out=pw[0:64, 0:128], in_=rw[:, wc:wc + 64], identity=ident[:]), reads=[rw, ident], writes=[pw])
                    P.op("pe", lambda e, rw=rw, ac=ac: e.transpose(out=pw[0:64, 128:256], in_=rw[:, ac:ac + 64], identity=ident[:]), reads=[rw, ident], writes=[pw])
                    P.op("act", lambda e: e.activation(out=twT[:], in_=pw[0:64, 0:128], func=AF.Tanh), reads=[pw], writes=[twT])
                    P.op("dve", lambda e: e.tensor_copy(out=adT[:], in_=pw[0:64, 128:256]), reads=[pw], writes=[adT])
                    P.op("pe", lambda e, dr=dr: e.matmul(pz[:], lhsT=twT[:], rhs=w2s[:, dr, :], start=True, stop=False), reads=[twT, w2s], writes=[pz])
                    P.op("pe", lambda e, dr=dr: e.matmul(pz[:], lhsT=ones[0:1, :], rhs=w0r[0:1, dr, :], start=False, stop=True), reads=[ones, w0r], writes=[pz])
                    P.op("act", lambda e: e.activation(out=T["sg"][:], in_=pz[:], func=AF.Sigmoid), reads=[pz], writes=[T["sg"]])
                    P.op("pe", lambda e, dr=dr: e.matmul(pz[:], lhsT=adT[:], rhs=a2s[:, dr, :], start=True, stop=False), reads=[adT, a2s], writes=[pz])
                    P.op("pe", lambda e, dr=dr: e.matmul(pz[:], lhsT=ones[0:1, :], rhs=a0r[0:1, dr, :], start=False, stop=True), reads=[ones, a0r], writes=[pz])
                    P.op("act", lambda e: e.activation(out=T["a_"][:], in_=pz[:], func=AF.Sigmoid), reads=[pz], writes=[T["a_"]])
                    P.op("pe", lambda e, m_incl=m_incl: e.matmul(pcw[:], lhsT=m_incl, rhs=T["sg"][:], start=True, stop=True), reads=[masks, T["sg"]], writes=[pcw])
                    P.op("act", lambda e: e.activation(out=T["eI"][:], in_=pcw[:], func=AF.Exp, scale=-E05), reads=[pcw], writes=[T["eI"]])
                    P.op("act", lambda e: e.activation(out=T["eInv"][:], in_=pcw[:], func=AF.Exp, scale=E05), reads=[pcw], writes=[T["eInv"]])
                    P.op("dve", lambda e: e.tensor_tensor(out=T["tmp"][:], in0=pcw[:], in1=T["sg"][:], op=ALU.subtract), reads=[pcw, T["sg"]], writes=[T["tmp"]])
                    P.op("act", lambda e: e.activation(out=T["eE"][:], in_=T["tmp"][:], func=AF.Exp, scale=-E05), reads=[T["tmp"]], writes=[T["eE"]])
                    P.op("pe", lambda e, m_strictT=m_strictT: e.matmul(pcw[:], lhsT=m_strictT, rhs=T["sg"][:], start=True, stop=True), reads=[masks, T["sg"]], writes=[pcw])
                    P.op("act", lambda e: e.activation(out=T["eR"][:], in_=pcw[:], func=AF.Exp, scale=-E05), reads=[pcw], writes=[T["eR"]])
                    for p in range(4):
                        P.op("pe", lambda e, p=p: e.matmul(pw[:, 256 + 2 * p:258 + 2 * p], lhsT=T["sg"][:, p * 128:(p + 1) * 128], rhs=masks[:, 8, 0:2], start=True, stop=True),
                             reads=[T["sg"], masks], writes=[pw])
                    P.op("act", lambda e: e.activation(out=WL[:].rearrange("p a c -> p (a c)"), in_=pw[:, 256:264], func=AF.Exp, scale=-E05), reads=[pw], writes=[WL])
                    r_, k_, v_, kk_ = rw[:, 0:512], rw[:, 512:1024], rw[:, 1024:1536], rw[:, 1920:2432]
                    P.op("dve", lambda e: e.tensor_tensor(out=T["tt_"][:], in0=T["a_"][:], in1=kv1[:], op=ALU.mult), reads=[T["a_"], kv1], writes=[T["tt_"]])
                    P.op("dve", lambda e: e.tensor_tensor(out=T["tt_"][:], in0=T["tt_"][:], in1=omk1[:], op=ALU.add), reads=[T["tt_"], omk1], writes=[T["tt_"]])
                    P.op("dve", lambda e, k_=k_: e.tensor_tensor(out=T["kd"][:], in0=k_, in1=T["tt_"][:], op=ALU.mult), reads=[rw, T["tt_"]], writes=[T["kd"]])
                    P.op("dve", lambda e, kk_=kk_: e.tensor_tensor(out=T["beta"][:], in0=T["a_"][:], in1=kk_, op=ALU.mult), reads=[rw, T["a_"]], writes=[T["beta"]])
                    P.op("dve", lambda e, r_=r_: e.tensor_tensor(out=T["Rt"][:], in0=r_, in1=T["eI"][:], op=ALU.mult), reads=[rw, T["eI"]], writes=[T["Rt"]])
                    P.op("dve", lambda e: e.tensor_tensor(out=T["Kt"][:], in0=T["kd"][:], in1=T["eInv"][:], op=ALU.mult), reads=[T["kd"], T["eInv"]], writes=[T["Kt"]])
                    P.op("dve", lambda e: e.tensor_tensor(out=T["Bt"][:], in0=T["beta"][:], in1=T["eInv"][:], op=ALU.mult), reads=[T["beta"], T["eInv"]], writes=[T["Bt"]])
                    P.op("dve", lambda e, kk_=kk_: e.tensor_tensor(out=T["At"][:], in0=kk_, in1=T["eE"][:], op=ALU.mult), reads=[rw, T["eE"]], writes=[T["At"]])
                    P.op("dve", lambda e: e.tensor_tensor(out=T["Kh"][:], in0=T["kd"][:], in1=T["eR"][:], op=ALU.mult), reads=[T["kd"], T["eR"]], writes=[T["Kh"]])
                    P.op("dve", lambda e: e.scalar_tensor_tensor(out=T["nBh"][:], in0=T["beta"][:], scalar=-1.0, in1=T["eR"][:], op0=ALU.mult, op1=ALU.mult), reads=[T["beta"], T["eR"]], writes=[T["nBh"]])
                    P.op("dve", lambda e, r_=r_: e.tensor_tensor(out=T["tmp"][:], in0=r_, in1=T["kd"][:], op=ALU.mult), reads=[rw, T["kd"]], writes=[T["tmp"]])
                    P.op("dve", lambda e: e.tensor_tensor(out=T["tmp"][:], in0=T["tmp"][:], in1=kv2[:], op=ALU.mult), reads=[T["tmp"], kv2], writes=[T["tmp"]])
                    P.op("dve", lambda e, bd=bd: e.tensor_reduce(out=bd[:], in_=T["tmp"][:].rearrange("p (h d) -> p h d", h=8), axis=AX.X, op=ALU.add), reads=[T["tmp"]], writes=[bd])
                    for ki, kn in enumerate(("Kt", "Bt", "At", "Rt")):
                        src = T[kn]
                        for p in range(4):
                            P.op("pe", lambda e, src=src, p=p: e.transpose(out=pft[:, p, :], in_=src[:, p * 128:(p + 1) * 128], identity=ident[:]), reads=[src, ident], writes=[pft])
                        if ki % 2:
                            P.op("act", lambda e, FT=FT, ki=ki: e.activation(out=FT[:, ki, :, :], in_=pft[:], func=AF.Copy), reads=[pft], writes=[FT])
                        else:
                            P.op("dve", lambda e, FT=FT, ki=ki: e.tensor_copy(out=FT[:, ki, :, :], in_=pft[:]), reads=[pft], writes=[FT])
                    KI, BI, AI, RI = 0, 1, 2, 3
                    for p in range(4 if "rwB1" not in dbg else 0):
                        for hi in range(2):
                            ho = hi * 64
                            fK, fB, fA, fR = (FT[ho:ho + 64, i, p, :] for i in (KI, BI, AI, RI))
                            A0, At0, P0 = Acur[0], Atc[0], Pc[0]
                            g = gmm(fB, fA, [FT])
                            P.op("dve", lambda e, g=g, A0=A0, m_strict=m_strict: e.tensor_tensor(out=A0[:], in0=g[:], in1=m_strict, op=ALU.mult), reads=[g, masks], writes=[A0])
                            P.op("dve", lambda e, A0=A0, P0=P0: e.tensor_tensor(out=P0[:], in0=ident[:], in1=A0[:], op=ALU.subtract), reads=[ident, A0], writes=[P0])
                            g = gmm(fA, fB, [FT])
                            P.op("dve", lambda e, g=g, At0=At0, m_strictT=m_strictT: e.tensor_tensor(out=At0[:], in0=g[:], in1=m_strictT, op=ALU.mult), reads=[g, masks], writes=[At0])
                            g = gmm(fK, fA, [FT])
                            P.op("dve", lambda e, g=g, hi=hi, m_strict=m_strict: e.tensor_tensor(out=Mh[hi][:], in0=g[:], in1=m_strict, op=ALU.mult), reads=[g, masks], writes=[Mh[hi]])
                            if want_out:
                                g = gmm(fK, fR, [FT])
                                P.op("dve", lambda e, g=g, hi=hi, m_incl=m_incl: e.tensor_tensor(out=G3h[hi][:], in0=g[:], in1=m_incl, op=ALU.mult), reads=[g, masks], writes=[G3h[hi]])
                                g = gmm(fB, fR, [FT])
                                P.op("dve", lambda e, g=g, hi=hi, m_incl=m_incl: e.scalar_tensor_tensor(out=nG4h[hi][:], in0=g[:], scalar=-1.0, in1=m_incl, op0=ALU.mult, op1=ALU.mult),
                                     reads=[g, masks], writes=[nG4h[hi]])
                            ia = 0
                            for kq in range(1, 6):
                                Ap, Atp, Pp = Acur[ia], Atc[ia], Pc[ia]
                                An, Atn = Acur[1 - ia], Atc[1 - ia]
                                Pn = Pc[1 - ia] if kq < 5 else Th[hi]
                                g = gmm(Ap[:], Atp[:], [Ap, Atp])
                                evac_copy(Atn, g)
                                if kq < 5:
                                    g = gmm(Atp[:], Ap[:], [Ap, Atp])
                                    evac_copy(An, g)
                                g = gmm(Atn[:], Pp[:], [Atn, Pp])
                                P.op("dve", lambda e, g=g, Pn=Pn, Pp=Pp: e.tensor_tensor(out=Pn[:], in0=g[:], in1=Pp[:], op=ALU.add), reads=[g, Pp], writes=[Pn])
                                ia = 1 - ia
                        for c in (chunks if "rwB2" not in dbg else ()):
                            rc0 = 64 * c
                            for hi in range(2):
                                ho = hi * 64; h = 2 * p + hi
                                P.op("pe", lambda e, FT=FT, ST=ST, ho=ho, p=p, hi=hi: e.matmul(pxy[:, 0, hi, :], lhsT=FT[ho:ho + 64, AI, p, :], rhs=ST[ho:ho + 64, p, :], start=True, stop=False),
                                     reads=[FT, ST], writes=[pxy], rg=ho)
                                P.op("pe", lambda e, rw=rw, hi=hi, h=h, rc0=rc0: e.matmul(pxy[:, 0, hi, :], lhsT=Mh[hi][rc0:rc0 + 64, :], rhs=rw[rc0:rc0 + 64, 1024 + h * 64:1024 + (h + 1) * 64], start=False, stop=True),
                                     reads=[Mh[hi], rw], writes=[pxy], rg=rc0)
                            P.op("act", lambda e, XT=XT, rc0=rc0: e.activation(out=XT[rc0:rc0 + 64], in_=pxy[rc0:rc0 + 64, 0], func=AF.Copy), reads=[pxy], writes=[XT])
                            for hi in range(2):
                                P.op("pe", lambda e, XT=XT, hi=hi, rc0=rc0: e.matmul(pxy[:, 1, hi, :], lhsT=Th[hi][rc0:rc0 + 64, :], rhs=XT[rc0:rc0 + 64, hi, :], start=True, stop=True),
                                     reads=[Th[hi], XT], writes=[pxy], rg=rc0)
                            P.op("dve", lambda e, UT=UT, rc0=rc0: e.tensor_copy(out=UT[rc0:rc0 + 64], in_=pxy[rc0:rc0 + 64, 1]), reads=[pxy], writes=[UT])
                            if want_out:
                                for hi in range(2):
                                    ho = hi * 64; h = 2 * p + hi
                                    P.op("pe", lambda e, FT=FT, ST=ST, ho=ho, p=p, hi=hi: e.matmul(pxy[:, 2, hi, :], lhsT=FT[ho:ho + 64, RI, p, :], rhs=ST[ho:ho + 64, p, :], start=True, stop=False),
                                         reads=[FT, ST], writes=[pxy], rg=ho)
                                    P.op("pe", lambda e, rw=rw, hi=hi, h=h, rc0=rc0: e.matmul(pxy[:, 2, hi, :], lhsT=G3h[hi][rc0:rc0 + 64, :], rhs=rw[rc0:rc0 + 64, 1024 + h * 64:1024 + (h + 1) * 64], start=False, stop=False),
                                         reads=[G3h[hi], rw], writes=[pxy], rg=rc0)
                                    P.op("pe", lambda e, UT=UT, hi=hi, rc0=rc0: e.matmul(pxy[:, 2, hi, :], lhsT=nG4h[hi][rc0:rc0 + 64, :], rhs=UT[rc0:rc0 + 64, hi, :], start=False, stop=True),
                                         reads=[nG4h[hi], UT], writes=[pxy], rg=rc0)
                                P.op("act", lambda e, y=y, rc0=rc0, p=p: e.activation(out=y[rc0:rc0 + 64, p * 128:(p + 1) * 128], in_=pxy[rc0:rc0 + 64, 2].rearrange("p a d -> p (a d)"), func=AF.Copy),
                                     reads=[pxy], writes=[y])
                            P.op("pe", lambda e, rw=rw, p=p, rc0=rc0: e.matmul(pS[:], lhsT=T["Kh"][rc0:rc0 + 64, p * 128:(p + 1) * 128], rhs=rw[rc0:rc0 + 64, 1024 + p * 128:1024 + (p + 1) * 128], start=True, stop=False),
                                 reads=[T["Kh"], rw], writes=[pS])
                            P.op("pe", lambda e, UT=UT, p=p, rc0=rc0: e.matmul(pS[:], lhsT=T["nBh"][rc0:rc0 + 64, p * 128:(p + 1) * 128], rhs=UT[rc0:rc0 + 64].rearrange("p a d -> p (a d)"), start=False, stop=True),
                                 reads=[T["nBh"], UT], writes=[pS])
                            for hi in range(2):
                                ho = hi * 64
                                P.op("dve", lambda e, ST=ST, ho=ho, p=p, c=c, hi=hi: e.scalar_tensor_tensor(out=ST[ho:ho + 64, p, :], in0=ST[ho:ho + 64, p, :], scalar=WL[ho:ho + 64, p, c:c + 1],
                                                                                                     in1=pS[ho:ho + 64, hi * 64:(hi + 1) * 64], op0=ALU.mult, op1=ALU.add), reads=[ST, WL, pS], writes=[ST])
                    if not want_out:
                        continue
                    if dr == 0:
                        P.dma("act", YS[t * 128:(t + 1) * 128, 0:512], y[:], reads=[y], writes=[("YS", t)])
                        P.dma("act", YS[t * 128:(t + 1) * 128, 512:520], bd[:], reads=[bd], writes=[("YSb", t)])
                    else:
                        yf = d["yf"]
                        P.dma("sp", yf[:], YS[t * 128:(t + 1) * 128, :], reads=[("YS", t), ("YSb", t)], writes=[yf])
                        y3 = y[:].rearrange("p (h d) -> p h d", h=8)
                        P.op("dve", lambda e, y=y, yf=yf: e.tensor_tensor(out=y[:], in0=y[:], in1=yf[:, 0:512], op=ALU.add), reads=[y, yf], writes=[y])
                        P.op("dve", lambda e, bd=bd, yf=yf: e.tensor_tensor(out=bd[:], in0=bd[:], in1=yf[:, 512:520], op=ALU.add), reads=[bd, yf], writes=[bd])
                        P.op("dve", lambda e, sm=sm, y3=y3: e.tensor_reduce(out=sm[:, 0, :], in_=y3, axis=AX.X, op=ALU.add), reads=[y], writes=[sm])
                        P.op("dve", lambda e, sm=sm: e.tensor_scalar_mul(out=sm[:, 0, :], in0=sm[:, 0, :], scalar1=-1.0 / 64), reads=[sm], writes=[sm])
                        P.op("dve", lambda e, sm=sm, y3=y3: e.tensor_tensor(out=y3, in0=y3, in1=sm[:, 0, :].unsqueeze(2).to_broadcast([128, 8, 64]), op=ALU.add), reads=[y, sm], writes=[y])
                        P.op("act", lambda e, y=y: e.activation(out=T["tmp"][:], in_=y[:], func=AF.Square), reads=[y], writes=[T["tmp"]])
                        P.op("dve", lambda e, sm=sm: e.tensor_reduce(out=sm[:, 1, :], in_=T["tmp"][:].rearrange("p (h d) -> p h d", h=8), axis=AX.X, op=ALU.add), reads=[T["tmp"]], writes=[sm])
                        P.op("act", lambda e, sm=sm: e.activation(out=sm[:, 1, :], in_=sm[:, 1, :], func=AF.Sqrt, scale=1.0 / 64, bias=lneps[:, 0:1]), reads=[sm, lneps], writes=[sm])
                        P.op("dve", lambda e, sm=sm: e.reciprocal(out=sm[:, 1, :], in_=sm[:, 1, :]), reads=[sm], writes=[sm])
                        P.op("dve", lambda e, sm=sm, y3=y3: e.tensor_tensor(out=y3, in0=y3, in1=sm[:, 1, :].unsqueeze(2).to_broadcast([128, 8, 64]), op=ALU.mult), reads=[y, sm], writes=[y])
                        P.op("dve", lambda e, y=y: e.tensor_tensor(out=y[:], in0=y[:], in1=ln0[:], op=ALU.mult), reads=[y, ln0], writes=[y])
                        P.op("dve", lambda e, y=y: e.tensor_tensor(out=y[:], in0=y[:], in1=ln1[:], op=ALU.add), reads=[y, ln1], writes=[y])
                        P.op("dve", lambda e, rw=rw, bd=bd: e.tensor_tensor(out=T["tmp"][:].rearrange("p (h d) -> p h d", h=8), in0=rw[:, 1024:1536].rearrange("p (h d) -> p h d", h=8),
                                                                      in1=bd[:].unsqueeze(2).to_broadcast([128, 8, 64]), op=ALU.mult), reads=[rw, bd], writes=[T["tmp"]])
                        P.op("dve", lambda e, y=y: e.tensor_tensor(out=y[:], in0=y[:], in1=T["tmp"][:], op=ALU.add), reads=[y, T["tmp"]], writes=[y])
                        P.op("pe", lambda e, rw=rw: e.transpose(out=pft[:, 0, :], in_=rw[:, 1792:1920], identity=ident[:]), reads=[rw, ident], writes=[pft])
                        P.op("act", lambda e: e.activation(out=sgT[:], in_=pft[:, 0, :], func=AF.Sigmoid), reads=[pft], writes=[sgT])
                        P.op("pe", lambda e: e.matmul(pz[:], lhsT=sgT[:], rhs=g2s[:], start=True, stop=True), reads=[sgT, g2s], writes=[pz])
                        P.op("dve", lambda e, y=y: e.tensor_tensor(out=y[:], in0=y[:], in1=pz[:], op=ALU.mult), reads=[y, pz], writes=[y])
                        P.dma("act", O[t * 128:(t + 1) * 128, 0:512], y[:], reads=[y], writes=[("O", t, 0)])
        P.pop()

    for l in range(2):
        nin = NIN[l]
        w_in = io["w_in_even"] if l == 0 else io["w_in_odd"]
        Xl = io["xin"] if l == 0 else X1
        if "U_in" not in dbg:
            P.push()
            MT = 4
            xts = [P.sbuf(f"p1x{i}", [128, D]) for i in range(3)]
            junk = P.sbuf("p1junk", [128, D])
            ss = [P.sbuf(f"p1ss{i}", [128, 1]) for i in range(3)]
            rstd = [P.sbuf(f"p1rs{i}", [128, 1]) for i in range(3)]
            xn = [P.sbuf(f"p1xn{i}", [128, D]) for i in range(2)]
            hxT = [P.sbuf(f"p1hxT{i}", [128, 8, MT * 128]) for i in range(2)]
            ptr = [P.psum(f"p1ptr{i}", [128, 4, 128]) for i in range(2)]
            wstg = [P.sbuf(f"p1ws{i}", [128, 8, 512]) for i in range(2)]
            wr = [P.sbuf(f"p1wr{i}", [128, 8, 512]) for i in range(2)]
            pu = [P.psum(f"p1pu{i}", [128, 512]) for i in range(4)]
            ut = [P.sbuf(f"p1ut{i}", [128, 512]) for i in range(4)]
            nblk = (nin + 511) // 512
            nmac = (cfg.ntile + MT - 1) // MT
            cnt = {"ti": 0, "ei": 0}

            def p1_norm(m):
                tiles = list(range(m * MT, min((m + 1) * MT, cfg.ntile)))
                hx = hxT[m % 2]
                for jj, t in enumerate(tiles):
                    cls = cfg.cls(t)
                    ti = cnt["ti"]; cnt["ti"] += 1
                    xt = xts[ti % 3]; s_ = ss[ti % 3]; r_ = rstd[ti % 3]; xn_ = xn[ti % 2]
                    P.dma("sp", xt[:], Xl[t * 128:(t + 1) * 128, :], reads=[("X", l, t)], writes=[xt])
                    P.op("act", lambda e, xt=xt, s_=s_: e.activation(out=junk[:], in_=xt[:], func=AF.Square, accum_out=s_[:]),
                         reads=[xt], writes=[junk, s_])
                    P.op("act", lambda e, s_=s_, r_=r_: e.activation(out=r_[:], in_=s_[:], func=AF.Sqrt, scale=1.0 / D, bias=epsc[:, 0:1]),
                         reads=[s_, epsc], writes=[r_])
                    P.op("dve", lambda e, r_=r_: e.reciprocal(out=r_[:], in_=r_[:]), reads=[r_], writes=[r_])
                    P.op("dve", lambda e, xt=xt, r_=r_, xn_=xn_: e.tensor_scalar(out=xn_[:], in0=xt[:], scalar1=r_[:, 0:1], scalar2=None, op0=ALU.mult),
                         reads=[xt, r_], writes=[xn_])
                    for half in range(2):
                        pt_ = ptr[half]
                        for kk in range(4):
                            k = half * 4 + kk
                            P.op("pe", lambda e, pt_=pt_, kk=kk, k=k, xn_=xn_: e.transpose(out=pt_[:, kk, :], in_=xn_[:, k * 128:(k + 1) * 128], identity=ident[:]),
                                 reads=[xn_, ident], writes=[pt_])
                        for kk in range(4):
                            k = half * 4 + kk
                            P.op("act", lambda e, pt_=pt_, kk=kk, k=k, hx=hx, jj=jj, cls=cls, ab=AB[l]: e.activation(
                                out=hx[:, k, jj * 128:(jj + 1) * 128].bitcast(F32R), in_=pt_[:, kk, :], func=AF.Identity,
                                scale=ab[:, 0, k, cls:cls + 1], bias=ab[:, 1, k, cls:cls + 1]),
                                reads=[pt_, AB[l]], writes=[hx])

            blocks = [(m, nb) for m in range(nmac) for nb in range(nblk)]

            def p1_loadw(i):
                m, nb = blocks[i]
                n0 = nb * 512
                nw = min(512, nin - n0)
                ws_, wr_ = wstg[i % 2], wr[i % 2]
                P.dma("sp", ws_[:, :, 0:nw], w_in[:, n0:n0 + nw].rearrange("(k p) n -> p k n", p=128), writes=[ws_])
                P.op("dve", lambda e, ws_=ws_, wr_=wr_, nw=nw: e.tensor_copy(out=wr_[:, 0:4, 0:nw].bitcast(F32R), in_=ws_[:, 0:4, 0:nw]),
                     reads=[ws_], writes=[(wr_.name, "a")])
                P.op("act", lambda e, ws_=ws_, wr_=wr_, nw=nw: e.activation(out=wr_[:, 4:8, 0:nw].bitcast(F32R), in_=ws_[:, 4:8, 0:nw], func=AF.Copy),
                     reads=[ws_], writes=[(wr_.name, "b")])

            p1_norm(0)
            p1_loadw(0)
            for i, (m, nb) in enumerate(blocks):
                if nb == 0 and m + 1 < nmac:
                    p1_norm(m + 1)
                if i + 1 < len(blocks):
                    p1_loadw(i + 1)
                tiles = list(range(m * MT, min((m + 1) * MT, cfg.ntile)))
                hx = hxT[m % 2]
                n0 = nb * 512
                nw = min(512, nin - n0)
                wr_ = wr[i % 2]
                for jj, t in enumerate(tiles):
                    ei = cnt["ei"]; cnt["ei"] += 1
                    ps = pu[ei % 4]; u_ = ut[ei % 4]
                    for k in range(8):
                        P.op("pe", lambda e, ps=ps, k=k, hx=hx, jj=jj, wr_=wr_, nw=nw: e.matmul(
                            ps[:, 0:nw], lhsT=hx[:, k, jj * 128:(jj + 1) * 128].bitcast(F32R), rhs=wr_[:, k, 0:nw].bitcast(F32R),
                            start=(k == 0), stop=(k == 7)), reads=[hx, (wr_.name, 'a' if k < 4 else 'b')], writes=[ps])
                    if ei % 2:
                        P.op("act", lambda e, ps=ps, u_=u_, nw=nw: e.activation(out=u_[:, 0:nw], in_=ps[:, 0:nw], func=AF.Copy), reads=[ps], writes=[u_])
                    else:
                        P.op("dve", lambda e, ps=ps, u_=u_, nw=nw: e.tensor_copy(out=u_[:, 0:nw], in_=ps[:, 0:nw]), reads=[ps], writes=[u_])
                    P.dma("act", U[t * 128:(t + 1) * 128, n0:n0 + nw], u_[:, 0:nw], reads=[u_], writes=[("U", t, nb)])
            P.pop()
        if stop_after == f"P1_{l}":
            break
        if stop_after is None:
            if l == 0:
                mlstm(l); diffattn(l)
            else:
                rwkv(l); gla(l)
            p3(l, Xl)
        elif stop_after == f"P3_{l}":
            p3(l, Xl); break
        elif l == 0 and stop_after in ("mlstm", "diff"):
            (mlstm if stop_after == "mlstm" else diffattn)(l); break
        elif l == 1 and stop_after in ("rwkv", "gla"):
            (rwkv if stop_after == "rwkv" else gla)(l); break

    P.emit()
    P.close()
    return nc

import numpy as np
TC = 256
def rope_table(TL):
    n = 8
    inv = (10000.0 ** (-np.arange(n, dtype=np.float32) / n)).astype(np.float32)
    row = np.repeat(np.arange(TL // 64, dtype=np.float32), 64)
    col = np.tile(np.arange(64, dtype=np.float32), TL // 64)
    ang = np.concatenate([row[:, None] * inv, col[:, None] * inv], axis=-1).astype(np.float32)
    return np.concatenate([np.cos(ang), np.sin(ang)], axis=-1).astype(np.float32)

def make_masks():
    m = np.zeros((9, 128, 128), np.float32)
    i = np.arange(128)
    S, T = i[:, None], i[None, :]
    blk = (S // 64 == T // 64)
    m[0] = (S <= T); m[1] = (S >= T)
    m[2] = (S <= T) * blk; m[3] = (S >= T) * blk
    m[4] = (S > T); m[5] = (S < T)
    m[6] = (S < T) * blk; m[7] = (S > T) * blk
    m[8, :64, 0] = 1.0; m[8, 64:, 1] = 1.0
    return m

def core_inputs(inp, core, NB, TL):
    b0 = core * NB
    xs = []
    for b in range(b0, b0 + NB):
        xs.append(inp["ctx"][b]); xs.append(inp["x"][b][:TL])
    m = {}
    m["xin"] = np.ascontiguousarray(np.concatenate(xs, axis=0))
    m["cvec"] = np.ascontiguousarray(np.concatenate([inp["c"][b0:b0 + NB], inp["c_ctx"][None, :]], axis=0))
    for k in ("ada_w", "ada_b", "norm_g", "w_out", "w_mlp_in", "w_mlp_out"):
        m[k] = inp[k]
    m["w_in_even"] = inp["w_in_even"][0]; m["w_in_odd"] = inp["w_in_odd"][0]
    m["mlstm_gate_b"] = inp["mlstm_gate_b"].reshape(1, 32); m["mlstm_norm_g"] = inp["mlstm_norm_g"].reshape(1, 512)
    m["diff_qk_g"] = inp["diff_qk_g"][0]; m["diff_lam"] = inp["diff_lam"].reshape(1, 128); m["diff_subln_g"] = inp["diff_subln_g"].reshape(1, 64)
    m["rwkv_mu"] = inp["rwkv_mu"][0]; m["rwkv_w0"] = inp["rwkv_w0"][0]; m["rwkv_w2"] = inp["rwkv_w2"][0]
    m["rwkv_a0"] = inp["rwkv_a0"][0]; m["rwkv_a2"] = inp["rwkv_a2"][0]; m["rwkv_g2"] = inp["rwkv_g2"][0]
    m["rwkv_kvec"] = inp["rwkv_kvec"][0]; m["rwkv_ln"] = inp["rwkv_ln"][0]
    m["gla_gate_w2"] = inp["gla_gate_w2"][0]; m["gla_gate_b"] = inp["gla_gate_b"][0]; m["gla_norm_g"] = inp["gla_norm_g"].reshape(1, 128)
    m["ident"] = np.eye(128, dtype=np.float32)
    m["rope"] = rope_table(TL)
    m["masks"] = make_masks()
    return {k: np.ascontiguousarray(np.asarray(v, dtype=np.float32)) for k, v in m.items()}


def kernel(**inputs):
    from concourse.bass_utils import run_bass_kernel_spmd
    inp = {k: np.asarray(v) for k, v in inputs.items()}
    NB, TL = 2, 4096
    cfg = Cfg(NB=NB, TL=TL)
    nc = build(cfg)
    in_maps = [core_inputs(inp, core, NB, TL) for core in range(8)]
    res = run_bass_kernel_spmd(nc, in_maps, core_ids=list(range(8)))
    outs = [np.asarray(r["out"]).reshape(NB, TL, D) for r in res.results]
    return np.concatenate(outs, axis=0).astype(np.float32)
```

```python
import contextlib
import numpy as np
import concourse.bass as bass
import concourse.mybir as mybir

F32 = mybir.dt.float32
F32R = mybir.dt.float32r
ALU = mybir.AluOpType
AF = mybir.ActivationFunctionType
AX = mybir.AxisListType

ENGS = ("pe", "act", "dve", "pool", "sp")
N_DMA_SEMS = 24
ATTACH_WAIT = True


class Op:
    __slots__ = ("eng", "fn", "deps", "signal", "count", "is_dma", "dsem", "dcount", "idx", "rg")


class Prog:
    def __init__(self, nc):
        self.nc = nc
        self.streams = {e: [] for e in ENGS}
        self.last_w = {}
        self.readers = {}
        self.stack = contextlib.ExitStack()
        self.n_dma = {}
        self.pstack = None
        self.bar = {}
        self.uid = 0
        self.recent_dma = {e: {} for e in ENGS}
        self.psum_names = set()

    def push(self):
        self.pstack = contextlib.ExitStack()

    def pop(self):
        self.barrier()
        self.pstack.close()
        self.pstack = None

    def barrier(self):
        lasts = []
        for e in ENGS:
            st = self.streams[e]
            for o in reversed(st):
                if not o.is_dma:
                    lasts.append(o)
                    break
            lasts.extend(self.recent_dma[e].values())
        for e in ENGS:
            self.bar[e] = list(lasts)

    def sbuf(self, name, shape, dtype=F32):
        self.uid += 1
        st = self.pstack if self.pstack is not None else self.stack
        return st.enter_context(self.nc.sbuf_tensor(f"{name}_{self.uid}", list(shape), dtype))

    def psum(self, name, shape, dtype=F32):
        self.uid += 1
        st = self.pstack if self.pstack is not None else self.stack
        self.psum_names.add(f"{name}_{self.uid}")
        return st.enter_context(self.nc.psum_tensor(f"{name}_{self.uid}", list(shape), dtype))

    def dram(self, name, shape, dtype=F32, kind="Internal"):
        return self.nc.dram_tensor(name, list(shape), dtype, kind=kind)

    def dma(self, eng, out, in_, reads=(), writes=(), **kw):
        return self.op(eng, lambda e: e.dma_start(out=out, in_=in_, **kw), reads, writes, is_dma=True)

    def op(self, eng, fn, reads=(), writes=(), is_dma=False, rg=None):
        o = Op()
        o.rg = rg
        o.eng = eng
        o.fn = fn
        o.signal = False
        o.count = 0
        o.is_dma = is_dma
        o.dsem = None
        o.dcount = 0
        reads = [r if isinstance(r, (str, tuple)) else r.name for r in reads]
        writes = [r if isinstance(r, (str, tuple)) else r.name for r in writes]
        deps = {}
        def add(d):
            if d is None:
                return
            if d.eng == "pe" and eng == "pe":
                return
            deps[id(d)] = d
        for r in reads:
            add(self.last_w.get(r))
            if r in self.psum_names:
                for rd in self.readers.get(r, ()):
                    if rd.eng != eng:
                        add(rd)
        for w in writes:
            add(self.last_w.get(w))
            if eng == "pe" and rg is not None:
                lw = self.last_w.get(w)
                if lw is not None and lw.eng == "pe" and lw.rg is not None and lw.rg != rg:
                    deps[id(lw)] = lw
            for rd in self.readers.get(w, ()):
                if rd is not o:
                    add(rd)
        if eng in self.bar:
            for d in self.bar.pop(eng):
                if d is not None and not (d.eng == eng and not d.is_dma and eng != "pe" and False):
                    deps[id(d)] = d
        o.deps = list(deps.values())
        for r in reads:
            self.readers.setdefault(r, []).append(o)
        for w in writes:
            self.last_w[w] = o
            self.readers[w] = []
        o.idx = len(self.streams[eng])
        self.streams[eng].append(o)
        if o.is_dma:
            k = self.n_dma.get(eng, 0)
            o.dsem = k % N_DMA_SEMS
            self.n_dma[eng] = k + 1
            self.recent_dma[eng][o.dsem] = o
        return o

    def emit(self):
        nc = self.nc
        for e in ENGS:
            for o in self.streams[e]:
                for d in o.deps:
                    if not d.is_dma:
                        d.signal = True
        for e in ENGS:
            c = 0
            for o in self.streams[e]:
                if o.signal:
                    c += 1
                    o.count = c
        st = self.stack
        esem = {e: st.enter_context(nc.semaphore("s_" + e)) for e in ("pe", "act", "dve", "pool")}
        dsems = {}
        for q in ENGS:
            if not any(o.is_dma for o in self.streams[q]):
                continue
            dsems[q] = [st.enter_context(nc.semaphore(f"d_{q}_{i}")) for i in range(N_DMA_SEMS)]
            cnt = [0] * N_DMA_SEMS
            for o in self.streams[q]:
                if o.is_dma:
                    cnt[o.dsem] += 16
                    o.dcount = cnt[o.dsem]
        block = st.enter_context(nc.Block())
        streams = self.streams

        def run(engname, eng):
            waited = {}
            for o in streams[engname]:
                need = {}
                for d in o.deps:
                    if d.is_dma:
                        key = ("d", d.eng, d.dsem)
                        val = d.dcount
                    else:
                        key = ("e", d.eng)
                        val = d.count
                    if need.get(key, 0) < val:
                        need[key] = val
                if o.is_dma and o.dcount > 16:
                    key = ("d", o.eng, o.dsem)
                    if need.get(key, 0) < o.dcount - 16:
                        need[key] = o.dcount - 16
                pend = []
                for key, val in need.items():
                    if waited.get(key, 0) >= val:
                        continue
                    waited[key] = val
                    sem = dsems[key[1]][key[2]] if key[0] == "d" else esem[key[1]]
                    pend.append((sem, val))
                attach = None
                if pend and ATTACH_WAIT and not o.is_dma:
                    attach = pend.pop()
                for sem, val in pend:
                    eng.wait_ge(sem, val)
                inst = o.fn(eng)
                if attach is not None:
                    inst._wait_ge(attach[0], eng.lower_val(attach[1]))
                if o.is_dma:
                    inst.then_inc(dsems[o.eng][o.dsem], 16)
                elif o.signal:
                    inst.then_inc(esem[o.eng], 1)
            return waited

        def fin(engname, eng):
            w = run(engname, eng)
            cnt = {}
            for o in streams[engname]:
                if o.is_dma:
                    cnt[o.dsem] = o.dcount
            for k, v in cnt.items():
                if w.get(("d", engname, k), 0) < v:
                    eng.wait_ge(dsems[engname][k], v)

        @block.tensor
        def _(pe):
            fin("pe", pe)

        @block.scalar
        def _(act):
            fin("act", act)

        @block.vector
        def _(dve):
            fin("dve", dve)

        @block.gpsimd
        def _(pool):
            fin("pool", pool)

        @block.sync
        def _(sp):
            fin("sp", sp)

    def close(self):
        self.stack.close()

import math
import numpy as np

D = 1024
EPS = 1e-6
A_IN, B_IN = 2080, 1536
EVEN_IN = A_IN + B_IN
C_IN, D_IN = 1920, 1568
ODD_IN = C_IN + D_IN
NIN = (EVEN_IN, ODD_IN)
HID = 4096
TC = 256


class Cfg:
    def __init__(self, NB=2, TL=4096, debug=()):
        self.NB = NB
        self.TL = TL
        self.TS = TC + TL
        self.NT = NB * self.TS
        self.ntile = self.NT // 128
        self.debug = set(debug)

    def cls(self, tile):
        b, r = divmod(tile * 128, self.TS)
        return self.NB if r < TC else b


def declare_io(nc, cfg):
    io = {}
    def inp(name, shape):
        io[name] = nc.dram_tensor(name, list(shape), F32, kind="ExternalInput").ap()
    inp("xin", [cfg.NT, D])
    inp("cvec", [cfg.NB + 1, D])
    inp("ada_w", [2, D, 6 * D]); inp("ada_b", [2, 6 * D]); inp("norm_g", [2, 2, D])
    inp("w_in_even", [D, EVEN_IN]); inp("w_in_odd", [D, ODD_IN])
    inp("w_out", [2, D, D]); inp("w_mlp_in", [2, D, HID]); inp("w_mlp_out", [2, HID, D])
    inp("mlstm_gate_b", [1, 32]); inp("mlstm_norm_g", [1, 512])
    inp("diff_qk_g", [2, 32]); inp("diff_lam", [1, 128]); inp("diff_subln_g", [1, 64])
    inp("rwkv_mu", [2, C_IN]); inp("rwkv_w0", [2, 512]); inp("rwkv_w2", [2, 64, 512])
    inp("rwkv_a0", [2, 512]); inp("rwkv_a2", [2, 64, 512]); inp("rwkv_g2", [128, 512])
    inp("rwkv_kvec", [3, 512]); inp("rwkv_ln", [2, 512])
    inp("gla_gate_w2", [2, 16, 256]); inp("gla_gate_b", [2, 256]); inp("gla_norm_g", [1, 128])
    inp("ident", [128, 128]); inp("rope", [cfg.TL, 32])
    inp("masks", [9, 128, 128])
    io["out"] = nc.dram_tensor("out", [cfg.NB * cfg.TL, D], F32, kind="ExternalOutput").ap()
    return io


def build(cfg, stop_after=None):
    nc = bass.Bass("TRN2", target_bir_lowering=False)
    io = declare_io(nc, cfg)
    P = Prog(nc)
    dbg = cfg.debug

    def scratch(name, shape):
        kind = "ExternalOutput" if name in dbg else "Internal"
        return nc.dram_tensor(name, list(shape), F32, kind=kind).ap()

    X1 = scratch("X1", [cfg.NT, D])
    if "U_in" in dbg:
        U = nc.dram_tensor("U", [cfg.NT, EVEN_IN], F32, kind="ExternalInput").ap()
    else:
        U = scratch("U", [cfg.NT, EVEN_IN])
    YS = scratch("YS", [cfg.NT, 520])
    RW = scratch("RW", [cfg.NT, 2432])
    NTS = cfg.TS // 128
    QKT = scratch("QKT", [cfg.NB, 2, 8, 64, cfg.TS])
    masks = P.sbuf("masks", [128, 9, 128])
    P.dma("sp", masks[:], io["masks"].rearrange("m p n -> p m n"), writes=[masks])

    def ukeys(t):
        return [("U", t, nb) for nb in range(8)]

    def bcast_load(dst, src_row, n):
        P.dma("sp", dst, src_row.to_broadcast([128, n]), writes=[dst.tensor.name if hasattr(dst, "tensor") else dst])

    if "O_in" in dbg:
        O = nc.dram_tensor("O", [cfg.NT, D], F32, kind="ExternalInput").ap()
    else:
        O = scratch("O", [cfg.NT, D])
    MODS = scratch("MODS", [2, cfg.NB + 1, 6 * D])
    NC = cfg.NB + 1

    ident = P.sbuf("ident", [128, 128])
    ones = P.sbuf("ones", [128, 128])
    P.dma("sp", ident[:], io["ident"], writes=[ident])
    P.op("dve", lambda e: e.memset(ones[:], 1.0), writes=[ones])
    epsc = P.sbuf("epsc", [128, 1])
    P.op("dve", lambda e: e.memset(epsc[:], EPS), writes=[epsc])
    AB = [P.sbuf(f"AB{l}", [128, 4, 8, NC]) for l in range(2)]

    P.push()
    crow = P.sbuf("crow", [NC, D])
    srow = P.sbuf("srow", [NC, D])
    sT = P.sbuf("sT", [128, 8, NC])
    P.dma("sp", crow[:], io["cvec"], writes=[crow])
    P.op("act", lambda e: e.activation(out=srow[:], in_=crow[:], func=AF.Silu), reads=[crow], writes=[srow])
    pt = P.psum("p0t", [128, 8, NC])
    for k in range(8):
        P.op("pe", lambda e, k=k: e.transpose(out=pt[:, k, :], in_=srow[:, k * 128:(k + 1) * 128], identity=ident[0:NC, 0:NC]),
             reads=[srow, ident], writes=[pt])
    P.op("dve", lambda e: e.tensor_copy(out=sT[:], in_=pt[:]), reads=[pt], writes=[sT])
    modrow = P.sbuf("modrow", [NC, 6 * D])
    gT = P.sbuf("gT", [128, 2, 8])
    wst = [P.sbuf(f"p0w{i}", [128, 8, 512]) for i in range(2)]
    brow = P.sbuf("p0b", [1, 6 * D])
    pm = [P.psum(f"p0m{i}", [NC, 512]) for i in range(2)]
    pmt = P.psum("p0mt", [128, 4, 8, NC])
    pg = P.psum("p0g", [128, 2, 8])
    grow = P.sbuf("p0grow", [1, 2 * D])
    for l in range(2):
        P.dma("sp", brow[:], io["ada_b"][l:l + 1, :], writes=[brow])
        for j in range(12):
            w = wst[j % 2]
            P.dma("sp", w[:], io["ada_w"][l, :, j * 512:(j + 1) * 512].rearrange("(k p) n -> p k n", p=128), writes=[w])
            ps = pm[j % 2]
            for k in range(8):
                P.op("pe", lambda e, k=k, w=w, ps=ps: e.matmul(ps[:], lhsT=sT[:, k, :], rhs=w[:, k, :], start=(k == 0), stop=False),
                     reads=[sT, w], writes=[ps])
            P.op("pe", lambda e, ps=ps, j=j: e.matmul(ps[:], lhsT=ones[0:1, 0:NC], rhs=brow[0:1, j * 512:(j + 1) * 512], start=False, stop=True),
                 reads=[ones, brow], writes=[ps])
            P.op("act", lambda e, ps=ps, j=j: e.activation(out=modrow[:, j * 512:(j + 1) * 512], in_=ps[:], func=AF.Copy),
                 reads=[ps], writes=[modrow])
        P.dma("sp", MODS[l], modrow[:], reads=[modrow], writes=[("MODS", l)])
        for qi, q in enumerate((0, 1, 3, 4)):
            for k in range(8):
                P.op("pe", lambda e, qi=qi, q=q, k=k: e.transpose(out=pmt[:, qi, k, :], in_=modrow[:, q * D + k * 128: q * D + (k + 1) * 128],
                                                                 identity=ident[0:NC, 0:NC]),
                     reads=[modrow, ident], writes=[pmt])
        P.dma("sp", grow[:], io["norm_g"][l:l + 1].rearrange("o j d -> o (j d)"), writes=[grow])
        for j in range(2):
            for k in range(8):
                P.op("pe", lambda e, j=j, k=k: e.transpose(out=pg[:, j, k:k + 1], in_=grow[0:1, j * D + k * 128: j * D + (k + 1) * 128], identity=ident[0:1, 0:1]),
                     reads=[grow, ident], writes=[pg])
        P.op("dve", lambda e: e.tensor_copy(out=gT[:], in_=pg[:]), reads=[pg], writes=[gT])
        ab = AB[l]
        P.op("dve", lambda e, ab=ab: e.tensor_copy(out=ab[:, 1], in_=pmt[:, 0]), reads=[pmt], writes=[ab])
        P.op("dve", lambda e, ab=ab: e.tensor_copy(out=ab[:, 3], in_=pmt[:, 2]), reads=[pmt], writes=[ab])
        for c in range(NC):
            P.op("dve", lambda e, ab=ab, c=c: e.scalar_tensor_tensor(out=ab[:, 0, :, c], in0=pmt[:, 1, :, c], scalar=1.0, in1=gT[:, 0, :],
                                                                   op0=ALU.add, op1=ALU.mult), reads=[pmt, gT], writes=[ab])
            P.op("dve", lambda e, ab=ab, c=c: e.scalar_tensor_tensor(out=ab[:, 2, :, c], in0=pmt[:, 3, :, c], scalar=1.0, in1=gT[:, 1, :],
                                                                   op0=ALU.add, op1=ALU.mult), reads=[pmt, gT], writes=[ab])
    P.pop()
    if stop_after == "P0":
        P.emit(); P.close(); return nc

    def norm_mod_T(xt, hxT, j, l, which, cls, sq_junk, ss, rstd, xn, ptr):
        P.op("act", lambda e: e.activation(out=sq_junk[:], in_=xt, func=AF.Square, accum_out=ss[:]),
             reads=[xt.tensor if hasattr(xt, "tensor") else xt], writes=[sq_junk, ss])
        return

    def p3(l, Xl):
        last = (l == 1)
        P.push()
        MT = 2
        if last:
            tl = [t for t in range(cfg.ntile) if cfg.cls(t) != cfg.NB]
        else:
            tl = list(range(cfg.ntile))
        macs = [tl[i:i + MT] for i in range(0, len(tl), MT)]
        NTK = MT * 128
        gbc = P.sbuf("p3gbc", [128, NC, 2, D])
        for c in range(NC):
            for gi, q in enumerate((2, 5)):
                P.dma("sp", gbc[:, c, gi, :], MODS[l, c:c + 1, q * D:(q + 1) * D].to_broadcast([128, D]),
                      reads=[("MODS", l)], writes=[gbc])
        xts = [P.sbuf(f"p3x{i}", [128, D]) for i in range(2 * MT)]
        obuf = [P.sbuf(f"p3o{i}", [128, D]) for i in range(2)]
        OT = [P.sbuf(f"p3OT{i}", [128, 8, NTK]) for i in range(2)]
        hx2T = P.sbuf("p3hx2T", [128, 8, NTK])
        hT = P.sbuf("p3hT", [128, 32, NTK])
        wstg = [P.sbuf(f"p3ws{i}", [128, 8, 512]) for i in range(2)]
        wr = [P.sbuf(f"p3wr{i}", [128, 8, 512]) for i in range(2)]
        tmp = [P.sbuf(f"p3tmp{i}", [128, 512]) for i in range(2)]
        hrl = [P.sbuf(f"p3hrl{i}", [128, NTK]) for i in range(2)]
        ss = [P.sbuf(f"p3ss{i}", [128, 1]) for i in range(2)]
        rstd = [P.sbuf(f"p3rs{i}", [128, 1]) for i in range(2)]
        ptr = [P.psum(f"p3ptr{i}", [128, 4, 128]) for i in range(2)]
        pu = [P.psum(f"p3pu{i}", [128, 512]) for i in range(2)]
        acc = [P.psum(f"p3acc{i}", [128, 512]) for i in range(2 * MT)]
        cnt = {"o": 0, "pu": 0, "tmp": 0, "hr": 0, "n": 0}
        w_out = io["w_out"][l]; w1 = io["w_mlp_in"][l]; w2 = io["w_mlp_out"][l]
        wblocks = []
        for m in range(len(macs)):
            for nb in range(2):
                wblocks.append(w_out[:, nb * 512:(nb + 1) * 512].rearrange("(k p) n -> p k n", p=128))
            for nb in range(8):
                wblocks.append(w1[:, nb * 512:(nb + 1) * 512].rearrange("(k p) n -> p k n", p=128))
            for nb in range(2):
                for kp in range(4):
                    wblocks.append(w2[kp * 1024:(kp + 1) * 1024, nb * 512:(nb + 1) * 512].rearrange("(k p) n -> p k n", p=128))
        wi = {"i": 0}

        def loadw(i):
            if i >= len(wblocks):
                return
            ws_, wr_ = wstg[i % 2], wr[i % 2]
            P.dma("sp", ws_[:, 0:4, :], wblocks[i][:, 0:4, :], writes=[(ws_.name, "a")])
            P.dma("sp", ws_[:, 4:8, :], wblocks[i][:, 4:8, :], writes=[(ws_.name, "b")])
            P.op("dve", lambda e, ws_=ws_, wr_=wr_: e.tensor_copy(out=wr_[:, 0:4, :].bitcast(F32R), in_=ws_[:, 0:4, :]), reads=[(ws_.name, "a")], writes=[(wr_.name, "a")])
            P.op("act", lambda e, ws_=ws_, wr_=wr_: e.activation(out=wr_[:, 4:8, :].bitcast(F32R), in_=ws_[:, 4:8, :], func=AF.Copy), reads=[(ws_.name, "b")], writes=[(wr_.name, "b")])

        def nextw():
            i = wi["i"]; wi["i"] += 1
            loadw(i + 1)
            return wr[i % 2]

        def transpose_tile(src, dst, jj, scale_bias=None):
            for half in range(2):
                pt_ = ptr[half]
                for kk in range(4):
                    k = half * 4 + kk
                    P.op("pe", lambda e, pt_=pt_, kk=kk, k=k: e.transpose(out=pt_[:, kk, :], in_=src[:, k * 128:(k + 1) * 128], identity=ident[:]),
                         reads=[src, ident], writes=[pt_])
                if scale_bias is None:
                    dsl = dst[:, half * 4:(half + 1) * 4, jj * 128:(jj + 1) * 128]
                    if half == 0:
                        P.op("act", lambda e, pt_=pt_, dsl=dsl: e.activation(out=dsl.bitcast(F32R), in_=pt_[:], func=AF.Copy), reads=[pt_], writes=[dst])
                    else:
                        P.op("dve", lambda e, pt_=pt_, dsl=dsl: e.tensor_copy(out=dsl.bitcast(F32R), in_=pt_[:]), reads=[pt_], writes=[dst])
                else:
                    ab, ja, jb, cls = scale_bias
                    for kk in range(4):
                        k = half * 4 + kk
                        P.op("act", lambda e, pt_=pt_, kk=kk, k=k: e.activation(
                            out=dst[:, k, jj * 128:(jj + 1) * 128].bitcast(F32R), in_=pt_[:, kk, :], func=AF.Identity,
                            scale=ab[:, ja, k, cls:cls + 1], bias=ab[:, jb, k, cls:cls + 1]), reads=[pt_, ab], writes=[dst])

        def prep_O(m):
            ot = OT[m % 2]
            for jj, t in enumerate(macs[m]):
                ob = obuf[cnt["o"] % 2]; cnt["o"] += 1
                P.dma("sp", ob[:], O[t * 128:(t + 1) * 128, :], reads=[("O", t, 0), ("O", t, 1)], writes=[ob])
                transpose_tile(ob, ot, jj)

        loadw(0)
        prep_O(0)
        for m, tiles in enumerate(macs):
            ntok = len(tiles) * 128
            ot = OT[m % 2]
            xs = [xts[(m % 2) * MT + jj] for jj in range(len(tiles))]
            for jj, t in enumerate(tiles):
                P.dma("sp", xs[jj][:], Xl[t * 128:(t + 1) * 128, :], reads=[("X", l, t)], writes=[xs[jj]])
            for nb in range(2):
                wr_ = nextw()
                for jj, t in enumerate(tiles):
                    cls = cfg.cls(t)
                    ps = pu[cnt["pu"] % 2]; cnt["pu"] += 1
                    tm = tmp[cnt["tmp"] % 2]; cnt["tmp"] += 1
                    for k in range(8):
                        P.op("pe", lambda e, ps=ps, k=k, jj=jj, wr_=wr_, ot=ot: e.matmul(ps[:], lhsT=ot[:, k, jj * 128:(jj + 1) * 128].bitcast(F32R),
                                                                              rhs=wr_[:, k, :].bitcast(F32R), start=(k == 0), stop=(k == 7)),
                             reads=[ot, (wr_.name, 'a' if k < 4 else 'b')], writes=[ps])
                    P.op("dve", lambda e, ps=ps, tm=tm, cls=cls, nb=nb: e.tensor_tensor(out=tm[:], in0=ps[:], in1=gbc[:, cls, 0, nb * 512:(nb + 1) * 512], op=ALU.mult),
                         reads=[ps, gbc], writes=[tm])
                    xj = xs[jj]
                    P.op("dve", lambda e, tm=tm, xj=xj, nb=nb: e.tensor_tensor(out=xj[:, nb * 512:(nb + 1) * 512], in0=xj[:, nb * 512:(nb + 1) * 512], in1=tm[:], op=ALU.add),
                         reads=[tm, xj], writes=[xj])
            if m + 1 < len(macs):
                prep_O(m + 1)
            for jj, t in enumerate(tiles):
                cls = cfg.cls(t)
                xj = xs[jj]
                n = cnt["n"]; cnt["n"] += 1
                s_ = ss[n % 2]; r_ = rstd[n % 2]; xn_ = obuf[cnt["o"] % 2]; cnt["o"] += 1
                P.op("act", lambda e, xj=xj, s_=s_, xn_=xn_: e.activation(out=xn_[:], in_=xj[:], func=AF.Square, accum_out=s_[:]),
                     reads=[xj], writes=[xn_, s_])
                P.op("act", lambda e, s_=s_, r_=r_: e.activation(out=r_[:], in_=s_[:], func=AF.Sqrt, scale=1.0 / D, bias=epsc[:, 0:1]),
                     reads=[s_, epsc], writes=[r_])
                P.op("dve", lambda e, r_=r_: e.reciprocal(out=r_[:], in_=r_[:]), reads=[r_], writes=[r_])
                P.op("dve", lambda e, xj=xj, r_=r_, xn_=xn_: e.tensor_scalar(out=xn_[:], in0=xj[:], scalar1=r_[:, 0:1], scalar2=None, op0=ALU.mult),
                     reads=[xj, r_], writes=[xn_])
                transpose_tile(xn_, hx2T, jj, scale_bias=(AB[l], 2, 3, cls))
            for nb in range(8):
                wr_ = nextw()
                for sub in range(4):
                    nchunk = nb * 4 + sub
                    ps = pu[cnt["pu"] % 2]; cnt["pu"] += 1
                    hr = hrl[cnt["hr"] % 2]; cnt["hr"] += 1
                    for k in range(8):
                        P.op("pe", lambda e, ps=ps, k=k, sub=sub, wr_=wr_, ntok=ntok: e.matmul(ps[:, 0:ntok], lhsT=wr_[:, k, sub * 128:(sub + 1) * 128].bitcast(F32R),
                                                                               rhs=hx2T[:, k, 0:ntok].bitcast(F32R), start=(k == 0), stop=(k == 7)),
                             reads=[hx2T, (wr_.name, 'a' if k < 4 else 'b')], writes=[ps])
                    P.op("act", lambda e, ps=ps, hr=hr, ntok=ntok: e.activation(out=hr[:, 0:ntok], in_=ps[:, 0:ntok], func=AF.Relu), reads=[ps], writes=[hr])
                    P.op("dve", lambda e, hr=hr, nchunk=nchunk, ntok=ntok: e.tensor_tensor(out=hT[:, nchunk, 0:ntok].bitcast(F32R), in0=hr[:, 0:ntok], in1=hr[:, 0:ntok], op=ALU.mult),
                         reads=[hr], writes=[hT])
            for nb in range(2):
                for kp in range(4):
                    wr_ = nextw()
                    for jj, t in enumerate(tiles):
                        ac = acc[nb * MT + jj]
                        for k in range(8):
                            P.op("pe", lambda e, ac=ac, k=k, kp=kp, jj=jj, wr_=wr_: e.matmul(ac[:], lhsT=hT[:, kp * 8 + k, jj * 128:(jj + 1) * 128].bitcast(F32R),
                                                                                        rhs=wr_[:, k, :].bitcast(F32R), start=(kp == 0 and k == 0), stop=(kp == 3 and k == 7)),
                                 reads=[hT, (wr_.name, 'a' if k < 4 else 'b')], writes=[ac])
                for jj, t in enumerate(tiles):
                    cls = cfg.cls(t)
                    ac = acc[nb * MT + jj]
                    tm = tmp[cnt["tmp"] % 2]; cnt["tmp"] += 1
                    xj = xs[jj]
                    P.op("dve", lambda e, ac=ac, tm=tm, cls=cls, nb=nb: e.tensor_tensor(out=tm[:], in0=ac[:], in1=gbc[:, cls, 1, nb * 512:(nb + 1) * 512], op=ALU.mult),
                         reads=[ac, gbc], writes=[tm])
                    P.op("dve", lambda e, tm=tm, xj=xj, nb=nb: e.tensor_tensor(out=xj[:, nb * 512:(nb + 1) * 512], in0=xj[:, nb * 512:(nb + 1) * 512], in1=tm[:], op=ALU.add),
                         reads=[tm, xj], writes=[xj])
            for jj, t in enumerate(tiles):
                xj = xs[jj]
                if last:
                    b, r = divmod(t * 128, cfg.TS)
                    row = b * cfg.TL + (r - TC)
                    P.dma("act", io["out"][row:row + 128, :], xj[:], reads=[xj], writes=[("OUT", t)])
                else:
                    P.dma("act", X1[t * 128:(t + 1) * 128, :], xj[:], reads=[xj], writes=[("X", 1, t)])
        P.pop()

    LN8 = math.log(0.125)

    def mlstm(l):
        P.push()
        NB = cfg.NB
        gb_bc = P.sbuf("mlgb", [128, 32]); ng_bc = P.sbuf("mlng", [128, 512])
        P.dma("sp", gb_bc[:], io["mlstm_gate_b"].to_broadcast([128, 32]), writes=[gb_bc])
        P.dma("sp", ng_bc[:], io["mlstm_norm_g"].to_broadcast([128, 512]), writes=[ng_bc])
        ln8 = P.sbuf("mlln8", [128, 1]); onec = P.sbuf("mlone", [128, 1]); eps_c = P.sbuf("mleps", [128, 1])
        P.op("dve", lambda e: e.memset(ln8[:], LN8), writes=[ln8])
        P.op("dve", lambda e: e.memset(onec[:], 1.0), writes=[onec])
        P.op("dve", lambda e: e.memset(eps_c[:], EPS), writes=[eps_c])
        B = []
        for b in range(NB):
            d = {}
            d["ua"] = [P.sbuf(f"mlua{b}{i}", [128, A_IN]) for i in range(2)]
            d["g"] = P.sbuf(f"mlg{b}", [128, 32])
            d["nlf"] = P.sbuf(f"mlnlf{b}", [128, 8])
            d["eb"] = P.sbuf(f"mleb{b}", [128, 8])
            d["vs"] = P.sbuf(f"mlvs{b}", [128, 8])
            d["eL"] = P.sbuf(f"mleL{b}", [128, 8])
            d["Vt"] = P.sbuf(f"mlVt{b}", [128, 8, 65])
            d["QT"] = P.sbuf(f"mlQT{b}", [128, 4, 128])
            d["KT"] = P.sbuf(f"mlKT{b}", [128, 4, 128])
            d["AT"] = [P.sbuf(f"mlAT{b}{i}", [128, 128]) for i in range(2)]
            d["C"] = P.sbuf(f"mlC{b}", [128, 4, 65])
            d["Ct"] = P.sbuf(f"mlCt{b}", [128, 65])
            d["small"] = P.sbuf(f"mlsm{b}", [128, 6, 8])
            d["hd"] = P.sbuf(f"mlhd{b}", [128, 8, 64])
            d["hf"] = P.sbuf(f"mlhf{b}", [128, 512])
            d["sq"] = P.sbuf(f"mlsq{b}", [128, 512])
            d["sg"] = P.sbuf(f"mlsg{b}", [128, 512])
            d["pT"] = P.psum(f"mlpT{b}", [128, 8, 128]) if b == 0 else None
            d["pa"] = P.psum(f"mlpa{b}", [128, 128])
            d["pn"] = P.psum(f"mlpn{b}", [128, 8, 128]) if b == 0 else None
            B.append(d)
        pn = B[0]["pn"]
        pc = P.psum("mlpc", [128, 2, 65])
        psm = P.psum("mlpsm", [128, 16])
        for dr in range(2):
            mk = masks[:, dr, :]
            for b in range(NB):
                P.op("dve", lambda e, b=b: e.memset(B[b]["C"][:], 0.0), writes=[B[b]["C"]])
            if dr == 0:
                order = list(range(NTS))
            else:
                order = [1, 0] + list(range(NTS - 1, 1, -1))
            for step, tt in enumerate(order):
                for b in range(NB):
                    d = B[b]
                    t = b * NTS + tt
                    ua = d["ua"][step % 2]
                    pT = B[0]["pT"]
                    P.dma("sp", ua[:], U[t * 128:(t + 1) * 128, 0:A_IN], reads=ukeys(t), writes=[ua])
                    g, nlf, eb, vs, eL, Vt, QT, KT, C, sm = d["g"], d["nlf"], d["eb"], d["vs"], d["eL"], d["Vt"], d["QT"], d["KT"], d["C"], d["small"]
                    P.op("dve", lambda e, ua=ua, g=g: e.tensor_tensor(out=g[:], in0=ua[:, 2048:2080], in1=gb_bc[:], op=ALU.add), reads=[ua, gb_bc], writes=[g])
                    ig = g[:, 16 * dr: 16 * dr + 8]
                    fg = g[:, 16 * dr + 8: 16 * dr + 16]
                    P.op("act", lambda e, fg=fg, nlf=nlf: e.activation(out=nlf[:], in_=fg, func=AF.Exp, scale=-1.0), reads=[g], writes=[nlf])
                    P.op("act", lambda e, nlf=nlf: e.activation(out=nlf[:], in_=nlf[:], func=AF.Ln, bias=onec[:, 0:1]), reads=[nlf, onec], writes=[nlf])
                    P.op("pe", lambda e, nlf=nlf, mk=mk: e.matmul(psm[:, 0:8], lhsT=mk, rhs=nlf[:], start=True, stop=True), reads=[masks, nlf], writes=[psm])
                    P.op("pe", lambda e, nlf=nlf: e.matmul(psm[:, 8:16], lhsT=ones[:], rhs=nlf[:], start=True, stop=True), reads=[ones, nlf], writes=[psm])
                    P.op("act", lambda e, eb=eb: e.activation(out=eb[:], in_=psm[:, 0:8], func=AF.Exp, scale=-1.0), reads=[psm], writes=[eb])
                    P.op("act", lambda e, eL=eL: e.activation(out=eL[:], in_=psm[:, 8:16], func=AF.Exp, scale=-1.0), reads=[psm], writes=[eL])
                    P.op("dve", lambda e, vs=vs, ig=ig: e.tensor_tensor(out=vs[:], in0=psm[:, 0:8], in1=ig, op=ALU.add), reads=[psm, g], writes=[vs])
                    P.op("act", lambda e, vs=vs: e.activation(out=vs[:], in_=vs[:], func=AF.Exp, bias=ln8[:, 0:1]), reads=[vs, ln8], writes=[vs])
                    P.op("dve", lambda e, ua=ua, vs=vs, Vt=Vt: e.tensor_tensor(out=Vt[:, :, 0:64], in0=ua[:, 1024:1536].rearrange("p (h d) -> p h d", h=8),
                                                                      in1=vs[:].unsqueeze(2).to_broadcast([128, 8, 64]), op=ALU.mult), reads=[ua, vs], writes=[Vt])
                    P.op("pool", lambda e, vs=vs, Vt=Vt: e.tensor_copy(out=Vt[:, :, 64], in_=vs[:]), reads=[vs], writes=[Vt])
                    for i in range(8):
                        P.op("pe", lambda e, i=i, ua=ua, pT=pT: e.transpose(out=pT[:, i, :], in_=ua[:, i * 128:(i + 1) * 128], identity=ident[:]), reads=[ua, ident], writes=[pT])
                    P.op("act", lambda e, QT=QT, pT=pT: e.activation(out=QT[:], in_=pT[:, 0:4, :], func=AF.Copy), reads=[pT], writes=[QT])
                    P.op("dve", lambda e, KT=KT, pT=pT: e.tensor_copy(out=KT[:], in_=pT[:, 4:8, :]), reads=[pT], writes=[KT])
                    for h in range(8):
                        hp, ho = h // 2, (h % 2) * 64
                        AT = d["AT"][h % 2]
                        pa = d["pa"]
                        P.op("pe", lambda e, KT=KT, QT=QT, hp=hp, ho=ho, pa=pa: e.matmul(pa[:], lhsT=KT[ho:ho + 64, hp, :], rhs=QT[ho:ho + 64, hp, :], start=True, stop=True),
                             reads=[KT, QT], writes=[pa])
                        P.op("dve", lambda e, AT=AT, pa=pa, mk=mk: e.tensor_tensor(out=AT[:], in0=pa[:], in1=mk, op=ALU.mult), reads=[pa, masks], writes=[AT])
                        P.op("pe", lambda e, AT=AT, Vt=Vt, h=h: e.matmul(pn[:, h, 0:65], lhsT=AT[:], rhs=Vt[:, h, :], start=True, stop=False), reads=[AT, Vt], writes=[pn])
                        P.op("pe", lambda e, QT=QT, C=C, h=h, hp=hp, ho=ho: e.matmul(pn[:, h, 0:65], lhsT=QT[ho:ho + 64, hp, :], rhs=C[ho:ho + 64, hp, :], start=False, stop=True),
                             reads=[QT, C], writes=[pn])
                    P.op("dve", lambda e, sm=sm, eb=eb: e.tensor_tensor(out=sm[:, 0, :], in0=pn[:, :, 64], in1=eb[:], op=ALU.mult), reads=[pn, eb], writes=[sm])
                    P.op("dve", lambda e, sm=sm: e.scalar_tensor_tensor(out=sm[:, 1, :], in0=sm[:, 0, :], scalar=-1.0, in1=sm[:, 0, :], op0=ALU.mult, op1=ALU.max), reads=[sm], writes=[sm])
                    P.op("dve", lambda e, sm=sm: e.tensor_scalar_max(out=sm[:, 2, :], in0=sm[:, 1, :], scalar1=1.0), reads=[sm], writes=[sm])
                    P.op("dve", lambda e, sm=sm: e.reciprocal(out=sm[:, 3, :], in_=sm[:, 2, :]), reads=[sm], writes=[sm])
                    P.op("dve", lambda e, sm=sm, eb=eb: e.tensor_tensor(out=sm[:, 4, :], in0=sm[:, 3, :], in1=eb[:], op=ALU.mult), reads=[sm, eb], writes=[sm])
                    hd = d["hd"]
                    P.op("dve", lambda e, sm=sm, hd=hd: e.tensor_tensor(out=hd[:], in0=pn[:, :, 0:64], in1=sm[:, 4, :].unsqueeze(2).to_broadcast([128, 8, 64]), op=ALU.mult),
                         reads=[pn, sm], writes=[hd])
                    for hp in range(4):
                        P.op("pe", lambda e, ua=ua, Vt=Vt, hp=hp: e.matmul(pc[:], lhsT=ua[:, 512 + hp * 128: 512 + (hp + 1) * 128], rhs=Vt[:, 2 * hp:2 * hp + 2, :], start=True, stop=True),
                             reads=[ua, Vt], writes=[pc])
                        for ho_i in range(2):
                            ho = ho_i * 64
                            h = 2 * hp + ho_i
                            Ct = d["Ct"]
                            P.op("dve", lambda e, C=C, Ct=Ct, hp=hp, ho=ho, ho_i=ho_i: e.tensor_tensor(out=Ct[ho:ho + 64, :], in0=pc[ho:ho + 64, ho_i, :], in1=C[ho:ho + 64, hp, :], op=ALU.add),
                                 reads=[pc, C], writes=[Ct])
                            P.op("act", lambda e, C=C, Ct=Ct, eL=eL, hp=hp, ho=ho, h=h: e.activation(out=C[ho:ho + 64, hp, :], in_=Ct[ho:ho + 64, :], func=AF.Copy, scale=eL[ho:ho + 64, h:h + 1]),
                                 reads=[Ct, eL], writes=[C])
                    hdf = hd[:].rearrange("p h d -> p (h d)")
                    if dr == 0:
                        P.dma("act", YS[t * 128:(t + 1) * 128, 0:512], hdf, reads=[hd], writes=[("YS", t)])
                    else:
                        hf, sq, sg = d["hf"], d["sq"], d["sg"]
                        P.dma("sp", hf[:], YS[t * 128:(t + 1) * 128, 0:512], reads=[("YS", t)], writes=[hf])
                        P.op("pool", lambda e, hf=hf, hdf=hdf: e.tensor_tensor(out=hf[:], in0=hf[:], in1=hdf, op=ALU.add), reads=[hf, hd], writes=[hf])
                        P.op("act", lambda e, hf=hf, sq=sq: e.activation(out=sq[:], in_=hf[:], func=AF.Square), reads=[hf], writes=[sq])
                        P.op("dve", lambda e, sq=sq, sm=sm: e.tensor_reduce(out=sm[:, 5, :], in_=sq[:].rearrange("p (h d) -> p h d", h=8), axis=AX.X, op=ALU.add), reads=[sq], writes=[sm])
                        P.op("act", lambda e, sm=sm: e.activation(out=sm[:, 5, :], in_=sm[:, 5, :], func=AF.Sqrt, scale=1.0 / 64, bias=eps_c[:, 0:1]), reads=[sm, eps_c], writes=[sm])
                        P.op("dve", lambda e, sm=sm: e.reciprocal(out=sm[:, 5, :], in_=sm[:, 5, :]), reads=[sm], writes=[sm])
                        P.op("dve", lambda e, hf=hf, sm=sm: e.tensor_tensor(out=hf[:].rearrange("p (h d) -> p h d", h=8), in0=hf[:].rearrange("p (h d) -> p h d", h=8),
                                                                      in1=sm[:, 5, :].unsqueeze(2).to_broadcast([128, 8, 64]), op=ALU.mult), reads=[hf, sm], writes=[hf])
                        P.op("pool", lambda e, hf=hf: e.tensor_tensor(out=hf[:], in0=hf[:], in1=ng_bc[:], op=ALU.mult), reads=[hf, ng_bc], writes=[hf])
                        P.op("act", lambda e, ua=ua, sg=sg: e.activation(out=sg[:], in_=ua[:, 1536:2048], func=AF.Sigmoid), reads=[ua], writes=[sg])
                        P.op("dve", lambda e, hf=hf, sg=sg: e.tensor_tensor(out=sg[:], in0=hf[:], in1=sg[:], op=ALU.mult), reads=[hf, sg], writes=[sg])
                        P.dma("act", O[t * 128:(t + 1) * 128, 0:512], sg[:], reads=[sg], writes=[("O", t, 0)])
        P.pop()

    def diffattn(l):
        NB = cfg.NB
        lam_init = 0.8 - 0.6 * math.exp(-0.3 * l)
        SC = 32 ** -0.5
        CSH = 4.0
        P.push()
        g2 = P.sbuf("dag2", [128, 2, 32]); eps_c = P.sbuf("daeps", [128, 1])
        P.dma("sp", g2[:, 0, :], io["diff_qk_g"][0:1, :].to_broadcast([128, 32]), writes=[g2])
        P.dma("sp", g2[:, 1, :], io["diff_qk_g"][1:2, :].to_broadcast([128, 32]), writes=[g2])
        P.op("dve", lambda e: e.memset(eps_c[:], EPS), writes=[eps_c])
        ubs = [P.sbuf(f"daub{i}", [128, 1024]) for i in range(2)]
        sqs = P.sbuf("dasq", [128, 1024]); rs = [P.sbuf(f"dars{i}", [128, 32]) for i in range(2)]
        qn = [P.sbuf(f"daqn{i}", [128, 1024]) for i in range(2)]
        qr = [P.sbuf(f"daqr{i}", [128, 1024]) for i in range(2)]
        tA = P.sbuf("datA", [128, 2, 512]); tB = P.sbuf("datB", [128, 2, 512])
        cs = [P.sbuf(f"dacs{i}", [128, 32]) for i in range(2)]
        QTt = [P.sbuf(f"daQTt{i}", [64, 16, 128]) for i in range(2)]
        pT = [P.psum(f"dapT{i}", [64, 8, 128]) for i in range(2)]
        for t in range(cfg.ntile):
            b, tt = divmod(t, NTS)
            isctx = tt < TC // 128
            ub = ubs[t % 2]; r_ = rs[t % 2]; qn_ = qn[t % 2]; qr_ = qr[t % 2]; cs_ = cs[t % 2]; qt_ = QTt[t % 2]
            P.dma("sp", ub[:], U[t * 128:(t + 1) * 128, A_IN:A_IN + 1024], reads=ukeys(t), writes=[ub])
            P.op("act", lambda e, ub=ub: e.activation(out=sqs[:], in_=ub[:], func=AF.Square), reads=[ub], writes=[sqs])
            P.op("dve", lambda e, r_=r_: e.tensor_reduce(out=r_[:], in_=sqs[:].rearrange("p (g d) -> p g d", g=32), axis=AX.X, op=ALU.add), reads=[sqs], writes=[r_])
            P.op("act", lambda e, r_=r_: e.activation(out=r_[:], in_=r_[:], func=AF.Sqrt, scale=1.0 / 32, bias=eps_c[:, 0:1]), reads=[r_, eps_c], writes=[r_])
            P.op("dve", lambda e, r_=r_: e.reciprocal(out=r_[:], in_=r_[:]), reads=[r_], writes=[r_])
            P.op("dve", lambda e, ub=ub, r_=r_, qn_=qn_: e.tensor_tensor(out=qn_[:].rearrange("p (g d) -> p g d", g=32), in0=ub[:].rearrange("p (g d) -> p g d", g=32),
                                                                   in1=r_[:].unsqueeze(2).to_broadcast([128, 32, 32]), op=ALU.mult), reads=[ub, r_], writes=[qn_])
            dst = qr_ if isctx else qn_
            P.op("pool", lambda e, qn_=qn_, dst=dst: e.tensor_tensor(out=dst[:].rearrange("p (a g d) -> p a g d", a=2, g=16), in0=qn_[:].rearrange("p (a g d) -> p a g d", a=2, g=16),
                                                                in1=g2[:].unsqueeze(2).to_broadcast([128, 2, 16, 32]), op=ALU.mult), reads=[qn_, g2], writes=[dst])
            if not isctx:
                lrow = (tt - TC // 128) * 128
                P.dma("sp", cs_[:], io["rope"][lrow:lrow + 128, :], writes=[cs_])
                v4 = qn_[:].rearrange("p (g i two) -> p g i two", g=32, two=2)
                o4 = qr_[:].rearrange("p (g i two) -> p g i two", g=32, two=2)
                x1, x2 = v4[:, :, :, 0], v4[:, :, :, 1]
                cosb = cs_[:, 0:16].unsqueeze(1).to_broadcast([128, 32, 16])
                sinb = cs_[:, 16:32].unsqueeze(1).to_broadcast([128, 32, 16])
                tA0 = tA[:, 0, :].rearrange("p (g i) -> p g i", g=32); tA1 = tA[:, 1, :].rearrange("p (g i) -> p g i", g=32)
                tB0 = tB[:, 0, :].rearrange("p (g i) -> p g i", g=32); tB1 = tB[:, 1, :].rearrange("p (g i) -> p g i", g=32)
                P.op("dve", lambda e, x1=x1, cosb=cosb, tA0=tA0: e.tensor_tensor(out=tA0, in0=x1, in1=cosb, op=ALU.mult), reads=[qn_, cs_], writes=[tA])
                P.op("dve", lambda e, x2=x2, sinb=sinb, tA1=tA1: e.tensor_tensor(out=tA1, in0=x2, in1=sinb, op=ALU.mult), reads=[qn_, cs_], writes=[tA])
                P.op("dve", lambda e, o4=o4, tA0=tA0, tA1=tA1: e.tensor_tensor(out=o4[:, :, :, 0], in0=tA0, in1=tA1, op=ALU.subtract), reads=[tA], writes=[qr_])
                P.op("dve", lambda e, x1=x1, sinb=sinb, tB0=tB0: e.tensor_tensor(out=tB0, in0=x1, in1=sinb, op=ALU.mult), reads=[qn_, cs_], writes=[tB])
                P.op("dve", lambda e, x2=x2, cosb=cosb, tB1=tB1: e.tensor_tensor(out=tB1, in0=x2, in1=cosb, op=ALU.mult), reads=[qn_, cs_], writes=[tB])
                P.op("dve", lambda e, o4=o4, tB0=tB0, tB1=tB1: e.tensor_tensor(out=o4[:, :, :, 1], in0=tB0, in1=tB1, op=ALU.add), reads=[tB], writes=[qr_])
            for a in range(2):
                pt_ = pT[a]
                for i in range(8):
                    c0 = a * 512 + i * 64
                    P.op("pe", lambda e, pt_=pt_, i=i, c0=c0, qr_=qr_: e.transpose(out=pt_[:, i, :], in_=qr_[:, c0:c0 + 64], identity=ident[:]), reads=[qr_, ident], writes=[pt_])
                if a == 0:
                    P.op("act", lambda e, pt_=pt_, qt_=qt_: e.activation(out=qt_[:, 0:8, :], in_=pt_[:], func=AF.Copy), reads=[pt_], writes=[qt_])
                else:
                    P.op("dve", lambda e, pt_=pt_, qt_=qt_: e.tensor_copy(out=qt_[:, 8:16, :], in_=pt_[:]), reads=[pt_], writes=[qt_])
            for a in range(2):
                P.dma("act", QKT[b, a, :, :, tt * 128:(tt + 1) * 128].rearrange("h p n -> p h n"), qt_[:, a * 8:(a + 1) * 8, :], reads=[qt_], writes=[("QKT", b, a, tt)])
        P.pop()
        P.push()
        lamr = P.sbuf("dalam", [128, 4, 32]); lt = P.sbuf("dalt", [128, 2, 32]); le = P.sbuf("dale", [128, 2]); nlam = P.sbuf("danlam", [128, 1])
        negc = P.sbuf("danegc", [128, 1]); eps2 = P.sbuf("daeps2", [128, 1]); gs = P.sbuf("dags", [128, 64])
        P.dma("sp", lamr[:].rearrange("p a d -> p (a d)"), io["diff_lam"].to_broadcast([128, 128]), writes=[lamr])
        P.dma("sp", gs[:], io["diff_subln_g"].to_broadcast([128, 64]), writes=[gs])
        P.op("dve", lambda e: e.memset(negc[:], -CSH), writes=[negc])
        P.op("dve", lambda e: e.memset(eps2[:], EPS), writes=[eps2])
        P.op("dve", lambda e: e.tensor_tensor(out=lt[:, 0, :], in0=lamr[:, 0, :], in1=lamr[:, 1, :], op=ALU.mult), reads=[lamr], writes=[lt])
        P.op("dve", lambda e: e.tensor_tensor(out=lt[:, 1, :], in0=lamr[:, 2, :], in1=lamr[:, 3, :], op=ALU.mult), reads=[lamr], writes=[lt])
        P.op("dve", lambda e: e.tensor_reduce(out=le[:], in_=lt[:], axis=AX.X, op=ALU.add), reads=[lt], writes=[le])
        P.op("act", lambda e: e.activation(out=le[:], in_=le[:], func=AF.Exp), reads=[le], writes=[le])
        P.op("dve", lambda e: e.tensor_tensor(out=nlam[:], in0=le[:, 1:2], in1=le[:, 0:1], op=ALU.subtract), reads=[le], writes=[nlam])
        P.op("dve", lambda e: e.tensor_scalar_add(out=nlam[:], in0=nlam[:], scalar1=-lam_init), reads=[nlam], writes=[nlam])
        P.op("dve", lambda e: e.tensor_scalar_mul(out=gs[:], in0=gs[:], scalar1=1.0 - lam_init), reads=[gs], writes=[gs])
        QTh = [P.sbuf(f"daQTh{i}", [64, cfg.TS]) for i in range(2)]
        KTh = [P.sbuf(f"daKTh{i}", [64, cfg.TS]) for i in range(2)]
        Vst = [P.sbuf(f"daVst{i}", [128, NTS, 64]) for i in range(2)]
        Vr = [P.sbuf(f"daVr{i}", [128, NTS, 65]) for i in range(2)]
        for i in range(2):
            for j0 in range(0, NTS, 128):
                j1 = min(NTS, j0 + 128)
                P.op("dve", lambda e, i=i, j0=j0, j1=j1: e.tensor_copy(out=Vr[i][:, j0:j1, 64].bitcast(F32R), in_=ones[:, 0:j1 - j0]), reads=[ones], writes=[Vr[i]])
        PT = [P.sbuf(f"daPT{i}", [128, 512]) for i in range(3)]
        accs = [P.sbuf(f"daaccs{i}", [65, 512]) for i in range(2)]
        Ot = [P.sbuf(f"daOt{i}", [128, 4, 64]) for i in range(2)]
        o2 = P.sbuf("dao2", [128, 4, 64]); sq = P.sbuf("dasq2", [128, 4, 64])
        sm = [P.sbuf(f"dasm{i}", [128, 4, 4]) for i in range(2)]
        ps = [P.psum(f"daps{i}", [128, 512]) for i in range(2)]
        acc = [P.psum(f"daacc{i}", [65, 512]) for i in range(2)]
        pTn = [P.psum(f"dapTn{i}", [128, 4, 65]) for i in range(2)]
        cnt = {"ps": 0, "pt": 0, "blk": 0}
        for b in range(NB):
            for h in range(8):
                it = b * 8 + h
                qth, kth, vst, vr = QTh[it % 2], KTh[it % 2], Vst[it % 2], Vr[it % 2]
                P.dma("sp", qth[:], QKT[b, 0, h], reads=[("QKT", b, 0, tt) for tt in range(NTS)], writes=[qth])
                P.dma("sp", kth[:], QKT[b, 1, h], reads=[("QKT", b, 1, tt) for tt in range(NTS)], writes=[kth])
                c0 = A_IN + 1024 + h * 64
                P.dma("sp", vst[:], U[b * cfg.TS:(b + 1) * cfg.TS, c0:c0 + 64].rearrange("(n p) d -> p n d", p=128),
                      reads=[k_ for tt in range(NTS) for k_ in ukeys(b * NTS + tt)], writes=[vst])
                P.op("pool", lambda e, qth=qth: e.tensor_copy(out=qth[:].bitcast(F32R), in_=qth[:]), reads=[qth], writes=[qth])
                P.op("pool", lambda e, kth=kth: e.tensor_copy(out=kth[:].bitcast(F32R), in_=kth[:]), reads=[kth], writes=[kth])
                P.op("pool", lambda e, vst=vst, vr=vr: e.tensor_copy(out=vr[:, :, 0:64].bitcast(F32R), in_=vst[:]), reads=[vst], writes=[vr])
                blocks = [(0, TC, list(range(TC // 128)))]
                q0 = TC
                while q0 < cfg.TS:
                    nq = min(512, cfg.TS - q0)
                    blocks.append((q0, nq, list(range(NTS))))
                    q0 += nq
                for (q0, nq, kts) in blocks:
                    nsub = nq // 128
                    for c in range(2):
                        for ki, kt in enumerate(kts):
                            p_ = ps[cnt["ps"] % 2]; cnt["ps"] += 1
                            pt_ = PT[cnt["pt"] % 3]; cnt["pt"] += 1
                            P.op("pe", lambda e, p_=p_, kth=kth, qth=qth, c=c, kt=kt, q0=q0, nq=nq: e.matmul(
                                p_[:, 0:nq], lhsT=kth[c * 32:(c + 1) * 32, kt * 128:(kt + 1) * 128].bitcast(F32R), rhs=qth[c * 32:(c + 1) * 32, q0:q0 + nq].bitcast(F32R),
                                start=True, stop=True), reads=[kth, qth], writes=[p_])
                            P.op("act", lambda e, p_=p_, pt_=pt_, nq=nq: e.activation(out=pt_[:, 0:nq].bitcast(F32R), in_=p_[:, 0:nq], func=AF.Exp, scale=SC, bias=negc[:, 0:1]),
                                 reads=[p_, negc], writes=[pt_])
                            P.op("pe", lambda e, pt_=pt_, vr=vr, c=c, kt=kt, nq=nq, ki=ki, nk=len(kts): e.matmul(
                                acc[c][:, 0:nq], lhsT=vr[:, kt, :].bitcast(F32R), rhs=pt_[:, 0:nq].bitcast(F32R), start=(ki == 0), stop=(ki == nk - 1)),
                                reads=[vr, pt_], writes=[acc[c]])
                        if c == 0:
                            P.op("dve", lambda e, c=c, nq=nq: e.tensor_copy(out=accs[c][:, 0:nq], in_=acc[c][:, 0:nq]), reads=[acc[c]], writes=[accs[c]])
                        else:
                            P.op("dve", lambda e, c=c, nq=nq: e.tensor_copy(out=accs[c][:, 0:nq], in_=acc[c][:, 0:nq]), reads=[acc[c]], writes=[accs[c]])
                        for j in range(nsub):
                            P.op("pe", lambda e, c=c, j=j: e.transpose(out=pTn[c][:, j, :], in_=accs[c][:, j * 128:(j + 1) * 128], identity=ident[0:65, 0:65]),
                                 reads=[accs[c], ident], writes=[pTn[c]])
                    bi = cnt["blk"]; cnt["blk"] += 1
                    ot = Ot[bi % 2]; sm_ = sm[bi % 2]
                    n_ = nsub
                    P.op("dve", lambda e, sm_=sm_, n_=n_: e.reciprocal(out=sm_[:, 0, 0:n_], in_=pTn[0][:, 0:n_, 64]), reads=[pTn[0]], writes=[sm_])
                    P.op("dve", lambda e, sm_=sm_, n_=n_: e.reciprocal(out=sm_[:, 1, 0:n_], in_=pTn[1][:, 0:n_, 64]), reads=[pTn[1]], writes=[sm_])
                    P.op("dve", lambda e, sm_=sm_, n_=n_: e.tensor_scalar(out=sm_[:, 1, 0:n_], in0=sm_[:, 1, 0:n_], scalar1=nlam[:, 0:1], scalar2=None, op0=ALU.mult), reads=[sm_, nlam], writes=[sm_])
                    P.op("dve", lambda e, sm_=sm_, n_=n_, ot=ot: e.tensor_tensor(out=ot[:, 0:n_, :], in0=pTn[0][:, 0:n_, 0:64], in1=sm_[:, 0, 0:n_].unsqueeze(2).to_broadcast([128, n_, 64]), op=ALU.mult),
                         reads=[pTn[0], sm_], writes=[ot])
                    P.op("dve", lambda e, sm_=sm_, n_=n_: e.tensor_tensor(out=o2[:, 0:n_, :], in0=pTn[1][:, 0:n_, 0:64], in1=sm_[:, 1, 0:n_].unsqueeze(2).to_broadcast([128, n_, 64]), op=ALU.mult),
                         reads=[pTn[1], sm_], writes=[o2])
                    P.op("pool", lambda e, n_=n_, ot=ot: e.tensor_tensor(out=ot[:, 0:n_, :], in0=ot[:, 0:n_, :], in1=o2[:, 0:n_, :], op=ALU.add), reads=[ot, o2], writes=[ot])
                    P.op("act", lambda e, n_=n_, ot=ot: e.activation(out=sq[:, 0:n_, :], in_=ot[:, 0:n_, :], func=AF.Square), reads=[ot], writes=[sq])
                    P.op("dve", lambda e, sm_=sm_, n_=n_: e.tensor_reduce(out=sm_[:, 2, 0:n_], in_=sq[:, 0:n_, :], axis=AX.X, op=ALU.add), reads=[sq], writes=[sm_])
                    P.op("act", lambda e, sm_=sm_, n_=n_: e.activation(out=sm_[:, 2, 0:n_], in_=sm_[:, 2, 0:n_], func=AF.Sqrt, scale=1.0 / 64, bias=eps2[:, 0:1]), reads=[sm_, eps2], writes=[sm_])
                    P.op("dve", lambda e, sm_=sm_, n_=n_: e.reciprocal(out=sm_[:, 2, 0:n_], in_=sm_[:, 2, 0:n_]), reads=[sm_], writes=[sm_])
                    P.op("dve", lambda e, sm_=sm_, n_=n_, ot=ot: e.tensor_tensor(out=ot[:, 0:n_, :], in0=ot[:, 0:n_, :], in1=sm_[:, 2, 0:n_].unsqueeze(2).to_broadcast([128, n_, 64]), op=ALU.mult),
                         reads=[ot, sm_], writes=[ot])
                    P.op("pool", lambda e, n_=n_, ot=ot: e.tensor_tensor(out=ot[:, 0:n_, :], in0=ot[:, 0:n_, :], in1=gs[:].unsqueeze(1).to_broadcast([128, n_, 64]), op=ALU.mult),
                         reads=[ot, gs], writes=[ot])
                    r0 = b * cfg.TS + q0
                    t0 = r0 // 128
                    P.dma("act", O[r0:r0 + nq, 512 + h * 64: 512 + (h + 1) * 64].rearrange("(n p) d -> p n d", p=128), ot[:, 0:n_, :], reads=[ot],
                          writes=[("O", t0 + j, 1) for j in range(n_)])
        P.pop()

    def gla(l):
        need_ctx = (l == 0) or ("ctx_out" in dbg)
        NB = cfg.NB
        c0 = C_IN
        P.push()
        gw2 = P.sbuf("glgw2", [16, 2, 256]); gbr = P.sbuf("glgb", [1, 2, 256]); ng_bc = P.sbuf("glng", [128, 128])
        onec = P.sbuf("glone", [128, 1]); eps_c = P.sbuf("gleps", [128, 1])
        P.dma("sp", gw2[:], io["gla_gate_w2"].rearrange("d k n -> k d n"), writes=[gw2])
        P.dma("sp", gbr[:], io["gla_gate_b"].rearrange("(o d) n -> o d n", o=1), writes=[gbr])
        P.dma("sp", ng_bc[:], io["gla_norm_g"].to_broadcast([128, 128]), writes=[ng_bc])
        P.op("dve", lambda e: e.memset(onec[:], 1.0), writes=[onec])
        P.op("dve", lambda e: e.memset(eps_c[:], EPS), writes=[eps_c])
        B = []
        for b in range(NB):
            d = {}
            d["ud"] = [P.sbuf(f"glud{b}{i}", [128, D_IN]) for i in range(2)]
            d["gdT"] = P.sbuf(f"glgdT{b}", [16, 128])
            d["nla"] = P.sbuf(f"glnla{b}", [128, 256])
            d["Kh"] = P.sbuf(f"glKh{b}", [128, 256])
            d["eq"] = P.sbuf(f"gleq{b}", [128, 2, 128]); d["ek"] = P.sbuf(f"glek{b}", [128, 2, 128])
            d["QT"] = P.sbuf(f"glQT{b}", [128, 2, 128]); d["KT"] = P.sbuf(f"glKT{b}", [128, 2, 128])
            d["AT"] = [P.sbuf(f"glAT{b}{i}", [128, 128]) for i in range(2)]
            d["S"] = P.sbuf(f"glS{b}", [128, 2, 128])
            d["od"] = P.sbuf(f"glod{b}", [128, 512]); d["of"] = P.sbuf(f"glof{b}", [128, 512])
            d["sq"] = P.sbuf(f"glsq{b}", [128, 512]); d["sm"] = P.sbuf(f"glsm{b}", [128, 4])
            d["sg"] = P.sbuf(f"glsg{b}", [128, 512])
            B.append(d)
        pg = P.psum("glpg", [16, 128]); pz = P.psum("glpz", [128, 256]); pcs = P.psum("glpcs", [128, 2, 256])
        pbT = P.psum("glpbT", [128, 2, 128]); pqk = P.psum("glpqk", [128, 4, 128]); pa = P.psum("glpa", [128, 128])
        po = P.psum("glpo", [128, 4, 128]); pc = P.psum("glpc", [128, 256])
        nct = TC // 128
        for dr in range(2):
            mk = masks[:, dr, :]
            mks = masks[:, 4 + dr, :]
            lastcol = 127 if dr == 0 else 0
            for b in range(NB):
                P.op("dve", lambda e, b=b: e.memset(B[b]["S"][:], 0.0), writes=[B[b]["S"]])
            order = list(range(NTS)) if dr == 0 else [1, 0] + list(range(NTS - 1, 1, -1))
            for step, tt in enumerate(order):
                want_out = need_ctx or tt >= nct
                for b in range(NB):
                    d = B[b]
                    t = b * NTS + tt
                    ud = d["ud"][step % 2]
                    gdT, nla, Kh, eq, ek, QT, KT, S = d["gdT"], d["nla"], d["Kh"], d["eq"], d["ek"], d["QT"], d["KT"], d["S"]
                    P.dma("sp", ud[:], U[t * 128:(t + 1) * 128, c0:c0 + D_IN], reads=ukeys(t), writes=[ud])
                    gc = 1536 + 16 * dr
                    P.op("pe", lambda e, ud=ud, gc=gc: e.transpose(out=pg[:], in_=ud[:, gc:gc + 16], identity=ident[:]), reads=[ud, ident], writes=[pg])
                    P.op("act", lambda e, gdT=gdT: e.activation(out=gdT[:], in_=pg[:], func=AF.Copy), reads=[pg], writes=[gdT])
                    P.op("pe", lambda e, gdT=gdT, dr=dr: e.matmul(pz[:], lhsT=gdT[:], rhs=gw2[:, dr, :], start=True, stop=False), reads=[gdT, gw2], writes=[pz])
                    P.op("pe", lambda e, dr=dr: e.matmul(pz[:], lhsT=ones[0:1, :], rhs=gbr[0:1, dr, :], start=False, stop=True), reads=[ones, gbr], writes=[pz])
                    P.op("act", lambda e, nla=nla: e.activation(out=nla[:], in_=pz[:], func=AF.Exp, scale=-1.0), reads=[pz], writes=[nla])
                    P.op("act", lambda e, nla=nla: e.activation(out=nla[:], in_=nla[:], func=AF.Ln, bias=onec[:, 0:1]), reads=[nla, onec], writes=[nla])
                    P.op("pe", lambda e, nla=nla, mks=mks: e.matmul(pcs[:, 1, :], lhsT=mks, rhs=nla[:], start=True, stop=True), reads=[masks, nla], writes=[pcs])
                    P.op("act", lambda e, Kh=Kh: e.activation(out=Kh[:], in_=pcs[:, 1, :], func=AF.Exp, scale=-1.0 / 16), reads=[pcs], writes=[Kh])
                    P.op("dve", lambda e, Kh=Kh, ud=ud: e.tensor_tensor(out=Kh[:], in0=Kh[:], in1=ud[:, 256:512], op=ALU.mult), reads=[Kh, ud], writes=[Kh])
                    for p in range(2):
                        P.op("pe", lambda e, nla=nla, p=p, mk=mk: e.matmul(pbT[:, p, :], lhsT=nla[:, p * 128:(p + 1) * 128], rhs=mk, start=True, stop=True), reads=[nla, masks], writes=[pbT])
                    P.op("act", lambda e, eq=eq: e.activation(out=eq[:], in_=pbT[:], func=AF.Exp, scale=-1.0 / 16), reads=[pbT], writes=[eq])
                    P.op("act", lambda e, ek=ek: e.activation(out=ek[:], in_=pbT[:], func=AF.Exp, scale=1.0 / 16), reads=[pbT], writes=[ek])
                    for i in range(4):
                        P.op("pe", lambda e, ud=ud, i=i: e.transpose(out=pqk[:, i, :], in_=ud[:, i * 128:(i + 1) * 128], identity=ident[:]), reads=[ud, ident], writes=[pqk])
                    P.op("dve", lambda e, QT=QT, eq=eq: e.scalar_tensor_tensor(out=QT[:], in0=pqk[:, 0:2, :], scalar=0.125, in1=eq[:], op0=ALU.mult, op1=ALU.mult), reads=[pqk, eq], writes=[QT])
                    P.op("dve", lambda e, KT=KT, ek=ek: e.tensor_tensor(out=KT[:], in0=pqk[:, 2:4, :], in1=ek[:], op=ALU.mult), reads=[pqk, ek], writes=[KT])
                    for h in range(4):
                        p, ho = h // 2, (h % 2) * 64
                        if want_out:
                            AT = d["AT"][h % 2]
                            P.op("pe", lambda e, KT=KT, QT=QT, p=p, ho=ho: e.matmul(pa[:], lhsT=KT[ho:ho + 64, p, :], rhs=QT[ho:ho + 64, p, :], start=True, stop=True), reads=[KT, QT], writes=[pa])
                            P.op("dve", lambda e, AT=AT, mk=mk: e.tensor_tensor(out=AT[:], in0=pa[:], in1=mk, op=ALU.mult), reads=[pa, masks], writes=[AT])
                            P.op("pe", lambda e, AT=AT, ud=ud, h=h: e.matmul(po[:, h, :], lhsT=AT[:], rhs=ud[:, 512 + h * 128: 512 + (h + 1) * 128], start=True, stop=False), reads=[AT, ud], writes=[po])
                            P.op("pe", lambda e, QT=QT, S=S, h=h, p=p, ho=ho: e.matmul(po[:, h, :], lhsT=QT[ho:ho + 64, p, :], rhs=S[ho:ho + 64, p, :], start=False, stop=True), reads=[QT, S], writes=[po])
                    od = d["od"]
                    if want_out:
                        P.op("act", lambda e, od=od: e.activation(out=od[:], in_=po[:].rearrange("p h d -> p (h d)"), func=AF.Copy), reads=[po], writes=[od])
                    for p in range(2):
                        P.op("pe", lambda e, Kh=Kh, ud=ud, p=p: e.matmul(pc[:], lhsT=Kh[:, p * 128:(p + 1) * 128], rhs=ud[:, 512 + p * 256: 512 + (p + 1) * 256], start=True, stop=True), reads=[Kh, ud], writes=[pc])
                        for hi in range(2):
                            ho = hi * 64
                            P.op("dve", lambda e, S=S, eq=eq, p=p, ho=ho, hi=hi, lastcol=lastcol: e.scalar_tensor_tensor(out=S[ho:ho + 64, p, :], in0=S[ho:ho + 64, p, :], scalar=eq[ho:ho + 64, p, lastcol:lastcol + 1],
                                                                                                 in1=pc[ho:ho + 64, hi * 128:(hi + 1) * 128], op0=ALU.mult, op1=ALU.add), reads=[S, eq, pc], writes=[S])
                    if not want_out:
                        continue
                    if dr == 0:
                        P.dma("act", YS[t * 128:(t + 1) * 128, 0:512], od[:], reads=[od], writes=[("YS", t)])
                    else:
                        of, sq, sm, sg = d["of"], d["sq"], d["sm"], d["sg"]
                        P.dma("sp", of[:], YS[t * 128:(t + 1) * 128, 0:512], reads=[("YS", t)], writes=[of])
                        P.op("pool", lambda e, of=of, od=od: e.tensor_tensor(out=of[:], in0=of[:], in1=od[:], op=ALU.add), reads=[of, od], writes=[of])
                        P.op("act", lambda e, of=of, sq=sq: e.activation(out=sq[:], in_=of[:], func=AF.Square), reads=[of], writes=[sq])
                        P.op("dve", lambda e, sq=sq, sm=sm: e.tensor_reduce(out=sm[:], in_=sq[:].rearrange("p (h d) -> p h d", h=4), axis=AX.X, op=ALU.add), reads=[sq], writes=[sm])
                        P.op("act", lambda e, sm=sm: e.activation(out=sm[:], in_=sm[:], func=AF.Sqrt, scale=1.0 / 128, bias=eps_c[:, 0:1]), reads=[sm, eps_c], writes=[sm])
                        P.op("dve", lambda e, sm=sm: e.reciprocal(out=sm[:], in_=sm[:]), reads=[sm], writes=[sm])
                        P.op("dve", lambda e, of=of, sm=sm: e.tensor_tensor(out=of[:].rearrange("p (h d) -> p h d", h=4), in0=of[:].rearrange("p (h d) -> p h d", h=4),
                                                                      in1=sm[:].unsqueeze(2).to_broadcast([128, 4, 128]), op=ALU.mult), reads=[of, sm], writes=[of])
                        P.op("pool", lambda e, of=of: e.tensor_tensor(out=of[:].rearrange("p (h d) -> p h d", h=4), in0=of[:].rearrange("p (h d) -> p h d", h=4),
                                                                  in1=ng_bc[:].unsqueeze(1).to_broadcast([128, 4, 128]), op=ALU.mult), reads=[of, ng_bc], writes=[of])
                        P.op("act", lambda e, ud=ud, sg=sg: e.activation(out=sg[:], in_=ud[:, 1024:1536], func=AF.Silu), reads=[ud], writes=[sg])
                        P.op("dve", lambda e, of=of, sg=sg: e.tensor_tensor(out=sg[:], in0=of[:], in1=sg[:], op=ALU.mult), reads=[of, sg], writes=[sg])
                        P.dma("act", O[t * 128:(t + 1) * 128, 512:1024], sg[:], reads=[sg], writes=[("O", t, 1)])
        P.pop()

    E05 = math.exp(-0.5)

    def rwkv(l):
        need_ctx = (l == 0) or ("ctx_out" in dbg)
        NB = cfg.NB
        nct = TC // 128
        P.push()
        mu0 = P.sbuf("rwmu0", [128, C_IN]); mu1 = P.sbuf("rwmu1", [128, C_IN]); muc = P.sbuf("rwmuc", [128, C_IN])
        kv0 = P.sbuf("rwkv0", [128, 512]); tiny = P.sbuf("rwtiny", [128, 1])
        P.dma("sp", mu0[:], io["rwkv_mu"][0:1, :].to_broadcast([128, C_IN]), writes=[mu0])
        P.dma("sp", mu1[:], io["rwkv_mu"][1:2, :].to_broadcast([128, C_IN]), writes=[mu1])
        P.dma("sp", kv0[:], io["rwkv_kvec"][0:1, :].to_broadcast([128, 512]), writes=[kv0])
        P.op("dve", lambda e: e.tensor_tensor(out=muc[:], in0=mu0[:], in1=mu1[:], op=ALU.add), reads=[mu0, mu1], writes=[muc])
        P.op("dve", lambda e: e.tensor_scalar(out=muc[:], in0=muc[:], scalar1=-1.0, scalar2=1.0, op0=ALU.mult, op1=ALU.add), reads=[muc], writes=[muc])
        cur = [P.sbuf(f"rwcur{i}", [128, C_IN]) for i in range(2)]
        prv = [P.sbuf(f"rwprv{i}", [128, C_IN]) for i in range(2)]
        nxt = [P.sbuf(f"rwnxt{i}", [128, C_IN]) for i in range(2)]
        rwt = [P.sbuf(f"rwrwt{i}", [128, 2432]) for i in range(2)]
        t1 = P.sbuf("rwt1", [128, C_IN]); sqk = P.sbuf("rwsqk", [128, 512]); ks = [P.sbuf(f"rwks{i}", [128, 8]) for i in range(2)]
        for t in range(cfg.ntile):
            b, tt = divmod(t, NTS)
            r0 = t * 128
            cu, pv, nx, rt, ks_ = cur[t % 2], prv[t % 2], nxt[t % 2], rwt[t % 2], ks[t % 2]
            seg_start = tt in (0, nct)
            seg_end = tt in (nct - 1, NTS - 1)
            P.dma("sp", cu[:], U[r0:r0 + 128, 0:C_IN], reads=ukeys(t), writes=[cu])
            if seg_start:
                P.op("pool", lambda e, pv=pv: e.memset(pv[:], 0.0), writes=[pv])
                P.dma("sp", pv[1:128, :], U[r0:r0 + 127, 0:C_IN], reads=ukeys(t), writes=[pv])
            else:
                P.dma("sp", pv[:], U[r0 - 1:r0 + 127, 0:C_IN], reads=ukeys(t) + ukeys(t - 1), writes=[pv])
            if seg_end:
                P.op("pool", lambda e, nx=nx: e.memset(nx[:], 0.0), writes=[nx])
                P.dma("sp", nx[0:127, :], U[r0 + 1:r0 + 128, 0:C_IN], reads=ukeys(t), writes=[nx])
            else:
                P.dma("sp", nx[:], U[r0 + 1:r0 + 129, 0:C_IN], reads=ukeys(t) + ukeys(t + 1), writes=[nx])
            us = rt[:, 0:C_IN]
            P.op("dve", lambda e, cu=cu, us=us: e.tensor_tensor(out=us, in0=cu[:], in1=muc[:], op=ALU.mult), reads=[cu, muc], writes=[rt])
            P.op("pool", lambda e, pv=pv: e.tensor_tensor(out=pv[:], in0=pv[:], in1=mu0[:], op=ALU.mult), reads=[pv, mu0], writes=[pv])
            P.op("dve", lambda e, nx=nx: e.tensor_tensor(out=nx[:], in0=nx[:], in1=mu1[:], op=ALU.mult), reads=[nx, mu1], writes=[nx])
            P.op("dve", lambda e, pv=pv, us=us: e.tensor_tensor(out=us, in0=us, in1=pv[:], op=ALU.add), reads=[rt, pv], writes=[rt])
            P.op("dve", lambda e, nx=nx, us=us: e.tensor_tensor(out=us, in0=us, in1=nx[:], op=ALU.add), reads=[rt, nx], writes=[rt])
            kkc = rt[:, 1920:2432]
            P.op("pool", lambda e, rt=rt, kkc=kkc: e.tensor_tensor(out=kkc, in0=rt[:, 512:1024], in1=kv0[:], op=ALU.mult), reads=[rt, kv0], writes=[rt])
            P.op("act", lambda e, kkc=kkc: e.activation(out=sqk[:], in_=kkc, func=AF.Square), reads=[rt], writes=[sqk])
            P.op("dve", lambda e, ks_=ks_: e.tensor_reduce(out=ks_[:], in_=sqk[:].rearrange("p (h d) -> p h d", h=8), axis=AX.X, op=ALU.add), reads=[sqk], writes=[ks_])
            P.op("dve", lambda e, ks_=ks_: e.tensor_scalar_max(out=ks_[:], in0=ks_[:], scalar1=1e-24), reads=[ks_], writes=[ks_])
            P.op("act", lambda e, ks_=ks_: e.activation(out=ks_[:], in_=ks_[:], func=AF.Sqrt), reads=[ks_], writes=[ks_])
            P.op("dve", lambda e, ks_=ks_: e.reciprocal(out=ks_[:], in_=ks_[:]), reads=[ks_], writes=[ks_])
            P.op("dve", lambda e, kkc=kkc, ks_=ks_: e.tensor_tensor(out=kkc.rearrange("p (h d) -> p h d", h=8), in0=kkc.rearrange("p (h d) -> p h d", h=8),
                                                              in1=ks_[:].unsqueeze(2).to_broadcast([128, 8, 64]), op=ALU.mult), reads=[rt, ks_], writes=[rt])
            P.dma("act", RW[r0:r0 + 128, :], rt[:], reads=[rt], writes=[("RW", t)])
        P.pop()
        if "rwA" in dbg:
            return
        P.push()
        w2s = P.sbuf("rww2", [64, 2, 512]); a2s = P.sbuf("rwa2", [64, 2, 512]); w0r = P.sbuf("rww0", [1, 2, 512]); a0r = P.sbuf("rwa0", [1, 2, 512])
        g2s = P.sbuf("rwg2", [128, 512]); kv1 = P.sbuf("rwkv1", [128, 512]); omk1 = P.sbuf("rwomk1", [128, 512]); kv2 = P.sbuf("rwkv2", [128, 512])
        ln0 = P.sbuf("rwln0", [128, 512]); ln1 = P.sbuf("rwln1", [128, 512]); lneps = P.sbuf("rwlneps", [128, 1])
        P.dma("sp", w2s[:], io["rwkv_w2"].rearrange("d k n -> k d n"), writes=[w2s])
        P.dma("sp", a2s[:], io["rwkv_a2"].rearrange("d k n -> k d n"), writes=[a2s])
        P.dma("sp", w0r[:], io["rwkv_w0"].rearrange("(o d) n -> o d n", o=1), writes=[w0r])
        P.dma("sp", a0r[:], io["rwkv_a0"].rearrange("(o d) n -> o d n", o=1), writes=[a0r])
        P.dma("sp", g2s[:], io["rwkv_g2"], writes=[g2s])
        P.dma("sp", kv1[:], io["rwkv_kvec"][1:2, :].to_broadcast([128, 512]), writes=[kv1])
        P.dma("sp", kv2[:], io["rwkv_kvec"][2:3, :].to_broadcast([128, 512]), writes=[kv2])
        P.dma("sp", ln0[:], io["rwkv_ln"][0:1, :].to_broadcast([128, 512]), writes=[ln0])
        P.dma("sp", ln1[:], io["rwkv_ln"][1:2, :].to_broadcast([128, 512]), writes=[ln1])
        P.op("dve", lambda e: e.tensor_scalar(out=omk1[:], in0=kv1[:], scalar1=-1.0, scalar2=1.0, op0=ALU.mult, op1=ALU.add), reads=[kv1], writes=[omk1])
        P.op("dve", lambda e: e.memset(lneps[:], 64e-5), writes=[lneps])
        nm = ["sg", "a_", "tt_", "kd", "beta", "eI", "eInv", "eE", "eR", "Rt", "Kt", "Bt", "At", "Kh", "nBh", "tmp"]
        T = {n: P.sbuf("rw_" + n, [128, 512]) for n in nm}
        twT = P.sbuf("rwtwT", [64, 128]); adT = P.sbuf("rwadT", [64, 128]); WL = P.sbuf("rwWL", [128, 4, 2])
        Acur = [P.sbuf(f"rwA{i}", [128, 128]) for i in range(2)]; Atc = [P.sbuf(f"rwAt{i}", [128, 128]) for i in range(2)]
        Pc = [P.sbuf(f"rwP{i}", [128, 128]) for i in range(2)]
        Th = [P.sbuf(f"rwTh{i}", [128, 128]) for i in range(2)]; Mh = [P.sbuf(f"rwMh{i}", [128, 128]) for i in range(2)]
        G3h = [P.sbuf(f"rwG3h{i}", [128, 128]) for i in range(2)]; nG4h = [P.sbuf(f"rwG4h{i}", [128, 128]) for i in range(2)]
        sgT = P.sbuf("rwsgT", [128, 128])
        B = []
        for b in range(NB):
            d = {}
            d["rw"] = P.sbuf(f"rwrw{b}", [128, 2432])
            d["FT"] = P.sbuf(f"rwFT{b}", [128, 4, 4, 128])
            d["ST"] = P.sbuf(f"rwST{b}", [128, 4, 64])
            d["XT"] = P.sbuf(f"rwXT{b}", [128, 2, 64]); d["UT"] = P.sbuf(f"rwUT{b}", [128, 2, 64])
            d["y"] = P.sbuf(f"rwy{b}", [128, 512]); d["bd"] = P.sbuf(f"rwbd{b}", [128, 8])
            d["yf"] = P.sbuf(f"rwyf{b}", [128, 520]); d["sm"] = P.sbuf(f"rwsm{b}", [128, 3, 8])
            B.append(d)
        pw = P.psum("rwpw", [128, 264]); pz = P.psum("rwpz", [128, 512]); pcw = P.psum("rwpcw", [128, 512]); pft = P.psum("rwpft", [128, 4, 128])
        pG = [P.psum(f"rwpG{i}", [128, 128]) for i in range(2)]; pxy = P.psum("rwpxy", [128, 3, 2, 64]); pS = P.psum("rwpS", [128, 128])
        gcnt = {"g": 0, "e": 0}

        def gmm(lhsT, rhs, reads):
            pg_ = pG[gcnt["g"] % 2]; gcnt["g"] += 1
            P.op("pe", lambda e, pg_=pg_, lhsT=lhsT, rhs=rhs: e.matmul(pg_[:], lhsT=lhsT, rhs=rhs, start=True, stop=True), reads=reads, writes=[pg_])
            return pg_

        def evac_copy(dst, src):
            gcnt["e"] += 1
            if gcnt["e"] % 2:
                P.op("act", lambda e, dst=dst, src=src: e.activation(out=dst[:], in_=src[:], func=AF.Copy), reads=[src], writes=[dst])
            else:
                P.op("dve", lambda e, dst=dst, src=src: e.tensor_copy(out=dst[:], in_=src[:]), reads=[src], writes=[dst])

        for dr in range(2):
            for b in range(NB):
                P.op("dve", lambda e, b=b: e.memset(B[b]["ST"][:], 0.0), writes=[B[b]["ST"]])
            order = list(range(NTS)) if dr == 0 else [1, 0] + list(range(NTS - 1, 1, -1))
            m_incl = masks[:, 2 + dr, :]; m_strict = masks[:, 6 + dr, :]; m_strictT = masks[:, 7 - dr, :]
            chunks = (0, 1) if dr == 0 else (1, 0)
            for step, tt in enumerate(order):
                want_out = need_ctx or tt >= nct
                for b in range(NB):
                    d = B[b]
                    t = b * NTS + tt
                    rw, FT, ST, XT, UT, y, bd, sm = d["rw"], d["FT"], d["ST"], d["XT"], d["UT"], d["y"], d["bd"], d["sm"]
                    P.dma("sp", rw[:], RW[t * 128:(t + 1) * 128, :], reads=[("RW", t)], writes=[rw])
                    wc = 1536 + 64 * dr; ac = 1664 + 64 * dr
                    P.op("pe", lambda e, rw=rw, wc=wc: e.transpose(out=pw[0:64, 0:128], in_=rw[:, wc:wc + 64], identity=ident[:]), reads=[rw, ident], writes=[pw])
                    P.op("pe", lambda e, rw=rw, ac=ac: e.transpose(out=pw[0:64, 128:256], in_=rw[:, ac:ac + 64], identity=ident[:]), reads=[rw, ident], writes=[pw])
                    P.op("act", lambda e: e.activation(out=twT[:], in_=pw[0:64, 0:128], func=AF.Tanh), reads=[pw], writes=[twT])
                    P.op("dve", lambda e: e.tensor_copy(out=adT[:], in_=pw[0:64, 128:256]), reads=[pw], writes=[adT])
                    P.op("pe", lambda e, dr=dr: e.matmul(pz[:], lhsT=twT[:], rhs=w2s[:, dr, :], start=True, stop=False), reads=[twT, w2s], writes=[pz])
                    P.op("pe", lambda e, dr=dr: e.matmul(pz[:], lhsT=ones[0:1, :], rhs=w0r[0:1, dr, :], start=False, stop=True), reads=[ones, w0r], writes=[pz])
                    P.op("act", lambda e: e.activation(out=T["sg"][:], in_=pz[:], func=AF.Sigmoid), reads=[pz], writes=[T["sg"]])
                    P.op("pe", lambda e, dr=dr: e.matmul(pz[:], lhsT=adT[:], rhs=a2s[:, dr, :], start=True, stop=False), reads=[adT, a2s], writes=[pz])
                    P.op("pe", lambda e, dr=dr: e.matmul(pz[:], lhsT=ones[0:1, :], rhs=a0r[0:1, dr, :], start=False, stop=True), reads=[ones, a0r], writes=[pz])
                    P.op("act", lambda e: e.activation(out=T["a_"][:], in_=pz[:], func=AF.Sigmoid), reads=[pz], writes=[T["a_"]])
                    P.op("pe", lambda e, m_incl=m_incl: e.matmul(pcw[:], lhsT=m_incl, rhs=T["sg"][:], start=True, stop=True), reads=[masks, T["sg"]], writes=[pcw])
                    P.op("act", lambda e: e.activation(out=T["eI"][:], in_=pcw[:], func=AF.Exp, scale=-E05), reads=[pcw], writes=[T["eI"]])
                    P.op("act", lambda e: e.activation(out=T["eInv"][:], in_=pcw[:], func=AF.Exp, scale=E05), reads=[pcw], writes=[T["eInv"]])
                    P.op("dve", lambda e: e.tensor_tensor(out=T["tmp"][:], in0=pcw[:], in1=T["sg"][:], op=ALU.subtract), reads=[pcw, T["sg"]], writes=[T["tmp"]])
                    P.op("act", lambda e: e.activation(out=T["eE"][:], in_=T["tmp"][:], func=AF.Exp, scale=-E05), reads=[T["tmp"]], writes=[T["eE"]])
                    P.op("pe", lambda e, m_strictT=m_strictT: e.matmul(pcw[:], lhsT=m_strictT, rhs=T["sg"][:], start=True, stop=True), reads=[masks, T["sg"]], writes=[pcw])
                    P.op("act", lambda e: e.activation(out=T["eR"][:], in_=pcw[:], func=AF.Exp, scale=-E05), reads=[pcw], writes=[T["eR"]])
                    for p in range(4):
                        P.op("pe", lambda e, p=p: e.matmul(pw[:, 256 + 2 * p:258 + 2 * p], lhsT=T["sg"][:, p * 128:(p + 1) * 128], rhs=masks[:, 8, 0:2], start=True, stop=True),
                             reads=[T["sg"], masks], writes=[pw])
                    P.op("act", lambda e: e.activation(out=WL[:].rearrange("p a c -> p (a c)"), in_=pw[:, 256:264], func=AF.Exp, scale=-E05), reads=[pw], writes=[WL])
                    r_, k_, v_, kk_ = rw[:, 0:512], rw[:, 512:1024], rw[:, 1024:1536], rw[:, 1920:2432]
                    P.op("dve", lambda e: e.tensor_tensor(out=T["tt_"][:], in0=T["a_"][:], in1=kv1[:], op=ALU.mult), reads=[T["a_"], kv1], writes=[T["tt_"]])
                    P.op("dve", lambda e: e.tensor_tensor(out=T["tt_"][:], in0=T["tt_"][:], in1=omk1[:], op=ALU.add), reads=[T["tt_"], omk1], writes=[T["tt_"]])
                    P.op("dve", lambda e, k_=k_: e.tensor_tensor(out=T["kd"][:], in0=k_, in1=T["tt_"][:], op=ALU.mult), reads=[rw, T["tt_"]], writes=[T["kd"]])
                    P.op("dve", lambda e, kk_=kk_: e.tensor_tensor(out=T["beta"][:], in0=T["a_"][:], in1=kk_, op=ALU.mult), reads=[rw, T["a_"]], writes=[T["beta"]])
                    P.op("dve", lambda e, r_=r_: e.tensor_tensor(out=T["Rt"][:], in0=r_, in1=T["eI"][:], op=ALU.mult), reads=[rw, T["eI"]], writes=[T["Rt"]])
                    P.op("dve", lambda e: e.tensor_tensor(out=T["Kt"][:], in0=T["kd"][:], in1=T["eInv"][:], op=ALU.mult), reads=[T["kd"], T["eInv"]], writes=[T["Kt"]])
                    P.op("dve", lambda e: e.tensor_tensor(out=T["Bt"][:], in0=T["beta"][:], in1=T["eInv"][:], op=ALU.mult), reads=[T["beta"], T["eInv"]], writes=[T["Bt"]])
                    P.op("dve", lambda e, kk_=kk_: e.tensor_tensor(out=T["At"][:], in0=kk_, in1=T["eE"][:], op=ALU.mult), reads=[rw, T["eE"]], writes=[T["At"]])
                    P.op("dve", lambda e: e.tensor_tensor(out=T["Kh"][:], in0=T["kd"][:], in1=T["eR"][:], op=ALU.mult), reads=[T["kd"], T["eR"]], writes=[T["Kh"]])
                    P.op("dve", lambda e: e.scalar_tensor_tensor(out=T["nBh"][:], in0=T["beta"][:], scalar=-1.0, in1=T["eR"][:], op0=ALU.mult, op1=ALU.mult), reads=[T["beta"], T["eR"]], writes=[T["nBh"]])
                    P.op("dve", lambda e, r_=r_: e.tensor_tensor(out=T["tmp"][:], in0=r_, in1=T["kd"][:], op=ALU.mult), reads=[rw, T["kd"]], writes=[T["tmp"]])
                    P.op("dve", lambda e: e.tensor_tensor(out=T["tmp"][:], in0=T["tmp"][:], in1=kv2[:], op=ALU.mult), reads=[T["tmp"], kv2], writes=[T["tmp"]])
                    P.op("dve", lambda e, bd=bd: e.tensor_reduce(out=bd[:], in_=T["tmp"][:].rearrange("p (h d) -> p h d", h=8), axis=AX.X, op=ALU.add), reads=[T["tmp"]], writes=[bd])
                    for ki, kn in enumerate(("Kt", "Bt", "At", "Rt")):
                        src = T[kn]
                        for p in range(4):
                            P.op("pe", lambda e, src=src, p=p: e.transpose(out=pft[:, p, :], in_=src[:, p * 128:(p + 1) * 128], identity=ident[:]), reads=[src, ident], writes=[pft])
                        if ki % 2:
                            P.op("act", lambda e, FT=FT, ki=ki: e.activation(out=FT[:, ki, :, :], in_=pft[:], func=AF.Copy), reads=[pft], writes=[FT])
                        else:
                            P.op("dve", lambda e, FT=FT, ki=ki: e.tensor_copy(out=FT[:, ki, :, :], in_=pft[:]), reads=[pft], writes=[FT])
                    KI, BI, AI, RI = 0, 1, 2, 3
                    for p in range(4 if "rwB1" not in dbg else 0):
                        for hi in range(2):
                            ho = hi * 64
                            fK, fB, fA, fR = (FT[ho:ho + 64, i, p, :] for i in (KI, BI, AI, RI))
                            A0, At0, P0 = Acur[0], Atc[0], Pc[0]
                            g = gmm(fB, fA, [FT])
                            P.op("dve", lambda e, g=g, A0=A0, m_strict=m_strict: e.tensor_tensor(out=A0[:], in0=g[:], in1=m_strict, op=ALU.mult), reads=[g, masks], writes=[A0])
                            P.op("dve", lambda e, A0=A0, P0=P0: e.tensor_tensor(out=P0[:], in0=ident[:], in1=A0[:], op=ALU.subtract), reads=[ident, A0], writes=[P0])
                            g = gmm(fA, fB, [FT])
                            P.op("dve", lambda e, g=g, At0=At0, m_strictT=m_strictT: e.tensor_tensor(out=At0[:], in0=g[:], in1=m_strictT, op=ALU.mult), reads=[g, masks], writes=[At0])
                            g = gmm(fK, fA, [FT])
                            P.op("dve", lambda e, g=g, hi=hi, m_strict=m_strict: e.tensor_tensor(out=Mh[hi][:], in0=g[:], in1=m_strict, op=ALU.mult), reads=[g, masks], writes=[Mh[hi]])
                            if want_out:
                                g = gmm(fK, fR, [FT])
                                P.op("dve", lambda e, g=g, hi=hi, m_incl=m_incl: e.tensor_tensor(out=G3h[hi][:], in0=g[:], in1=m_incl, op=ALU.mult), reads=[g, masks], writes=[G3h[hi]])
                                g = gmm(fB, fR, [FT])
                                P.op("dve", lambda e, g=g, hi=hi, m_incl=m_incl: e.scalar_tensor_tensor(out=nG4h[hi][:], in0=g[:], scalar=-1.0, in1=m_incl, op0=ALU.mult, op1=ALU.mult),
                                     reads=[g, masks], writes=[nG4h[hi]])
                            ia = 0
                            for kq in range(1, 6):
                                Ap, Atp, Pp = Acur[ia], Atc[ia], Pc[ia]
                                An, Atn = Acur[1 - ia], Atc[1 - ia]
                                Pn = Pc[1 - ia] if kq < 5 else Th[hi]
                                g = gmm(Ap[:], Atp[:], [Ap, Atp])
                                evac_copy(Atn, g)
                                if kq < 5:
                                    g = gmm(Atp[:], Ap[:], [Ap, Atp])
                                    evac_copy(An, g)
                                g = gmm(Atn[:], Pp[:], [Atn, Pp])
                                P.op("dve", lambda e, g=g, Pn=Pn, Pp=Pp: e.tensor_tensor(out=Pn[:], in0=g[:], in1=Pp[:], op=ALU.add), reads=[g, Pp], writes=[Pn])
                                ia = 1 - ia
                        for c in (chunks if "rwB2" not in dbg else ()):
                            rc0 = 64 * c
                            for hi in range(2):
                                ho = hi * 64; h = 2 * p + hi
                                P.op("pe", lambda e, FT=FT, ST=ST, ho=ho, p=p, hi=hi: e.matmul(pxy[:, 0, hi, :], lhsT=FT[ho:ho + 64, AI, p, :], rhs=ST[ho:ho + 64, p, :], start=True, stop=False),
                                     reads=[FT, ST], writes=[pxy], rg=ho)
                                P.op("pe", lambda e, rw=rw, hi=hi, h=h, rc0=rc0: e.matmul(pxy[:, 0, hi, :], lhsT=Mh[hi][rc0:rc0 + 64, :], rhs=rw[rc0:rc0 + 64, 1024 + h * 64:1024 + (h + 1) * 64], start=False, stop=True),
                                     reads=[Mh[hi], rw], writes=[pxy], rg=rc0)
                            P.op("act", lambda e, XT=XT, rc0=rc0: e.activation(out=XT[rc0:rc0 + 64], in_=pxy[rc0:rc0 + 64, 0], func=AF.Copy), reads=[pxy], writes=[XT])
                            for hi in range(2):
                                P.op("pe", lambda e, XT=XT, hi=hi, rc0=rc0: e.matmul(pxy[:, 1, hi, :], lhsT=Th[hi][rc0:rc0 + 64, :], rhs=XT[rc0:rc0 + 64, hi, :], start=True, stop=True),
                                     reads=[Th[hi], XT], writes=[pxy], rg=rc0)
                            P.op("dve", lambda e, UT=UT, rc0=rc0: e.tensor_copy(out=UT[rc0:rc0 + 64], in_=pxy[rc0:rc0 + 64, 1]), reads=[pxy], writes=[UT])
                            if want_out:
                                for hi in range(2):
                                    ho = hi * 64; h = 2 * p + hi
                                    P.op("pe", lambda e, FT=FT, ST=ST, ho=ho, p=p, hi=hi: e.matmul(pxy[:, 2, hi, :], lhsT=FT[ho:ho + 64, RI, p, :], rhs=ST[ho:ho + 64, p, :], start=True, stop=False),
                                         reads=[FT, ST], writes=[pxy], rg=ho)
                                    P.op("pe", lambda e, rw=rw, hi=hi, h=h, rc0=rc0: e.matmul(pxy[:, 2, hi, :], lhsT=G3h[hi][rc0:rc0 + 64, :], rhs=rw[rc0:rc0 + 64, 1024 + h * 64:1024 + (h + 1) * 64], start=False, stop=False),
                                         reads=[G3h[hi], rw], writes=[pxy], rg=rc0)
                                    P.op("pe", lambda e, UT=UT, hi=hi, rc0=rc0: e.matmul(pxy[:, 2, hi, :], lhsT=nG4h[hi][rc0:rc0 + 64, :], rhs=UT[rc0:rc0 + 64, hi, :], start=False, stop=True),
                                         reads=[nG4h[hi], UT], writes=[pxy], rg=rc0)
                                P.op("act", lambda e, y=y, rc0=rc0, p=p: e.activation(out=y[rc0:rc0 + 64, p * 128:(p + 1) * 128], in_=pxy[rc0:rc0 + 64, 2].rearrange("p a d -> p (a d)"), func=AF.Copy),
                                     reads=[pxy], writes=[y])
                            P.op("pe", lambda e, rw=rw, p=p, rc0=rc0: e.matmul(pS[:], lhsT=T["Kh"][rc0:rc0 + 64, p * 128:(p + 1) * 128], rhs=rw[rc0:rc0 + 64, 1024 + p * 128:1024 + (p + 1) * 128], start=True, stop=False),
                                 reads=[T["Kh"], rw], writes=[pS])
                            P.op("pe", lambda e, UT=UT, p=p, rc0=rc0: e.matmul(pS[:], lhsT=T["nBh"][rc0:rc0 + 64, p * 128:(p + 1) * 128], rhs=UT[rc0:rc0 + 64].rearrange("p a d -> p (a d)"), start=False, stop=True),
                                 reads=[T["nBh"], UT], writes=[pS])
                            for hi in range(2):
                                ho = hi * 64
                                P.op("dve", lambda e, ST=ST, ho=ho, p=p, c=c, hi=hi: e.scalar_tensor_tensor(out=ST[ho:ho + 64, p, :], in0=ST[ho:ho + 64, p, :], scalar=WL[ho:ho + 64, p, c:c + 1],
                                                                                                     in1=pS[ho:ho + 64, hi * 64:(hi + 1) * 64], op0=ALU.mult, op1=ALU.add), reads=[ST, WL, pS], writes=[ST])
                    if not want_out:
                        continue
                    if dr == 0:
                        P.dma("act", YS[t * 128:(t + 1) * 128, 0:512], y[:], reads=[y], writes=[("YS", t)])
                        P.dma("act", YS[t * 128:(t + 1) * 128, 512:520], bd[:], reads=[bd], writes=[("YSb", t)])
                    else:
                        yf = d["yf"]
                        P.dma("sp", yf[:], YS[t * 128:(t + 1) * 128, :], reads=[("YS", t), ("YSb", t)], writes=[yf])
                        y3 = y[:].rearrange("p (h d) -> p h d", h=8)
                        P.op("dve", lambda e, y=y, yf=yf: e.tensor_tensor(out=y[:], in0=y[:], in1=yf[:, 0:512], op=ALU.add), reads=[y, yf], writes=[y])
                        P.op("dve", lambda e, bd=bd, yf=yf: e.tensor_tensor(out=bd[:], in0=bd[:], in1=yf[:, 512:520], op=ALU.add), reads=[bd, yf], writes=[bd])
                        P.op("dve", lambda e, sm=sm, y3=y3: e.tensor_reduce(out=sm[:, 0, :], in_=y3, axis=AX.X, op=ALU.add), reads=[y], writes=[sm])
                        P.op("dve", lambda e, sm=sm: e.tensor_scalar_mul(out=sm[:, 0, :], in0=sm[:, 0, :], scalar1=-1.0 / 64), reads=[sm], writes=[sm])
                        P.op("dve", lambda e, sm=sm, y3=y3: e.tensor_tensor(out=y3, in0=y3, in1=sm[:, 0, :].unsqueeze(2).to_broadcast([128, 8, 64]), op=ALU.add), reads=[y, sm], writes=[y])
                        P.op("act", lambda e, y=y: e.activation(out=T["tmp"][:], in_=y[:], func=AF.Square), reads=[y], writes=[T["tmp"]])
                        P.op("dve", lambda e, sm=sm: e.tensor_reduce(out=sm[:, 1, :], in_=T["tmp"][:].rearrange("p (h d) -> p h d", h=8), axis=AX.X, op=ALU.add), reads=[T["tmp"]], writes=[sm])
                        P.op("act", lambda e, sm=sm: e.activation(out=sm[:, 1, :], in_=sm[:, 1, :], func=AF.Sqrt, scale=1.0 / 64, bias=lneps[:, 0:1]), reads=[sm, lneps], writes=[sm])
                        P.op("dve", lambda e, sm=sm: e.reciprocal(out=sm[:, 1, :], in_=sm[:, 1, :]), reads=[sm], writes=[sm])
                        P.op("dve", lambda e, sm=sm, y3=y3: e.tensor_tensor(out=y3, in0=y3, in1=sm[:, 1, :].unsqueeze(2).to_broadcast([128, 8, 64]), op=ALU.mult), reads=[y, sm], writes=[y])
                        P.op("dve", lambda e, y=y: e.tensor_tensor(out=y[:], in0=y[:], in1=ln0[:], op=ALU.mult), reads=[y, ln0], writes=[y])
                        P.op("dve", lambda e, y=y: e.tensor_tensor(out=y[:], in0=y[:], in1=ln1[:], op=ALU.add), reads=[y, ln1], writes=[y])
                        P.op("dve", lambda e, rw=rw, bd=bd: e.tensor_tensor(out=T["tmp"][:].rearrange("p (h d) -> p h d", h=8), in0=rw[:, 1024:1536].rearrange("p (h d) -> p h d", h=8),
                                                                      in1=bd[:].unsqueeze(2).to_broadcast([128, 8, 64]), op=ALU.mult), reads=[rw, bd], writes=[T["tmp"]])
                        P.op("dve", lambda e, y=y: e.tensor_tensor(out=y[:], in0=y[:], in1=T["tmp"][:], op=ALU.add), reads=[y, T["tmp"]], writes=[y])
                        P.op("pe", lambda e, rw=rw: e.transpose(out=pft[:, 0, :], in_=rw[:, 1792:1920], identity=ident[:]), reads=[rw, ident], writes=[pft])
                        P.op("act", lambda e: e.activation(out=sgT[:], in_=pft[:, 0, :], func=AF.Sigmoid), reads=[pft], writes=[sgT])
                        P.op("pe", lambda e: e.matmul(pz[:], lhsT=sgT[:], rhs=g2s[:], start=True, stop=True), reads=[sgT, g2s], writes=[pz])
                        P.op("dve", lambda e, y=y: e.tensor_tensor(out=y[:], in0=y[:], in1=pz[:], op=ALU.mult), reads=[y, pz], writes=[y])
                        P.dma("act", O[t * 128:(t + 1) * 128, 0:512], y[:], reads=[y], writes=[("O", t, 0)])
        P.pop()

    for l in range(2):
        nin = NIN[l]
        w_in = io["w_in_even"] if l == 0 else io["w_in_odd"]
        Xl = io["xin"] if l == 0 else X1
        if "U_in" not in dbg:
            P.push()
            MT = 4
            xts = [P.sbuf(f"p1x{i}", [128, D]) for i in range(3)]
            junk = P.sbuf("p1junk", [128, D])
            ss = [P.sbuf(f"p1ss{i}", [128, 1]) for i in range(3)]
            rstd = [P.sbuf(f"p1rs{i}", [128, 1]) for i in range(3)]
            xn = [P.sbuf(f"p1xn{i}", [128, D]) for i in range(2)]
            hxT = [P.sbuf(f"p1hxT{i}", [128, 8, MT * 128]) for i in range(2)]
            ptr = [P.psum(f"p1ptr{i}", [128, 4, 128]) for i in range(2)]
            wstg = [P.sbuf(f"p1ws{i}", [128, 8, 512]) for i in range(2)]
            wr = [P.sbuf(f"p1wr{i}", [128, 8, 512]) for i in range(2)]
            pu = [P.psum(f"p1pu{i}", [128, 512]) for i in range(4)]
            ut = [P.sbuf(f"p1ut{i}", [128, 512]) for i in range(4)]
            nblk = (nin + 511) // 512
            nmac = (cfg.ntile + MT - 1) // MT
            cnt = {"ti": 0, "ei": 0}

            def p1_norm(m):
                tiles = list(range(m * MT, min((m + 1) * MT, cfg.ntile)))
                hx = hxT[m % 2]
                for jj, t in enumerate(tiles):
                    cls = cfg.cls(t)
                    ti = cnt["ti"]; cnt["ti"] += 1
                    xt = xts[ti % 3]; s_ = ss[ti % 3]; r_ = rstd[ti % 3]; xn_ = xn[ti % 2]
                    P.dma("sp", xt[:], Xl[t * 128:(t + 1) * 128, :], reads=[("X", l, t)], writes=[xt])
                    P.op("act", lambda e, xt=xt, s_=s_: e.activation(out=junk[:], in_=xt[:], func=AF.Square, accum_out=s_[:]),
                         reads=[xt], writes=[junk, s_])
                    P.op("act", lambda e, s_=s_, r_=r_: e.activation(out=r_[:], in_=s_[:], func=AF.Sqrt, scale=1.0 / D, bias=epsc[:, 0:1]),
                         reads=[s_, epsc], writes=[r_])
                    P.op("dve", lambda e, r_=r_: e.reciprocal(out=r_[:], in_=r_[:]), reads=[r_], writes=[r_])
                    P.op("dve", lambda e, xt=xt, r_=r_, xn_=xn_: e.tensor_scalar(out=xn_[:], in0=xt[:], scalar1=r_[:, 0:1], scalar2=None, op0=ALU.mult),
                         reads=[xt, r_], writes=[xn_])
                    for half in range(2):
                        pt_ = ptr[half]
                        for kk in range(4):
                            k = half * 4 + kk
                            P.op("pe", lambda e, pt_=pt_, kk=kk, k=k, xn_=xn_: e.transpose(out=pt_[:, kk, :], in_=xn_[:, k * 128:(k + 1) * 128], identity=ident[:]),
                                 reads=[xn_, ident], writes=[pt_])
                        for kk in range(4):
                            k = half * 4 + kk
                            P.op("act", lambda e, pt_=pt_, kk=kk, k=k, hx=hx, jj=jj, cls=cls, ab=AB[l]: e.activation(
                                out=hx[:, k, jj * 128:(jj + 1) * 128].bitcast(F32R), in_=pt_[:, kk, :], func=AF.Identity,
                                scale=ab[:, 0, k, cls:cls + 1], bias=ab[:, 1, k, cls:cls + 1]),
                                reads=[pt_, AB[l]], writes=[hx])

            blocks = [(m, nb) for m in range(nmac) for nb in range(nblk)]

            def p1_loadw(i):
                m, nb = blocks[i]
                n0 = nb * 512
                nw = min(512, nin - n0)
                ws_, wr_ = wstg[i % 2], wr[i % 2]
                P.dma("sp", ws_[:, :, 0:nw], w_in[:, n0:n0 + nw].rearrange("(k p) n -> p k n", p=128), writes=[ws_])
                P.op("dve", lambda e, ws_=ws_, wr_=wr_, nw=nw: e.tensor_copy(out=wr_[:, 0:4, 0:nw].bitcast(F32R), in_=ws_[:, 0:4, 0:nw]),
                     reads=[ws_], writes=[(wr_.name, "a")])
                P.op("act", lambda e, ws_=ws_, wr_=wr_, nw=nw: e.activation(out=wr_[:, 4:8, 0:nw].bitcast(F32R), in_=ws_[:, 4:8, 0:nw], func=AF.Copy),
                     reads=[ws_], writes=[(wr_.name, "b")])

            p1_norm(0)
            p1_loadw(0)
            for i, (m, nb) in enumerate(blocks):
                if nb == 0 and m + 1 < nmac:
                    p1_norm(m + 1)
                if i + 1 < len(blocks):
                    p1_loadw(i + 1)
                tiles = list(range(m * MT, min((m + 1) * MT, cfg.ntile)))
                hx = hxT[m % 2]
                n0 = nb * 512
                nw = min(512, nin - n0)
                wr_ = wr[i % 2]
                for jj, t in enumerate(tiles):
                    ei = cnt["ei"]; cnt["ei"] += 1
                    ps = pu[ei % 4]; u_ = ut[ei % 4]
                    for k in range(8):
                        P.op("pe", lambda e, ps=ps, k=k, hx=hx, jj=jj, wr_=wr_, nw=nw: e.matmul(
                            ps[:, 0:nw], lhsT=hx[:, k, jj * 128:(jj + 1) * 128].bitcast(F32R), rhs=wr_[:, k, 0:nw].bitcast(F32R),
                            start=(k == 0), stop=(k == 7)), reads=[hx, (wr_.name, 'a' if k < 4 else 'b')], writes=[ps])
                    if ei % 2:
                        P.op("act", lambda e, ps=ps, u_=u_, nw=nw: e.activation(out=u_[:, 0:nw], in_=ps[:, 0:nw], func=AF.Copy), reads=[ps], writes=[u_])
                    else:
                        P.op("dve", lambda e, ps=ps, u_=u_, nw=nw: e.tensor_copy(out=u_[:, 0:nw], in_=ps[:, 0:nw]), reads=[ps], writes=[u_])
                    P.dma("act", U[t * 128:(t + 1) * 128, n0:n0 + nw], u_[:, 0:nw], reads=[u_], writes=[("U", t, nb)])
            P.pop()
        if stop_after == f"P1_{l}":
            break
        if stop_after is None:
            if l == 0:
                mlstm(l); diffattn(l)
            else:
                rwkv(l); gla(l)
            p3(l, Xl)
        elif stop_after == f"P3_{l}":
            p3(l, Xl); break
        elif l == 0 and stop_after in ("mlstm", "diff"):
            (mlstm if stop_after == "mlstm" else diffattn)(l); break
        elif l == 1 and stop_after in ("rwkv", "gla"):
            (rwkv if stop_after == "rwkv" else gla)(l); break

    P.emit()
    P.close()
    return nc

import numpy as np
TC = 256
def rope_table(TL):
    n = 8
    inv = (10000.0 ** (-np.arange(n, dtype=np.float32) / n)).astype(np.float32)
    row = np.repeat(np.arange(TL // 64, dtype=np.float32), 64)
    col = np.tile(np.arange(64, dtype=np.float32), TL // 64)
    ang = np.concatenate([row[:, None] * inv, col[:, None] * inv], axis=-1).astype(np.float32)
    return np.concatenate([np.cos(ang), np.sin(ang)], axis=-1).astype(np.float32)

def make_masks():
    m = np.zeros((9, 128, 128), np.float32)
    i = np.arange(128)
    S, T = i[:, None], i[None, :]
    blk = (S // 64 == T // 64)
    m[0] = (S <= T); m[1] = (S >= T)
    m[2] = (S <= T) * blk; m[3] = (S >= T) * blk
    m[4] = (S > T); m[5] = (S < T)
    m[6] = (S < T) * blk; m[7] = (S > T) * blk
    m[8, :64, 0] = 1.0; m[8, 64:, 1] = 1.0
    return m

def core_inputs(inp, core, NB, TL):
    b0 = core * NB
    xs = []
    for b in range(b0, b0 + NB):
        xs.append(inp["ctx"][b]); xs.append(inp["x"][b][:TL])
    m = {}
    m["xin"] = np.ascontiguousarray(np.concatenate(xs, axis=0))
    m["cvec"] = np.ascontiguousarray(np.concatenate([inp["c"][b0:b0 + NB], inp["c_ctx"][None, :]], axis=0))
    for k in ("ada_w", "ada_b", "norm_g", "w_out", "w_mlp_in", "w_mlp_out"):
        m[k] = inp[k]
    m["w_in_even"] = inp["w_in_even"][0]; m["w_in_odd"] = inp["w_in_odd"][0]
    m["mlstm_gate_b"] = inp["mlstm_gate_b"].reshape(1, 32); m["mlstm_norm_g"] = inp["mlstm_norm_g"].reshape(1, 512)
    m["diff_qk_g"] = inp["diff_qk_g"][0]; m["diff_lam"] = inp["diff_lam"].reshape(1, 128); m["diff_subln_g"] = inp["diff_subln_g"].reshape(1, 64)
    m["rwkv_mu"] = inp["rwkv_mu"][0]; m["rwkv_w0"] = inp["rwkv_w0"][0]; m["rwkv_w2"] = inp["rwkv_w2"][0]
    m["rwkv_a0"] = inp["rwkv_a0"][0]; m["rwkv_a2"] = inp["rwkv_a2"][0]; m["rwkv_g2"] = inp["rwkv_g2"][0]
    m["rwkv_kvec"] = inp["rwkv_kvec"][0]; m["rwkv_ln"] = inp["rwkv_ln"][0]
    m["gla_gate_w2"] = inp["gla_gate_w2"][0]; m["gla_gate_b"] = inp["gla_gate_b"][0]; m["gla_norm_g"] = inp["gla_norm_g"].reshape(1, 128)
    m["ident"] = np.eye(128, dtype=np.float32)
    m["rope"] = rope_table(TL)
    m["masks"] = make_masks()
    return {k: np.ascontiguousarray(np.asarray(v, dtype=np.float32)) for k, v in m.items()}


def kernel(**inputs):
    from concourse.bass_utils import run_bass_kernel_spmd
    inp = {k: np.asarray(v) for k, v in inputs.items()}
    NB, TL = 2, 4096
    cfg = Cfg(NB=NB, TL=TL)
    nc = build(cfg)
    in_maps = [core_inputs(inp, core, NB, TL) for core in range(8)]
    res = run_bass_kernel_spmd(nc, in_maps, core_ids=list(range(8)))
    outs = [np.asarray(r["out"]).reshape(NB, TL, D) for r in res.results]
    return np.concatenate(outs, axis=0).astype(np.float32)
```
